# Optimizing a Trainium2 kernel written in Bass

```python
import math
import jax, jax.numpy as jnp
from jax import lax
import numpy as np

D_MODEL = 1024
BATCH = 32
SEQ = 256
DEPTH = 1
DEC_BATCH = 8
DEC_SEQ = 1024
PAST_LEN = 512

GRID_W = 64
D_RNN = 1024
N_RNN_HEADS = 4
RNN_HEAD_DIM = D_RNN // N_RNN_HEADS
RNN_CONV_W = 4
RNN_CONV_LEFT = 2
RGLRU_C = 8.0
D_HY = 1024
HY_ORDER = 2
HY_CONV_W = 3
HY_CONV_LEFT = 1
HY_EMB_BANDS = 16
HY_EMB_DIM = 1 + 2 * HY_EMB_BANDS
HY_FILTER_HIDDEN = 64
HY_DECAY_TARGET = 1e-2
HY_FAST_DECAY_PCT = 0.3
HY_SLOW_DECAY_PCT = 1.5
HY_MIN_DECAY = math.log(HY_DECAY_TARGET) / HY_SLOW_DECAY_PCT
HY_MAX_DECAY = math.log(HY_DECAY_TARGET) / HY_FAST_DECAY_PCT
D_IN = 2 * D_RNN + 3 * D_HY + 2 * D_MODEL
N_KEYS = 128
N_EXPERTS = N_KEYS * N_KEYS
PEER_HEADS = 8
PEER_KEY_DIM = 256
PEER_HALF = PEER_KEY_DIM // 2
PEER_TOPK = 16
PEER_TOKEN_BLOCK = 128
DN_ALPHA = (2.0 * DEPTH) ** 0.25
DN_BETA = (8.0 * DEPTH) ** -0.25
LN_EPS = 1e-5

kernel_name = "hybrid_rglru_hyena_peer_prefix_step"


def layer_norm(x, g, b):
    xf = x.astype(jnp.float32)
    mu = jnp.mean(xf, axis=-1, keepdims=True)
    var = jnp.mean(jnp.square(xf - mu), axis=-1, keepdims=True)
    return ((xf - mu) * lax.rsqrt(var + LN_EPS) * g.astype(jnp.float32) + b.astype(jnp.float32)).astype(x.dtype)


def depthwise_conv_centred(x, w, b, pad_left):
    k_w, ch = w.shape
    y = lax.conv_general_dilated(x, w[:, None, :].astype(x.dtype), window_strides=(1,),
                                 padding=[(pad_left, k_w - 1 - pad_left)],
                                 dimension_numbers=("NWC", "WIO", "NWC"), feature_group_count=ch)
    return y + b.astype(x.dtype)


def grid_pos_embed(n_tokens):
    rows = n_tokens // GRID_W
    t = jnp.arange(rows * GRID_W)
    r = (t // GRID_W).astype(jnp.float32)
    col = (t % GRID_W).astype(jnp.float32)
    quarter = D_MODEL // 4
    omega = 1.0 / (10000.0 ** (jnp.arange(quarter, dtype=jnp.float32) / quarter))
    er = r[:, None] * omega[None, :]
    ec = col[:, None] * omega[None, :]
    return jnp.concatenate([jnp.sin(er), jnp.cos(er), jnp.sin(ec), jnp.cos(ec)], axis=-1)


def linear_scan(a, u, h0, reverse):
    def combine(left, right):
        a_l, b_l = left
        a_r, b_r = right
        return a_l * a_r, a_r * b_l + b_r
    a_cum, b_cum = lax.associative_scan(combine, (a, u), axis=1, reverse=reverse)
    return a_cum * h0[:, None, :] + b_cum


def rglru_bidir(x, gate_w, gate_b, lam, h0):
    bsz, seq_len, _ = x.shape
    xh = x.reshape(bsz, seq_len, N_RNN_HEADS, RNN_HEAD_DIM)
    gates = jnp.einsum("blhi,dghij->dgblhj", xh, gate_w).reshape(2, 2, bsz, seq_len, D_RNN)
    gates = jax.nn.sigmoid((gates + gate_b[:, :, None, None, :]).astype(jnp.float32))
    r_gate, i_gate = gates[:, 0], gates[:, 1]
    log_a = -RGLRU_C * r_gate * jax.nn.softplus(-lam.astype(jnp.float32))[:, None, None, :]
    a = jnp.exp(log_a)
    u = jnp.sqrt(-jnp.expm1(2.0 * log_a)) * (i_gate * x.astype(jnp.float32)[None])
    h0f = h0.astype(jnp.float32)
    h_fwd = linear_scan(a[0], u[0], h0f[:, 0], reverse=False)
    h_bwd = linear_scan(a[1], u[1], h0f[:, 1], reverse=True)
    final_state = jnp.stack([h_fwd[:, -1], h_bwd[:, 0]], axis=1)
    return h_fwd + h_bwd, final_state


def hyena_filter_spectra(seq_len, w1, b1, w2, b2, w3, b3, sin_freq):
    f32 = jnp.float32
    t_idx = jnp.arange(seq_len, dtype=f32)
    t_norm = t_idx / max(seq_len - 1, 1)
    w = 2.0 * math.pi * t_idx / seq_len
    bands = jnp.linspace(1e-4, HY_EMB_BANDS - 1, HY_EMB_BANDS, dtype=f32)
    fw = w[:, None] * bands[None, :]
    z = jnp.concatenate([t_norm[:, None], jnp.cos(fw), -jnp.sin(fw)], axis=-1)
    freq = sin_freq.astype(f32)
    hid = jnp.sin(freq * (z @ w1.astype(f32) + b1.astype(f32)))
    hid = jnp.sin(freq * (hid @ w2.astype(f32) + b2.astype(f32)))
    filt = (hid @ w3.astype(f32) + b3.astype(f32)).reshape(seq_len, 2, HY_ORDER, D_HY)
    deltas = jnp.linspace(HY_MIN_DECAY, HY_MAX_DECAY, D_HY, dtype=f32)
    decay = jnp.exp(-t_norm[:, None] * jnp.abs(deltas)[None, :])
    filt = filt * decay[:, None, None, :]
    k = jnp.concatenate([filt[:, 0], jnp.zeros((1, HY_ORDER, D_HY), f32), jnp.flip(filt[1:, 1], axis=0)], axis=0)
    return jnp.fft.rfft(k, axis=0)


def fft_long_conv(u, kf, skip):
    seq_len = u.shape[1]
    y = jnp.fft.irfft(jnp.fft.rfft(u, n=2 * seq_len, axis=1) * kf[None], n=2 * seq_len, axis=1)[:, :seq_len]
    return y + u * skip


def hyena_branch(hy, p):
    hy = depthwise_conv_centred(hy, p["hy_conv_w"], p["hy_conv_b"], HY_CONV_LEFT).astype(jnp.float32)
    x1, x2, v = jnp.split(hy, 3, axis=-1)
    kf = hyena_filter_spectra(hy.shape[1], p["hy_ffn_w1"], p["hy_ffn_b1"], p["hy_ffn_w2"], p["hy_ffn_b2"],
                              p["hy_ffn_w3"], p["hy_ffn_b3"], p["hy_sin_freq"])
    skip = p["hy_skip"].astype(jnp.float32)
    z = x1 * fft_long_conv(v, kf[:, 0], skip[0])
    z = x2 * fft_long_conv(z, kf[:, 1], skip[1])
    return z


def token_mixer(h, h0, p):
    dt = h.dtype
    proj = h @ p["w_in"]
    rnn_x, rnn_g, hy, g_merge = jnp.split(proj, [D_RNN, 2 * D_RNN, 2 * D_RNN + 3 * D_HY], axis=-1)
    rnn_x = depthwise_conv_centred(rnn_x, p["rnn_conv_w"], p["rnn_conv_b"], RNN_CONV_LEFT)
    y_rnn, final_state = rglru_bidir(rnn_x, p["rnn_gate_w"], p["rnn_gate_b"], p["rnn_lambda"], h0)
    y_a = (y_rnn.astype(dt) * jax.nn.gelu(rnn_g)) @ p["w_branch_a"]
    y_b = hyena_branch(hy, p).astype(dt) @ p["w_branch_b"]
    g_a, g_b = jnp.split(jax.nn.sigmoid(g_merge), 2, axis=-1)
    return (g_a * y_a + g_b * y_b) @ p["w_out"], final_state


def peer(h, w_query, sub_keys, u_tab, v_tab):
    bsz, seq_len, dm = h.shape
    n_tok = bsz * seq_len
    xt = h.reshape(n_tok // PEER_TOKEN_BLOCK, PEER_TOKEN_BLOCK, dm)

    def block(xb):
        q = (xb @ w_query).reshape(PEER_TOKEN_BLOCK, PEER_HEADS, 2, PEER_HALF)
        s = jnp.einsum("thpk,pnk->thpn", q, sub_keys).astype(jnp.float32)
        top_s, top_i = lax.top_k(s, PEER_TOPK)
        cand_s = top_s[:, :, 0, :, None] + top_s[:, :, 1, None, :]
        cand_i = top_i[:, :, 0, :, None] * N_KEYS + top_i[:, :, 1, None, :]
        cand_s = cand_s.reshape(PEER_TOKEN_BLOCK, PEER_HEADS, PEER_TOPK * PEER_TOPK)
        cand_i = cand_i.reshape(PEER_TOKEN_BLOCK, PEER_HEADS, PEER_TOPK * PEER_TOPK)
        best_s, best_j = lax.top_k(cand_s, PEER_TOPK)
        idx = jnp.take_along_axis(cand_i, best_j, axis=-1)
        g = jax.nn.softmax(best_s, axis=-1)
        u = u_tab[idx]
        act = jax.nn.gelu(jnp.einsum("td,thkd->thk", xb, u).astype(jnp.float32))
        v = v_tab[idx]
        return jnp.einsum("thk,thkd->td", (g * act).astype(xb.dtype), v)

    return lax.map(block, xt).reshape(bsz, seq_len, dm)


def trunk_layer(x, cond, h0, p):
    mod = jax.nn.silu(cond) @ p["w_ada"] + p["b_ada"]
    sh1, sc1, g1, sh2, sc2, g2 = jnp.split(mod[:, None, :], 6, axis=-1)
    y, final_state = token_mixer(x * (1 + sc1) + sh1, h0, p)
    x = layer_norm(DN_ALPHA * x + g1 * y, p["ln1_g"], p["ln1_b"])
    y = peer(x * (1 + sc2) + sh2, p["peer_w_query"], p["peer_sub_keys"], p["peer_u"], p["peer_v"])
    x = layer_norm(DN_ALPHA * x + g2 * y, p["ln2_g"], p["ln2_b"])
    return x, final_state


def setup_inputs(seed: int = 0) -> dict:
    key = jax.random.key(seed)
    ks = iter(jax.random.split(key, 40))
    f32 = jnp.float32

    def nrm(shape, scale):
        return jax.random.normal(next(ks), shape, f32) * scale

    a0 = jax.random.uniform(next(ks), (DEPTH, 2, D_RNN), f32, 0.9, 0.999)
    s0 = a0 ** (1.0 / RGLRU_C)
    rnn_lambda = jnp.log(s0) - jnp.log1p(-s0)
    return {
        "x_prompt": nrm((BATCH, SEQ, D_MODEL), 1.0),
        "x_sample": nrm((DEC_BATCH, DEC_SEQ, D_MODEL), 1.0),
        "state_rglru": nrm((DEC_BATCH, DEPTH, 2, D_RNN), 0.5),
        "c": nrm((DEC_BATCH, D_MODEL), 1.0),
        "c_ctx": nrm((D_MODEL,), 1.0),
        "w_ada": nrm((DEPTH, D_MODEL, 6 * D_MODEL), 0.5 * D_MODEL ** -0.5),
        "b_ada": nrm((DEPTH, 6 * D_MODEL), 0.02),
        "w_in": nrm((DEPTH, D_MODEL, D_IN), D_MODEL ** -0.5),
        "rnn_conv_w": nrm((DEPTH, RNN_CONV_W, D_RNN), RNN_CONV_W ** -0.5),
        "rnn_conv_b": nrm((DEPTH, D_RNN), 0.02),
        "rnn_gate_w": nrm((DEPTH, 2, 2, N_RNN_HEADS, RNN_HEAD_DIM, RNN_HEAD_DIM), RNN_HEAD_DIM ** -0.5),
        "rnn_gate_b": nrm((DEPTH, 2, 2, D_RNN), 0.02),
        "rnn_lambda": rnn_lambda,
        "hy_conv_w": nrm((DEPTH, HY_CONV_W, 3 * D_HY), HY_CONV_W ** -0.5),
        "hy_conv_b": nrm((DEPTH, 3 * D_HY), 0.02),
        "hy_ffn_w1": nrm((DEPTH, HY_EMB_DIM, HY_FILTER_HIDDEN), HY_EMB_DIM ** -0.5),
        "hy_ffn_b1": nrm((DEPTH, HY_FILTER_HIDDEN), 0.1),
        "hy_ffn_w2": nrm((DEPTH, HY_FILTER_HIDDEN, HY_FILTER_HIDDEN), HY_FILTER_HIDDEN ** -0.5),
        "hy_ffn_b2": nrm((DEPTH, HY_FILTER_HIDDEN), 0.1),
        "hy_ffn_w3": nrm((DEPTH, HY_FILTER_HIDDEN, 2 * HY_ORDER * D_HY), 0.05 * HY_FILTER_HIDDEN ** -0.5),
        "hy_ffn_b3": nrm((DEPTH, 2 * HY_ORDER * D_HY), 0.005),
        "hy_sin_freq": 1.0 + nrm((DEPTH, HY_FILTER_HIDDEN), 0.05),
        "hy_skip": nrm((DEPTH, HY_ORDER, D_HY), 0.5),
        "w_branch_a": nrm((DEPTH, D_RNN, D_MODEL), D_RNN ** -0.5),
        "w_branch_b": nrm((DEPTH, D_HY, D_MODEL), D_HY ** -0.5),
        "w_out": nrm((DEPTH, D_MODEL, D_MODEL), DN_BETA * D_MODEL ** -0.5),
        "ln1_g": 1.0 + nrm((DEPTH, D_MODEL), 0.02),
        "ln1_b": nrm((DEPTH, D_MODEL), 0.02),
        "ln2_g": 1.0 + nrm((DEPTH, D_MODEL), 0.02),
        "ln2_b": nrm((DEPTH, D_MODEL), 0.02),
        "peer_w_query": nrm((DEPTH, D_MODEL, PEER_HEADS * PEER_KEY_DIM), D_MODEL ** -0.5),
        "peer_sub_keys": nrm((DEPTH, 2, N_KEYS, PEER_HALF), PEER_HALF ** -0.5),
        "peer_u": nrm((DEPTH, N_EXPERTS, D_MODEL), D_MODEL ** -0.5),
        "peer_v": nrm((DEPTH, N_EXPERTS, D_MODEL), DN_BETA * PEER_HEADS ** -0.5),
    }


def reference(x_prompt, x_sample, state_rglru, c, c_ctx, w_ada, b_ada, w_in, rnn_conv_w, rnn_conv_b,
              rnn_gate_w, rnn_gate_b, rnn_lambda, hy_conv_w, hy_conv_b, hy_ffn_w1, hy_ffn_b1, hy_ffn_w2,
              hy_ffn_b2, hy_ffn_w3, hy_ffn_b3, hy_sin_freq, hy_skip, w_branch_a, w_branch_b, w_out,
              ln1_g, ln1_b, ln2_g, ln2_b, peer_w_query, peer_sub_keys, peer_u, peer_v):
    def params_of(l):
        return {
            "w_ada": w_ada[l], "b_ada": b_ada[l], "w_in": w_in[l],
            "rnn_conv_w": rnn_conv_w[l], "rnn_conv_b": rnn_conv_b[l],
            "rnn_gate_w": rnn_gate_w[l], "rnn_gate_b": rnn_gate_b[l], "rnn_lambda": rnn_lambda[l],
            "hy_conv_w": hy_conv_w[l], "hy_conv_b": hy_conv_b[l],
            "hy_ffn_w1": hy_ffn_w1[l], "hy_ffn_b1": hy_ffn_b1[l], "hy_ffn_w2": hy_ffn_w2[l],
            "hy_ffn_b2": hy_ffn_b2[l], "hy_ffn_w3": hy_ffn_w3[l], "hy_ffn_b3": hy_ffn_b3[l],
            "hy_sin_freq": hy_sin_freq[l], "hy_skip": hy_skip[l],
            "w_branch_a": w_branch_a[l], "w_branch_b": w_branch_b[l], "w_out": w_out[l],
            "ln1_g": ln1_g[l], "ln1_b": ln1_b[l], "ln2_g": ln2_g[l], "ln2_b": ln2_b[l],
            "peer_w_query": peer_w_query[l], "peer_sub_keys": peer_sub_keys[l],
            "peer_u": peer_u[l], "peer_v": peer_v[l],
        }

    xp = x_prompt
    cond_ctx = jnp.broadcast_to(c_ctx, (x_prompt.shape[0], D_MODEL))
    ctx_states = []
    for l in range(DEPTH):
        h0 = jnp.zeros((x_prompt.shape[0], 2, D_RNN), jnp.float32)
        xp, st = trunk_layer(xp, cond_ctx, h0, params_of(l))
        ctx_states.append(st.astype(x_prompt.dtype))
    new_state_rglru = jnp.stack(ctx_states, axis=1)

    xs = x_sample + grid_pos_embed(x_sample.shape[1]).astype(x_sample.dtype)[None]
    for l in range(DEPTH):
        xs, _ = trunk_layer(xs, c, state_rglru[:, l], params_of(l))

    return (xp, xs, new_state_rglru)
```

```python
import math
import os
HYSKIP = os.environ.get('HYSKIP', '')
PEER_SK = int(os.environ.get('PEER_SK', '2'))
PEER_FS = int(os.environ.get('PEER_FS', '2'))
PEER_GS = int(os.environ.get('PEER_GS', '4'))
from contextlib import ExitStack

import ml_dtypes
import numpy as np

import concourse.bass as bass
import concourse.mybir as mybir
from concourse.bass_utils import run_bass_kernel_spmd

F32 = mybir.dt.float32
BF16 = mybir.dt.bfloat16
I32 = mybir.dt.int32
U32 = mybir.dt.uint32
AF = mybir.ActivationFunctionType
ALU = mybir.AluOpType
AX = mybir.AxisListType

ENGS = ["sync", "scalar", "vector", "gpsimd", "tensor"]
DBGOPS = []
NOSYNC_ENGS = set(os.environ.get('NOSYNC', '').split(',')) - {''}
N_CORES = 8
DM = 1024
ALPHA = 2.0 ** 0.25
LN_EPS = 1e-5
RGLRU_C = 8.0


class Buf:
    __slots__ = ("name", "last_w", "readers")

    def __init__(self, name=""):
        self.name = name
        self.last_w = None
        self.readers = {}


class Op:
    __slots__ = ("eng", "fn", "deps", "key", "pos", "is_dma", "sig", "val", "waits",
                 "nosame", "vc_issue", "vc_done")


class Sched:
    def __init__(self, nc, dma_slots=8, same_engine_sync=True):
        self.nc = nc
        self.ops = []
        self.per_eng = {e: [] for e in ENGS}
        self.npos = {}
        self.dma_slots = dma_slots
        self.dma_count = {e: 0 for e in ENGS}
        self.slot_last = {}
        self.same_engine_sync = same_engine_sync
        self.enabled = True

    def _new(self, eng, fn, dma, nosame):
        o = Op()
        o.eng = eng
        o.fn = fn
        o.is_dma = dma
        o.sig = dma
        o.nosame = nosame
        return o

    def op(self, eng, fn, reads=(), writes=(), dma=False, nosame=False):
        if not self.enabled:
            return None
        o = self._new(eng, fn, dma, nosame)
        deps = []
        for b in reads:
            if b.last_w is not None:
                deps.append(b.last_w)
        for b in writes:
            if b.last_w is not None:
                deps.append(b.last_w)
            deps.extend(b.readers.values())
        if dma:
            slot = self.dma_count[eng] % self.dma_slots
            self.dma_count[eng] += 1
            o.key = ("dma", eng, slot)
            prev = self.slot_last.get(o.key)
            if prev is not None:
                deps.append(prev)
            self.slot_last[o.key] = o
        else:
            o.key = eng
        o.pos = self.npos.get(o.key, 0)
        self.npos[o.key] = o.pos + 1
        o.deps = [d for d in deps if d is not o]
        rk = o.key if not dma else ("dmaop", id(o))
        for b in reads:
            b.readers[rk] = o
        for b in writes:
            b.last_w = o
            b.readers = {}
        self.ops.append(o)
        self.per_eng[eng].append(o)
        return o

    def dma(self, eng, out, in_, reads=(), writes=(), **kw):
        return self.op(eng, lambda e: e.dma_start(out=out, in_=in_, **kw), reads, writes, dma=True)

    def barrier(self):
        if not self.enabled:
            return
        lasts = [self.per_eng[e][-1] for e in ENGS if self.per_eng[e]]
        lasts = [o for o in lasts if o.fn is not None]
        lasts += list(self.slot_last.values())
        for e in ENGS:
            o = self._new(e, None, False, False)
            o.key = e
            o.pos = self.npos.get(e, 0)
            self.npos[e] = o.pos + 1
            o.deps = list(lasts)
            self.ops.append(o)
            self.per_eng[e].append(o)

    def finalize(self):
        last_on_eng = {}
        for o in self.ops:
            vc = {}
            prev = last_on_eng.get(o.eng)
            if prev is not None:
                vc.update(prev.vc_issue)
            waits = []
            best = {}
            for d in o.deps:
                if d.key not in best or best[d.key].pos < d.pos:
                    best[d.key] = d
            for k, d in best.items():
                if (not d.is_dma) and (not o.is_dma) and d.eng == o.eng and (
                        o.nosame or not self.same_engine_sync or o.eng in NOSYNC_ENGS):
                    continue
                if vc.get(k, -1) >= d.pos:
                    continue
                waits.append(d)
                d.sig = True
                for kk, vv in d.vc_done.items():
                    if vc.get(kk, -1) < vv:
                        vc[kk] = vv
            o.waits = waits
            o.vc_issue = vc
            vd = dict(vc)
            if o.fn is not None:
                vd[o.key] = o.pos
            o.vc_done = vd
            last_on_eng[o.eng] = o
        cnt = {}
        for o in self.ops:
            if o.is_dma:
                o.val = 16 * (o.pos + 1)
            elif o.sig:
                cnt[o.key] = cnt.get(o.key, 0) + 1
                o.val = cnt[o.key]

    def emit(self, stack):
        nc = self.nc
        fin = self._new("sync", None, False, False)
        fin.key = "sync"
        fin.pos = self.npos.get("sync", 0)
        fin.deps = list(self.slot_last.values())
        self.ops.append(fin)
        self.per_eng["sync"].append(fin)
        self.finalize()
        sems = {}
        for e in ENGS:
            sems[e] = stack.enter_context(nc.semaphore("s_" + e))
        for k in self.slot_last.keys():
            sems[k] = stack.enter_context(nc.semaphore("d_%s_%d" % (k[1], k[2])))
        per_eng = self.per_eng

        def run(engname, eng):
            for o in per_eng[engname]:
                for d in o.waits:
                    eng.wait_ge(sems[d.key], d.val)
                if o.fn is None:
                    continue
                inst = o.fn(eng)
                if o.sig:
                    inst.then_inc(sems[o.key], 16 if o.is_dma else 1)

        with nc.Block() as block:
            @block.sync
            def _(e):
                run("sync", e)

            @block.scalar
            def _(e):
                run("scalar", e)

            @block.vector
            def _(e):
                run("vector", e)

            @block.gpsimd
            def _(e):
                run("gpsimd", e)

            @block.tensor
            def _(e):
                run("tensor", e)


class Tl:
    def __init__(self, t, nbuf=1, name=""):
        self.t = t
        self.bs = [Buf(name + str(i)) for i in range(nbuf)]
        self.b = self.bs[0]


class Ring:
    def __init__(self, tiles):
        self.tiles = tiles
        self.i = 0

    def next(self):
        t = self.tiles[self.i % len(self.tiles)]
        self.i += 1
        return t


def build(debug=(), stop=None):
    nc = bass.Bass("TRN2", target_bir_lowering=False)
    S = Sched(nc)
    debug = set(debug)
    dbg_names = []

    def din(name, shape, dt=F32):
        return nc.dram_tensor(name, list(shape), dt, kind="ExternalInput").ap()

    def dout(name, shape, dt=F32):
        return nc.dram_tensor(name, list(shape), dt, kind="ExternalOutput").ap()

    xin = [din("xp", [1024, DM]), din("xs", [1024, DM])]
    pos_d = din("pos", [1024, DM])
    condT_d = din("condT", [128, 8, 2])
    h0T_d = din("h0T", [128, 2, 8])
    w_ada_d = din("w_ada", [DM, 6 * DM])
    b_adaT_d = din("b_adaT", [128, 2, 8])
    b_ada_rows_d = din("b_ada_rows", [4, DM])
    w_in_d = din("w_in", [DM, 7168])
    rcw_d = din("rcw", [128, 8, 4])
    rcb_d = din("rcb", [128, 8])
    gate_w_d = din("gate_w", [2, 2, 4, 256, 256])
    gbT_d = din("gbT", [128, 2, 2, 8])
    lamT_d = din("lamT", [128, 2, 8])
    hcw_d = din("hcw", [128, 24, 3])
    hcb_d = din("hcb", [128, 24])
    hy_w1_d = din("hy_w1", [33, 64])
    hy_b1_d = din("hy_b1", [64, 1])
    hy_w2_d = din("hy_w2", [64, 64])
    hy_b2_d = din("hy_b2", [64, 1])
    hy_freq_d = din("hy_freq", [64, 1])
    hy_w3_d = din("hy_w3", [64, 4096])
    hy_b3_d = din("hy_b3", [4096])
    skipT_d = din("skipT", [128, 2, 8])
    w_a_d = din("w_a", [DM, DM])
    w_b_d = din("w_b", [DM, DM])
    w_o_d = din("w_o", [DM, DM])
    ln_d = [din("ln1_g", [DM]), din("ln1_b", [DM]), din("ln2_g", [DM]), din("ln2_b", [DM])]
    wq_d = din("wq", [DM, 2048])
    skT_d = din("skT", [128, 2, 128])
    pu_d = din("peer_u", [16384, DM])
    pv_d = din("peer_v", [16384, DM])
    identf_d = din("identf", [128, 128])
    identb_d = din("identb", [128, 128], BF16)
    zT_d = [din("zT256", [33, 256]), din("zT1024", [33, 1024])]
    dec_d = [din("dec256", [256, DM]), din("dec1024", [1024, DM])]
    C_d = [din("C256", [256, 256], BF16), din("C1024", [1024, 1024], BF16)]
    Sm_d = [din("S256", [256, 256], BF16), din("S1024", [1024, 1024], BF16)]
    ST_d = [din("ST256", [256, 256], BF16), din("ST1024", [1024, 1024], BF16)]
    wtab_d = [din("wtab256", [128, 4, 2]), din("wtab1024", [128, 4, 8])]
    mask0_d = din("mask0", [128, 1])

    yout = [dout("yp", [1024, DM]), dout("ys", [1024, DM])]
    ns_d = dout("ns", [64, 128])
    modrow_d = nc.dram_tensor("modrow", [2, 4, DM], F32, kind="Internal").ap()
    modrow_b = Buf("modrow")
    uv_d = nc.dram_tensor("uv_bf16", [16384, 2 * DM], BF16, kind="Internal").ap()
    uv_b = Buf("uv")
    x2_d = nc.dram_tensor("x2_scratch", [2, 1024, DM], F32, kind="Internal").ap()
    x2_db = [[Buf("x2d") for _ in range(8)] for _ in range(2)]

    top = ExitStack()
    with top:
        uid = [0]

        def alloc(scope, name, shape, dt=F32, nbuf=1):
            uid[0] += 1
            t = scope.enter_context(nc.sbuf_tensor("s%d_%s" % (uid[0], name), list(shape), dt))
            return Tl(t, nbuf, name)

        def ring(scope, name, shape, dt, n):
            return Ring([alloc(scope, "%s_%d" % (name, i), shape, dt) for i in range(n)])

        _pb = [Tl(top.enter_context(nc.psum_tensor("pb%d" % i, [128, 512], F32)), 1, "pb%d" % i) for i in range(6)]
        pbanks6 = Ring(_pb)
        pbanks4 = Ring(_pb[:4])
        pacc = _pb[4:6]
        pbanks = pbanks6
        pbf = Ring([Tl(top.enter_context(nc.psum_tensor("pbf%d" % i, [128, 1024], BF16)), 1, "pbf%d" % i)
                    for i in range(2)])

        def dbg(name, tl, ap, dt=F32):
            if name not in debug:
                return
            o = dout("dbg_" + name, list(ap.shape), dt)
            dbg_names.append("dbg_" + name)
            S.dma("sync", o, ap, reads=tl.bs)

        def E(eng, meth, reads, writes, nosame=False, **kw):
            return S.op(eng, lambda e: getattr(e, meth)(**kw), reads, writes, nosame=nosame)

        def MM(out, lhsT, rhs, start, stop, reads, writes):
            return S.op("tensor", lambda e: e.matmul(out, lhsT=lhsT, rhs=rhs, start=start, stop=stop),
                        reads, writes, nosame=True)

        def GATHER(out, table, idx, reads, writes):
            return S.op("gpsimd", lambda e: e.indirect_dma_start(
                out=out, out_offset=None, in_=table, in_offset=bass.IndirectOffsetOnAxis(ap=idx, axis=0)),
                reads, writes, dma=True)

        def TR(out, in_, ident, reads, writes):
            return S.op("tensor", lambda e: e.transpose(out=out, in_=in_, identity=ident),
                        reads, writes, nosame=True)

        def load(eng, tl, dram_ap, sb_ap=None, extra_reads=()):
            S.dma(eng, sb_ap if sb_ap is not None else tl.t[:], dram_ap, reads=list(extra_reads), writes=tl.bs)

        for cq in range(4):
            r0, r1 = cq * 4096, (cq + 1) * 4096
            S.dma("gpsimd", uv_d[r0:r1, 0:DM], pu_d[r0:r1, :], writes=[uv_b])
            S.dma("gpsimd", uv_d[r0:r1, DM:2 * DM], pv_d[r0:r1, :], writes=[uv_b])
        identf = alloc(top, "identf", [128, 128]); load("sync", identf, identf_d)
        identb = alloc(top, "identb", [128, 128], BF16); load("sync", identb, identb_d)
        mask0 = alloc(top, "mask0", [128, 1]); load("sync", mask0, mask0_d)
        epsb = alloc(top, "epsb", [128, 1])
        E("vector", "memset", [], epsb.bs, ap=epsb.t[:], constant=LN_EPS)
        rcw = alloc(top, "rcw", [128, 8, 4]); load("sync", rcw, rcw_d)
        rcb = alloc(top, "rcb", [128, 8]); load("sync", rcb, rcb_d)
        gbT = alloc(top, "gbT", [128, 2, 2, 8]); load("sync", gbT, gbT_d)
        lamT = alloc(top, "lamT", [128, 2, 8]); load("sync", lamT, lamT_d)
        h0T = alloc(top, "h0T", [128, 2, 8]); load("sync", h0T, h0T_d)
        hcw = alloc(top, "hcw", [128, 24, 3]); load("sync", hcw, hcw_d)
        hcb = alloc(top, "hcb", [128, 24]); load("sync", hcb, hcb_d)
        skipT = alloc(top, "skipT", [128, 2, 8]); load("sync", skipT, skipT_d)
        skT = alloc(top, "skT", [128, 2, 128]); load("sync", skT, skT_d)
        nsp = alloc(top, "nsp", [128, 2, 8])
        E("scalar", "activation", lamT.bs, nsp.bs, out=nsp.t[:], in_=lamT.t[:], func=AF.Exp, scale=-1.0)
        E("scalar", "activation", nsp.bs, nsp.bs, out=nsp.t[:], in_=nsp.t[:], func=AF.Ln, bias=1.0, scale=1.0)
        E("vector", "tensor_scalar", nsp.bs, nsp.bs, out=nsp.t[:], in0=nsp.t[:], scalar1=-RGLRU_C, scalar2=None,
          op0=ALU.mult)
        modT = alloc(top, "modT", [128, 2, 8, 2])
        nst = alloc(top, "nst", [128, 4, 2, 8])

        with ExitStack() as ph:
            condT = alloc(ph, "condT", [128, 8, 2]); load("sync", condT, condT_d)
            condS = alloc(ph, "condS", [128, 8, 2])
            E("scalar", "activation", condT.bs, condS.bs, out=condS.t[:], in_=condT.t[:], func=AF.Silu)
            b_adaT = alloc(ph, "b_adaT", [128, 2, 8]); load("sync", b_adaT, b_adaT_d)
            brow = alloc(ph, "brow", [1, 4, DM])
            load("sync", brow, b_ada_rows_d.rearrange("(o a) n -> o a n", o=1))
            wa_ring = ring(ph, "wa", [128, 8, DM], F32, 2)
            rowt = ring(ph, "rowt", [1, DM], F32, 2)
            for ty in range(6):
                wa = wa_ring.next()
                load("sync", wa, w_ada_d[:, ty * DM:(ty + 1) * DM].rearrange("(kc p) n -> p kc n", p=128))
                if ty < 2:
                    pb = pbanks.next()
                    for m in range(8):
                        for kc in range(8):
                            MM(pb.t[:, m * 2:m * 2 + 2], wa.t[:, kc, m * 128:(m + 1) * 128], condS.t[:, kc, :],
                               kc == 0, kc == 7, wa.bs + condS.bs, pb.bs)
                    E("vector", "tensor_tensor", pb.bs + b_adaT.bs, modT.bs,
                      out=modT.t[:, ty], in0=pb.t[:, 0:16].rearrange("p (m j) -> p m j", j=2),
                      in1=b_adaT.t[:, ty].unsqueeze(2).to_broadcast([128, 8, 2]), op=ALU.add)
                    if ty == 1:
                        E("vector", "tensor_scalar", modT.bs, modT.bs, out=modT.t[:, 1], in0=modT.t[:, 1],
                          scalar1=1.0, scalar2=None, op0=ALU.add)
                else:
                    for j in range(2):
                        rt = rowt.next()
                        for hf in range(2):
                            pb = pbanks.next()
                            for kc in range(8):
                                MM(pb.t[0:1, :], condS.t[:, kc, j:j + 1], wa.t[:, kc, hf * 512:(hf + 1) * 512],
                                   kc == 0, kc == 7, wa.bs + condS.bs, pb.bs)
                            E("vector", "tensor_tensor", pb.bs + brow.bs, rt.bs,
                              out=rt.t[0:1, hf * 512:(hf + 1) * 512], in0=pb.t[0:1, :],
                              in1=brow.t[0:1, ty - 2, hf * 512:(hf + 1) * 512], op=ALU.add)
                        if ty == 4:
                            E("vector", "tensor_scalar", rt.bs, rt.bs, out=rt.t[:], in0=rt.t[:], scalar1=1.0,
                              scalar2=None, op0=ALU.add)
                        S.dma("sync", modrow_d[j, ty - 2:ty - 1, :], rt.t[0:1, :], reads=rt.bs, writes=[modrow_b])
            S.barrier()

        def load_row(tl, dram_row, extra=()):
            S.dma("sync", tl.t[:], dram_row.partition_broadcast(128), reads=list(extra), writes=tl.bs)

        def layer_norm_tile(scope_tiles, xt, g_row, b_row, eng2="gpsimd"):
            stats, mv, rstd = scope_tiles
            E("vector", "bn_stats", xt.bs, stats.bs, out=stats.t[:, 0, :], in_=xt.t[:, 0:512])
            E("vector", "bn_stats", xt.bs, stats.bs, out=stats.t[:, 1, :], in_=xt.t[:, 512:1024])
            E("vector", "bn_aggr", stats.bs, mv.bs, out=mv.t[:], in_=stats.t[:].rearrange("p a b -> p (a b)"))
            E("scalar", "activation", mv.bs + epsb.bs, rstd.bs, out=rstd.t[:], in_=mv.t[:, 1:2], func=AF.Sqrt,
              bias=epsb.t[:], scale=1.0)
            E("vector", "reciprocal", rstd.bs, rstd.bs, out=rstd.t[:], in_=rstd.t[:])
            E("vector", "tensor_scalar", xt.bs + mv.bs + rstd.bs, xt.bs, out=xt.t[:], in0=xt.t[:],
              scalar1=mv.t[:, 0:1], scalar2=rstd.t[:], op0=ALU.subtract, op1=ALU.mult)
            E(eng2, "tensor_tensor", xt.bs + g_row.bs, xt.bs, out=xt.t[:], in0=xt.t[:], in1=g_row.t[:], op=ALU.mult)
            E(eng2, "tensor_tensor", xt.bs + b_row.bs, xt.bs, out=xt.t[:], in0=xt.t[:], in1=b_row.t[:], op=ALU.add)

        def conv(eng, out_tl, out_ap, in_tl, in_ap, w_tl, w_ap, b_ap, ntap, left, nseq, L):
            o3 = out_ap.rearrange("p (s l) -> p s l", s=nseq)
            i3 = in_ap.rearrange("p (s l) -> p s l", s=nseq)
            E(eng, "tensor_scalar", in_tl.bs + w_tl[0].bs + w_tl[1].bs, out_tl.bs, out=out_ap, in0=in_ap,
              scalar1=w_ap[:, left:left + 1], scalar2=b_ap, op0=ALU.mult, op1=ALU.add)
            for j in range(ntap):
                o = j - left
                if o == 0:
                    continue
                lo_out = max(0, -o)
                hi_out = L - max(0, o)
                E(eng, "scalar_tensor_tensor", in_tl.bs + out_tl.bs + w_tl[0].bs, out_tl.bs,
                  out=o3[:, :, lo_out:hi_out], in0=i3[:, :, lo_out + o:hi_out + o], scalar=w_ap[:, j:j + 1],
                  in1=o3[:, :, lo_out:hi_out], op0=ALU.mult, op1=ALU.add)

        for pi in range(2):
            nseq, L = (4, 256) if pi == 0 else (1, 1024)
            pbanks = pbanks6
            ntc = L // 128
            x_d = xin[pi]
            with ExitStack() as pp:
              with ExitStack() as mxs:
                h1T = alloc(mxs, "h1T", [128, 8, 1024], BF16)
                ybT = alloc(mxs, "ybT", [128, 8, 1024], BF16, nbuf=8)

                with ExitStack() as ph:
                    xr = ring(ph, "xt", [128, DM], F32, 2)
                    pr = ring(ph, "pt", [128, DM], F32, 2)
                    for i in range(8):
                        xt = xr.next()
                        load("sync", xt, x_d[i * 128:(i + 1) * 128, :])
                        if pi == 1:
                            pt = pr.next()
                            load("gpsimd", pt, pos_d[i * 128:(i + 1) * 128, :])
                            E("vector", "tensor_tensor", xt.bs + pt.bs, xt.bs, out=xt.t[:], in0=xt.t[:], in1=pt.t[:],
                              op=ALU.add)
                        for hf in range(2):
                            pb = pbanks.next()
                            for cc in range(4):
                                c = hf * 4 + cc
                                TR(pb.t[:, cc * 128:(cc + 1) * 128], xt.t[:, c * 128:(c + 1) * 128], identf.t[:],
                                   xt.bs + identf.bs, pb.bs)
                            for cc in range(4):
                                c = hf * 4 + cc
                                E("scalar", "activation", pb.bs + modT.bs, h1T.bs,
                                  out=h1T.t[:, c, i * 128:(i + 1) * 128], in_=pb.t[:, cc * 128:(cc + 1) * 128],
                                  func=AF.Identity, bias=modT.t[:, 0, c, pi:pi + 1], scale=modT.t[:, 1, c, pi:pi + 1])
                    dbg("h1T%d" % pi, h1T, h1T.t[:], BF16)
                    S.barrier()
                    if stop == (pi, 1): S.enabled = False

                with ExitStack() as ph:
                    li = pi
                    Cm = alloc(ph, "Cm", [128, ntc, L], BF16)
                    Sm = alloc(ph, "Sm", [128, ntc, L], BF16)
                    STm = alloc(ph, "STm", [128, ntc, L], BF16)
                    load("sync", Cm, C_d[li].rearrange("(tc p) f -> p tc f", p=128))
                    load("sync", Sm, Sm_d[li].rearrange("(tc p) f -> p tc f", p=128))
                    load("sync", STm, ST_d[li].rearrange("(tc p) f -> p tc f", p=128))
                    wtab = alloc(ph, "wtab", [128, 4, ntc]); load("sync", wtab, wtab_d[li])
                    hid2 = alloc(ph, "hid2", [64, L])
                    with ExitStack() as sub:
                        zT = alloc(sub, "zT", [33, L]); load("sync", zT, zT_d[li])
                        w1 = alloc(sub, "hw1", [33, 64]); load("sync", w1, hy_w1_d)
                        w2 = alloc(sub, "hw2", [64, 64]); load("sync", w2, hy_w2_d)
                        hb1 = alloc(sub, "hb1", [64, 1]); load("sync", hb1, hy_b1_d)
                        hb2 = alloc(sub, "hb2", [64, 1]); load("sync", hb2, hy_b2_d)
                        hfr = alloc(sub, "hfr", [64, 1]); load("sync", hfr, hy_freq_d)
                        fb = alloc(sub, "fb", [64, 2])
                        E("vector", "tensor_tensor", hb1.bs + hfr.bs, fb.bs, out=fb.t[:, 0:1], in0=hb1.t[:], in1=hfr.t[:],
                          op=ALU.mult)
                        E("vector", "tensor_tensor", hb2.bs + hfr.bs, fb.bs, out=fb.t[:, 1:2], in0=hb2.t[:], in1=hfr.t[:],
                          op=ALU.mult)
                        hid = [alloc(sub, "hid1", [64, L]), hid2]
                        sarg = alloc(sub, "sarg", [64, L])
                        sint = alloc(sub, "sint", [64, L], I32)
                        sflt = alloc(sub, "sflt", [64, L])
                        for layer in range(2):
                            src = zT if layer == 0 else hid[0]
                            wl = w1 if layer == 0 else w2
                            kdim = 33 if layer == 0 else 64
                            blk = min(L, 512)
                            for b0 in range(0, L, blk):
                                pb = pbanks.next()
                                MM(pb.t[0:64, 0:blk], wl.t[0:kdim, :], src.t[0:kdim, b0:b0 + blk], True, True,
                                   wl.bs + src.bs, pb.bs)
                                E("vector", "tensor_scalar", pb.bs + hfr.bs + fb.bs, sarg.bs, out=sarg.t[:, b0:b0 + blk],
                                  in0=pb.t[0:64, 0:blk], scalar1=hfr.t[:], scalar2=fb.t[:, layer:layer + 1],
                                  op0=ALU.mult, op1=ALU.add)
                            E("vector", "tensor_scalar", sarg.bs, sarg.bs, out=sarg.t[:], in0=sarg.t[:],
                              scalar1=float(1.0 / (2 * math.pi)), scalar2=8.0, op0=ALU.mult, op1=ALU.add)
                            E("vector", "tensor_copy", sarg.bs, sint.bs, out=sint.t[:], in_=sarg.t[:])
                            E("vector", "tensor_copy", sint.bs, sflt.bs, out=sflt.t[:], in_=sint.t[:])
                            E("vector", "tensor_tensor", sarg.bs + sflt.bs, sarg.bs, out=sarg.t[:], in0=sarg.t[:],
                              in1=sflt.t[:], op=ALU.subtract)
                            E("vector", "tensor_single_scalar", sarg.bs, sflt.bs, out=sflt.t[:], in_=sarg.t[:], scalar=0.5,
                              op=ALU.is_gt)
                            E("vector", "tensor_tensor", sarg.bs + sflt.bs, sarg.bs, out=sarg.t[:], in0=sarg.t[:],
                              in1=sflt.t[:], op=ALU.subtract)
                            E("scalar", "activation", sarg.bs, hid[layer].bs, out=hid[layer].t[:], in_=sarg.t[:],
                              func=AF.Sin, scale=float(2 * math.pi))
                        S.barrier()
                        dbg("hid2_%d" % pi, hid2, hid2.t[:])
                        if stop == (pi, 1.5): S.enabled = False

                    w3c_r = ring(ph, "w3c", [64, 4, 128], F32, 2)
                    b3c_r = ring(ph, "b3c", [128, 4, 128], F32, 2)
                    dec_r = ring(ph, "decf", [128, ntc, 128], F32, 2)
                    kf = alloc(ph, "kf", [128, 4, 128])
                    kff = alloc(ph, "kff", [128, 2, 128])
                    kfb = alloc(ph, "kfb", [128, 2, 128])
                    kpm = alloc(ph, "kpm", [128, ntc, 2, 2, 128], BF16)
                    TA = alloc(ph, "TA", [128, ntc, 2, 128])
                    TB = alloc(ph, "TB", [128, ntc, 2, 128])
                    TD0 = alloc(ph, "TD0", [128, 2, 128])
                    tmpd = alloc(ph, "tmpd", [128, 2, 128])
                    wg32_r = ring(ph, "wg32", [128, 8, 3, 128], F32, 1)
                    wgb_r = ring(ph, "wgb", [128, 8, 3, 128], BF16, 2)
                    hpre = alloc(ph, "hpre", [128, 3, 1024], BF16, nbuf=3)
                    hc = alloc(ph, "hc", [128, 3, 1024], F32, nbuf=3)
                    wb = alloc(ph, "wb", [128, 1024], BF16)
                    wT = alloc(ph, "wT", [128, 8, 128], BF16)
                    tt_r = [alloc(ph, "tt%d" % q, [128, 512]) for q in range(4)]
                    ysz = 2 * ntc * 128 if pi == 1 else 2 * 2 * 4 * 128
                    YT = alloc(ph, "YT", [128, ysz], BF16)
                    tmpz = alloc(ph, "tmpz", [128, 1024])
                    z1 = alloc(ph, "z1", [128, 1024])

                    w_in_v = w_in_d.rearrange("(kc p) n -> p kc n", p=128)
                    w3_v = hy_w3_d.rearrange("k (q n) -> k q n", q=4)
                    b3_v = hy_b3_d.rearrange("(q n) -> q n", q=4)
                    dec_v = dec_d[li].rearrange("(tc p) n -> p tc n", p=128)

                    for c in range(8):
                        w3c = w3c_r.next(); load("sync", w3c, w3_v[:, :, c * 128:(c + 1) * 128])
                        b3c = b3c_r.next()
                        S.dma("sync", b3c.t[:], b3_v[:, c * 128:(c + 1) * 128].partition_broadcast(128), writes=b3c.bs)
                        decf = dec_r.next(); load("sync", decf, dec_v[:, :, c * 128:(c + 1) * 128])
                        wg32 = wg32_r.next()
                        for q in range(3):
                            S.dma("sync", wg32.t[:, :, q, :],
                                  w_in_v[:, :, 2048 + q * 1024 + c * 128:2048 + q * 1024 + (c + 1) * 128],
                                  writes=wg32.bs)
                        wgb = wgb_r.next()
                        E("scalar", "copy", wg32.bs, wgb.bs, out=wgb.t[:].rearrange("p a b c -> p (a b c)"), in_=wg32.t[:].rearrange("p a b c -> p (a b c)"))
                        if c == 0:
                            dbg("b3c%d" % pi, b3c, b3c.t[:]); dbg("wgb%d" % pi, wgb, wgb.t[:], BF16)
                            if stop == (pi, 1.55): S.enabled = False
                        for tc in range(ntc):
                            pb = pbanks.next()
                            MM(pb.t[:, :], hid2.t[:, tc * 128:(tc + 1) * 128], w3c.t[:].rearrange("k q n -> k (q n)"),
                               True, True, hid2.bs + w3c.bs, pb.bs)
                            E("vector", "tensor_tensor", pb.bs + b3c.bs, kf.bs, out=kf.t[:].rearrange("p q n -> p (q n)"),
                              in0=pb.t[:, :], in1=b3c.t[:].rearrange("p q n -> p (q n)"), op=ALU.add)
                            dbc = decf.t[:, tc, :].unsqueeze(1).to_broadcast([128, 2, 128])
                            E("gpsimd", "tensor_tensor", kf.bs + decf.bs, kff.bs, out=kff.t[:], in0=kf.t[:, 0:2, :], in1=dbc,
                              op=ALU.mult)
                            E("gpsimd", "tensor_tensor", kf.bs + decf.bs, kfb.bs, out=kfb.t[:], in0=kf.t[:, 2:4, :], in1=dbc,
                              op=ALU.mult)
                            if tc == 0:
                                E("vector", "tensor_scalar", kfb.bs + mask0.bs, kfb.bs, out=kfb.t[:], in0=kfb.t[:],
                                  scalar1=mask0.t[:], scalar2=None, op0=ALU.mult)
                            E("gpsimd", "tensor_tensor", kff.bs + kfb.bs, kpm.bs, out=kpm.t[:, tc, 0, :, :], in0=kff.t[:],
                              in1=kfb.t[:], op=ALU.add)
                            E("gpsimd", "tensor_tensor", kff.bs + kfb.bs, kpm.bs, out=kpm.t[:, tc, 1, :, :], in0=kff.t[:],
                              in1=kfb.t[:], op=ALU.subtract)
                        if c == 0:
                            dbg("kpm%d" % pi, kpm, kpm.t[:], BF16)
                            if stop == (pi, 1.57): S.enabled = False
                        for fc in range(ntc):
                            pa = pbanks.next()
                            for tc in range(ntc):
                                MM(pa.t[:, 0:256], Cm.t[:, tc, fc * 128:(fc + 1) * 128], kpm.t[:, tc, 0, :, :].rearrange("p o n -> p (o n)"),
                                   tc == 0, tc == ntc - 1, Cm.bs + kpm.bs, pa.bs)
                            for tc in range(ntc):
                                MM(pa.t[:, 256:512], Sm.t[:, tc, fc * 128:(fc + 1) * 128], kpm.t[:, tc, 1, :, :].rearrange("p o n -> p (o n)"),
                                   tc == 0, tc == ntc - 1, Sm.bs + kpm.bs, pa.bs)
                            if 'A' not in HYSKIP: E("scalar", "activation", pa.bs + wtab.bs, TA.bs, out=TA.t[:, fc].rearrange("p o n -> p (o n)"),
                              in_=pa.t[:, 0:256], func=AF.Identity,
                              scale=wtab.t[:, 0, fc:fc + 1])
                            if 'B' not in HYSKIP: E("scalar", "activation", pa.bs + wtab.bs, TB.bs, out=TB.t[:, fc].rearrange("p o n -> p (o n)"),
                              in_=pa.t[:, 256:512], func=AF.Identity,
                              scale=wtab.t[:, 1, fc:fc + 1])
                            if fc == 0 and 'D' not in HYSKIP:
                                pd = pbanks.next()
                                for tc in range(ntc):
                                    MM(pd.t[:, 0:256], Sm.t[:, tc, 0:128], kpm.t[:, tc, 0, :, :].rearrange("p o n -> p (o n)"),
                                       tc == 0, tc == ntc - 1, Sm.bs + kpm.bs, pd.bs)
                                E("scalar", "activation", pa.bs + wtab.bs, TD0.bs, out=TD0.t[:].rearrange("p o n -> p (o n)"),
                                  in_=pa.t[:, 0:256], func=AF.Identity, scale=wtab.t[:, 2, 0:1])
                                E("scalar", "activation", pd.bs + wtab.bs, tmpd.bs, out=tmpd.t[:].rearrange("p o n -> p (o n)"),
                                  in_=pd.t[:, 0:256], func=AF.Identity, scale=wtab.t[:, 3, 0:1])
                                E("gpsimd", "tensor_tensor", TD0.bs + tmpd.bs, TD0.bs, out=TD0.t[:], in0=TD0.t[:],
                                  in1=tmpd.t[:], op=ALU.add)
                        if c == 0:
                            dbg("TA%d" % pi, TA, TA.t[:]); dbg("TB%d" % pi, TB, TB.t[:]); dbg("TD0_%d" % pi, TD0, TD0.t[:])
                            if stop == (pi, 1.6): S.enabled = False
                        for q in range(3):
                            for blk in range(2):
                                pb = pbanks.next()
                                for kc in range(8):
                                    MM(pb.t[:, :], wgb.t[:, kc, q, :], h1T.t[:, kc, blk * 512:(blk + 1) * 512],
                                       kc == 0, kc == 7, wgb.bs + h1T.bs, pb.bs)
                                E("scalar", "copy", pb.bs, [hpre.bs[q]], out=hpre.t[:, q, blk * 512:(blk + 1) * 512],
                                  in_=pb.t[:, :])
                            cc = q * 8 + c
                            o_tl = Tl(None); o_tl.bs = [hc.bs[q]]
                            i_tl = Tl(None); i_tl.bs = [hpre.bs[q]]
                            conv("vector", o_tl, hc.t[:, q, :], i_tl, hpre.t[:, q, :],
                                 (hcw, hcb), hcw.t[:, cc, :], hcb.t[:, cc:cc + 1], 3, 1, nseq, L)
                        if c == 0:
                            dbg("hc%d" % pi, hc, hc.t[:])
                            if stop == (pi, 1.7): S.enabled = False
                        for od in range(2):
                            if od == 0:
                                w_ap, w_bs = hc.t[:, 2, :], [hc.bs[2]]
                                x_ap, x_bs = hc.t[:, 0, :], [hc.bs[0]]
                            else:
                                w_ap, w_bs = z1.t[:], z1.bs
                                x_ap, x_bs = hc.t[:, 1, :], [hc.bs[1]]
                            E("scalar", "copy", w_bs, wb.bs, out=wb.t[:], in_=w_ap)
                            pt = pbf.next()
                            for tt in range(8):
                                slot = (tt % 2) * 4 + tt // 2 if pi == 0 else tt
                                TR(pt.t[:, slot * 128:(slot + 1) * 128], wb.t[:, tt * 128:(tt + 1) * 128], identb.t[:],
                                   wb.bs + identb.bs, pt.bs)
                            E("vector", "tensor_copy", pt.bs, wT.bs, out=wT.t[:].rearrange("p a n -> p (a n)"), in_=pt.t[:, :])
                            if pi == 0:
                                YTv = YT.t[:].rearrange("p (fc r s n) -> p fc r s n", fc=2, r=2, s=4)
                                wTv = wT.t[:].rearrange("p (tc s) n -> p tc (s n)", s=4)
                                groups = [(fc, 1) for fc in range(2)]
                            else:
                                YTv = YT.t[:].rearrange("p (fc r n) -> p fc r n", fc=ntc, r=2)
                                groups = [(0, 4), (4, 4)]
                            for (f0, nf) in groups:
                                pre = pbanks.next()
                                pim = pbanks.next()
                                if pi == 0:
                                    fc = f0
                                    for (pbk, Mx) in ((pre, Cm), (pim, Sm)):
                                        for tc in range(ntc):
                                            MM(pbk.t[:, :], Mx.t[:, tc, fc * 128:(fc + 1) * 128], wTv[:, tc, :],
                                               tc == 0, tc == ntc - 1, Mx.bs + wT.bs, pbk.bs)
                                    ta = TA.t[:, fc, od, :].unsqueeze(1).to_broadcast([128, 4, 128])
                                    tb_ = TB.t[:, fc, od, :].unsqueeze(1).to_broadcast([128, 4, 128])
                                    if fc == 0:
                                        td = TD0.t[:, od, :].unsqueeze(1).to_broadcast([128, 4, 128])
                                        td_bs = TD0.bs
                                    else:
                                        td = ta
                                        td_bs = TA.bs
                                    ure = pre.t[:, :].rearrange("p (s n) -> p s n", s=4)
                                    uim = pim.t[:, :].rearrange("p (s n) -> p s n", s=4)
                                    yre = YTv[:, fc, 0, :, :]
                                    yim = YTv[:, fc, 1, :, :]
                                    shp = "p (s n) -> p s n"
                                    tv = [t_.t[:, :].rearrange(shp, s=4) for t_ in tt_r]
                                    E("vector", "tensor_tensor", pre.bs + TA.bs, tt_r[0].bs, out=tv[0], in0=ure, in1=ta, op=ALU.mult)
                                    E("vector", "tensor_tensor", pim.bs + TB.bs, tt_r[1].bs, out=tv[1], in0=uim, in1=tb_, op=ALU.mult)
                                    E("vector", "tensor_tensor", pre.bs + TB.bs, tt_r[2].bs, out=tv[2], in0=ure, in1=tb_, op=ALU.mult)
                                    E("vector", "tensor_tensor", pim.bs + td_bs, tt_r[3].bs, out=tv[3], in0=uim, in1=td, op=ALU.mult)
                                    E("gpsimd", "tensor_tensor", tt_r[0].bs + tt_r[1].bs, YT.bs, out=yre, in0=tv[0], in1=tv[1], op=ALU.subtract)
                                    E("gpsimd", "tensor_tensor", tt_r[2].bs + tt_r[3].bs, YT.bs, out=yim, in0=tv[2], in1=tv[3], op=ALU.add)
                                else:
                                    for ff in range(nf):
                                        fc = f0 + ff
                                        for (pbk, Mx) in ((pre, Cm), (pim, Sm)):
                                            for tc in range(ntc):
                                                MM(pbk.t[:, ff * 128:(ff + 1) * 128], Mx.t[:, tc, fc * 128:(fc + 1) * 128],
                                                   wT.t[:, tc, :], tc == 0, tc == ntc - 1, Mx.bs + wT.bs, pbk.bs)
                                    ta = TA.t[:, f0:f0 + nf, od, :]
                                    tb_ = TB.t[:, f0:f0 + nf, od, :]
                                    ure = pre.t[:, :].rearrange("p (s n) -> p s n", s=4)
                                    uim = pim.t[:, :].rearrange("p (s n) -> p s n", s=4)
                                    yre = YTv[:, f0:f0 + nf, 0, :]
                                    yim = YTv[:, f0:f0 + nf, 1, :]
                                    tv = [t_.t[:, :].rearrange("p (s n) -> p s n", s=4) for t_ in tt_r]
                                    E("vector", "tensor_tensor", pre.bs + TA.bs, tt_r[0].bs, out=tv[0], in0=ure, in1=ta, op=ALU.mult)
                                    E("vector", "tensor_tensor", pim.bs + TB.bs, tt_r[1].bs, out=tv[1], in0=uim, in1=tb_, op=ALU.mult)
                                    E("vector", "tensor_tensor", pre.bs + TB.bs, tt_r[2].bs, out=tv[2], in0=ure, in1=tb_, op=ALU.mult)
                                    E("vector", "tensor_tensor", pim.bs + TA.bs, tt_r[3].bs, out=tv[3], in0=uim, in1=ta, op=ALU.mult)
                                    if f0 == 0:
                                        E("vector", "tensor_tensor", pim.bs + TD0.bs, tt_r[3].bs, out=tt_r[3].t[:, 0:128],
                                          in0=pim.t[:, 0:128], in1=TD0.t[:, od, :], op=ALU.mult)
                                    E("gpsimd", "tensor_tensor", tt_r[0].bs + tt_r[1].bs, YT.bs, out=yre, in0=tv[0], in1=tv[1], op=ALU.subtract)
                                    E("gpsimd", "tensor_tensor", tt_r[2].bs + tt_r[3].bs, YT.bs, out=yim, in0=tv[2], in1=tv[3], op=ALU.add)
                            sk = skipT.t[:, od, c:c + 1]
                            if od == 0:
                                o_ap_full, o_bs = z1.t[:], z1.bs
                            else:
                                o_ap_full, o_bs = ybT.t[:, c, :], [ybT.bs[c]]
                            for blk in range(2):
                                pb = pbanks.next()
                                if pi == 0:
                                    for s2 in range(2):
                                        sq = blk * 2 + s2
                                        k = 0
                                        for fc in range(2):
                                            for r, Mx in ((0, Cm), (1, STm)):
                                                MM(pb.t[:, s2 * 256:(s2 + 1) * 256], YTv[:, fc, r, sq, :], Mx.t[:, fc, :],
                                                   k == 0, k == 3, YT.bs + Mx.bs, pb.bs)
                                                k += 1
                                else:
                                    k = 0
                                    for fc in range(ntc):
                                        for r, Mx in ((0, Cm), (1, STm)):
                                            MM(pb.t[:, :], YTv[:, fc, r, :], Mx.t[:, fc, blk * 512:(blk + 1) * 512],
                                               k == 0, k == 2 * ntc - 1, YT.bs + Mx.bs, pb.bs)
                                            k += 1
                                sl = slice(blk * 512, (blk + 1) * 512)
                                E("vector", "scalar_tensor_tensor", w_bs + skipT.bs + pb.bs, tmpz.bs, out=tmpz.t[:, sl],
                                  in0=w_ap[:, sl], scalar=sk, in1=pb.t[:, :], op0=ALU.mult, op1=ALU.add)
                                E("gpsimd", "tensor_tensor", tmpz.bs + x_bs, o_bs, out=o_ap_full[:, sl], in0=tmpz.t[:, sl],
                                  in1=x_ap[:, sl], op=ALU.mult)
                    dbg("ybT%d" % pi, ybT, ybT.t[:], BF16)
                    S.barrier()
                    if stop == (pi, 2): S.enabled = False

                yaT = alloc(mxs, "yaT", [128, 8, 1024], BF16, nbuf=8)
                with ExitStack() as ph:
                    w32_r = ring(ph, "rw32", [128, 8, 256], F32, 2)
                    wbf_r = ring(ph, "rwbf", [128, 8, 256], BF16, 2)
                    gw32 = alloc(ph, "gw32", [128, 4, 2, 256], F32)
                    gwb = alloc(ph, "gwb", [128, 4, 2, 256], BF16)
                    xpre = alloc(ph, "xpre", [128, 2, 1024], F32, nbuf=2)
                    xc = alloc(ph, "xc", [128, 2, 1024], F32, nbuf=2)
                    xcb = alloc(ph, "xcb", [128, 2, 1024], BF16)
                    rg = alloc(ph, "rg", [128, 1024]); ig = alloc(ph, "ig", [128, 1024])
                    at = alloc(ph, "at", [128, 1024]); st_ = alloc(ph, "st", [128, 1024]); ut = alloc(ph, "ut", [128, 1024])
                    hdir = [alloc(ph, "hf", [128, 1024]), alloc(ph, "hb", [128, 1024])]
                    glu = alloc(ph, "glu", [128, 2, 1024], F32, nbuf=2)
                    w_in_v = w_in_d.rearrange("(kc p) n -> p kc n", p=128)
                    for hd in range(4):
                        wx32 = w32_r.next(); load("sync", wx32, w_in_v[:, :, hd * 256:(hd + 1) * 256])
                        wxb = wbf_r.next(); E("vector", "tensor_copy", wx32.bs, wxb.bs, out=wxb.t[:], in_=wx32.t[:])
                        wg32 = w32_r.next(); load("sync", wg32, w_in_v[:, :, 1024 + hd * 256:1024 + (hd + 1) * 256])
                        wgb2 = wbf_r.next(); E("vector", "tensor_copy", wg32.bs, wgb2.bs, out=wgb2.t[:], in_=wg32.t[:])
                        for d in range(2):
                            for g in range(2):
                                S.dma("sync", gw32.t[:, d * 2 + g, :, :],
                                      gate_w_d[d, g, hd].rearrange("(ic p) j -> p ic j", p=128), writes=gw32.bs)
                        E("vector", "tensor_copy", gw32.bs, gwb.bs, out=gwb.t[:], in_=gw32.t[:])
                        for jc in range(2):
                            for blk in range(2):
                                sl = slice(blk * 512, (blk + 1) * 512)
                                pb = pbanks.next()
                                for kc in range(8):
                                    MM(pb.t[:, :], wxb.t[:, kc, jc * 128:(jc + 1) * 128], h1T.t[:, kc, sl],
                                       kc == 0, kc == 7, wxb.bs + h1T.bs, pb.bs)
                                E("scalar", "copy", pb.bs, [xpre.bs[jc]], out=xpre.t[:, jc, sl], in_=pb.t[:, :])
                                pb = pbanks.next()
                                for kc in range(8):
                                    MM(pb.t[:, :], wgb2.t[:, kc, jc * 128:(jc + 1) * 128], h1T.t[:, kc, sl],
                                       kc == 0, kc == 7, wgb2.bs + h1T.bs, pb.bs)
                                E("scalar", "activation", pb.bs, [glu.bs[jc]], out=glu.t[:, jc, sl], in_=pb.t[:, :],
                                  func=AF.Gelu_apprx_tanh)
                            ch = hd * 2 + jc
                            o_tl = Tl(None); o_tl.bs = [xc.bs[jc]]
                            i_tl = Tl(None); i_tl.bs = [xpre.bs[jc]]
                            conv("vector", o_tl, xc.t[:, jc, :], i_tl, xpre.t[:, jc, :], (rcw, rcb), rcw.t[:, ch, :],
                                 rcb.t[:, ch:ch + 1], 4, 2, nseq, L)
                            E("scalar", "copy", [xc.bs[jc]], xcb.bs, out=xcb.t[:, jc, :], in_=xc.t[:, jc, :])
                        for jc in range(2):
                            ch = hd * 2 + jc
                            for d in range(2):
                                for g in range(2):
                                    dst = rg if g == 0 else ig
                                    for blk in range(2):
                                        sl = slice(blk * 512, (blk + 1) * 512)
                                        pb = pbanks.next()
                                        for ic in range(2):
                                            MM(pb.t[:, :], gwb.t[:, d * 2 + g, ic, jc * 128:(jc + 1) * 128], xcb.t[:, ic, sl],
                                               ic == 0, ic == 1, gwb.bs + xcb.bs, pb.bs)
                                        E("scalar", "activation", pb.bs + gbT.bs, dst.bs, out=dst.t[:, sl], in_=pb.t[:, :],
                                          func=AF.Sigmoid, bias=gbT.t[:, d, g, ch:ch + 1], scale=1.0)
                                E("scalar", "activation", rg.bs + nsp.bs, at.bs, out=at.t[:], in_=rg.t[:], func=AF.Exp,
                                  scale=nsp.t[:, d, ch:ch + 1])
                                E("gpsimd", "tensor_tensor", at.bs, st_.bs, out=st_.t[:], in0=at.t[:], in1=at.t[:], op=ALU.mult)
                                E("scalar", "activation", st_.bs, st_.bs, out=st_.t[:], in_=st_.t[:], func=AF.Sqrt,
                                  bias=1.0, scale=-1.0)
                                E("gpsimd", "tensor_tensor", ig.bs + [xc.bs[jc]], ig.bs, out=ig.t[:], in0=ig.t[:],
                                  in1=xc.t[:, jc, :], op=ALU.mult)
                                E("vector", "tensor_tensor", ig.bs + st_.bs, ut.bs, out=ut.t[:], in0=ig.t[:], in1=st_.t[:],
                                  op=ALU.mult)
                                hd_t = hdir[d]
                                for sq in range(nseq):
                                    lo, hi = sq * L, (sq + 1) * L
                                    init = h0T.t[:, d, ch:ch + 1] if pi == 1 else 0.0
                                    if d == 0:
                                        o_, a_, u_ = hd_t.t[:, lo:hi], at.t[:, lo:hi], ut.t[:, lo:hi]
                                    else:
                                        o_, a_, u_ = hd_t.t[:, lo:hi][:, ::-1], at.t[:, lo:hi][:, ::-1], ut.t[:, lo:hi][:, ::-1]
                                    E("vector", "tensor_tensor_scan", at.bs + ut.bs + h0T.bs, hd_t.bs, out=o_, data0=a_,
                                      data1=u_, initial=init, op0=ALU.mult, op1=ALU.add)
                                if pi == 0:
                                    hv = hd_t.t[:].rearrange("p (s l) -> p s l", s=4)
                                    col = L - 1 if d == 0 else 0
                                    E("gpsimd", "tensor_copy", hd_t.bs, nst.bs, out=nst.t[:, :, d, ch:ch + 1],
                                      in_=hv[:, :, col:col + 1])
                            E("vector", "tensor_tensor", hdir[0].bs + hdir[1].bs, hdir[0].bs, out=hdir[0].t[:], in0=hdir[0].t[:],
                              in1=hdir[1].t[:], op=ALU.add)
                            E("gpsimd", "tensor_tensor", hdir[0].bs + [glu.bs[jc]], [yaT.bs[ch]], out=yaT.t[:, ch, :],
                              in0=hdir[0].t[:], in1=glu.t[:, jc, :], op=ALU.mult)
                    dbg("yaT%d" % pi, yaT, yaT.t[:], BF16)
                    if pi == 0:
                        dbg("nst", nst, nst.t[:])
                    S.barrier()
                    if stop == (pi, 3): S.enabled = False

                mT = alloc(mxs, "mT", [128, 8, 1024], BF16, nbuf=8)
                with ExitStack() as ph:
                    w32_r = ring(ph, "mw32", [128, 8, 256], F32, 3)
                    wbf_r = ring(ph, "mwbf", [128, 8, 256], BF16, 8)
                    gat = ring(ph, "gat", [128, 512], F32, 2)
                    gbt = ring(ph, "gbt", [128, 512], F32, 2)
                    mat = ring(ph, "mat", [128, 512], F32, 2)
                    w_in_v = w_in_d.rearrange("(kc p) n -> p kc n", p=128)
                    wa_v = w_a_d.rearrange("(kc p) n -> p kc n", p=128)
                    wb_v = w_b_d.rearrange("(kc p) n -> p kc n", p=128)
                    for cg in range(4):
                        ws = []
                        for src in (wa_v[:, :, cg * 256:(cg + 1) * 256], wb_v[:, :, cg * 256:(cg + 1) * 256],
                                    w_in_v[:, :, 5120 + cg * 256:5120 + (cg + 1) * 256],
                                    w_in_v[:, :, 6144 + cg * 256:6144 + (cg + 1) * 256]):
                            w32 = w32_r.next(); load("sync", w32, src)
                            wbf = wbf_r.next(); E("scalar", "copy", w32.bs, wbf.bs, out=wbf.t[:].rearrange("p a b -> p (a b)"), in_=w32.t[:].rearrange("p a b -> p (a b)"))
                            ws.append(wbf)
                        for jc in range(2):
                            c = cg * 2 + jc
                            cs = slice(jc * 128, (jc + 1) * 128)
                            for blk in range(2):
                                sl = slice(blk * 512, (blk + 1) * 512)
                                pya = pbanks.next()
                                for kc in range(8):
                                    MM(pya.t[:, :], ws[0].t[:, kc, cs], yaT.t[:, kc, sl], kc == 0, kc == 7, ws[0].bs + yaT.bs, pya.bs)
                                pyb = pbanks.next()
                                for kc in range(8):
                                    MM(pyb.t[:, :], ws[1].t[:, kc, cs], ybT.t[:, kc, sl], kc == 0, kc == 7, ws[1].bs + ybT.bs, pyb.bs)
                                pga = pbanks.next()
                                for kc in range(8):
                                    MM(pga.t[:, :], ws[2].t[:, kc, cs], h1T.t[:, kc, sl], kc == 0, kc == 7, ws[2].bs + h1T.bs, pga.bs)
                                pgb = pbanks.next()
                                for kc in range(8):
                                    MM(pgb.t[:, :], ws[3].t[:, kc, cs], h1T.t[:, kc, sl], kc == 0, kc == 7, ws[3].bs + h1T.bs, pgb.bs)
                                ga = gat.next(); gb = gbt.next(); ma = mat.next()
                                E("scalar", "activation", pga.bs, ga.bs, out=ga.t[:], in_=pga.t[:, :], func=AF.Sigmoid)
                                E("scalar", "activation", pgb.bs, gb.bs, out=gb.t[:], in_=pgb.t[:, :], func=AF.Sigmoid)
                                E("vector", "tensor_tensor", pya.bs + ga.bs, ma.bs, out=ma.t[:], in0=pya.t[:, :], in1=ga.t[:], op=ALU.mult)
                                E("vector", "tensor_tensor", pyb.bs + gb.bs, gb.bs, out=gb.t[:], in0=pyb.t[:, :], in1=gb.t[:], op=ALU.mult)
                                E("gpsimd", "tensor_tensor", ma.bs + gb.bs, [mT.bs[c]], out=mT.t[:, c, sl], in0=ma.t[:], in1=gb.t[:], op=ALU.add)
                    dbg("mT%d" % pi, mT, mT.t[:], BF16)
                    S.barrier()
                    if stop == (pi, 4): S.enabled = False

                with ExitStack() as ph:
                    wo = alloc(ph, "wo", [128, 8, DM], BF16, nbuf=4)
                    w32_r = ring(ph, "ow32", [128, 8, 256], F32, 2)
                    wo_v = w_o_d.rearrange("(kc p) n -> p kc n", p=128)
                    for cg in range(4):
                        w32 = w32_r.next(); load("sync", w32, wo_v[:, :, cg * 256:(cg + 1) * 256])
                        E("vector", "tensor_copy", w32.bs, [wo.bs[cg]], out=wo.t[:, :, cg * 256:(cg + 1) * 256], in_=w32.t[:])
                    g1r = alloc(ph, "g1r", [128, DM]); load_row(g1r, modrow_d[pi, 0, :], [modrow_b])
                    lg = alloc(ph, "lg", [128, DM]); load_row(lg, ln_d[0])
                    lb = alloc(ph, "lb", [128, DM]); load_row(lb, ln_d[1])
                    xr = ring(ph, "xt3", [128, DM], F32, 2)
                    pr = ring(ph, "pt3", [128, DM], F32, 2)
                    yt_r = ring(ph, "yt3", [128, DM], F32, 2)
                    lnt = (alloc(ph, "stats", [128, 2, 6]), alloc(ph, "mv", [128, 2]), alloc(ph, "rstd", [128, 1]))
                    for i in range(8):
                        xt = xr.next()
                        load("sync", xt, x_d[i * 128:(i + 1) * 128, :])
                        if pi == 1:
                            pt = pr.next()
                            load("sync", pt, pos_d[i * 128:(i + 1) * 128, :])
                            E("gpsimd", "tensor_tensor", xt.bs + pt.bs, xt.bs, out=xt.t[:], in0=xt.t[:], in1=pt.t[:], op=ALU.add)
                        yt = yt_r.next()
                        for hf in range(2):
                            pb = pbanks.next()
                            for kc in range(8):
                                MM(pb.t[:, :], mT.t[:, kc, i * 128:(i + 1) * 128], wo.t[:, kc, hf * 512:(hf + 1) * 512],
                                   kc == 0, kc == 7, mT.bs + wo.bs, pb.bs)
                            E("vector", "tensor_tensor", pb.bs + g1r.bs, yt.bs, out=yt.t[:, hf * 512:(hf + 1) * 512],
                              in0=pb.t[:, :], in1=g1r.t[:, hf * 512:(hf + 1) * 512], op=ALU.mult)
                        E("vector", "scalar_tensor_tensor", xt.bs + yt.bs, yt.bs, out=yt.t[:], in0=xt.t[:],
                          scalar=ALPHA, in1=yt.t[:], op0=ALU.mult, op1=ALU.add)
                        layer_norm_tile(lnt, yt, lg, lb)
                        S.dma("gpsimd", x2_d[pi, i * 128:(i + 1) * 128, :], yt.t[:], reads=yt.bs, writes=[x2_db[pi][i]])
                        if i == 0:
                            dbg("x2_%d" % pi, yt, yt.t[:])
                    S.barrier()
                    if stop == (pi, 5): S.enabled = False

              with ExitStack() as ph:
                  pbanks = pbanks4
                  wqb = alloc(ph, "wqb", [128, 8, 2048], BF16, nbuf=8)
                  UVN = int(os.environ.get('PEER_UVN', '16'))
                  uv_r = ring(ph, "uvg", [128, 2 * DM], BF16, UVN)
                  with ExitStack() as sub:
                      w32_r = ring(sub, "qw32", [128, 8, 256], F32, 2)
                      wq_v = wq_d.rearrange("(kc p) n -> p kc n", p=128)
                      for cg in range(8):
                          w32 = w32_r.next(); load("sync", w32, wq_v[:, :, cg * 256:(cg + 1) * 256])
                          E("scalar", "copy", w32.bs, [wqb.bs[cg]], out=wqb.t[:, :, cg * 256:(cg + 1) * 256], in_=w32.t[:])
                      S.barrier()
                  sh2r = alloc(ph, "sh2r", [128, DM]); load_row(sh2r, modrow_d[pi, 1, :], [modrow_b])
                  sc2r = alloc(ph, "sc2r", [128, DM]); load_row(sc2r, modrow_d[pi, 2, :], [modrow_b])
                  g2r = alloc(ph, "g2r", [128, DM]); load_row(g2r, modrow_d[pi, 3, :], [modrow_b])
                  lg = alloc(ph, "lg2", [128, DM]); load_row(lg, ln_d[2])
                  lb = alloc(ph, "lb2", [128, DM]); load_row(lb, ln_d[3])
                  lnt = (alloc(ph, "stats2", [128, 2, 6]), alloc(ph, "mv2", [128, 2]), alloc(ph, "rstd2", [128, 1]))
                  h2 = alloc(ph, "h2", [128, DM])
                  h2b_l = [alloc(ph, "h2b%d" % q, [128, DM], BF16) for q in range(2)]
                  h2T = alloc(ph, "h2T", [128, 8, 128], BF16)
                  qT = alloc(ph, "qT", [128, 16, 128])
                  sc = alloc(ph, "sc", [128, 16, 128])
                  scw = alloc(ph, "scw", [128, 128]); scw2 = alloc(ph, "scw2", [128, 128])
                  tops = alloc(ph, "tops", [128, 16, 16], nbuf=16); topi = alloc(ph, "topi", [128, 16, 16], U32, nbuf=16)

                  def _sub(tl, k):
                      w = Tl(None); w.bs = [tl.bs[k]]; return w
                  tops_b = [_sub(tops, k) for k in range(16)]; topi_b = [_sub(topi, k) for k in range(16)]
                  topif = alloc(ph, "topif", [128, 16, 16])
                  cand = alloc(ph, "cand", [128, 8, 16, 16]); candw = alloc(ph, "candw", [128, 256]); candw2 = alloc(ph, "candw2", [128, 256])
                  bs_ = alloc(ph, "bests", [128, 8, 16], nbuf=8); bp = alloc(ph, "bestp", [128, 8, 16], U32, nbuf=8)
                  bs_b = [_sub(bs_, k) for k in range(8)]; bp_b = [_sub(bp, k) for k in range(8)]
                  k1 = alloc(ph, "k1", [128, 8, 16], U32); k2 = alloc(ph, "k2", [128, 8, 16], U32)
                  k1f = alloc(ph, "k1f", [128, 8, 16]); k2f = alloc(ph, "k2f", [128, 8, 16])
                  oh = cand; i1f = alloc(ph, "i1f", [128, 8, 16]); i2f = alloc(ph, "i2f", [128, 8, 16])
                  io16 = alloc(ph, "io16", [128, 16]); io16i = alloc(ph, "io16i", [128, 16], I32)
                  E("gpsimd", "iota", [], io16i.bs, out=io16i.t[:], pattern=[[1, 16]], base=0, channel_multiplier=0)
                  E("vector", "tensor_copy", io16i.bs, io16.bs, out=io16.t[:], in_=io16i.t[:])
                  eidx_l = [alloc(ph, "eidx%d" % q, [128, 128], I32) for q in range(2)]
                  gw_l = [alloc(ph, "gw%d" % q, [128, 8, 16]) for q in range(2)]
                  zs = alloc(ph, "zs", [128, 8])
                  act = alloc(ph, "act", [128, 128]); wgt = alloc(ph, "wgt", [128, 128])
                  dg_r = ring(ph, "dg", [128, 128], BF16, 8)
                  ot_r = ring(ph, "ot", [128, DM], F32, 2)
                  x2_r = ring(ph, "x2t", [128, DM], F32, 2)

                  def top16x2(args_a, args_b):
                      (sa, sapa, wa_, wapa, osa, oia, tsa, tia) = args_a
                      (sb_, sapb, wb_, wapb, osb, oib, tsb, tib) = args_b
                      for (st, sap, os_, ts) in ((sa, sapa, osa, tsa), (sb_, sapb, osb, tsb)):
                          E("vector", "max", st.bs, ts.bs, out=os_[:, 0:8], in_=sap)
                      for (st, sap, os_, oi, ts, ti) in ((sa, sapa, osa, oia, tsa, tia), (sb_, sapb, osb, oib, tsb, tib)):
                          E("vector", "max_index", st.bs + ts.bs, ti.bs, out=oi[:, 0:8], in_max=os_[:, 0:8], in_values=sap)
                      for (st, sap, wt, wap, os_, ts) in ((sa, sapa, wa_, wapa, osa, tsa), (sb_, sapb, wb_, wapb, osb, tsb)):
                          E("vector", "match_replace", st.bs + ts.bs, wt.bs, out=wap, in_to_replace=os_[:, 0:8], in_values=sap, imm_value=-1e30)
                      for (wt, wap, os_, ts) in ((wa_, wapa, osa, tsa), (wb_, wapb, osb, tsb)):
                          E("vector", "max", wt.bs, ts.bs, out=os_[:, 8:16], in_=wap)
                      for (wt, wap, os_, oi, ts, ti) in ((wa_, wapa, osa, oia, tsa, tia), (wb_, wapb, osb, oib, tsb, tib)):
                          E("vector", "max_index", wt.bs + ts.bs, ti.bs, out=oi[:, 8:16], in_max=os_[:, 8:16], in_values=wap)

                  def top16(src_tl, src_ap, work_tl, work_ap, out_s, out_i, o_tl_s, o_tl_i):
                      E("vector", "max", src_tl.bs, o_tl_s.bs, out=out_s[:, 0:8], in_=src_ap)
                      E("vector", "max_index", src_tl.bs + o_tl_s.bs, o_tl_i.bs, out=out_i[:, 0:8], in_max=out_s[:, 0:8], in_values=src_ap)
                      E("vector", "match_replace", src_tl.bs + o_tl_s.bs, work_tl.bs, out=work_ap, in_to_replace=out_s[:, 0:8],
                        in_values=src_ap, imm_value=-1e30)
                      E("vector", "max", work_tl.bs, o_tl_s.bs, out=out_s[:, 8:16], in_=work_ap)
                      E("vector", "max_index", work_tl.bs + o_tl_s.bs, o_tl_i.bs, out=out_i[:, 8:16], in_max=out_s[:, 8:16], in_values=work_ap)

                  def front(i):
                      sl_ = i % 2
                      h2b = h2b_l[sl_]; eidx = eidx_l[sl_]; gw = gw_l[sl_]
                      x2t = x2_r.next()
                      S.dma("sync", x2t.t[:], x2_d[pi, i * 128:(i + 1) * 128, :], reads=[x2_db[pi][i]], writes=x2t.bs)
                      yield
                      E("vector", "tensor_tensor", x2t.bs + sc2r.bs, h2.bs, out=h2.t[:], in0=x2t.t[:], in1=sc2r.t[:], op=ALU.mult)
                      E("vector", "tensor_tensor", h2.bs + sh2r.bs, h2.bs, out=h2.t[:], in0=h2.t[:], in1=sh2r.t[:], op=ALU.add)
                      yield
                      E("scalar", "copy", h2.bs, h2b.bs, out=h2b.t[:], in_=h2.t[:])
                      yield
                      pt = pbf.next()
                      for c in range(8):
                          TR(pt.t[:, c * 128:(c + 1) * 128], h2b.t[:, c * 128:(c + 1) * 128], identb.t[:], h2b.bs + identb.bs, pt.bs)
                      yield
                      E("scalar", "copy", pt.bs, h2T.bs, out=h2T.t[:].rearrange("p a n -> p (a n)"), in_=pt.t[:, :])
                      yield
                      for g4 in range(4):
                          pb = pbanks.next()
                          for gg in range(4):
                              hp = g4 * 4 + gg
                              for kc in range(8):
                                  MM(pb.t[:, gg * 128:(gg + 1) * 128], wqb.t[:, kc, hp * 128:(hp + 1) * 128], h2T.t[:, kc, :],
                                     kc == 0, kc == 7, wqb.bs + h2T.bs, pb.bs)
                          yield
                          E("scalar", "copy", pb.bs, qT.bs, out=qT.t[:, g4 * 4:(g4 + 1) * 4, :].rearrange("p a n -> p (a n)"), in_=pb.t[:, :])
                          yield
                      for g4 in range(4):
                          pb = pbanks.next()
                          for gg in range(4):
                              hp = g4 * 4 + gg
                              MM(pb.t[:, gg * 128:(gg + 1) * 128], qT.t[:, hp, :], skT.t[:, hp % 2, :], True, True,
                                 qT.bs + skT.bs, pb.bs)
                          yield
                          E("scalar", "copy", pb.bs, sc.bs, out=sc.t[:, g4 * 4:(g4 + 1) * 4, :].rearrange("p a n -> p (a n)"), in_=pb.t[:, :])
                          yield
                      for hp in range(0, 16, 2):
                          top16x2((sc, sc.t[:, hp, :], scw, scw.t[:], tops.t[:, hp, :], topi.t[:, hp, :], tops_b[hp], topi_b[hp]),
                                  (sc, sc.t[:, hp + 1, :], scw2, scw2.t[:], tops.t[:, hp + 1, :], topi.t[:, hp + 1, :], tops_b[hp + 1], topi_b[hp + 1]))
                          yield
                          yield
                      E("vector", "tensor_copy", topi.bs, topif.bs, out=topif.t[:], in_=topi.t[:])
                      tv = tops.t[:].rearrange("p (h q) k -> p h q k", q=2)
                      tiv = topif.t[:].rearrange("p (h q) k -> p h q k", q=2)
                      E("vector", "tensor_tensor", tops.bs, cand.bs, out=cand.t[:],
                        in0=tv[:, :, 0, :].unsqueeze(3).to_broadcast([128, 8, 16, 16]),
                        in1=tv[:, :, 1, :].unsqueeze(2).to_broadcast([128, 8, 16, 16]), op=ALU.add)
                      yield
                      for h in range(0, 8, 2):
                          top16x2((cand, cand.t[:, h].rearrange("p a b -> p (a b)"), candw, candw.t[:], bs_.t[:, h, :], bp.t[:, h, :], bs_b[h], bp_b[h]),
                                  (cand, cand.t[:, h + 1].rearrange("p a b -> p (a b)"), candw2, candw2.t[:], bs_.t[:, h + 1, :], bp.t[:, h + 1, :], bs_b[h + 1], bp_b[h + 1]))
                          yield
                          yield
                      E("vector", "tensor_single_scalar", bp.bs, k1.bs, out=k1.t[:], in_=bp.t[:], scalar=4, op=ALU.logical_shift_right)
                      E("vector", "tensor_single_scalar", bp.bs, k2.bs, out=k2.t[:], in_=bp.t[:], scalar=15, op=ALU.bitwise_and)
                      E("vector", "tensor_copy", k1.bs, k1f.bs, out=k1f.t[:], in_=k1.t[:])
                      E("vector", "tensor_copy", k2.bs, k2f.bs, out=k2f.t[:], in_=k2.t[:])
                      yield
                      iob = io16.t[:].unsqueeze(1).unsqueeze(1).to_broadcast([128, 8, 16, 16])
                      for (kf_, q, dst) in ((k1f, 0, i1f), (k2f, 1, i2f)):
                          E("vector", "tensor_tensor", kf_.bs + io16.bs, oh.bs, out=oh.t[:],
                            in0=kf_.t[:].unsqueeze(3).to_broadcast([128, 8, 16, 16]), in1=iob, op=ALU.is_equal)
                          yield
                          E("vector", "tensor_tensor", oh.bs + topif.bs, oh.bs, out=oh.t[:], in0=oh.t[:],
                            in1=tiv[:, :, q, :].unsqueeze(2).to_broadcast([128, 8, 16, 16]), op=ALU.mult)
                          yield
                          E("vector", "tensor_reduce", oh.bs, dst.bs, out=dst.t[:], in_=oh.t[:], axis=AX.X, op=ALU.add)
                          yield
                      E("vector", "scalar_tensor_tensor", i1f.bs + i2f.bs, i1f.bs, out=i1f.t[:], in0=i1f.t[:], scalar=128.0,
                        in1=i2f.t[:], op0=ALU.mult, op1=ALU.add)
                      E("vector", "tensor_copy", i1f.bs, eidx.bs, out=eidx.t[:].rearrange("p (h k) -> p h k", h=8), in_=i1f.t[:])
                      yield
                      E("vector", "tensor_tensor", bs_.bs, gw.bs, out=gw.t[:], in0=bs_.t[:],
                        in1=bs_.t[:, :, 0:1].to_broadcast([128, 8, 16]), op=ALU.subtract)
                      yield
                      E("scalar", "activation", gw.bs, gw.bs, out=gw.t[:], in_=gw.t[:], func=AF.Exp)
                      yield
                      E("vector", "tensor_reduce", gw.bs, zs.bs, out=zs.t[:], in_=gw.t[:], axis=AX.X, op=ALU.add)
                      E("vector", "reciprocal", zs.bs, zs.bs, out=zs.t[:], in_=zs.t[:])
                      E("vector", "tensor_tensor", gw.bs + zs.bs, gw.bs, out=gw.t[:], in0=gw.t[:],
                        in1=zs.t[:].unsqueeze(2).to_broadcast([128, 8, 16]), op=ALU.mult)
                      x2_of[i] = x2t
                      yield

                  def back(i, nxt):
                      sl_ = i % 2
                      h2b = h2b_l[sl_]; eidx = eidx_l[sl_]; gw = gw_l[sl_]
                      x2t = x2_of[i]
                      gwf = gw.t[:].rearrange("p h k -> p (h k)")
                      uvs_of = {}
                      GS = PEER_GS
                      NG = 128 // GS
                      SK = PEER_SK
                      assert (SK + 1) * GS + GS - 1 <= UVN + GS - 1 and UVN >= (SK + 2) * GS - 0, "gather ring too shallow for skew"
                      for jg in range(NG + SK):
                          if jg < NG:
                              uvs = []
                              for jj in range(GS):
                                  j = jg * GS + jj
                                  uv = uv_r.next()
                                  uvs.append(uv)
                                  GATHER(uv.t[:], uv_d, eidx.t[:, j:j + 1], eidx.bs + [uv_b], uv.bs)
                                  E("vector", "tensor_tensor", uv.bs + h2b.bs, uv.bs, out=uv.t[:, 0:DM], in0=uv.t[:, 0:DM], in1=h2b.t[:],
                                    op=ALU.mult)
                                  E("scalar", "activation", uv.bs, uv.bs + (act.bs if jj in (0, GS - 1) else []), out=uv.t[:, 0:DM],
                                    in_=uv.t[:, 0:DM], func=AF.Identity, accum_out=act.t[:, j:j + 1])
                              uvs_of[jg] = uvs
                          if nxt is not None and jg < NG:
                              for _ in range(PEER_FS):
                                  next(nxt, None)
                          if jg >= SK:
                              g_ = jg - SK
                              grp = slice(g_ * GS, (g_ + 1) * GS)
                              uvs = uvs_of.pop(g_)
                              E("scalar", "activation", act.bs, wgt.bs, out=wgt.t[:, grp], in_=act.t[:, grp], func=AF.Gelu_apprx_tanh)
                              E("vector", "tensor_tensor", wgt.bs + gw.bs, wgt.bs, out=wgt.t[:, grp], in0=wgt.t[:, grp], in1=gwf[:, grp], op=ALU.mult)
                              for jj in range(GS):
                                  j = g_ * GS + jj
                                  uv = uvs[jj]
                                  dg = dg_r.next()
                                  E("scalar", "activation", identf.bs + wgt.bs, dg.bs, out=dg.t[:], in_=identf.t[:], func=AF.Identity,
                                    scale=wgt.t[:, j:j + 1])
                                  for hf in range(2):
                                      MM(pacc[hf].t[:, :], dg.t[:], uv.t[:, DM + hf * 512:DM + (hf + 1) * 512], j == 0, j == 127,
                                         dg.bs + uv.bs, pacc[hf].bs)
                      ot = ot_r.next()
                      for hf in range(2):
                          E("vector", "tensor_tensor", pacc[hf].bs + g2r.bs, ot.bs, out=ot.t[:, hf * 512:(hf + 1) * 512], in0=pacc[hf].t[:, :],
                            in1=g2r.t[:, hf * 512:(hf + 1) * 512], op=ALU.mult)
                      E("vector", "scalar_tensor_tensor", x2t.bs + ot.bs, ot.bs, out=ot.t[:], in0=x2t.t[:], scalar=ALPHA,
                        in1=ot.t[:], op0=ALU.mult, op1=ALU.add)
                      layer_norm_tile(lnt, ot, lg, lb, eng2="vector")
                      S.dma("sync", yout[pi][i * 128:(i + 1) * 128, :], ot.t[:], reads=ot.bs)

                  x2_of = {}
                  g0 = front(0)
                  for _ in g0:
                      pass
                  for i in range(8):
                      nxt = front(i + 1) if i + 1 < 8 else None
                      back(i, nxt)
                      if nxt is not None:
                          for _ in nxt:
                              pass
                  S.barrier()

        S.enabled = True
        with ExitStack() as ph:
            nso = alloc(ph, "nso", [64, 128])
            if stop is None or stop >= (0, 3):
                pb = pbanks.next()
                TR(pb.t[0:64, 0:128], nst.t[:].rearrange("p s d c -> p (s d c)"), identf.t[:], nst.bs + identf.bs, pb.bs)
                E("vector", "tensor_copy", pb.bs, nso.bs, out=nso.t[:], in_=pb.t[0:64, 0:128])
                S.dma("sync", ns_d, nso.t[:], reads=nso.bs)
            S.emit(top)
    return nc, dbg_names


def _consts():
    bf = ml_dtypes.bfloat16
    c = {}
    c["identf"] = np.eye(128, dtype=np.float32)
    c["identb"] = np.eye(128, dtype=np.float32).astype(bf)
    m0 = np.ones((128, 1), np.float32); m0[0, 0] = 0.0
    c["mask0"] = m0
    t = np.arange(1024)
    r = (t // 64).astype(np.float32); col = (t % 64).astype(np.float32)
    quarter = DM // 4
    omega = (1.0 / (10000.0 ** (np.arange(quarter, dtype=np.float32) / quarter))).astype(np.float32)
    er = r[:, None] * omega[None, :]; ec = col[:, None] * omega[None, :]
    c["pos"] = np.concatenate([np.sin(er), np.cos(er), np.sin(ec), np.cos(ec)], axis=-1).astype(np.float32)
    deltas = np.linspace(math.log(1e-2) / 1.5, math.log(1e-2) / 0.3, DM, dtype=np.float32)
    bands = np.linspace(1e-4, 15, 16, dtype=np.float32)
    for L in (256, 1024):
        ti = np.arange(L, dtype=np.float32)
        tn = ti / max(L - 1, 1)
        w = (2.0 * math.pi * ti / L).astype(np.float32)
        fw = w[:, None] * bands[None, :]
        z = np.concatenate([tn[:, None], np.cos(fw), -np.sin(fw)], axis=-1).astype(np.float32)
        c["zT%d" % L] = np.ascontiguousarray(z.T)
        c["dec%d" % L] = np.exp(-tn[:, None] * np.abs(deltas)[None, :]).astype(np.float32)
        tt = np.arange(L, dtype=np.float64)
        ang = np.pi * np.outer(tt, tt) / L
        C = np.cos(ang)
        Sm = -np.sin(ang)
        Sm[:, 0] = (-1.0) ** tt
        c["C%d" % L] = C.astype(np.float32).astype(bf)
        c["S%d" % L] = Sm.astype(np.float32).astype(bf)
        c["ST%d" % L] = np.ascontiguousarray(Sm.T).astype(np.float32).astype(bf)
        nfc = L // 128
        f = np.arange(L)
        wfre = np.where(f == 0, 1.0 / (2 * L), 1.0 / L)
        wB = np.where(f == 0, 0.0, 1.0 / L)
        mD = np.where(f == 0, 0.0, 1.0 / L)
        m2 = np.where(f == 0, 1.0 / (2 * L), 0.0)
        tab = np.stack([wfre, wB, mD, m2], 0).reshape(4, nfc, 128).transpose(2, 0, 1)
        c["wtab%d" % L] = np.ascontiguousarray(tab).astype(np.float32)
    return c


def _chunkT(v):
    v = np.asarray(v)
    lead = v.shape[:-1]
    n = v.shape[-1] // 128
    v = v.reshape(lead + (n, 128))
    return np.ascontiguousarray(np.moveaxis(v, -1, 0))


def _in_maps(inp):
    f = lambda a: np.ascontiguousarray(np.asarray(a, dtype=np.float32))
    cst = _consts()
    shared = dict(cst)
    shared["w_ada"] = f(inp["w_ada"][0])
    b_ada = f(inp["b_ada"][0])
    shared["b_adaT"] = _chunkT(b_ada[:2048].reshape(2, 1024))
    shared["b_ada_rows"] = np.ascontiguousarray(b_ada[2048:].reshape(4, 1024))
    shared["w_in"] = f(inp["w_in"][0])
    shared["rcw"] = np.ascontiguousarray(_chunkT(f(inp["rnn_conv_w"][0])).transpose(0, 2, 1))
    shared["rcb"] = _chunkT(f(inp["rnn_conv_b"][0]))
    shared["gate_w"] = f(inp["rnn_gate_w"][0])
    shared["gbT"] = _chunkT(f(inp["rnn_gate_b"][0]))
    shared["lamT"] = _chunkT(f(inp["rnn_lambda"][0]))
    shared["hcw"] = np.ascontiguousarray(_chunkT(f(inp["hy_conv_w"][0])).transpose(0, 2, 1))
    shared["hcb"] = _chunkT(f(inp["hy_conv_b"][0]))
    shared["hy_w1"] = f(inp["hy_ffn_w1"][0])
    shared["hy_b1"] = f(inp["hy_ffn_b1"][0]).reshape(64, 1)
    shared["hy_w2"] = f(inp["hy_ffn_w2"][0])
    shared["hy_b2"] = f(inp["hy_ffn_b2"][0]).reshape(64, 1)
    shared["hy_freq"] = f(inp["hy_sin_freq"][0]).reshape(64, 1)
    shared["hy_w3"] = f(inp["hy_ffn_w3"][0])
    shared["hy_b3"] = f(inp["hy_ffn_b3"][0])
    shared["skipT"] = _chunkT(f(inp["hy_skip"][0]))
    shared["w_a"] = f(inp["w_branch_a"][0])
    shared["w_b"] = f(inp["w_branch_b"][0])
    shared["w_o"] = f(inp["w_out"][0])
    shared["ln1_g"] = f(inp["ln1_g"][0]); shared["ln1_b"] = f(inp["ln1_b"][0])
    shared["ln2_g"] = f(inp["ln2_g"][0]); shared["ln2_b"] = f(inp["ln2_b"][0])
    shared["wq"] = f(inp["peer_w_query"][0])
    shared["skT"] = np.ascontiguousarray(f(inp["peer_sub_keys"][0]).transpose(2, 0, 1))
    shared["peer_u"] = f(inp["peer_u"][0])
    shared["peer_v"] = f(inp["peer_v"][0])
    xp = f(inp["x_prompt"]); xs = f(inp["x_sample"]); stt = f(inp["state_rglru"]); cc = f(inp["c"]); cctx = f(inp["c_ctx"])
    maps = []
    for i in range(N_CORES):
        m = dict(shared)
        m["xp"] = np.ascontiguousarray(xp[4 * i:4 * i + 4].reshape(1024, DM))
        m["xs"] = np.ascontiguousarray(xs[i])
        cond = np.stack([cctx, cc[i]], 0)
        m["condT"] = _chunkT(cond).transpose(0, 2, 1).copy()
        m["h0T"] = _chunkT(stt[i, 0])
        maps.append(m)
    return maps


_CACHE = {}


def kernel(**inputs):
    if "nc" not in _CACHE:
        _CACHE["nc"] = build()
    nc, _ = _CACHE["nc"]
    maps = _in_maps(inputs)
    res = run_bass_kernel_spmd(nc, maps, core_ids=list(range(N_CORES)))
    yp = np.zeros((32, 256, DM), np.float32)
    ys = np.zeros((8, 1024, DM), np.float32)
    ns = np.zeros((32, 1, 2, 1024), np.float32)
    for i in range(N_CORES):
        r = res.results[i]
        yp[4 * i:4 * i + 4] = np.asarray(r["yp"]).reshape(4, 256, DM)
        ys[i] = np.asarray(r["ys"])
        ns[4 * i:4 * i + 4, 0] = np.asarray(r["ns"]).reshape(4, 2, 1024)
    return yp, ys, ns
```

```python
import math
import os
HYSKIP = os.environ.get('HYSKIP', '')
PEER_SK = int(os.environ.get('PEER_SK', '2'))
PEER_FS = int(os.environ.get('PEER_FS', '2'))
PEER_GS = int(os.environ.get('PEER_GS', '4'))
from contextlib import ExitStack

import ml_dtypes
import numpy as np

import concourse.bass as bass
import concourse.mybir as mybir
from concourse.bass_utils import run_bass_kernel_spmd

F32 = mybir.dt.float32
BF16 = mybir.dt.bfloat16
I32 = mybir.dt.int32
U32 = mybir.dt.uint32
AF = mybir.ActivationFunctionType
ALU = mybir.AluOpType
AX = mybir.AxisListType

ENGS = ["sync", "scalar", "vector", "gpsimd", "tensor"]
DBGOPS = []
NOSYNC_ENGS = set(os.environ.get('NOSYNC', '').split(',')) - {''}
N_CORES = 8
DM = 1024
ALPHA = 2.0 ** 0.25
LN_EPS = 1e-5
RGLRU_C = 8.0


class Buf:
    __slots__ = ("name", "last_w", "readers")

    def __init__(self, name=""):
        self.name = name
        self.last_w = None
        self.readers = {}


class Op:
    __slots__ = ("eng", "fn", "deps", "key", "pos", "is_dma", "sig", "val", "waits",
                 "nosame", "vc_issue", "vc_done")


class Sched:
    def __init__(self, nc, dma_slots=16, same_engine_sync=True):
        self.nc = nc
        self.ops = []
        self.per_eng = {e: [] for e in ENGS}
        self.npos = {}
        self.dma_slots = dma_slots
        self.dma_count = {e: 0 for e in ENGS}
        self.slot_last = {}
        self.same_engine_sync = same_engine_sync
        self.enabled = True

    def _new(self, eng, fn, dma, nosame):
        o = Op()
        o.eng = eng
        o.fn = fn
        o.is_dma = dma
        o.sig = dma
        o.nosame = nosame
        return o

    def op(self, eng, fn, reads=(), writes=(), dma=False, nosame=False):
        if not self.enabled:
            return None
        o = self._new(eng, fn, dma, nosame)
        deps = []
        for b in reads:
            if b.last_w is not None:
                deps.append(b.last_w)
        for b in writes:
            if b.last_w is not None:
                deps.append(b.last_w)
            deps.extend(b.readers.values())
        if dma:
            slot = self.dma_count[eng] % self.dma_slots
            self.dma_count[eng] += 1
            o.key = ("dma", eng, slot)
            prev = self.slot_last.get(o.key)
            if prev is not None:
                deps.append(prev)
            self.slot_last[o.key] = o
        else:
            o.key = eng
        o.pos = self.npos.get(o.key, 0)
        self.npos[o.key] = o.pos + 1
        o.deps = [d for d in deps if d is not o]
        rk = o.key if not dma else ("dmaop", id(o))
        for b in reads:
            b.readers[rk] = o
        for b in writes:
            b.last_w = o
            b.readers = {}
        self.ops.append(o)
        self.per_eng[eng].append(o)
        return o

    def dma(self, eng, out, in_, reads=(), writes=(), **kw):
        return self.op(eng, lambda e: e.dma_start(out=out, in_=in_, **kw), reads, writes, dma=True)

    def barrier(self):
        if not self.enabled:
            return
        lasts = [self.per_eng[e][-1] for e in ENGS if self.per_eng[e]]
        lasts = [o for o in lasts if o.fn is not None]
        lasts += list(self.slot_last.values())
        for e in ENGS:
            o = self._new(e, None, False, False)
            o.key = e
            o.pos = self.npos.get(e, 0)
            self.npos[e] = o.pos + 1
            o.deps = list(lasts)
            self.ops.append(o)
            self.per_eng[e].append(o)

    def finalize(self):
        last_on_eng = {}
        for o in self.ops:
            vc = {}
            prev = last_on_eng.get(o.eng)
            if prev is not None:
                vc.update(prev.vc_issue)
            waits = []
            best = {}
            for d in o.deps:
                if d.key not in best or best[d.key].pos < d.pos:
                    best[d.key] = d
            for k, d in best.items():
                if (not d.is_dma) and (not o.is_dma) and d.eng == o.eng and (
                        o.nosame or not self.same_engine_sync or o.eng in NOSYNC_ENGS):
                    continue
                if vc.get(k, -1) >= d.pos:
                    continue
                waits.append(d)
                d.sig = True
                for kk, vv in d.vc_done.items():
                    if vc.get(kk, -1) < vv:
                        vc[kk] = vv
            o.waits = waits
            o.vc_issue = vc
            vd = dict(vc)
            if o.fn is not None:
                vd[o.key] = o.pos
            o.vc_done = vd
            last_on_eng[o.eng] = o
        cnt = {}
        for o in self.ops:
            if o.is_dma:
                o.val = 16 * (o.pos + 1)
            elif o.sig:
                cnt[o.key] = cnt.get(o.key, 0) + 1
                o.val = cnt[o.key]

    def emit(self, stack):
        nc = self.nc
        fin = self._new("sync", None, False, False)
        fin.key = "sync"
        fin.pos = self.npos.get("sync", 0)
        fin.deps = list(self.slot_last.values())
        self.ops.append(fin)
        self.per_eng["sync"].append(fin)
        self.finalize()
        sems = {}
        for e in ENGS:
            sems[e] = stack.enter_context(nc.semaphore("s_" + e))
        for k in self.slot_last.keys():
            sems[k] = stack.enter_context(nc.semaphore("d_%s_%d" % (k[1], k[2])))
        per_eng = self.per_eng

        def run(engname, eng):
            for o in per_eng[engname]:
                for d in o.waits:
                    eng.wait_ge(sems[d.key], d.val)
                if o.fn is None:
                    continue
                inst = o.fn(eng)
                if o.sig:
                    inst.then_inc(sems[o.key], 16 if o.is_dma else 1)

        with nc.Block() as block:
            @block.sync
            def _(e):
                run("sync", e)

            @block.scalar
            def _(e):
                run("scalar", e)

            @block.vector
            def _(e):
                run("vector", e)

            @block.gpsimd
            def _(e):
                run("gpsimd", e)

            @block.tensor
            def _(e):
                run("tensor", e)


class Tl:
    def __init__(self, t, nbuf=1, name=""):
        self.t = t
        self.bs = [Buf(name + str(i)) for i in range(nbuf)]
        self.b = self.bs[0]


class Ring:
    def __init__(self, tiles):
        self.tiles = tiles
        self.i = 0

    def next(self):
        t = self.tiles[self.i % len(self.tiles)]
        self.i += 1
        return t


def build(debug=(), stop=None):
    nc = bass.Bass("TRN2", target_bir_lowering=False)
    S = Sched(nc)
    debug = set(debug)
    dbg_names = []

    def din(name, shape, dt=F32):
        return nc.dram_tensor(name, list(shape), dt, kind="ExternalInput").ap()

    def dout(name, shape, dt=F32):
        return nc.dram_tensor(name, list(shape), dt, kind="ExternalOutput").ap()

    xin = [din("xp", [1024, DM]), din("xs", [1024, DM])]
    pos_d = din("pos", [1024, DM])
    condT_d = din("condT", [128, 8, 2])
    h0T_d = din("h0T", [128, 2, 8])
    w_ada_d = din("w_ada", [DM, 6 * DM])
    b_adaT_d = din("b_adaT", [128, 2, 8])
    b_ada_rows_d = din("b_ada_rows", [4, DM])
    w_in_d = din("w_in", [DM, 7168])
    rcw_d = din("rcw", [128, 8, 4])
    rcb_d = din("rcb", [128, 8])
    gate_w_d = din("gate_w", [2, 2, 4, 256, 256])
    gbT_d = din("gbT", [128, 2, 2, 8])
    lamT_d = din("lamT", [128, 2, 8])
    hcw_d = din("hcw", [128, 24, 3])
    hcb_d = din("hcb", [128, 24])
    hy_w1_d = din("hy_w1", [33, 64])
    hy_b1_d = din("hy_b1", [64, 1])
    hy_w2_d = din("hy_w2", [64, 64])
    hy_b2_d = din("hy_b2", [64, 1])
    hy_freq_d = din("hy_freq", [64, 1])
    hy_w3_d = din("hy_w3", [64, 4096])
    hy_b3_d = din("hy_b3", [4096])
    skipT_d = din("skipT", [128, 2, 8])
    w_a_d = din("w_a", [DM, DM])
    w_b_d = din("w_b", [DM, DM])
    w_o_d = din("w_o", [DM, DM])
    ln_d = [din("ln1_g", [DM]), din("ln1_b", [DM]), din("ln2_g", [DM]), din("ln2_b", [DM])]
    wq_d = din("wq", [DM, 2048])
    skT_d = din("skT", [128, 2, 128])
    pu_d = din("peer_u", [16384, DM])
    pv_d = din("peer_v", [16384, DM])
    identf_d = din("identf", [128, 128])
    identb_d = din("identb", [128, 128], BF16)
    zT_d = [din("zT256", [33, 256]), din("zT1024", [33, 1024])]
    dec_d = [din("dec256", [256, DM]), din("dec1024", [1024, DM])]
    C_d = [din("C256", [256, 256], BF16), din("C1024", [1024, 1024], BF16)]
    Sm_d = [din("S256", [256, 256], BF16), din("S1024", [1024, 1024], BF16)]
    ST_d = [din("ST256", [256, 256], BF16), din("ST1024", [1024, 1024], BF16)]
    wtab_d = [din("wtab256", [128, 4, 2]), din("wtab1024", [128, 4, 8])]
    mask0_d = din("mask0", [128, 1])

    yout = [dout("yp", [1024, DM]), dout("ys", [1024, DM])]
    ns_d = dout("ns", [64, 128])
    modrow_d = nc.dram_tensor("modrow", [2, 4, DM], F32, kind="Internal").ap()
    modrow_b = Buf("modrow")
    w_in_bd = nc.dram_tensor("w_in_bf16", [DM, 7168], BF16, kind="Internal").ap()
    w_a_bd = nc.dram_tensor("w_a_bf16", [DM, DM], BF16, kind="Internal").ap()
    w_b_bd = nc.dram_tensor("w_b_bf16", [DM, DM], BF16, kind="Internal").ap()
    w_o_bd = nc.dram_tensor("w_o_bf16", [DM, DM], BF16, kind="Internal").ap()
    wq_bd = nc.dram_tensor("wq_bf16", [DM, 2048], BF16, kind="Internal").ap()
    gate_w_bd = nc.dram_tensor("gate_w_bf16", [2, 2, 4, 256, 256], BF16, kind="Internal").ap()
    wcv_b = {k: Buf("wcv_" + k) for k in ("w_in", "w_a", "w_b", "w_o", "wq", "gate")}
    uv_d = nc.dram_tensor("uv_bf16", [16384, 2 * DM], BF16, kind="Internal").ap()
    uv_b = Buf("uv")
    x2_d = nc.dram_tensor("x2_scratch", [2, 1024, DM], F32, kind="Internal").ap()
    x2_db = [[Buf("x2d") for _ in range(8)] for _ in range(2)]

    top = ExitStack()
    with top:
        uid = [0]

        def alloc(scope, name, shape, dt=F32, nbuf=1):
            uid[0] += 1
            t = scope.enter_context(nc.sbuf_tensor("s%d_%s" % (uid[0], name), list(shape), dt))
            return Tl(t, nbuf, name)

        def ring(scope, name, shape, dt, n):
            return Ring([alloc(scope, "%s_%d" % (name, i), shape, dt) for i in range(n)])

        _pb = [Tl(top.enter_context(nc.psum_tensor("pb%d" % i, [128, 512], F32)), 1, "pb%d" % i) for i in range(6)]
        pbanks6 = Ring(_pb)
        pbanks4 = Ring(_pb[:4])
        pacc = _pb[4:6]
        pbanks = pbanks6
        pbf = Ring([Tl(top.enter_context(nc.psum_tensor("pbf%d" % i, [128, 1024], BF16)), 1, "pbf%d" % i)
                    for i in range(2)])

        def dbg(name, tl, ap, dt=F32):
            if name not in debug:
                return
            o = dout("dbg_" + name, list(ap.shape), dt)
            dbg_names.append("dbg_" + name)
            S.dma("sync", o, ap, reads=tl.bs)

        def E(eng, meth, reads, writes, nosame=False, **kw):
            return S.op(eng, lambda e: getattr(e, meth)(**kw), reads, writes, nosame=nosame)

        def MM(out, lhsT, rhs, start, stop, reads, writes):
            return S.op("tensor", lambda e: e.matmul(out, lhsT=lhsT, rhs=rhs, start=start, stop=stop),
                        reads, writes, nosame=True)

        def GATHER(out, table, idx, reads, writes):
            return S.op("gpsimd", lambda e: e.indirect_dma_start(
                out=out, out_offset=None, in_=table, in_offset=bass.IndirectOffsetOnAxis(ap=idx, axis=0)),
                reads, writes, dma=True)

        def TR(out, in_, ident, reads, writes):
            return S.op("tensor", lambda e: e.transpose(out=out, in_=in_, identity=ident),
                        reads, writes, nosame=True)

        def load(eng, tl, dram_ap, sb_ap=None, extra_reads=()):
            S.dma(eng, sb_ap if sb_ap is not None else tl.t[:], dram_ap, reads=list(extra_reads), writes=tl.bs)

        S.dma("gpsimd", w_in_bd, w_in_d, writes=[wcv_b["w_in"]])
        S.dma("gpsimd", w_b_bd, w_b_d, writes=[wcv_b["w_b"]])
        S.dma("gpsimd", gate_w_bd.rearrange("a b c i j -> (a b c i) j"), gate_w_d.rearrange("a b c i j -> (a b c i) j"),
              writes=[wcv_b["gate"]])
        S.dma("gpsimd", w_a_bd, w_a_d, writes=[wcv_b["w_a"]])
        S.dma("gpsimd", w_o_bd, w_o_d, writes=[wcv_b["w_o"]])
        S.dma("gpsimd", wq_bd, wq_d, writes=[wcv_b["wq"]])
        for cq in range(4):
            r0, r1 = cq * 4096, (cq + 1) * 4096
            S.dma("gpsimd", uv_d[r0:r1, 0:DM], pu_d[r0:r1, :], writes=[uv_b])
            S.dma("gpsimd", uv_d[r0:r1, DM:2 * DM], pv_d[r0:r1, :], writes=[uv_b])
        identf = alloc(top, "identf", [128, 128]); load("sync", identf, identf_d)
        identb = alloc(top, "identb", [128, 128], BF16); load("sync", identb, identb_d)
        mask0 = alloc(top, "mask0", [128, 1]); load("sync", mask0, mask0_d)
        epsb = alloc(top, "epsb", [128, 1])
        E("vector", "memset", [], epsb.bs, ap=epsb.t[:], constant=LN_EPS)
        rcw = alloc(top, "rcw", [128, 8, 4]); load("sync", rcw, rcw_d)
        rcb = alloc(top, "rcb", [128, 8]); load("sync", rcb, rcb_d)
        gbT = alloc(top, "gbT", [128, 2, 2, 8]); load("sync", gbT, gbT_d)
        lamT = alloc(top, "lamT", [128, 2, 8]); load("sync", lamT, lamT_d)
        h0T = alloc(top, "h0T", [128, 2, 8]); load("sync", h0T, h0T_d)
        hcw = alloc(top, "hcw", [128, 24, 3]); load("sync", hcw, hcw_d)
        hcb = alloc(top, "hcb", [128, 24]); load("sync", hcb, hcb_d)
        skipT = alloc(top, "skipT", [128, 2, 8]); load("sync", skipT, skipT_d)
        skT = alloc(top, "skT", [128, 2, 128]); load("sync", skT, skT_d)
        nsp = alloc(top, "nsp", [128, 2, 8])
        E("scalar", "activation", lamT.bs, nsp.bs, out=nsp.t[:], in_=lamT.t[:], func=AF.Exp, scale=-1.0)
        E("scalar", "activation", nsp.bs, nsp.bs, out=nsp.t[:], in_=nsp.t[:], func=AF.Ln, bias=1.0, scale=1.0)
        E("vector", "tensor_scalar", nsp.bs, nsp.bs, out=nsp.t[:], in0=nsp.t[:], scalar1=-RGLRU_C, scalar2=None,
          op0=ALU.mult)
        modT = alloc(top, "modT", [128, 2, 8, 2])
        nst = alloc(top, "nst", [128, 4, 2, 8])

        with ExitStack() as ph:
            condT = alloc(ph, "condT", [128, 8, 2]); load("sync", condT, condT_d)
            condS = alloc(ph, "condS", [128, 8, 2])
            E("scalar", "activation", condT.bs, condS.bs, out=condS.t[:], in_=condT.t[:], func=AF.Silu)
            b_adaT = alloc(ph, "b_adaT", [128, 2, 8]); load("sync", b_adaT, b_adaT_d)
            brow = alloc(ph, "brow", [1, 4, DM])
            load("sync", brow, b_ada_rows_d.rearrange("(o a) n -> o a n", o=1))
            wa_ring = ring(ph, "wa", [128, 8, DM], F32, 2)
            rowt = ring(ph, "rowt", [1, DM], F32, 2)
            for ty in range(6):
                wa = wa_ring.next()
                load("sync", wa, w_ada_d[:, ty * DM:(ty + 1) * DM].rearrange("(kc p) n -> p kc n", p=128))
                if ty < 2:
                    pb = pbanks.next()
                    for m in range(8):
                        for kc in range(8):
                            MM(pb.t[:, m * 2:m * 2 + 2], wa.t[:, kc, m * 128:(m + 1) * 128], condS.t[:, kc, :],
                               kc == 0, kc == 7, wa.bs + condS.bs, pb.bs)
                    E("vector", "tensor_tensor", pb.bs + b_adaT.bs, modT.bs,
                      out=modT.t[:, ty], in0=pb.t[:, 0:16].rearrange("p (m j) -> p m j", j=2),
                      in1=b_adaT.t[:, ty].unsqueeze(2).to_broadcast([128, 8, 2]), op=ALU.add)
                    if ty == 1:
                        E("vector", "tensor_scalar", modT.bs, modT.bs, out=modT.t[:, 1], in0=modT.t[:, 1],
                          scalar1=1.0, scalar2=None, op0=ALU.add)
                else:
                    for j in range(2):
                        rt = rowt.next()
                        for hf in range(2):
                            pb = pbanks.next()
                            for kc in range(8):
                                MM(pb.t[0:1, :], condS.t[:, kc, j:j + 1], wa.t[:, kc, hf * 512:(hf + 1) * 512],
                                   kc == 0, kc == 7, wa.bs + condS.bs, pb.bs)
                            E("vector", "tensor_tensor", pb.bs + brow.bs, rt.bs,
                              out=rt.t[0:1, hf * 512:(hf + 1) * 512], in0=pb.t[0:1, :],
                              in1=brow.t[0:1, ty - 2, hf * 512:(hf + 1) * 512], op=ALU.add)
                        if ty == 4:
                            E("vector", "tensor_scalar", rt.bs, rt.bs, out=rt.t[:], in0=rt.t[:], scalar1=1.0,
                              scalar2=None, op0=ALU.add)
                        S.dma("sync", modrow_d[j, ty - 2:ty - 1, :], rt.t[0:1, :], reads=rt.bs, writes=[modrow_b])
            S.barrier()

        def load_row(tl, dram_row, extra=()):
            S.dma("sync", tl.t[:], dram_row.partition_broadcast(128), reads=list(extra), writes=tl.bs)

        def layer_norm_tile(scope_tiles, xt, g_row, b_row, eng2="gpsimd"):
            stats, mv, rstd = scope_tiles
            E("vector", "bn_stats", xt.bs, stats.bs, out=stats.t[:, 0, :], in_=xt.t[:, 0:512])
            E("vector", "bn_stats", xt.bs, stats.bs, out=stats.t[:, 1, :], in_=xt.t[:, 512:1024])
            E("vector", "bn_aggr", stats.bs, mv.bs, out=mv.t[:], in_=stats.t[:].rearrange("p a b -> p (a b)"))
            E("scalar", "activation", mv.bs + epsb.bs, rstd.bs, out=rstd.t[:], in_=mv.t[:, 1:2], func=AF.Sqrt,
              bias=epsb.t[:], scale=1.0)
            E("vector", "reciprocal", rstd.bs, rstd.bs, out=rstd.t[:], in_=rstd.t[:])
            E("vector", "tensor_scalar", xt.bs + mv.bs + rstd.bs, xt.bs, out=xt.t[:], in0=xt.t[:],
              scalar1=mv.t[:, 0:1], scalar2=rstd.t[:], op0=ALU.subtract, op1=ALU.mult)
            E(eng2, "tensor_tensor", xt.bs + g_row.bs, xt.bs, out=xt.t[:], in0=xt.t[:], in1=g_row.t[:], op=ALU.mult)
            E(eng2, "tensor_tensor", xt.bs + b_row.bs, xt.bs, out=xt.t[:], in0=xt.t[:], in1=b_row.t[:], op=ALU.add)

        def conv(eng, out_tl, out_ap, in_tl, in_ap, w_tl, w_ap, b_ap, ntap, left, nseq, L):
            o3 = out_ap.rearrange("p (s l) -> p s l", s=nseq)
            i3 = in_ap.rearrange("p (s l) -> p s l", s=nseq)
            E(eng, "tensor_scalar", in_tl.bs + w_tl[0].bs + w_tl[1].bs, out_tl.bs, out=out_ap, in0=in_ap,
              scalar1=w_ap[:, left:left + 1], scalar2=b_ap, op0=ALU.mult, op1=ALU.add)
            for j in range(ntap):
                o = j - left
                if o == 0:
                    continue
                lo_out = max(0, -o)
                hi_out = L - max(0, o)
                E(eng, "scalar_tensor_tensor", in_tl.bs + out_tl.bs + w_tl[0].bs, out_tl.bs,
                  out=o3[:, :, lo_out:hi_out], in0=i3[:, :, lo_out + o:hi_out + o], scalar=w_ap[:, j:j + 1],
                  in1=o3[:, :, lo_out:hi_out], op0=ALU.mult, op1=ALU.add)

        for pi in range(2):
            nseq, L = (4, 256) if pi == 0 else (1, 1024)
            pbanks = pbanks6
            ntc = L // 128
            x_d = xin[pi]
            with ExitStack() as pp:
              with ExitStack() as mxs:
                h1T = alloc(mxs, "h1T", [128, 8, 1024], BF16)
                ybT = alloc(mxs, "ybT", [128, 8, 1024], BF16, nbuf=8)

                with ExitStack() as ph:
                    xr = ring(ph, "xt", [128, DM], F32, 2)
                    pr = ring(ph, "pt", [128, DM], F32, 2)
                    for i in range(8):
                        xt = xr.next()
                        load("sync", xt, x_d[i * 128:(i + 1) * 128, :])
                        if pi == 1:
                            pt = pr.next()
                            load("gpsimd", pt, pos_d[i * 128:(i + 1) * 128, :])
                            E("vector", "tensor_tensor", xt.bs + pt.bs, xt.bs, out=xt.t[:], in0=xt.t[:], in1=pt.t[:],
                              op=ALU.add)
                        for hf in range(2):
                            pb = pbanks.next()
                            for cc in range(4):
                                c = hf * 4 + cc
                                TR(pb.t[:, cc * 128:(cc + 1) * 128], xt.t[:, c * 128:(c + 1) * 128], identf.t[:],
                                   xt.bs + identf.bs, pb.bs)
                            for cc in range(4):
                                c = hf * 4 + cc
                                E("scalar", "activation", pb.bs + modT.bs, h1T.bs,
                                  out=h1T.t[:, c, i * 128:(i + 1) * 128], in_=pb.t[:, cc * 128:(cc + 1) * 128],
                                  func=AF.Identity, bias=modT.t[:, 0, c, pi:pi + 1], scale=modT.t[:, 1, c, pi:pi + 1])
                    dbg("h1T%d" % pi, h1T, h1T.t[:], BF16)
                    S.barrier()
                    if stop == (pi, 1): S.enabled = False

                with ExitStack() as ph:
                    li = pi
                    Cm = alloc(ph, "Cm", [128, ntc, L], BF16)
                    Sm = alloc(ph, "Sm", [128, ntc, L], BF16)
                    STm = alloc(ph, "STm", [128, ntc, L], BF16)
                    load("sync", Cm, C_d[li].rearrange("(tc p) f -> p tc f", p=128))
                    load("sync", Sm, Sm_d[li].rearrange("(tc p) f -> p tc f", p=128))
                    load("sync", STm, ST_d[li].rearrange("(tc p) f -> p tc f", p=128))
                    wtab = alloc(ph, "wtab", [128, 4, ntc]); load("sync", wtab, wtab_d[li])
                    hid2 = alloc(ph, "hid2", [64, L])
                    with ExitStack() as sub:
                        zT = alloc(sub, "zT", [33, L]); load("sync", zT, zT_d[li])
                        w1 = alloc(sub, "hw1", [33, 64]); load("sync", w1, hy_w1_d)
                        w2 = alloc(sub, "hw2", [64, 64]); load("sync", w2, hy_w2_d)
                        hb1 = alloc(sub, "hb1", [64, 1]); load("sync", hb1, hy_b1_d)
                        hb2 = alloc(sub, "hb2", [64, 1]); load("sync", hb2, hy_b2_d)
                        hfr = alloc(sub, "hfr", [64, 1]); load("sync", hfr, hy_freq_d)
                        fb = alloc(sub, "fb", [64, 2])
                        E("vector", "tensor_tensor", hb1.bs + hfr.bs, fb.bs, out=fb.t[:, 0:1], in0=hb1.t[:], in1=hfr.t[:],
                          op=ALU.mult)
                        E("vector", "tensor_tensor", hb2.bs + hfr.bs, fb.bs, out=fb.t[:, 1:2], in0=hb2.t[:], in1=hfr.t[:],
                          op=ALU.mult)
                        hid = [alloc(sub, "hid1", [64, L]), hid2]
                        sarg = alloc(sub, "sarg", [64, L])
                        sint = alloc(sub, "sint", [64, L], I32)
                        sflt = alloc(sub, "sflt", [64, L])
                        for layer in range(2):
                            src = zT if layer == 0 else hid[0]
                            wl = w1 if layer == 0 else w2
                            kdim = 33 if layer == 0 else 64
                            blk = min(L, 512)
                            for b0 in range(0, L, blk):
                                pb = pbanks.next()
                                MM(pb.t[0:64, 0:blk], wl.t[0:kdim, :], src.t[0:kdim, b0:b0 + blk], True, True,
                                   wl.bs + src.bs, pb.bs)
                                E("vector", "tensor_scalar", pb.bs + hfr.bs + fb.bs, sarg.bs, out=sarg.t[:, b0:b0 + blk],
                                  in0=pb.t[0:64, 0:blk], scalar1=hfr.t[:], scalar2=fb.t[:, layer:layer + 1],
                                  op0=ALU.mult, op1=ALU.add)
                            E("vector", "tensor_scalar", sarg.bs, sarg.bs, out=sarg.t[:], in0=sarg.t[:],
                              scalar1=float(1.0 / (2 * math.pi)), scalar2=8.0, op0=ALU.mult, op1=ALU.add)
                            E("vector", "tensor_copy", sarg.bs, sint.bs, out=sint.t[:], in_=sarg.t[:])
                            E("vector", "tensor_copy", sint.bs, sflt.bs, out=sflt.t[:], in_=sint.t[:])
                            E("vector", "tensor_tensor", sarg.bs + sflt.bs, sarg.bs, out=sarg.t[:], in0=sarg.t[:],
                              in1=sflt.t[:], op=ALU.subtract)
                            E("vector", "tensor_single_scalar", sarg.bs, sflt.bs, out=sflt.t[:], in_=sarg.t[:], scalar=0.5,
                              op=ALU.is_gt)
                            E("vector", "tensor_tensor", sarg.bs + sflt.bs, sarg.bs, out=sarg.t[:], in0=sarg.t[:],
                              in1=sflt.t[:], op=ALU.subtract)
                            E("scalar", "activation", sarg.bs, hid[layer].bs, out=hid[layer].t[:], in_=sarg.t[:],
                              func=AF.Sin, scale=float(2 * math.pi))
                        S.barrier()
                        dbg("hid2_%d" % pi, hid2, hid2.t[:])
                        if stop == (pi, 1.5): S.enabled = False

                    w3c_r = ring(ph, "w3c", [64, 4, 128], F32, 2)
                    b3c_r = ring(ph, "b3c", [128, 4, 128], F32, 2)
                    dec_r = ring(ph, "decf", [128, ntc, 128], F32, 2)
                    kf = alloc(ph, "kf", [128, 4, 128])
                    kff = alloc(ph, "kff", [128, 2, 128])
                    kfb = alloc(ph, "kfb", [128, 2, 128])
                    kpm = alloc(ph, "kpm", [128, ntc, 2, 2, 128], BF16)
                    TA = alloc(ph, "TA", [128, ntc, 2, 128])
                    TB = alloc(ph, "TB", [128, ntc, 2, 128])
                    TD0 = alloc(ph, "TD0", [128, 2, 128])
                    tmpd = alloc(ph, "tmpd", [128, 2, 128])
                    wgb_r = ring(ph, "wgb", [128, 8, 3, 128], BF16, 3)
                    hpre = alloc(ph, "hpre", [128, 3, 1024], BF16, nbuf=3)
                    hc = alloc(ph, "hc", [128, 3, 1024], F32, nbuf=3)
                    wb = alloc(ph, "wb", [128, 1024], BF16)
                    wT = alloc(ph, "wT", [128, 8, 128], BF16)
                    tt_r = [alloc(ph, "tt%d" % q, [128, 512]) for q in range(4)]
                    ysz = 2 * ntc * 128 if pi == 1 else 2 * 2 * 4 * 128
                    YT = alloc(ph, "YT", [128, ysz], BF16)
                    tmpz = alloc(ph, "tmpz", [128, 1024])
                    z1 = alloc(ph, "z1", [128, 1024])

                    w_in_bv = w_in_bd.rearrange("(kc p) n -> p kc n", p=128)
                    w3_v = hy_w3_d.rearrange("k (q n) -> k q n", q=4)
                    b3_v = hy_b3_d.rearrange("(q n) -> q n", q=4)
                    dec_v = dec_d[li].rearrange("(tc p) n -> p tc n", p=128)

                    for c in range(8):
                        w3c = w3c_r.next(); load("sync", w3c, w3_v[:, :, c * 128:(c + 1) * 128])
                        b3c = b3c_r.next()
                        S.dma("sync", b3c.t[:], b3_v[:, c * 128:(c + 1) * 128].partition_broadcast(128), writes=b3c.bs)
                        decf = dec_r.next(); load("sync", decf, dec_v[:, :, c * 128:(c + 1) * 128])
                        wgb = wgb_r.next()
                        for q in range(3):
                            S.dma("sync", wgb.t[:, :, q, :],
                                  w_in_bv[:, :, 2048 + q * 1024 + c * 128:2048 + q * 1024 + (c + 1) * 128],
                                  reads=[wcv_b["w_in"]], writes=wgb.bs)
                        if c == 0:
                            dbg("b3c%d" % pi, b3c, b3c.t[:])
                            if stop == (pi, 1.55): S.enabled = False
                        for tc in range(ntc):
                            pb = pbanks.next()
                            MM(pb.t[:, :], hid2.t[:, tc * 128:(tc + 1) * 128], w3c.t[:].rearrange("k q n -> k (q n)"),
                               True, True, hid2.bs + w3c.bs, pb.bs)
                            E("vector", "tensor_tensor", pb.bs + b3c.bs, kf.bs, out=kf.t[:].rearrange("p q n -> p (q n)"),
                              in0=pb.t[:, :], in1=b3c.t[:].rearrange("p q n -> p (q n)"), op=ALU.add)
                            dbc = decf.t[:, tc, :].unsqueeze(1).to_broadcast([128, 2, 128])
                            E("gpsimd", "tensor_tensor", kf.bs + decf.bs, kff.bs, out=kff.t[:], in0=kf.t[:, 0:2, :], in1=dbc,
                              op=ALU.mult)
                            E("gpsimd", "tensor_tensor", kf.bs + decf.bs, kfb.bs, out=kfb.t[:], in0=kf.t[:, 2:4, :], in1=dbc,
                              op=ALU.mult)
                            if tc == 0:
                                E("vector", "tensor_scalar", kfb.bs + mask0.bs, kfb.bs, out=kfb.t[:], in0=kfb.t[:],
                                  scalar1=mask0.t[:], scalar2=None, op0=ALU.mult)
                            E("gpsimd", "tensor_tensor", kff.bs + kfb.bs, kpm.bs, out=kpm.t[:, tc, 0, :, :], in0=kff.t[:],
                              in1=kfb.t[:], op=ALU.add)
                            E("gpsimd", "tensor_tensor", kff.bs + kfb.bs, kpm.bs, out=kpm.t[:, tc, 1, :, :], in0=kff.t[:],
                              in1=kfb.t[:], op=ALU.subtract)
                        if c == 0:
                            dbg("kpm%d" % pi, kpm, kpm.t[:], BF16)
                            if stop == (pi, 1.57): S.enabled = False
                        for fc in range(ntc):
                            pa = pbanks.next()
                            for tc in range(ntc):
                                MM(pa.t[:, 0:256], Cm.t[:, tc, fc * 128:(fc + 1) * 128], kpm.t[:, tc, 0, :, :].rearrange("p o n -> p (o n)"),
                                   tc == 0, tc == ntc - 1, Cm.bs + kpm.bs, pa.bs)
                            for tc in range(ntc):
                                MM(pa.t[:, 256:512], Sm.t[:, tc, fc * 128:(fc + 1) * 128], kpm.t[:, tc, 1, :, :].rearrange("p o n -> p (o n)"),
                                   tc == 0, tc == ntc - 1, Sm.bs + kpm.bs, pa.bs)
                            if 'A' not in HYSKIP: E("scalar", "activation", pa.bs + wtab.bs, TA.bs, out=TA.t[:, fc].rearrange("p o n -> p (o n)"),
                              in_=pa.t[:, 0:256], func=AF.Identity,
                              scale=wtab.t[:, 0, fc:fc + 1])
                            if 'B' not in HYSKIP: E("scalar", "activation", pa.bs + wtab.bs, TB.bs, out=TB.t[:, fc].rearrange("p o n -> p (o n)"),
                              in_=pa.t[:, 256:512], func=AF.Identity,
                              scale=wtab.t[:, 1, fc:fc + 1])
                            if fc == 0 and 'D' not in HYSKIP:
                                pd = pbanks.next()
                                for tc in range(ntc):
                                    MM(pd.t[:, 0:256], Sm.t[:, tc, 0:128], kpm.t[:, tc, 0, :, :].rearrange("p o n -> p (o n)"),
                                       tc == 0, tc == ntc - 1, Sm.bs + kpm.bs, pd.bs)
                                E("scalar", "activation", pa.bs + wtab.bs, TD0.bs, out=TD0.t[:].rearrange("p o n -> p (o n)"),
                                  in_=pa.t[:, 0:256], func=AF.Identity, scale=wtab.t[:, 2, 0:1])
                                E("scalar", "activation", pd.bs + wtab.bs, tmpd.bs, out=tmpd.t[:].rearrange("p o n -> p (o n)"),
                                  in_=pd.t[:, 0:256], func=AF.Identity, scale=wtab.t[:, 3, 0:1])
                                E("gpsimd", "tensor_tensor", TD0.bs + tmpd.bs, TD0.bs, out=TD0.t[:], in0=TD0.t[:],
                                  in1=tmpd.t[:], op=ALU.add)
                        if c == 0:
                            dbg("TA%d" % pi, TA, TA.t[:]); dbg("TB%d" % pi, TB, TB.t[:]); dbg("TD0_%d" % pi, TD0, TD0.t[:])
                            if stop == (pi, 1.6): S.enabled = False
                        for q in range(3):
                            for blk in range(2):
                                pb = pbanks.next()
                                for kc in range(8):
                                    MM(pb.t[:, :], wgb.t[:, kc, q, :], h1T.t[:, kc, blk * 512:(blk + 1) * 512],
                                       kc == 0, kc == 7, wgb.bs + h1T.bs, pb.bs)
                                E("scalar", "copy", pb.bs, [hpre.bs[q]], out=hpre.t[:, q, blk * 512:(blk + 1) * 512],
                                  in_=pb.t[:, :])
                            cc = q * 8 + c
                            o_tl = Tl(None); o_tl.bs = [hc.bs[q]]
                            i_tl = Tl(None); i_tl.bs = [hpre.bs[q]]
                            conv("vector", o_tl, hc.t[:, q, :], i_tl, hpre.t[:, q, :],
                                 (hcw, hcb), hcw.t[:, cc, :], hcb.t[:, cc:cc + 1], 3, 1, nseq, L)
                        if c == 0:
                            dbg("hc%d" % pi, hc, hc.t[:])
                            if stop == (pi, 1.7): S.enabled = False
                        for od in range(2):
                            if od == 0:
                                w_ap, w_bs = hc.t[:, 2, :], [hc.bs[2]]
                                x_ap, x_bs = hc.t[:, 0, :], [hc.bs[0]]
                            else:
                                w_ap, w_bs = z1.t[:], z1.bs
                                x_ap, x_bs = hc.t[:, 1, :], [hc.bs[1]]
                            E("scalar", "copy", w_bs, wb.bs, out=wb.t[:], in_=w_ap)
                            pt = pbf.next()
                            for tt in range(8):
                                slot = (tt % 2) * 4 + tt // 2 if pi == 0 else tt
                                TR(pt.t[:, slot * 128:(slot + 1) * 128], wb.t[:, tt * 128:(tt + 1) * 128], identb.t[:],
                                   wb.bs + identb.bs, pt.bs)
                            E("vector", "tensor_copy", pt.bs, wT.bs, out=wT.t[:].rearrange("p a n -> p (a n)"), in_=pt.t[:, :])
                            if pi == 0:
                                YTv = YT.t[:].rearrange("p (fc r s n) -> p fc r s n", fc=2, r=2, s=4)
                                wTv = wT.t[:].rearrange("p (tc s) n -> p tc (s n)", s=4)
                                groups = [(fc, 1) for fc in range(2)]
                            else:
                                YTv = YT.t[:].rearrange("p (fc r n) -> p fc r n", fc=ntc, r=2)
                                groups = [(0, 4), (4, 4)]
                            for (f0, nf) in groups:
                                pre = pbanks.next()
                                pim = pbanks.next()
                                if pi == 0:
                                    fc = f0
                                    for (pbk, Mx) in ((pre, Cm), (pim, Sm)):
                                        for tc in range(ntc):
                                            MM(pbk.t[:, :], Mx.t[:, tc, fc * 128:(fc + 1) * 128], wTv[:, tc, :],
                                               tc == 0, tc == ntc - 1, Mx.bs + wT.bs, pbk.bs)
                                    ta = TA.t[:, fc, od, :].unsqueeze(1).to_broadcast([128, 4, 128])
                                    tb_ = TB.t[:, fc, od, :].unsqueeze(1).to_broadcast([128, 4, 128])
                                    if fc == 0:
                                        td = TD0.t[:, od, :].unsqueeze(1).to_broadcast([128, 4, 128])
                                        td_bs = TD0.bs
                                    else:
                                        td = ta
                                        td_bs = TA.bs
                                    ure = pre.t[:, :].rearrange("p (s n) -> p s n", s=4)
                                    uim = pim.t[:, :].rearrange("p (s n) -> p s n", s=4)
                                    yre = YTv[:, fc, 0, :, :]
                                    yim = YTv[:, fc, 1, :, :]
                                    shp = "p (s n) -> p s n"
                                    tv = [t_.t[:, :].rearrange(shp, s=4) for t_ in tt_r]
                                    E("vector", "tensor_tensor", pre.bs + TA.bs, tt_r[0].bs, out=tv[0], in0=ure, in1=ta, op=ALU.mult)
                                    E("vector", "tensor_tensor", pim.bs + TB.bs, tt_r[1].bs, out=tv[1], in0=uim, in1=tb_, op=ALU.mult)
                                    E("vector", "tensor_tensor", pre.bs + TB.bs, tt_r[2].bs, out=tv[2], in0=ure, in1=tb_, op=ALU.mult)
                                    E("vector", "tensor_tensor", pim.bs + td_bs, tt_r[3].bs, out=tv[3], in0=uim, in1=td, op=ALU.mult)
                                    E("gpsimd", "tensor_tensor", tt_r[0].bs + tt_r[1].bs, YT.bs, out=yre, in0=tv[0], in1=tv[1], op=ALU.subtract)
                                    E("gpsimd", "tensor_tensor", tt_r[2].bs + tt_r[3].bs, YT.bs, out=yim, in0=tv[2], in1=tv[3], op=ALU.add)
                                else:
                                    for ff in range(nf):
                                        fc = f0 + ff
                                        for (pbk, Mx) in ((pre, Cm), (pim, Sm)):
                                            for tc in range(ntc):
                                                MM(pbk.t[:, ff * 128:(ff + 1) * 128], Mx.t[:, tc, fc * 128:(fc + 1) * 128],
                                                   wT.t[:, tc, :], tc == 0, tc == ntc - 1, Mx.bs + wT.bs, pbk.bs)
                                    ta = TA.t[:, f0:f0 + nf, od, :]
                                    tb_ = TB.t[:, f0:f0 + nf, od, :]
                                    ure = pre.t[:, :].rearrange("p (s n) -> p s n", s=4)
                                    uim = pim.t[:, :].rearrange("p (s n) -> p s n", s=4)
                                    yre = YTv[:, f0:f0 + nf, 0, :]
                                    yim = YTv[:, f0:f0 + nf, 1, :]
                                    tv = [t_.t[:, :].rearrange("p (s n) -> p s n", s=4) for t_ in tt_r]
                                    E("vector", "tensor_tensor", pre.bs + TA.bs, tt_r[0].bs, out=tv[0], in0=ure, in1=ta, op=ALU.mult)
                                    E("vector", "tensor_tensor", pim.bs + TB.bs, tt_r[1].bs, out=tv[1], in0=uim, in1=tb_, op=ALU.mult)
                                    E("vector", "tensor_tensor", pre.bs + TB.bs, tt_r[2].bs, out=tv[2], in0=ure, in1=tb_, op=ALU.mult)
                                    E("vector", "tensor_tensor", pim.bs + TA.bs, tt_r[3].bs, out=tv[3], in0=uim, in1=ta, op=ALU.mult)
                                    if f0 == 0:
                                        E("vector", "tensor_tensor", pim.bs + TD0.bs, tt_r[3].bs, out=tt_r[3].t[:, 0:128],
                                          in0=pim.t[:, 0:128], in1=TD0.t[:, od, :], op=ALU.mult)
                                    E("gpsimd", "tensor_tensor", tt_r[0].bs + tt_r[1].bs, YT.bs, out=yre, in0=tv[0], in1=tv[1], op=ALU.subtract)
                                    E("gpsimd", "tensor_tensor", tt_r[2].bs + tt_r[3].bs, YT.bs, out=yim, in0=tv[2], in1=tv[3], op=ALU.add)
                            sk = skipT.t[:, od, c:c + 1]
                            if od == 0:
                                o_ap_full, o_bs = z1.t[:], z1.bs
                            else:
                                o_ap_full, o_bs = ybT.t[:, c, :], [ybT.bs[c]]
                            for blk in range(2):
                                pb = pbanks.next()
                                if pi == 0:
                                    for s2 in range(2):
                                        sq = blk * 2 + s2
                                        k = 0
                                        for fc in range(2):
                                            for r, Mx in ((0, Cm), (1, STm)):
                                                MM(pb.t[:, s2 * 256:(s2 + 1) * 256], YTv[:, fc, r, sq, :], Mx.t[:, fc, :],
                                                   k == 0, k == 3, YT.bs + Mx.bs, pb.bs)
                                                k += 1
                                else:
                                    k = 0
                                    for fc in range(ntc):
                                        for r, Mx in ((0, Cm), (1, STm)):
                                            MM(pb.t[:, :], YTv[:, fc, r, :], Mx.t[:, fc, blk * 512:(blk + 1) * 512],
                                               k == 0, k == 2 * ntc - 1, YT.bs + Mx.bs, pb.bs)
                                            k += 1
                                sl = slice(blk * 512, (blk + 1) * 512)
                                E("vector", "scalar_tensor_tensor", w_bs + skipT.bs + pb.bs, tmpz.bs, out=tmpz.t[:, sl],
                                  in0=w_ap[:, sl], scalar=sk, in1=pb.t[:, :], op0=ALU.mult, op1=ALU.add)
                                E("gpsimd", "tensor_tensor", tmpz.bs + x_bs, o_bs, out=o_ap_full[:, sl], in0=tmpz.t[:, sl],
                                  in1=x_ap[:, sl], op=ALU.mult)
                    dbg("ybT%d" % pi, ybT, ybT.t[:], BF16)
                    S.barrier()
                    if stop == (pi, 2): S.enabled = False

                yaT = alloc(mxs, "yaT", [128, 8, 1024], BF16, nbuf=8)
                with ExitStack() as ph:
                    wbf_r = ring(ph, "rwbf", [128, 8, 256], BF16, 4)
                    gwb_r = ring(ph, "gwb", [128, 4, 2, 256], BF16, 2)
                    xpre = alloc(ph, "xpre", [128, 2, 1024], F32, nbuf=2)
                    xc = alloc(ph, "xc", [128, 2, 1024], F32, nbuf=2)
                    xcb = alloc(ph, "xcb", [128, 2, 1024], BF16)
                    rg = alloc(ph, "rg", [128, 1024]); ig = alloc(ph, "ig", [128, 1024])
                    at = alloc(ph, "at", [128, 1024]); st_ = alloc(ph, "st", [128, 1024]); ut = alloc(ph, "ut", [128, 1024])
                    hdir = [alloc(ph, "hf", [128, 1024]), alloc(ph, "hb", [128, 1024])]
                    glu = alloc(ph, "glu", [128, 2, 1024], F32, nbuf=2)
                    w_in_bv = w_in_bd.rearrange("(kc p) n -> p kc n", p=128)
                    for hd in range(4):
                        wxb = wbf_r.next()
                        S.dma("sync", wxb.t[:], w_in_bv[:, :, hd * 256:(hd + 1) * 256], reads=[wcv_b["w_in"]], writes=wxb.bs)
                        wgb2 = wbf_r.next()
                        S.dma("sync", wgb2.t[:], w_in_bv[:, :, 1024 + hd * 256:1024 + (hd + 1) * 256], reads=[wcv_b["w_in"]],
                              writes=wgb2.bs)
                        gwb = gwb_r.next()
                        for d in range(2):
                            for g in range(2):
                                S.dma("sync", gwb.t[:, d * 2 + g, :, :],
                                      gate_w_bd[d, g, hd].rearrange("(ic p) j -> p ic j", p=128), reads=[wcv_b["gate"]],
                                      writes=gwb.bs)
                        for jc in range(2):
                            for blk in range(2):
                                sl = slice(blk * 512, (blk + 1) * 512)
                                pb = pbanks.next()
                                for kc in range(8):
                                    MM(pb.t[:, :], wxb.t[:, kc, jc * 128:(jc + 1) * 128], h1T.t[:, kc, sl],
                                       kc == 0, kc == 7, wxb.bs + h1T.bs, pb.bs)
                                E("scalar", "copy", pb.bs, [xpre.bs[jc]], out=xpre.t[:, jc, sl], in_=pb.t[:, :])
                                pb = pbanks.next()
                                for kc in range(8):
                                    MM(pb.t[:, :], wgb2.t[:, kc, jc * 128:(jc + 1) * 128], h1T.t[:, kc, sl],
                                       kc == 0, kc == 7, wgb2.bs + h1T.bs, pb.bs)
                                E("scalar", "activation", pb.bs, [glu.bs[jc]], out=glu.t[:, jc, sl], in_=pb.t[:, :],
                                  func=AF.Gelu_apprx_tanh)
                            ch = hd * 2 + jc
                            o_tl = Tl(None); o_tl.bs = [xc.bs[jc]]
                            i_tl = Tl(None); i_tl.bs = [xpre.bs[jc]]
                            conv("vector", o_tl, xc.t[:, jc, :], i_tl, xpre.t[:, jc, :], (rcw, rcb), rcw.t[:, ch, :],
                                 rcb.t[:, ch:ch + 1], 4, 2, nseq, L)
                            E("scalar", "copy", [xc.bs[jc]], xcb.bs, out=xcb.t[:, jc, :], in_=xc.t[:, jc, :])
                        for jc in range(2):
                            ch = hd * 2 + jc
                            for d in range(2):
                                for g in range(2):
                                    dst = rg if g == 0 else ig
                                    for blk in range(2):
                                        sl = slice(blk * 512, (blk + 1) * 512)
                                        pb = pbanks.next()
                                        for ic in range(2):
                                            MM(pb.t[:, :], gwb.t[:, d * 2 + g, ic, jc * 128:(jc + 1) * 128], xcb.t[:, ic, sl],
                                               ic == 0, ic == 1, gwb.bs + xcb.bs, pb.bs)
                                        E("scalar", "activation", pb.bs + gbT.bs, dst.bs, out=dst.t[:, sl], in_=pb.t[:, :],
                                          func=AF.Sigmoid, bias=gbT.t[:, d, g, ch:ch + 1], scale=1.0)
                                E("scalar", "activation", rg.bs + nsp.bs, at.bs, out=at.t[:], in_=rg.t[:], func=AF.Exp,
                                  scale=nsp.t[:, d, ch:ch + 1])
                                E("gpsimd", "tensor_tensor", at.bs, st_.bs, out=st_.t[:], in0=at.t[:], in1=at.t[:], op=ALU.mult)
                                E("scalar", "activation", st_.bs, st_.bs, out=st_.t[:], in_=st_.t[:], func=AF.Sqrt,
                                  bias=1.0, scale=-1.0)
                                E("gpsimd", "tensor_tensor", ig.bs + [xc.bs[jc]], ig.bs, out=ig.t[:], in0=ig.t[:],
                                  in1=xc.t[:, jc, :], op=ALU.mult)
                                E("vector", "tensor_tensor", ig.bs + st_.bs, ut.bs, out=ut.t[:], in0=ig.t[:], in1=st_.t[:],
                                  op=ALU.mult)
                                hd_t = hdir[d]
                                for sq in range(nseq):
                                    lo, hi = sq * L, (sq + 1) * L
                                    init = h0T.t[:, d, ch:ch + 1] if pi == 1 else 0.0
                                    if d == 0:
                                        o_, a_, u_ = hd_t.t[:, lo:hi], at.t[:, lo:hi], ut.t[:, lo:hi]
                                    else:
                                        o_, a_, u_ = hd_t.t[:, lo:hi][:, ::-1], at.t[:, lo:hi][:, ::-1], ut.t[:, lo:hi][:, ::-1]
                                    E("vector", "tensor_tensor_scan", at.bs + ut.bs + h0T.bs, hd_t.bs, out=o_, data0=a_,
                                      data1=u_, initial=init, op0=ALU.mult, op1=ALU.add)
                                if pi == 0:
                                    hv = hd_t.t[:].rearrange("p (s l) -> p s l", s=4)
                                    col = L - 1 if d == 0 else 0
                                    E("gpsimd", "tensor_copy", hd_t.bs, nst.bs, out=nst.t[:, :, d, ch:ch + 1],
                                      in_=hv[:, :, col:col + 1])
                            E("vector", "tensor_tensor", hdir[0].bs + hdir[1].bs, hdir[0].bs, out=hdir[0].t[:], in0=hdir[0].t[:],
                              in1=hdir[1].t[:], op=ALU.add)
                            E("gpsimd", "tensor_tensor", hdir[0].bs + [glu.bs[jc]], [yaT.bs[ch]], out=yaT.t[:, ch, :],
                              in0=hdir[0].t[:], in1=glu.t[:, jc, :], op=ALU.mult)
                    dbg("yaT%d" % pi, yaT, yaT.t[:], BF16)
                    if pi == 0:
                        dbg("nst", nst, nst.t[:])
                    S.barrier()
                    if stop == (pi, 3): S.enabled = False

                mT = alloc(mxs, "mT", [128, 8, 1024], BF16, nbuf=8)
                with ExitStack() as ph:
                    wbf_r = ring(ph, "mwbf", [128, 8, 256], BF16, 12)
                    gat = ring(ph, "gat", [128, 512], F32, 2)
                    gbt = ring(ph, "gbt", [128, 512], F32, 2)
                    mat = ring(ph, "mat", [128, 512], F32, 2)
                    w_in_v = w_in_bd.rearrange("(kc p) n -> p kc n", p=128)
                    wa_v = w_a_bd.rearrange("(kc p) n -> p kc n", p=128)
                    wb_v = w_b_bd.rearrange("(kc p) n -> p kc n", p=128)
                    for cg in range(4):
                        ws = []
                        for (src, bk) in ((wa_v[:, :, cg * 256:(cg + 1) * 256], "w_a"), (wb_v[:, :, cg * 256:(cg + 1) * 256], "w_b"),
                                          (w_in_v[:, :, 5120 + cg * 256:5120 + (cg + 1) * 256], "w_in"),
                                          (w_in_v[:, :, 6144 + cg * 256:6144 + (cg + 1) * 256], "w_in")):
                            wbf = wbf_r.next()
                            S.dma("sync", wbf.t[:], src, reads=[wcv_b[bk]], writes=wbf.bs)
                            ws.append(wbf)
                        for jc in range(2):
                            c = cg * 2 + jc
                            cs = slice(jc * 128, (jc + 1) * 128)
                            for blk in range(2):
                                sl = slice(blk * 512, (blk + 1) * 512)
                                pya = pbanks.next()
                                for kc in range(8):
                                    MM(pya.t[:, :], ws[0].t[:, kc, cs], yaT.t[:, kc, sl], kc == 0, kc == 7, ws[0].bs + yaT.bs, pya.bs)
                                pyb = pbanks.next()
                                for kc in range(8):
                                    MM(pyb.t[:, :], ws[1].t[:, kc, cs], ybT.t[:, kc, sl], kc == 0, kc == 7, ws[1].bs + ybT.bs, pyb.bs)
                                pga = pbanks.next()
                                for kc in range(8):
                                    MM(pga.t[:, :], ws[2].t[:, kc, cs], h1T.t[:, kc, sl], kc == 0, kc == 7, ws[2].bs + h1T.bs, pga.bs)
                                pgb = pbanks.next()
                                for kc in range(8):
                                    MM(pgb.t[:, :], ws[3].t[:, kc, cs], h1T.t[:, kc, sl], kc == 0, kc == 7, ws[3].bs + h1T.bs, pgb.bs)
                                ga = gat.next(); gb = gbt.next(); ma = mat.next()
                                E("scalar", "activation", pga.bs, ga.bs, out=ga.t[:], in_=pga.t[:, :], func=AF.Sigmoid)
                                E("scalar", "activation", pgb.bs, gb.bs, out=gb.t[:], in_=pgb.t[:, :], func=AF.Sigmoid)
                                E("vector", "tensor_tensor", pya.bs + ga.bs, ma.bs, out=ma.t[:], in0=pya.t[:, :], in1=ga.t[:], op=ALU.mult)
                                E("vector", "tensor_tensor", pyb.bs + gb.bs, gb.bs, out=gb.t[:], in0=pyb.t[:, :], in1=gb.t[:], op=ALU.mult)
                                E("gpsimd", "tensor_tensor", ma.bs + gb.bs, [mT.bs[c]], out=mT.t[:, c, sl], in0=ma.t[:], in1=gb.t[:], op=ALU.add)
                    dbg("mT%d" % pi, mT, mT.t[:], BF16)
                    S.barrier()
                    if stop == (pi, 4): S.enabled = False

                with ExitStack() as ph:
                    wo = alloc(ph, "wo", [128, 8, DM], BF16, nbuf=4)
                    wo_v = w_o_bd.rearrange("(kc p) n -> p kc n", p=128)
                    for cg in range(4):
                        S.dma("sync", wo.t[:, :, cg * 256:(cg + 1) * 256], wo_v[:, :, cg * 256:(cg + 1) * 256],
                              reads=[wcv_b["w_o"]], writes=[wo.bs[cg]])
                    g1r = alloc(ph, "g1r", [128, DM]); load_row(g1r, modrow_d[pi, 0, :], [modrow_b])
                    lg = alloc(ph, "lg", [128, DM]); load_row(lg, ln_d[0])
                    lb = alloc(ph, "lb", [128, DM]); load_row(lb, ln_d[1])
                    xr = ring(ph, "xt3", [128, DM], F32, 2)
                    pr = ring(ph, "pt3", [128, DM], F32, 2)
                    yt_r = ring(ph, "yt3", [128, DM], F32, 2)
                    lnt = (alloc(ph, "stats", [128, 2, 6]), alloc(ph, "mv", [128, 2]), alloc(ph, "rstd", [128, 1]))
                    for i in range(8):
                        xt = xr.next()
                        load("sync", xt, x_d[i * 128:(i + 1) * 128, :])
                        if pi == 1:
                            pt = pr.next()
                            load("sync", pt, pos_d[i * 128:(i + 1) * 128, :])
                            E("gpsimd", "tensor_tensor", xt.bs + pt.bs, xt.bs, out=xt.t[:], in0=xt.t[:], in1=pt.t[:], op=ALU.add)
                        yt = yt_r.next()
                        for hf in range(2):
                            pb = pbanks.next()
                            for kc in range(8):
                                MM(pb.t[:, :], mT.t[:, kc, i * 128:(i + 1) * 128], wo.t[:, kc, hf * 512:(hf + 1) * 512],
                                   kc == 0, kc == 7, mT.bs + wo.bs, pb.bs)
                            E("vector", "tensor_tensor", pb.bs + g1r.bs, yt.bs, out=yt.t[:, hf * 512:(hf + 1) * 512],
                              in0=pb.t[:, :], in1=g1r.t[:, hf * 512:(hf + 1) * 512], op=ALU.mult)
                        E("vector", "scalar_tensor_tensor", xt.bs + yt.bs, yt.bs, out=yt.t[:], in0=xt.t[:],
                          scalar=ALPHA, in1=yt.t[:], op0=ALU.mult, op1=ALU.add)
                        layer_norm_tile(lnt, yt, lg, lb)
                        S.dma("gpsimd", x2_d[pi, i * 128:(i + 1) * 128, :], yt.t[:], reads=yt.bs, writes=[x2_db[pi][i]])
                        if i == 0:
                            dbg("x2_%d" % pi, yt, yt.t[:])
                    S.barrier()
                    if stop == (pi, 5): S.enabled = False

              with ExitStack() as ph:
                  pbanks = pbanks4
                  wqb = alloc(ph, "wqb", [128, 8, 2048], BF16, nbuf=8)
                  UVN = int(os.environ.get('PEER_UVN', '16'))
                  uv_r = ring(ph, "uvg", [128, 2 * DM], BF16, UVN)
                  wq_v = wq_bd.rearrange("(kc p) n -> p kc n", p=128)
                  for cg in range(8):
                      S.dma("sync", wqb.t[:, :, cg * 256:(cg + 1) * 256], wq_v[:, :, cg * 256:(cg + 1) * 256],
                            reads=[wcv_b["wq"]], writes=[wqb.bs[cg]])
                  sh2r = alloc(ph, "sh2r", [128, DM]); load_row(sh2r, modrow_d[pi, 1, :], [modrow_b])
                  sc2r = alloc(ph, "sc2r", [128, DM]); load_row(sc2r, modrow_d[pi, 2, :], [modrow_b])
                  g2r = alloc(ph, "g2r", [128, DM]); load_row(g2r, modrow_d[pi, 3, :], [modrow_b])
                  lg = alloc(ph, "lg2", [128, DM]); load_row(lg, ln_d[2])
                  lb = alloc(ph, "lb2", [128, DM]); load_row(lb, ln_d[3])
                  lnt = (alloc(ph, "stats2", [128, 2, 6]), alloc(ph, "mv2", [128, 2]), alloc(ph, "rstd2", [128, 1]))
                  h2 = alloc(ph, "h2", [128, DM])
                  h2b_l = [alloc(ph, "h2b%d" % q, [128, DM], BF16) for q in range(2)]
                  h2T = alloc(ph, "h2T", [128, 8, 128], BF16)
                  qT = alloc(ph, "qT", [128, 16, 128])
                  sc = alloc(ph, "sc", [128, 16, 128])
                  scw = alloc(ph, "scw", [128, 128]); scw2 = alloc(ph, "scw2", [128, 128])
                  tops = alloc(ph, "tops", [128, 16, 16], nbuf=16); topi = alloc(ph, "topi", [128, 16, 16], U32, nbuf=16)

                  def _sub(tl, k):
                      w = Tl(None); w.bs = [tl.bs[k]]; return w
                  tops_b = [_sub(tops, k) for k in range(16)]; topi_b = [_sub(topi, k) for k in range(16)]
                  topif = alloc(ph, "topif", [128, 16, 16])
                  cand = alloc(ph, "cand", [128, 8, 16, 16]); candw = alloc(ph, "candw", [128, 256]); candw2 = alloc(ph, "candw2", [128, 256])
                  bs_ = alloc(ph, "bests", [128, 8, 16], nbuf=8); bp = alloc(ph, "bestp", [128, 8, 16], U32, nbuf=8)
                  bs_b = [_sub(bs_, k) for k in range(8)]; bp_b = [_sub(bp, k) for k in range(8)]
                  k1 = alloc(ph, "k1", [128, 8, 16], U32); k2 = alloc(ph, "k2", [128, 8, 16], U32)
                  k1f = alloc(ph, "k1f", [128, 8, 16]); k2f = alloc(ph, "k2f", [128, 8, 16])
                  oh = cand; i1f = alloc(ph, "i1f", [128, 8, 16]); i2f = alloc(ph, "i2f", [128, 8, 16])
                  io16 = alloc(ph, "io16", [128, 16]); io16i = alloc(ph, "io16i", [128, 16], I32)
                  E("gpsimd", "iota", [], io16i.bs, out=io16i.t[:], pattern=[[1, 16]], base=0, channel_multiplier=0)
                  E("vector", "tensor_copy", io16i.bs, io16.bs, out=io16.t[:], in_=io16i.t[:])
                  eidx_l = [alloc(ph, "eidx%d" % q, [128, 128], I32) for q in range(2)]
                  gw_l = [alloc(ph, "gw%d" % q, [128, 8, 16]) for q in range(2)]
                  zs = alloc(ph, "zs", [128, 8])
                  act = alloc(ph, "act", [128, 128]); wgt = alloc(ph, "wgt", [128, 128])
                  dg_r = ring(ph, "dg", [128, 128], BF16, 8)
                  ot_r = ring(ph, "ot", [128, DM], F32, 2)
                  x2_r = ring(ph, "x2t", [128, DM], F32, 2)

                  def top16x2(args_a, args_b):
                      (sa, sapa, wa_, wapa, osa, oia, tsa, tia) = args_a
                      (sb_, sapb, wb_, wapb, osb, oib, tsb, tib) = args_b
                      for (st, sap, os_, ts) in ((sa, sapa, osa, tsa), (sb_, sapb, osb, tsb)):
                          E("vector", "max", st.bs, ts.bs, out=os_[:, 0:8], in_=sap)
                      for (st, sap, os_, oi, ts, ti) in ((sa, sapa, osa, oia, tsa, tia), (sb_, sapb, osb, oib, tsb, tib)):
                          E("vector", "max_index", st.bs + ts.bs, ti.bs, out=oi[:, 0:8], in_max=os_[:, 0:8], in_values=sap)
                      for (st, sap, wt, wap, os_, ts) in ((sa, sapa, wa_, wapa, osa, tsa), (sb_, sapb, wb_, wapb, osb, tsb)):
                          E("vector", "match_replace", st.bs + ts.bs, wt.bs, out=wap, in_to_replace=os_[:, 0:8], in_values=sap, imm_value=-1e30)
                      for (wt, wap, os_, ts) in ((wa_, wapa, osa, tsa), (wb_, wapb, osb, tsb)):
                          E("vector", "max", wt.bs, ts.bs, out=os_[:, 8:16], in_=wap)
                      for (wt, wap, os_, oi, ts, ti) in ((wa_, wapa, osa, oia, tsa, tia), (wb_, wapb, osb, oib, tsb, tib)):
                          E("vector", "max_index", wt.bs + ts.bs, ti.bs, out=oi[:, 8:16], in_max=os_[:, 8:16], in_values=wap)

                  def top16(src_tl, src_ap, work_tl, work_ap, out_s, out_i, o_tl_s, o_tl_i):
                      E("vector", "max", src_tl.bs, o_tl_s.bs, out=out_s[:, 0:8], in_=src_ap)
                      E("vector", "max_index", src_tl.bs + o_tl_s.bs, o_tl_i.bs, out=out_i[:, 0:8], in_max=out_s[:, 0:8], in_values=src_ap)
                      E("vector", "match_replace", src_tl.bs + o_tl_s.bs, work_tl.bs, out=work_ap, in_to_replace=out_s[:, 0:8],
                        in_values=src_ap, imm_value=-1e30)
                      E("vector", "max", work_tl.bs, o_tl_s.bs, out=out_s[:, 8:16], in_=work_ap)
                      E("vector", "max_index", work_tl.bs + o_tl_s.bs, o_tl_i.bs, out=out_i[:, 8:16], in_max=out_s[:, 8:16], in_values=work_ap)

                  def front(i):
                      sl_ = i % 2
                      h2b = h2b_l[sl_]; eidx = eidx_l[sl_]; gw = gw_l[sl_]
                      x2t = x2_r.next()
                      S.dma("sync", x2t.t[:], x2_d[pi, i * 128:(i + 1) * 128, :], reads=[x2_db[pi][i]], writes=x2t.bs)
                      yield
                      E("vector", "tensor_tensor", x2t.bs + sc2r.bs, h2.bs, out=h2.t[:], in0=x2t.t[:], in1=sc2r.t[:], op=ALU.mult)
                      E("vector", "tensor_tensor", h2.bs + sh2r.bs, h2.bs, out=h2.t[:], in0=h2.t[:], in1=sh2r.t[:], op=ALU.add)
                      yield
                      E("scalar", "copy", h2.bs, h2b.bs, out=h2b.t[:], in_=h2.t[:])
                      yield
                      pt = pbf.next()
                      for c in range(8):
                          TR(pt.t[:, c * 128:(c + 1) * 128], h2b.t[:, c * 128:(c + 1) * 128], identb.t[:], h2b.bs + identb.bs, pt.bs)
                      yield
                      E("scalar", "copy", pt.bs, h2T.bs, out=h2T.t[:].rearrange("p a n -> p (a n)"), in_=pt.t[:, :])
                      yield
                      for g4 in range(4):
                          pb = pbanks.next()
                          for gg in range(4):
                              hp = g4 * 4 + gg
                              for kc in range(8):
                                  MM(pb.t[:, gg * 128:(gg + 1) * 128], wqb.t[:, kc, hp * 128:(hp + 1) * 128], h2T.t[:, kc, :],
                                     kc == 0, kc == 7, wqb.bs + h2T.bs, pb.bs)
                          yield
                          E("scalar", "copy", pb.bs, qT.bs, out=qT.t[:, g4 * 4:(g4 + 1) * 4, :].rearrange("p a n -> p (a n)"), in_=pb.t[:, :])
                          yield
                      for g4 in range(4):
                          pb = pbanks.next()
                          for gg in range(4):
                              hp = g4 * 4 + gg
                              MM(pb.t[:, gg * 128:(gg + 1) * 128], qT.t[:, hp, :], skT.t[:, hp % 2, :], True, True,
                                 qT.bs + skT.bs, pb.bs)
                          yield
                          E("scalar", "copy", pb.bs, sc.bs, out=sc.t[:, g4 * 4:(g4 + 1) * 4, :].rearrange("p a n -> p (a n)"), in_=pb.t[:, :])
                          yield
                      for hp in range(0, 16, 2):
                          top16x2((sc, sc.t[:, hp, :], scw, scw.t[:], tops.t[:, hp, :], topi.t[:, hp, :], tops_b[hp], topi_b[hp]),
                                  (sc, sc.t[:, hp + 1, :], scw2, scw2.t[:], tops.t[:, hp + 1, :], topi.t[:, hp + 1, :], tops_b[hp + 1], topi_b[hp + 1]))
                          yield
                          yield
                      E("vector", "tensor_copy", topi.bs, topif.bs, out=topif.t[:], in_=topi.t[:])
                      tv = tops.t[:].rearrange("p (h q) k -> p h q k", q=2)
                      tiv = topif.t[:].rearrange("p (h q) k -> p h q k", q=2)
                      E("vector", "tensor_tensor", tops.bs, cand.bs, out=cand.t[:],
                        in0=tv[:, :, 0, :].unsqueeze(3).to_broadcast([128, 8, 16, 16]),
                        in1=tv[:, :, 1, :].unsqueeze(2).to_broadcast([128, 8, 16, 16]), op=ALU.add)
                      yield
                      for h in range(0, 8, 2):
                          top16x2((cand, cand.t[:, h].rearrange("p a b -> p (a b)"), candw, candw.t[:], bs_.t[:, h, :], bp.t[:, h, :], bs_b[h], bp_b[h]),
                                  (cand, cand.t[:, h + 1].rearrange("p a b -> p (a b)"), candw2, candw2.t[:], bs_.t[:, h + 1, :], bp.t[:, h + 1, :], bs_b[h + 1], bp_b[h + 1]))
                          yield
                          yield
                      E("vector", "tensor_single_scalar", bp.bs, k1.bs, out=k1.t[:], in_=bp.t[:], scalar=4, op=ALU.logical_shift_right)
                      E("vector", "tensor_single_scalar", bp.bs, k2.bs, out=k2.t[:], in_=bp.t[:], scalar=15, op=ALU.bitwise_and)
                      E("vector", "tensor_copy", k1.bs, k1f.bs, out=k1f.t[:], in_=k1.t[:])
                      E("vector", "tensor_copy", k2.bs, k2f.bs, out=k2f.t[:], in_=k2.t[:])
                      yield
                      iob = io16.t[:].unsqueeze(1).unsqueeze(1).to_broadcast([128, 8, 16, 16])
                      for (kf_, q, dst) in ((k1f, 0, i1f), (k2f, 1, i2f)):
                          E("vector", "tensor_tensor", kf_.bs + io16.bs, oh.bs, out=oh.t[:],
                            in0=kf_.t[:].unsqueeze(3).to_broadcast([128, 8, 16, 16]), in1=iob, op=ALU.is_equal)
                          yield
                          E("vector", "tensor_tensor", oh.bs + topif.bs, oh.bs, out=oh.t[:], in0=oh.t[:],
                            in1=tiv[:, :, q, :].unsqueeze(2).to_broadcast([128, 8, 16, 16]), op=ALU.mult)
                          yield
                          E("vector", "tensor_reduce", oh.bs, dst.bs, out=dst.t[:], in_=oh.t[:], axis=AX.X, op=ALU.add)
                          yield
                      E("vector", "scalar_tensor_tensor", i1f.bs + i2f.bs, i1f.bs, out=i1f.t[:], in0=i1f.t[:], scalar=128.0,
                        in1=i2f.t[:], op0=ALU.mult, op1=ALU.add)
                      E("vector", "tensor_copy", i1f.bs, eidx.bs, out=eidx.t[:].rearrange("p (h k) -> p h k", h=8), in_=i1f.t[:])
                      yield
                      E("vector", "tensor_tensor", bs_.bs, gw.bs, out=gw.t[:], in0=bs_.t[:],
                        in1=bs_.t[:, :, 0:1].to_broadcast([128, 8, 16]), op=ALU.subtract)
                      yield
                      E("scalar", "activation", gw.bs, gw.bs, out=gw.t[:], in_=gw.t[:], func=AF.Exp)
                      yield
                      E("vector", "tensor_reduce", gw.bs, zs.bs, out=zs.t[:], in_=gw.t[:], axis=AX.X, op=ALU.add)
                      E("vector", "reciprocal", zs.bs, zs.bs, out=zs.t[:], in_=zs.t[:])
                      E("vector", "tensor_tensor", gw.bs + zs.bs, gw.bs, out=gw.t[:], in0=gw.t[:],
                        in1=zs.t[:].unsqueeze(2).to_broadcast([128, 8, 16]), op=ALU.mult)
                      x2_of[i] = x2t
                      yield

                  def back(i, nxt):
                      sl_ = i % 2
                      h2b = h2b_l[sl_]; eidx = eidx_l[sl_]; gw = gw_l[sl_]
                      x2t = x2_of[i]
                      gwf = gw.t[:].rearrange("p h k -> p (h k)")
                      uvs_of = {}
                      GS = PEER_GS
                      NG = 128 // GS
                      SK = PEER_SK
                      assert (SK + 1) * GS + GS - 1 <= UVN + GS - 1 and UVN >= (SK + 2) * GS - 0, "gather ring too shallow for skew"
                      for jg in range(NG + SK):
                          if jg < NG:
                              uvs = []
                              for jj in range(GS):
                                  j = jg * GS + jj
                                  uv = uv_r.next()
                                  uvs.append(uv)
                                  GATHER(uv.t[:], uv_d, eidx.t[:, j:j + 1], eidx.bs + [uv_b], uv.bs)
                                  E("vector", "tensor_tensor", uv.bs + h2b.bs, uv.bs, out=uv.t[:, 0:DM], in0=uv.t[:, 0:DM], in1=h2b.t[:],
                                    op=ALU.mult)
                                  E("scalar", "activation", uv.bs, uv.bs + (act.bs if jj in (0, GS - 1) else []), out=uv.t[:, 0:DM],
                                    in_=uv.t[:, 0:DM], func=AF.Identity, accum_out=act.t[:, j:j + 1])
                              uvs_of[jg] = uvs
                          if nxt is not None and jg < NG:
                              for _ in range(PEER_FS):
                                  next(nxt, None)
                          if jg >= SK:
                              g_ = jg - SK
                              grp = slice(g_ * GS, (g_ + 1) * GS)
                              uvs = uvs_of.pop(g_)
                              E("scalar", "activation", act.bs, wgt.bs, out=wgt.t[:, grp], in_=act.t[:, grp], func=AF.Gelu_apprx_tanh)
                              E("vector", "tensor_tensor", wgt.bs + gw.bs, wgt.bs, out=wgt.t[:, grp], in0=wgt.t[:, grp], in1=gwf[:, grp], op=ALU.mult)
                              for jj in range(GS):
                                  j = g_ * GS + jj
                                  uv = uvs[jj]
                                  dg = dg_r.next()
                                  E("scalar", "activation", identf.bs + wgt.bs, dg.bs, out=dg.t[:], in_=identf.t[:], func=AF.Identity,
                                    scale=wgt.t[:, j:j + 1])
                                  for hf in range(2):
                                      MM(pacc[hf].t[:, :], dg.t[:], uv.t[:, DM + hf * 512:DM + (hf + 1) * 512], j == 0, j == 127,
                                         dg.bs + uv.bs, pacc[hf].bs)
                      ot = ot_r.next()
                      for hf in range(2):
                          E("vector", "tensor_tensor", pacc[hf].bs + g2r.bs, ot.bs, out=ot.t[:, hf * 512:(hf + 1) * 512], in0=pacc[hf].t[:, :],
                            in1=g2r.t[:, hf * 512:(hf + 1) * 512], op=ALU.mult)
                      E("vector", "scalar_tensor_tensor", x2t.bs + ot.bs, ot.bs, out=ot.t[:], in0=x2t.t[:], scalar=ALPHA,
                        in1=ot.t[:], op0=ALU.mult, op1=ALU.add)
                      layer_norm_tile(lnt, ot, lg, lb, eng2="vector")
                      S.dma("sync", yout[pi][i * 128:(i + 1) * 128, :], ot.t[:], reads=ot.bs)

                  x2_of = {}
                  g0 = front(0)
                  for _ in g0:
                      pass
                  for i in range(8):
                      nxt = front(i + 1) if i + 1 < 8 else None
                      back(i, nxt)
                      if nxt is not None:
                          for _ in nxt:
                              pass
                  S.barrier()

        S.enabled = True
        with ExitStack() as ph:
            nso = alloc(ph, "nso", [64, 128])
            if stop is None or stop >= (0, 3):
                pb = pbanks.next()
                TR(pb.t[0:64, 0:128], nst.t[:].rearrange("p s d c -> p (s d c)"), identf.t[:], nst.bs + identf.bs, pb.bs)
                E("vector", "tensor_copy", pb.bs, nso.bs, out=nso.t[:], in_=pb.t[0:64, 0:128])
                S.dma("sync", ns_d, nso.t[:], reads=nso.bs)
            S.emit(top)
    return nc, dbg_names


def _consts():
    bf = ml_dtypes.bfloat16
    c = {}
    c["identf"] = np.eye(128, dtype=np.float32)
    c["identb"] = np.eye(128, dtype=np.float32).astype(bf)
    m0 = np.ones((128, 1), np.float32); m0[0, 0] = 0.0
    c["mask0"] = m0
    t = np.arange(1024)
    r = (t // 64).astype(np.float32); col = (t % 64).astype(np.float32)
    quarter = DM // 4
    omega = (1.0 / (10000.0 ** (np.arange(quarter, dtype=np.float32) / quarter))).astype(np.float32)
    er = r[:, None] * omega[None, :]; ec = col[:, None] * omega[None, :]
    c["pos"] = np.concatenate([np.sin(er), np.cos(er), np.sin(ec), np.cos(ec)], axis=-1).astype(np.float32)
    deltas = np.linspace(math.log(1e-2) / 1.5, math.log(1e-2) / 0.3, DM, dtype=np.float32)
    bands = np.linspace(1e-4, 15, 16, dtype=np.float32)
    for L in (256, 1024):
        ti = np.arange(L, dtype=np.float32)
        tn = ti / max(L - 1, 1)
        w = (2.0 * math.pi * ti / L).astype(np.float32)
        fw = w[:, None] * bands[None, :]
        z = np.concatenate([tn[:, None], np.cos(fw), -np.sin(fw)], axis=-1).astype(np.float32)
        c["zT%d" % L] = np.ascontiguousarray(z.T)
        c["dec%d" % L] = np.exp(-tn[:, None] * np.abs(deltas)[None, :]).astype(np.float32)
        tt = np.arange(L, dtype=np.float64)
        ang = np.pi * np.outer(tt, tt) / L
        C = np.cos(ang)
        Sm = -np.sin(ang)
        Sm[:, 0] = (-1.0) ** tt
        c["C%d" % L] = C.astype(np.float32).astype(bf)
        c["S%d" % L] = Sm.astype(np.float32).astype(bf)
        c["ST%d" % L] = np.ascontiguousarray(Sm.T).astype(np.float32).astype(bf)
        nfc = L // 128
        f = np.arange(L)
        wfre = np.where(f == 0, 1.0 / (2 * L), 1.0 / L)
        wB = np.where(f == 0, 0.0, 1.0 / L)
        mD = np.where(f == 0, 0.0, 1.0 / L)
        m2 = np.where(f == 0, 1.0 / (2 * L), 0.0)
        tab = np.stack([wfre, wB, mD, m2], 0).reshape(4, nfc, 128).transpose(2, 0, 1)
        c["wtab%d" % L] = np.ascontiguousarray(tab).astype(np.float32)
    return c


def _chunkT(v):
    v = np.asarray(v)
    lead = v.shape[:-1]
    n = v.shape[-1] // 128
    v = v.reshape(lead + (n, 128))
    return np.ascontiguousarray(np.moveaxis(v, -1, 0))


def _in_maps(inp):
    f = lambda a: np.ascontiguousarray(np.asarray(a, dtype=np.float32))
    cst = _consts()
    shared = dict(cst)
    shared["w_ada"] = f(inp["w_ada"][0])
    b_ada = f(inp["b_ada"][0])
    shared["b_adaT"] = _chunkT(b_ada[:2048].reshape(2, 1024))
    shared["b_ada_rows"] = np.ascontiguousarray(b_ada[2048:].reshape(4, 1024))
    shared["w_in"] = f(inp["w_in"][0])
    shared["rcw"] = np.ascontiguousarray(_chunkT(f(inp["rnn_conv_w"][0])).transpose(0, 2, 1))
    shared["rcb"] = _chunkT(f(inp["rnn_conv_b"][0]))
    shared["gate_w"] = f(inp["rnn_gate_w"][0])
    shared["gbT"] = _chunkT(f(inp["rnn_gate_b"][0]))
    shared["lamT"] = _chunkT(f(inp["rnn_lambda"][0]))
    shared["hcw"] = np.ascontiguousarray(_chunkT(f(inp["hy_conv_w"][0])).transpose(0, 2, 1))
    shared["hcb"] = _chunkT(f(inp["hy_conv_b"][0]))
    shared["hy_w1"] = f(inp["hy_ffn_w1"][0])
    shared["hy_b1"] = f(inp["hy_ffn_b1"][0]).reshape(64, 1)
    shared["hy_w2"] = f(inp["hy_ffn_w2"][0])
    shared["hy_b2"] = f(inp["hy_ffn_b2"][0]).reshape(64, 1)
    shared["hy_freq"] = f(inp["hy_sin_freq"][0]).reshape(64, 1)
    shared["hy_w3"] = f(inp["hy_ffn_w3"][0])
    shared["hy_b3"] = f(inp["hy_ffn_b3"][0])
    shared["skipT"] = _chunkT(f(inp["hy_skip"][0]))
    shared["w_a"] = f(inp["w_branch_a"][0])
    shared["w_b"] = f(inp["w_branch_b"][0])
    shared["w_o"] = f(inp["w_out"][0])
    shared["ln1_g"] = f(inp["ln1_g"][0]); shared["ln1_b"] = f(inp["ln1_b"][0])
    shared["ln2_g"] = f(inp["ln2_g"][0]); shared["ln2_b"] = f(inp["ln2_b"][0])
    shared["wq"] = f(inp["peer_w_query"][0])
    shared["skT"] = np.ascontiguousarray(f(inp["peer_sub_keys"][0]).transpose(2, 0, 1))
    shared["peer_u"] = f(inp["peer_u"][0])
    shared["peer_v"] = f(inp["peer_v"][0])
    xp = f(inp["x_prompt"]); xs = f(inp["x_sample"]); stt = f(inp["state_rglru"]); cc = f(inp["c"]); cctx = f(inp["c_ctx"])
    maps = []
    for i in range(N_CORES):
        m = dict(shared)
        m["xp"] = np.ascontiguousarray(xp[4 * i:4 * i + 4].reshape(1024, DM))
        m["xs"] = np.ascontiguousarray(xs[i])
        cond = np.stack([cctx, cc[i]], 0)
        m["condT"] = _chunkT(cond).transpose(0, 2, 1).copy()
        m["h0T"] = _chunkT(stt[i, 0])
        maps.append(m)
    return maps


_CACHE = {}


def kernel(**inputs):
    if "nc" not in _CACHE:
        _CACHE["nc"] = build()
    nc, _ = _CACHE["nc"]
    maps = _in_maps(inputs)
    res = run_bass_kernel_spmd(nc, maps, core_ids=list(range(N_CORES)))
    yp = np.zeros((32, 256, DM), np.float32)
    ys = np.zeros((8, 1024, DM), np.float32)
    ns = np.zeros((32, 1, 2, 1024), np.float32)
    for i in range(N_CORES):
        r = res.results[i]
        yp[4 * i:4 * i + 4] = np.asarray(r["yp"]).reshape(4, 256, DM)
        ys[i] = np.asarray(r["ys"])
        ns[4 * i:4 * i + 4, 0] = np.asarray(r["ns"]).reshape(4, 2, 1024)
    return yp, ys, ns
```

```python
import math
import os
HYSKIP = os.environ.get('HYSKIP', '')
PEER_SK = int(os.environ.get('PEER_SK', '2'))
PEER_FS = int(os.environ.get('PEER_FS', '3'))
PEER_GS = int(os.environ.get('PEER_GS', '4'))
from contextlib import ExitStack

import ml_dtypes
import numpy as np

import concourse.bass as bass
import concourse.mybir as mybir
from concourse.bass_utils import run_bass_kernel_spmd

F32 = mybir.dt.float32
BF16 = mybir.dt.bfloat16
I32 = mybir.dt.int32
U32 = mybir.dt.uint32
AF = mybir.ActivationFunctionType
ALU = mybir.AluOpType
AX = mybir.AxisListType

ENGS = ["sync", "scalar", "vector", "gpsimd", "tensor"]
DBGOPS = []
NOSYNC_ENGS = set(os.environ.get('NOSYNC', '').split(',')) - {''}
N_CORES = 8
DM = 1024
ALPHA = 2.0 ** 0.25
LN_EPS = 1e-5
RGLRU_C = 8.0


class Buf:
    __slots__ = ("name", "last_w", "readers")

    def __init__(self, name=""):
        self.name = name
        self.last_w = None
        self.readers = {}


class Op:
    __slots__ = ("eng", "fn", "deps", "key", "pos", "is_dma", "sig", "val", "waits",
                 "nosame", "vc_issue", "vc_done")


class Sched:
    def __init__(self, nc, dma_slots=16, same_engine_sync=True):
        self.nc = nc
        self.ops = []
        self.per_eng = {e: [] for e in ENGS}
        self.npos = {}
        self.dma_slots = dma_slots
        self.dma_count = {e: 0 for e in ENGS}
        self.slot_last = {}
        self.same_engine_sync = same_engine_sync
        self.enabled = True

    def _new(self, eng, fn, dma, nosame):
        o = Op()
        o.eng = eng
        o.fn = fn
        o.is_dma = dma
        o.sig = dma
        o.nosame = nosame
        return o

    def op(self, eng, fn, reads=(), writes=(), dma=False, nosame=False):
        if not self.enabled:
            return None
        o = self._new(eng, fn, dma, nosame)
        deps = []
        for b in reads:
            if b.last_w is not None:
                deps.append(b.last_w)
        for b in writes:
            if b.last_w is not None:
                deps.append(b.last_w)
            deps.extend(b.readers.values())
        if dma:
            slot = self.dma_count[eng] % self.dma_slots
            self.dma_count[eng] += 1
            o.key = ("dma", eng, slot)
            prev = self.slot_last.get(o.key)
            if prev is not None:
                deps.append(prev)
            self.slot_last[o.key] = o
        else:
            o.key = eng
        o.pos = self.npos.get(o.key, 0)
        self.npos[o.key] = o.pos + 1
        o.deps = [d for d in deps if d is not o]
        rk = o.key if not dma else ("dmaop", id(o))
        for b in reads:
            b.readers[rk] = o
        for b in writes:
            b.last_w = o
            b.readers = {}
        self.ops.append(o)
        self.per_eng[eng].append(o)
        return o

    def dma(self, eng, out, in_, reads=(), writes=(), **kw):
        return self.op(eng, lambda e: e.dma_start(out=out, in_=in_, **kw), reads, writes, dma=True)

    def barrier(self):
        if not self.enabled:
            return
        lasts = [self.per_eng[e][-1] for e in ENGS if self.per_eng[e]]
        lasts = [o for o in lasts if o.fn is not None]
        lasts += list(self.slot_last.values())
        for e in ENGS:
            o = self._new(e, None, False, False)
            o.key = e
            o.pos = self.npos.get(e, 0)
            self.npos[e] = o.pos + 1
            o.deps = list(lasts)
            self.ops.append(o)
            self.per_eng[e].append(o)

    def finalize(self):
        last_on_eng = {}
        for o in self.ops:
            vc = {}
            prev = last_on_eng.get(o.eng)
            if prev is not None:
                vc.update(prev.vc_issue)
            waits = []
            best = {}
            for d in o.deps:
                if d.key not in best or best[d.key].pos < d.pos:
                    best[d.key] = d
            for k, d in best.items():
                if (not d.is_dma) and (not o.is_dma) and d.eng == o.eng and (
                        o.nosame or not self.same_engine_sync or o.eng in NOSYNC_ENGS):
                    continue
                if vc.get(k, -1) >= d.pos:
                    continue
                waits.append(d)
                d.sig = True
                for kk, vv in d.vc_done.items():
                    if vc.get(kk, -1) < vv:
                        vc[kk] = vv
            o.waits = waits
            o.vc_issue = vc
            vd = dict(vc)
            if o.fn is not None:
                vd[o.key] = o.pos
            o.vc_done = vd
            last_on_eng[o.eng] = o
        cnt = {}
        for o in self.ops:
            if o.is_dma:
                o.val = 16 * (o.pos + 1)
            elif o.sig:
                cnt[o.key] = cnt.get(o.key, 0) + 1
                o.val = cnt[o.key]

    def emit(self, stack):
        nc = self.nc
        fin = self._new("sync", None, False, False)
        fin.key = "sync"
        fin.pos = self.npos.get("sync", 0)
        fin.deps = list(self.slot_last.values())
        self.ops.append(fin)
        self.per_eng["sync"].append(fin)
        self.finalize()
        sems = {}
        for e in ENGS:
            sems[e] = stack.enter_context(nc.semaphore("s_" + e))
        for k in self.slot_last.keys():
            sems[k] = stack.enter_context(nc.semaphore("d_%s_%d" % (k[1], k[2])))
        per_eng = self.per_eng

        def run(engname, eng):
            for o in per_eng[engname]:
                for d in o.waits:
                    eng.wait_ge(sems[d.key], d.val)
                if o.fn is None:
                    continue
                inst = o.fn(eng)
                if o.sig:
                    inst.then_inc(sems[o.key], 16 if o.is_dma else 1)

        with nc.Block() as block:
            @block.sync
            def _(e):
                run("sync", e)

            @block.scalar
            def _(e):
                run("scalar", e)

            @block.vector
            def _(e):
                run("vector", e)

            @block.gpsimd
            def _(e):
                run("gpsimd", e)

            @block.tensor
            def _(e):
                run("tensor", e)


class Tl:
    def __init__(self, t, nbuf=1, name=""):
        self.t = t
        self.bs = [Buf(name + str(i)) for i in range(nbuf)]
        self.b = self.bs[0]


class Ring:
    def __init__(self, tiles):
        self.tiles = tiles
        self.i = 0

    def next(self):
        t = self.tiles[self.i % len(self.tiles)]
        self.i += 1
        return t


def build(debug=(), stop=None):
    nc = bass.Bass("TRN2", target_bir_lowering=False)
    S = Sched(nc)
    debug = set(debug)
    dbg_names = []

    def din(name, shape, dt=F32):
        return nc.dram_tensor(name, list(shape), dt, kind="ExternalInput").ap()

    def dout(name, shape, dt=F32):
        return nc.dram_tensor(name, list(shape), dt, kind="ExternalOutput").ap()

    xin = [din("xp", [1024, DM]), din("xs", [1024, DM])]
    pos_d = din("pos", [1024, DM])
    condT_d = din("condT", [128, 8, 2])
    h0T_d = din("h0T", [128, 2, 8])
    w_ada_d = din("w_ada", [DM, 6 * DM])
    b_adaT_d = din("b_adaT", [128, 2, 8])
    b_ada_rows_d = din("b_ada_rows", [4, DM])
    w_in_d = din("w_in", [DM, 7168])
    rcw_d = din("rcw", [128, 8, 4])
    rcb_d = din("rcb", [128, 8])
    gate_w_d = din("gate_w", [2, 2, 4, 256, 256])
    gbT_d = din("gbT", [128, 2, 2, 8])
    lamT_d = din("lamT", [128, 2, 8])
    hcw_d = din("hcw", [128, 24, 3])
    hcb_d = din("hcb", [128, 24])
    hy_w1_d = din("hy_w1", [33, 64])
    hy_b1_d = din("hy_b1", [64, 1])
    hy_w2_d = din("hy_w2", [64, 64])
    hy_b2_d = din("hy_b2", [64, 1])
    hy_freq_d = din("hy_freq", [64, 1])
    hy_w3_d = din("hy_w3", [64, 4096])
    hy_b3_d = din("hy_b3", [4096])
    skipT_d = din("skipT", [128, 2, 8])
    w_a_d = din("w_a", [DM, DM])
    w_b_d = din("w_b", [DM, DM])
    w_o_d = din("w_o", [DM, DM])
    ln_d = [din("ln1_g", [DM]), din("ln1_b", [DM]), din("ln2_g", [DM]), din("ln2_b", [DM])]
    wq_d = din("wq", [DM, 2048])
    skT_d = din("skT", [128, 2, 128])
    pu_d = din("peer_u", [16384, DM])
    pv_d = din("peer_v", [16384, DM])
    identf_d = din("identf", [128, 128])
    identb_d = din("identb", [128, 128], BF16)
    zT_d = [din("zT256", [33, 256]), din("zT1024", [33, 1024])]
    dec_d = [din("dec256", [256, DM]), din("dec1024", [1024, DM])]
    C_d = [din("C256", [256, 256], BF16), din("C1024", [1024, 1024], BF16)]
    Sm_d = [din("S256", [256, 256], BF16), din("S1024", [1024, 1024], BF16)]
    ST_d = [din("ST256", [256, 256], BF16), din("ST1024", [1024, 1024], BF16)]
    wtab_d = [din("wtab256", [128, 4, 2]), din("wtab1024", [128, 4, 8])]
    mask0_d = din("mask0", [128, 1])

    yout = [dout("yp", [1024, DM]), dout("ys", [1024, DM])]
    ns_d = dout("ns", [64, 128])
    modrow_d = nc.dram_tensor("modrow", [2, 4, DM], F32, kind="Internal").ap()
    modrow_b = Buf("modrow")
    w_in_bd = nc.dram_tensor("w_in_bf16", [DM, 7168], BF16, kind="Internal").ap()
    w_a_bd = nc.dram_tensor("w_a_bf16", [DM, DM], BF16, kind="Internal").ap()
    w_b_bd = nc.dram_tensor("w_b_bf16", [DM, DM], BF16, kind="Internal").ap()
    w_o_bd = nc.dram_tensor("w_o_bf16", [DM, DM], BF16, kind="Internal").ap()
    wq_bd = nc.dram_tensor("wq_bf16", [DM, 2048], BF16, kind="Internal").ap()
    gate_w_bd = nc.dram_tensor("gate_w_bf16", [2, 2, 4, 256, 256], BF16, kind="Internal").ap()
    wcv_b = {k: Buf("wcv_" + k) for k in ("w_in", "w_a", "w_b", "w_o", "wq", "gate")}
    uv_d = nc.dram_tensor("uv_bf16", [16384, 2 * DM], BF16, kind="Internal").ap()
    uv_b = Buf("uv")
    x2_d = nc.dram_tensor("x2_scratch", [2, 1024, DM], F32, kind="Internal").ap()
    x2_db = [[Buf("x2d") for _ in range(8)] for _ in range(2)]

    top = ExitStack()
    with top:
        uid = [0]

        def alloc(scope, name, shape, dt=F32, nbuf=1):
            uid[0] += 1
            t = scope.enter_context(nc.sbuf_tensor("s%d_%s" % (uid[0], name), list(shape), dt))
            return Tl(t, nbuf, name)

        def ring(scope, name, shape, dt, n):
            return Ring([alloc(scope, "%s_%d" % (name, i), shape, dt) for i in range(n)])

        _pb = [Tl(top.enter_context(nc.psum_tensor("pb%d" % i, [128, 512], F32)), 1, "pb%d" % i) for i in range(6)]
        pbanks6 = Ring(_pb)
        pbanks4 = Ring(_pb[:4])
        pacc = _pb[4:6]
        pbanks = pbanks6
        pbf = Ring([Tl(top.enter_context(nc.psum_tensor("pbf%d" % i, [128, 1024], BF16)), 1, "pbf%d" % i)
                    for i in range(2)])

        def dbg(name, tl, ap, dt=F32):
            if name not in debug:
                return
            o = dout("dbg_" + name, list(ap.shape), dt)
            dbg_names.append("dbg_" + name)
            S.dma("sync", o, ap, reads=tl.bs)

        def E(eng, meth, reads, writes, nosame=False, **kw):
            return S.op(eng, lambda e: getattr(e, meth)(**kw), reads, writes, nosame=nosame)

        def MM(out, lhsT, rhs, start, stop, reads, writes):
            return S.op("tensor", lambda e: e.matmul(out, lhsT=lhsT, rhs=rhs, start=start, stop=stop),
                        reads, writes, nosame=True)

        def GATHER(out, table, idx, reads, writes):
            return S.op("gpsimd", lambda e: e.indirect_dma_start(
                out=out, out_offset=None, in_=table, in_offset=bass.IndirectOffsetOnAxis(ap=idx, axis=0)),
                reads, writes, dma=True)

        def TR(out, in_, ident, reads, writes):
            return S.op("tensor", lambda e: e.transpose(out=out, in_=in_, identity=ident),
                        reads, writes, nosame=True)

        def load(eng, tl, dram_ap, sb_ap=None, extra_reads=()):
            S.dma(eng, sb_ap if sb_ap is not None else tl.t[:], dram_ap, reads=list(extra_reads), writes=tl.bs)

        S.dma("gpsimd", w_in_bd, w_in_d, writes=[wcv_b["w_in"]])
        S.dma("gpsimd", w_b_bd, w_b_d, writes=[wcv_b["w_b"]])
        S.dma("gpsimd", gate_w_bd.rearrange("a b c i j -> (a b c i) j"), gate_w_d.rearrange("a b c i j -> (a b c i) j"),
              writes=[wcv_b["gate"]])
        S.dma("gpsimd", w_a_bd, w_a_d, writes=[wcv_b["w_a"]])
        S.dma("gpsimd", w_o_bd, w_o_d, writes=[wcv_b["w_o"]])
        S.dma("gpsimd", wq_bd, wq_d, writes=[wcv_b["wq"]])
        for cq in range(4):
            r0, r1 = cq * 4096, (cq + 1) * 4096
            S.dma("gpsimd", uv_d[r0:r1, 0:DM], pu_d[r0:r1, :], writes=[uv_b])
            S.dma("gpsimd", uv_d[r0:r1, DM:2 * DM], pv_d[r0:r1, :], writes=[uv_b])
        identf = alloc(top, "identf", [128, 128]); load("sync", identf, identf_d)
        identb = alloc(top, "identb", [128, 128], BF16); load("sync", identb, identb_d)
        mask0 = alloc(top, "mask0", [128, 1]); load("sync", mask0, mask0_d)
        epsb = alloc(top, "epsb", [128, 1])
        E("vector", "memset", [], epsb.bs, ap=epsb.t[:], constant=LN_EPS)
        rcw = alloc(top, "rcw", [128, 8, 4]); load("sync", rcw, rcw_d)
        rcb = alloc(top, "rcb", [128, 8]); load("sync", rcb, rcb_d)
        gbT = alloc(top, "gbT", [128, 2, 2, 8]); load("sync", gbT, gbT_d)
        lamT = alloc(top, "lamT", [128, 2, 8]); load("sync", lamT, lamT_d)
        h0T = alloc(top, "h0T", [128, 2, 8]); load("sync", h0T, h0T_d)
        hcw = alloc(top, "hcw", [128, 24, 3]); load("sync", hcw, hcw_d)
        hcb = alloc(top, "hcb", [128, 24]); load("sync", hcb, hcb_d)
        skipT = alloc(top, "skipT", [128, 2, 8]); load("sync", skipT, skipT_d)
        skT = alloc(top, "skT", [128, 2, 128]); load("sync", skT, skT_d)
        nsp = alloc(top, "nsp", [128, 2, 8])
        E("scalar", "activation", lamT.bs, nsp.bs, out=nsp.t[:], in_=lamT.t[:], func=AF.Exp, scale=-1.0)
        E("scalar", "activation", nsp.bs, nsp.bs, out=nsp.t[:], in_=nsp.t[:], func=AF.Ln, bias=1.0, scale=1.0)
        E("vector", "tensor_scalar", nsp.bs, nsp.bs, out=nsp.t[:], in0=nsp.t[:], scalar1=-RGLRU_C, scalar2=None,
          op0=ALU.mult)
        modT = alloc(top, "modT", [128, 2, 8, 2])
        nst = alloc(top, "nst", [128, 4, 2, 8])

        with ExitStack() as ph:
            condT = alloc(ph, "condT", [128, 8, 2]); load("sync", condT, condT_d)
            condS = alloc(ph, "condS", [128, 8, 2])
            E("scalar", "activation", condT.bs, condS.bs, out=condS.t[:], in_=condT.t[:], func=AF.Silu)
            b_adaT = alloc(ph, "b_adaT", [128, 2, 8]); load("sync", b_adaT, b_adaT_d)
            brow = alloc(ph, "brow", [1, 4, DM])
            load("sync", brow, b_ada_rows_d.rearrange("(o a) n -> o a n", o=1))
            wa_ring = ring(ph, "wa", [128, 8, DM], F32, 2)
            rowt = ring(ph, "rowt", [1, DM], F32, 2)
            for ty in range(6):
                wa = wa_ring.next()
                load("sync", wa, w_ada_d[:, ty * DM:(ty + 1) * DM].rearrange("(kc p) n -> p kc n", p=128))
                if ty < 2:
                    pb = pbanks.next()
                    for m in range(8):
                        for kc in range(8):
                            MM(pb.t[:, m * 2:m * 2 + 2], wa.t[:, kc, m * 128:(m + 1) * 128], condS.t[:, kc, :],
                               kc == 0, kc == 7, wa.bs + condS.bs, pb.bs)
                    E("vector", "tensor_tensor", pb.bs + b_adaT.bs, modT.bs,
                      out=modT.t[:, ty], in0=pb.t[:, 0:16].rearrange("p (m j) -> p m j", j=2),
                      in1=b_adaT.t[:, ty].unsqueeze(2).to_broadcast([128, 8, 2]), op=ALU.add)
                    if ty == 1:
                        E("vector", "tensor_scalar", modT.bs, modT.bs, out=modT.t[:, 1], in0=modT.t[:, 1],
                          scalar1=1.0, scalar2=None, op0=ALU.add)
                else:
                    for j in range(2):
                        rt = rowt.next()
                        for hf in range(2):
                            pb = pbanks.next()
                            for kc in range(8):
                                MM(pb.t[0:1, :], condS.t[:, kc, j:j + 1], wa.t[:, kc, hf * 512:(hf + 1) * 512],
                                   kc == 0, kc == 7, wa.bs + condS.bs, pb.bs)
                            E("vector", "tensor_tensor", pb.bs + brow.bs, rt.bs,
                              out=rt.t[0:1, hf * 512:(hf + 1) * 512], in0=pb.t[0:1, :],
                              in1=brow.t[0:1, ty - 2, hf * 512:(hf + 1) * 512], op=ALU.add)
                        if ty == 4:
                            E("vector", "tensor_scalar", rt.bs, rt.bs, out=rt.t[:], in0=rt.t[:], scalar1=1.0,
                              scalar2=None, op0=ALU.add)
                        S.dma("sync", modrow_d[j, ty - 2:ty - 1, :], rt.t[0:1, :], reads=rt.bs, writes=[modrow_b])
            S.barrier()

        def load_row(tl, dram_row, extra=()):
            S.dma("sync", tl.t[:], dram_row.partition_broadcast(128), reads=list(extra), writes=tl.bs)

        def layer_norm_tile(scope_tiles, xt, g_row, b_row, eng2="gpsimd"):
            stats, mv, rstd = scope_tiles
            E("vector", "bn_stats", xt.bs, stats.bs, out=stats.t[:, 0, :], in_=xt.t[:, 0:512])
            E("vector", "bn_stats", xt.bs, stats.bs, out=stats.t[:, 1, :], in_=xt.t[:, 512:1024])
            E("vector", "bn_aggr", stats.bs, mv.bs, out=mv.t[:], in_=stats.t[:].rearrange("p a b -> p (a b)"))
            E("scalar", "activation", mv.bs + epsb.bs, rstd.bs, out=rstd.t[:], in_=mv.t[:, 1:2], func=AF.Sqrt,
              bias=epsb.t[:], scale=1.0)
            E("vector", "reciprocal", rstd.bs, rstd.bs, out=rstd.t[:], in_=rstd.t[:])
            E("vector", "tensor_scalar", xt.bs + mv.bs + rstd.bs, xt.bs, out=xt.t[:], in0=xt.t[:],
              scalar1=mv.t[:, 0:1], scalar2=rstd.t[:], op0=ALU.subtract, op1=ALU.mult)
            E(eng2, "tensor_tensor", xt.bs + g_row.bs, xt.bs, out=xt.t[:], in0=xt.t[:], in1=g_row.t[:], op=ALU.mult)
            E(eng2, "tensor_tensor", xt.bs + b_row.bs, xt.bs, out=xt.t[:], in0=xt.t[:], in1=b_row.t[:], op=ALU.add)

        def conv(eng, out_tl, out_ap, in_tl, in_ap, w_tl, w_ap, b_ap, ntap, left, nseq, L):
            o3 = out_ap.rearrange("p (s l) -> p s l", s=nseq)
            i3 = in_ap.rearrange("p (s l) -> p s l", s=nseq)
            E(eng, "tensor_scalar", in_tl.bs + w_tl[0].bs + w_tl[1].bs, out_tl.bs, out=out_ap, in0=in_ap,
              scalar1=w_ap[:, left:left + 1], scalar2=b_ap, op0=ALU.mult, op1=ALU.add)
            for j in range(ntap):
                o = j - left
                if o == 0:
                    continue
                lo_out = max(0, -o)
                hi_out = L - max(0, o)
                E(eng, "scalar_tensor_tensor", in_tl.bs + out_tl.bs + w_tl[0].bs, out_tl.bs,
                  out=o3[:, :, lo_out:hi_out], in0=i3[:, :, lo_out + o:hi_out + o], scalar=w_ap[:, j:j + 1],
                  in1=o3[:, :, lo_out:hi_out], op0=ALU.mult, op1=ALU.add)

        for pi in range(2):
            nseq, L = (4, 256) if pi == 0 else (1, 1024)
            pbanks = pbanks6
            ntc = L // 128
            x_d = xin[pi]
            with ExitStack() as pp:
              with ExitStack() as mxs:
                h1T = alloc(mxs, "h1T", [128, 8, 1024], BF16)
                ybT = alloc(mxs, "ybT", [128, 8, 1024], BF16, nbuf=8)

                with ExitStack() as ph:
                    xr = ring(ph, "xt", [128, DM], F32, 2)
                    pr = ring(ph, "pt", [128, DM], F32, 2)
                    for i in range(8):
                        xt = xr.next()
                        load("sync", xt, x_d[i * 128:(i + 1) * 128, :])
                        if pi == 1:
                            pt = pr.next()
                            load("gpsimd", pt, pos_d[i * 128:(i + 1) * 128, :])
                            E("vector", "tensor_tensor", xt.bs + pt.bs, xt.bs, out=xt.t[:], in0=xt.t[:], in1=pt.t[:],
                              op=ALU.add)
                        for hf in range(2):
                            pb = pbanks.next()
                            for cc in range(4):
                                c = hf * 4 + cc
                                TR(pb.t[:, cc * 128:(cc + 1) * 128], xt.t[:, c * 128:(c + 1) * 128], identf.t[:],
                                   xt.bs + identf.bs, pb.bs)
                            for cc in range(4):
                                c = hf * 4 + cc
                                E("scalar", "activation", pb.bs + modT.bs, h1T.bs,
                                  out=h1T.t[:, c, i * 128:(i + 1) * 128], in_=pb.t[:, cc * 128:(cc + 1) * 128],
                                  func=AF.Identity, bias=modT.t[:, 0, c, pi:pi + 1], scale=modT.t[:, 1, c, pi:pi + 1])
                    dbg("h1T%d" % pi, h1T, h1T.t[:], BF16)
                    S.barrier()
                    if stop == (pi, 1): S.enabled = False

                with ExitStack() as ph:
                    li = pi
                    Cm = alloc(ph, "Cm", [128, ntc, L], BF16)
                    Sm = alloc(ph, "Sm", [128, ntc, L], BF16)
                    STm = alloc(ph, "STm", [128, ntc, L], BF16)
                    load("sync", Cm, C_d[li].rearrange("(tc p) f -> p tc f", p=128))
                    load("sync", Sm, Sm_d[li].rearrange("(tc p) f -> p tc f", p=128))
                    load("sync", STm, ST_d[li].rearrange("(tc p) f -> p tc f", p=128))
                    wtab = alloc(ph, "wtab", [128, 4, ntc]); load("sync", wtab, wtab_d[li])
                    hid2 = alloc(ph, "hid2", [64, L])
                    with ExitStack() as sub:
                        zT = alloc(sub, "zT", [33, L]); load("sync", zT, zT_d[li])
                        w1 = alloc(sub, "hw1", [33, 64]); load("sync", w1, hy_w1_d)
                        w2 = alloc(sub, "hw2", [64, 64]); load("sync", w2, hy_w2_d)
                        hb1 = alloc(sub, "hb1", [64, 1]); load("sync", hb1, hy_b1_d)
                        hb2 = alloc(sub, "hb2", [64, 1]); load("sync", hb2, hy_b2_d)
                        hfr = alloc(sub, "hfr", [64, 1]); load("sync", hfr, hy_freq_d)
                        fb = alloc(sub, "fb", [64, 2])
                        E("vector", "tensor_tensor", hb1.bs + hfr.bs, fb.bs, out=fb.t[:, 0:1], in0=hb1.t[:], in1=hfr.t[:],
                          op=ALU.mult)
                        E("vector", "tensor_tensor", hb2.bs + hfr.bs, fb.bs, out=fb.t[:, 1:2], in0=hb2.t[:], in1=hfr.t[:],
                          op=ALU.mult)
                        hid = [alloc(sub, "hid1", [64, L]), hid2]
                        sarg = alloc(sub, "sarg", [64, L])
                        sint = alloc(sub, "sint", [64, L], I32)
                        sflt = alloc(sub, "sflt", [64, L])
                        for layer in range(2):
                            src = zT if layer == 0 else hid[0]
                            wl = w1 if layer == 0 else w2
                            kdim = 33 if layer == 0 else 64
                            blk = min(L, 512)
                            for b0 in range(0, L, blk):
                                pb = pbanks.next()
                                MM(pb.t[0:64, 0:blk], wl.t[0:kdim, :], src.t[0:kdim, b0:b0 + blk], True, True,
                                   wl.bs + src.bs, pb.bs)
                                E("vector", "tensor_scalar", pb.bs + hfr.bs + fb.bs, sarg.bs, out=sarg.t[:, b0:b0 + blk],
                                  in0=pb.t[0:64, 0:blk], scalar1=hfr.t[:], scalar2=fb.t[:, layer:layer + 1],
                                  op0=ALU.mult, op1=ALU.add)
                            E("vector", "tensor_scalar", sarg.bs, sarg.bs, out=sarg.t[:], in0=sarg.t[:],
                              scalar1=float(1.0 / (2 * math.pi)), scalar2=8.0, op0=ALU.mult, op1=ALU.add)
                            E("vector", "tensor_copy", sarg.bs, sint.bs, out=sint.t[:], in_=sarg.t[:])
                            E("vector", "tensor_copy", sint.bs, sflt.bs, out=sflt.t[:], in_=sint.t[:])
                            E("vector", "tensor_tensor", sarg.bs + sflt.bs, sarg.bs, out=sarg.t[:], in0=sarg.t[:],
                              in1=sflt.t[:], op=ALU.subtract)
                            E("vector", "tensor_single_scalar", sarg.bs, sflt.bs, out=sflt.t[:], in_=sarg.t[:], scalar=0.5,
                              op=ALU.is_gt)
                            E("vector", "tensor_tensor", sarg.bs + sflt.bs, sarg.bs, out=sarg.t[:], in0=sarg.t[:],
                              in1=sflt.t[:], op=ALU.subtract)
                            E("scalar", "activation", sarg.bs, hid[layer].bs, out=hid[layer].t[:], in_=sarg.t[:],
                              func=AF.Sin, scale=float(2 * math.pi))
                        S.barrier()
                        dbg("hid2_%d" % pi, hid2, hid2.t[:])
                        if stop == (pi, 1.5): S.enabled = False

                    w3c_r = ring(ph, "w3c", [64, 4, 128], F32, 2)
                    b3c_r = ring(ph, "b3c", [128, 4, 128], F32, 2)
                    dec_r = ring(ph, "decf", [128, ntc, 128], F32, 2)
                    wgb_r = ring(ph, "wgb", [128, 8, 3, 128], BF16, 3)
                    ysz = 2 * ntc * 128 if pi == 1 else 2 * 2 * 4 * 128
                    nset = 2 if pi == 0 else 1

                    def mkset(q):
                        return (alloc(ph, "kf%d" % q, [128, 4, 128]), alloc(ph, "kff%d" % q, [128, 2, 128]),
                                alloc(ph, "kfb%d" % q, [128, 2, 128]),
                                alloc(ph, "kpm%d" % q, [128, ntc, 2, 2, 128], BF16),
                                alloc(ph, "TA%d" % q, [128, ntc, 2, 128]), alloc(ph, "TB%d" % q, [128, ntc, 2, 128]),
                                alloc(ph, "TD0%d" % q, [128, 2, 128]), alloc(ph, "tmpd%d" % q, [128, 2, 128]),
                                alloc(ph, "hpre%d" % q, [128, 3, 1024], BF16, nbuf=3), alloc(ph, "hc%d" % q, [128, 3, 1024], F32, nbuf=3),
                                alloc(ph, "wb%d" % q, [128, 1024], BF16), alloc(ph, "wT%d" % q, [128, 8, 128], BF16),
                                [alloc(ph, "tt%d_%d" % (q, r), [128, 512]) for r in range(4)],
                                alloc(ph, "YT%d" % q, [128, ysz], BF16), alloc(ph, "tmpz%d" % q, [128, 1024]),
                                alloc(ph, "z1%d" % q, [128, 1024]))
                    bsets = [mkset(q) for q in range(nset)]

                    w_in_bv = w_in_bd.rearrange("(kc p) n -> p kc n", p=128)
                    w3_v = hy_w3_d.rearrange("k (q n) -> k q n", q=4)
                    b3_v = hy_b3_d.rearrange("(q n) -> q n", q=4)
                    dec_v = dec_d[li].rearrange("(tc p) n -> p tc n", p=128)

                    def stA(c):
                        (kf, kff, kfb, kpm, TA, TB, TD0, tmpd, hpre, hc, wb, wT, tt_r, YT, tmpz, z1) = bsets[c % nset]
                        w3c = w3c_r.next(); load("sync", w3c, w3_v[:, :, c * 128:(c + 1) * 128])
                        b3c = b3c_r.next()
                        S.dma("sync", b3c.t[:], b3_v[:, c * 128:(c + 1) * 128].partition_broadcast(128), writes=b3c.bs)
                        decf = dec_r.next(); load("sync", decf, dec_v[:, :, c * 128:(c + 1) * 128])
                        wgb = wgb_r.next()
                        for q in range(3):
                            S.dma("sync", wgb.t[:, :, q, :],
                                  w_in_bv[:, :, 2048 + q * 1024 + c * 128:2048 + q * 1024 + (c + 1) * 128],
                                  reads=[wcv_b["w_in"]], writes=wgb.bs)
                        if c == 0:
                            dbg("b3c%d" % pi, b3c, b3c.t[:])
                            if stop == (pi, 1.55): S.enabled = False
                        for tc in range(ntc):
                            pb = pbanks.next()
                            MM(pb.t[:, :], hid2.t[:, tc * 128:(tc + 1) * 128], w3c.t[:].rearrange("k q n -> k (q n)"),
                               True, True, hid2.bs + w3c.bs, pb.bs)
                            E("vector", "tensor_tensor", pb.bs + b3c.bs, kf.bs, out=kf.t[:].rearrange("p q n -> p (q n)"),
                              in0=pb.t[:, :], in1=b3c.t[:].rearrange("p q n -> p (q n)"), op=ALU.add)
                            dbc = decf.t[:, tc, :].unsqueeze(1).to_broadcast([128, 2, 128])
                            E("gpsimd", "tensor_tensor", kf.bs + decf.bs, kff.bs, out=kff.t[:], in0=kf.t[:, 0:2, :], in1=dbc,
                              op=ALU.mult)
                            E("gpsimd", "tensor_tensor", kf.bs + decf.bs, kfb.bs, out=kfb.t[:], in0=kf.t[:, 2:4, :], in1=dbc,
                              op=ALU.mult)
                            if tc == 0:
                                E("vector", "tensor_scalar", kfb.bs + mask0.bs, kfb.bs, out=kfb.t[:], in0=kfb.t[:],
                                  scalar1=mask0.t[:], scalar2=None, op0=ALU.mult)
                            E("gpsimd", "tensor_tensor", kff.bs + kfb.bs, kpm.bs, out=kpm.t[:, tc, 0, :, :], in0=kff.t[:],
                              in1=kfb.t[:], op=ALU.add)
                            E("gpsimd", "tensor_tensor", kff.bs + kfb.bs, kpm.bs, out=kpm.t[:, tc, 1, :, :], in0=kff.t[:],
                              in1=kfb.t[:], op=ALU.subtract)
                        if c == 0:
                            dbg("kpm%d" % pi, kpm, kpm.t[:], BF16)
                            if stop == (pi, 1.57): S.enabled = False
                        for fc in range(ntc):
                            pa = pbanks.next()
                            for tc in range(ntc):
                                MM(pa.t[:, 0:256], Cm.t[:, tc, fc * 128:(fc + 1) * 128], kpm.t[:, tc, 0, :, :].rearrange("p o n -> p (o n)"),
                                   tc == 0, tc == ntc - 1, Cm.bs + kpm.bs, pa.bs)
                            for tc in range(ntc):
                                MM(pa.t[:, 256:512], Sm.t[:, tc, fc * 128:(fc + 1) * 128], kpm.t[:, tc, 1, :, :].rearrange("p o n -> p (o n)"),
                                   tc == 0, tc == ntc - 1, Sm.bs + kpm.bs, pa.bs)
                            if 'A' not in HYSKIP: E("scalar", "activation", pa.bs + wtab.bs, TA.bs, out=TA.t[:, fc].rearrange("p o n -> p (o n)"),
                              in_=pa.t[:, 0:256], func=AF.Identity,
                              scale=wtab.t[:, 0, fc:fc + 1])
                            if 'B' not in HYSKIP: E("scalar", "activation", pa.bs + wtab.bs, TB.bs, out=TB.t[:, fc].rearrange("p o n -> p (o n)"),
                              in_=pa.t[:, 256:512], func=AF.Identity,
                              scale=wtab.t[:, 1, fc:fc + 1])
                            if fc == 0 and 'D' not in HYSKIP:
                                pd = pbanks.next()
                                for tc in range(ntc):
                                    MM(pd.t[:, 0:256], Sm.t[:, tc, 0:128], kpm.t[:, tc, 0, :, :].rearrange("p o n -> p (o n)"),
                                       tc == 0, tc == ntc - 1, Sm.bs + kpm.bs, pd.bs)
                                E("scalar", "activation", pa.bs + wtab.bs, TD0.bs, out=TD0.t[:].rearrange("p o n -> p (o n)"),
                                  in_=pa.t[:, 0:256], func=AF.Identity, scale=wtab.t[:, 2, 0:1])
                                E("scalar", "activation", pd.bs + wtab.bs, tmpd.bs, out=tmpd.t[:].rearrange("p o n -> p (o n)"),
                                  in_=pd.t[:, 0:256], func=AF.Identity, scale=wtab.t[:, 3, 0:1])
                                E("gpsimd", "tensor_tensor", TD0.bs + tmpd.bs, TD0.bs, out=TD0.t[:], in0=TD0.t[:],
                                  in1=tmpd.t[:], op=ALU.add)
                        if c == 0:
                            dbg("TA%d" % pi, TA, TA.t[:]); dbg("TB%d" % pi, TB, TB.t[:]); dbg("TD0_%d" % pi, TD0, TD0.t[:])
                            if stop == (pi, 1.6): S.enabled = False
                        for q in range(3):
                            for blk in range(2):
                                pb = pbanks.next()
                                for kc in range(8):
                                    MM(pb.t[:, :], wgb.t[:, kc, q, :], h1T.t[:, kc, blk * 512:(blk + 1) * 512],
                                       kc == 0, kc == 7, wgb.bs + h1T.bs, pb.bs)
                                E("scalar", "copy", pb.bs, [hpre.bs[q]], out=hpre.t[:, q, blk * 512:(blk + 1) * 512],
                                  in_=pb.t[:, :])
                            cc = q * 8 + c
                            o_tl = Tl(None); o_tl.bs = [hc.bs[q]]
                            i_tl = Tl(None); i_tl.bs = [hpre.bs[q]]
                            conv("vector", o_tl, hc.t[:, q, :], i_tl, hpre.t[:, q, :],
                                 (hcw, hcb), hcw.t[:, cc, :], hcb.t[:, cc:cc + 1], 3, 1, nseq, L)
                        if c == 0:
                            dbg("hc%d" % pi, hc, hc.t[:])
                            if stop == (pi, 1.7): S.enabled = False
                    def stB(c):
                        (kf, kff, kfb, kpm, TA, TB, TD0, tmpd, hpre, hc, wb, wT, tt_r, YT, tmpz, z1) = bsets[c % nset]
                        for od in range(2):
                            if od == 0:
                                w_ap, w_bs = hc.t[:, 2, :], [hc.bs[2]]
                                x_ap, x_bs = hc.t[:, 0, :], [hc.bs[0]]
                            else:
                                w_ap, w_bs = z1.t[:], z1.bs
                                x_ap, x_bs = hc.t[:, 1, :], [hc.bs[1]]
                            E("scalar", "copy", w_bs, wb.bs, out=wb.t[:], in_=w_ap)
                            pt = pbf.next()
                            for tt in range(8):
                                slot = (tt % 2) * 4 + tt // 2 if pi == 0 else tt
                                TR(pt.t[:, slot * 128:(slot + 1) * 128], wb.t[:, tt * 128:(tt + 1) * 128], identb.t[:],
                                   wb.bs + identb.bs, pt.bs)
                            E("vector", "tensor_copy", pt.bs, wT.bs, out=wT.t[:].rearrange("p a n -> p (a n)"), in_=pt.t[:, :])
                            if pi == 0:
                                YTv = YT.t[:].rearrange("p (fc r s n) -> p fc r s n", fc=2, r=2, s=4)
                                wTv = wT.t[:].rearrange("p (tc s) n -> p tc (s n)", s=4)
                                groups = [(fc, 1) for fc in range(2)]
                            else:
                                YTv = YT.t[:].rearrange("p (fc r n) -> p fc r n", fc=ntc, r=2)
                                groups = [(0, 4), (4, 4)]
                            for (f0, nf) in groups:
                                pre = pbanks.next()
                                pim = pbanks.next()
                                if pi == 0:
                                    fc = f0
                                    for (pbk, Mx) in ((pre, Cm), (pim, Sm)):
                                        for tc in range(ntc):
                                            MM(pbk.t[:, :], Mx.t[:, tc, fc * 128:(fc + 1) * 128], wTv[:, tc, :],
                                               tc == 0, tc == ntc - 1, Mx.bs + wT.bs, pbk.bs)
                                    ta = TA.t[:, fc, od, :].unsqueeze(1).to_broadcast([128, 4, 128])
                                    tb_ = TB.t[:, fc, od, :].unsqueeze(1).to_broadcast([128, 4, 128])
                                    if fc == 0:
                                        td = TD0.t[:, od, :].unsqueeze(1).to_broadcast([128, 4, 128])
                                        td_bs = TD0.bs
                                    else:
                                        td = ta
                                        td_bs = TA.bs
                                    ure = pre.t[:, :].rearrange("p (s n) -> p s n", s=4)
                                    uim = pim.t[:, :].rearrange("p (s n) -> p s n", s=4)
                                    yre = YTv[:, fc, 0, :, :]
                                    yim = YTv[:, fc, 1, :, :]
                                    shp = "p (s n) -> p s n"
                                    tv = [t_.t[:, :].rearrange(shp, s=4) for t_ in tt_r]
                                    E("vector", "tensor_tensor", pre.bs + TA.bs, tt_r[0].bs, out=tv[0], in0=ure, in1=ta, op=ALU.mult)
                                    E("vector", "tensor_tensor", pim.bs + TB.bs, tt_r[1].bs, out=tv[1], in0=uim, in1=tb_, op=ALU.mult)
                                    E("vector", "tensor_tensor", pre.bs + TB.bs, tt_r[2].bs, out=tv[2], in0=ure, in1=tb_, op=ALU.mult)
                                    E("vector", "tensor_tensor", pim.bs + td_bs, tt_r[3].bs, out=tv[3], in0=uim, in1=td, op=ALU.mult)
                                    E("gpsimd", "tensor_tensor", tt_r[0].bs + tt_r[1].bs, YT.bs, out=yre, in0=tv[0], in1=tv[1], op=ALU.subtract)
                                    E("gpsimd", "tensor_tensor", tt_r[2].bs + tt_r[3].bs, YT.bs, out=yim, in0=tv[2], in1=tv[3], op=ALU.add)
                                else:
                                    for ff in range(nf):
                                        fc = f0 + ff
                                        for (pbk, Mx) in ((pre, Cm), (pim, Sm)):
                                            for tc in range(ntc):
                                                MM(pbk.t[:, ff * 128:(ff + 1) * 128], Mx.t[:, tc, fc * 128:(fc + 1) * 128],
                                                   wT.t[:, tc, :], tc == 0, tc == ntc - 1, Mx.bs + wT.bs, pbk.bs)
                                    ta = TA.t[:, f0:f0 + nf, od, :]
                                    tb_ = TB.t[:, f0:f0 + nf, od, :]
                                    ure = pre.t[:, :].rearrange("p (s n) -> p s n", s=4)
                                    uim = pim.t[:, :].rearrange("p (s n) -> p s n", s=4)
                                    yre = YTv[:, f0:f0 + nf, 0, :]
                                    yim = YTv[:, f0:f0 + nf, 1, :]
                                    tv = [t_.t[:, :].rearrange("p (s n) -> p s n", s=4) for t_ in tt_r]
                                    E("vector", "tensor_tensor", pre.bs + TA.bs, tt_r[0].bs, out=tv[0], in0=ure, in1=ta, op=ALU.mult)
                                    E("vector", "tensor_tensor", pim.bs + TB.bs, tt_r[1].bs, out=tv[1], in0=uim, in1=tb_, op=ALU.mult)
                                    E("vector", "tensor_tensor", pre.bs + TB.bs, tt_r[2].bs, out=tv[2], in0=ure, in1=tb_, op=ALU.mult)
                                    E("vector", "tensor_tensor", pim.bs + TA.bs, tt_r[3].bs, out=tv[3], in0=uim, in1=ta, op=ALU.mult)
                                    if f0 == 0:
                                        E("vector", "tensor_tensor", pim.bs + TD0.bs, tt_r[3].bs, out=tt_r[3].t[:, 0:128],
                                          in0=pim.t[:, 0:128], in1=TD0.t[:, od, :], op=ALU.mult)
                                    E("gpsimd", "tensor_tensor", tt_r[0].bs + tt_r[1].bs, YT.bs, out=yre, in0=tv[0], in1=tv[1], op=ALU.subtract)
                                    E("gpsimd", "tensor_tensor", tt_r[2].bs + tt_r[3].bs, YT.bs, out=yim, in0=tv[2], in1=tv[3], op=ALU.add)
                            sk = skipT.t[:, od, c:c + 1]
                            if od == 0:
                                o_ap_full, o_bs = z1.t[:], z1.bs
                            else:
                                o_ap_full, o_bs = ybT.t[:, c, :], [ybT.bs[c]]
                            for blk in range(2):
                                pb = pbanks.next()
                                if pi == 0:
                                    for s2 in range(2):
                                        sq = blk * 2 + s2
                                        k = 0
                                        for fc in range(2):
                                            for r, Mx in ((0, Cm), (1, STm)):
                                                MM(pb.t[:, s2 * 256:(s2 + 1) * 256], YTv[:, fc, r, sq, :], Mx.t[:, fc, :],
                                                   k == 0, k == 3, YT.bs + Mx.bs, pb.bs)
                                                k += 1
                                else:
                                    k = 0
                                    for fc in range(ntc):
                                        for r, Mx in ((0, Cm), (1, STm)):
                                            MM(pb.t[:, :], YTv[:, fc, r, :], Mx.t[:, fc, blk * 512:(blk + 1) * 512],
                                               k == 0, k == 2 * ntc - 1, YT.bs + Mx.bs, pb.bs)
                                            k += 1
                                sl = slice(blk * 512, (blk + 1) * 512)
                                E("vector", "scalar_tensor_tensor", w_bs + skipT.bs + pb.bs, tmpz.bs, out=tmpz.t[:, sl],
                                  in0=w_ap[:, sl], scalar=sk, in1=pb.t[:, :], op0=ALU.mult, op1=ALU.add)
                                E("gpsimd", "tensor_tensor", tmpz.bs + x_bs, o_bs, out=o_ap_full[:, sl], in0=tmpz.t[:, sl],
                                  in1=x_ap[:, sl], op=ALU.mult)
                    if nset == 2:
                        stA(0)
                        for c in range(8):
                            if c + 1 < 8:
                                stA(c + 1)
                            stB(c)
                    else:
                        for c in range(8):
                            stA(c)
                            stB(c)
                    dbg("ybT%d" % pi, ybT, ybT.t[:], BF16)
                    S.barrier()
                    if stop == (pi, 2): S.enabled = False

                yaT = alloc(mxs, "yaT", [128, 8, 1024], BF16, nbuf=8)
                with ExitStack() as ph:
                    wbf_r = ring(ph, "rwbf", [128, 8, 256], BF16, 4)
                    gwb_r = ring(ph, "gwb", [128, 4, 2, 256], BF16, 2)
                    xpre = alloc(ph, "xpre", [128, 2, 1024], F32, nbuf=2)
                    xc = alloc(ph, "xc", [128, 2, 1024], F32, nbuf=2)
                    xcb = alloc(ph, "xcb", [128, 2, 1024], BF16)
                    rg = alloc(ph, "rg", [128, 1024]); ig = alloc(ph, "ig", [128, 1024])
                    at = alloc(ph, "at", [128, 1024]); st_ = alloc(ph, "st", [128, 1024]); ut = alloc(ph, "ut", [128, 1024])
                    hdir = [alloc(ph, "hf", [128, 1024]), alloc(ph, "hb", [128, 1024])]
                    glu = alloc(ph, "glu", [128, 2, 1024], F32, nbuf=2)
                    w_in_bv = w_in_bd.rearrange("(kc p) n -> p kc n", p=128)
                    for hd in range(4):
                        wxb = wbf_r.next()
                        S.dma("sync", wxb.t[:], w_in_bv[:, :, hd * 256:(hd + 1) * 256], reads=[wcv_b["w_in"]], writes=wxb.bs)
                        wgb2 = wbf_r.next()
                        S.dma("sync", wgb2.t[:], w_in_bv[:, :, 1024 + hd * 256:1024 + (hd + 1) * 256], reads=[wcv_b["w_in"]],
                              writes=wgb2.bs)
                        gwb = gwb_r.next()
                        for d in range(2):
                            for g in range(2):
                                S.dma("sync", gwb.t[:, d * 2 + g, :, :],
                                      gate_w_bd[d, g, hd].rearrange("(ic p) j -> p ic j", p=128), reads=[wcv_b["gate"]],
                                      writes=gwb.bs)
                        for jc in range(2):
                            for blk in range(2):
                                sl = slice(blk * 512, (blk + 1) * 512)
                                pb = pbanks.next()
                                for kc in range(8):
                                    MM(pb.t[:, :], wxb.t[:, kc, jc * 128:(jc + 1) * 128], h1T.t[:, kc, sl],
                                       kc == 0, kc == 7, wxb.bs + h1T.bs, pb.bs)
                                E("scalar", "copy", pb.bs, [xpre.bs[jc]], out=xpre.t[:, jc, sl], in_=pb.t[:, :])
                                pb = pbanks.next()
                                for kc in range(8):
                                    MM(pb.t[:, :], wgb2.t[:, kc, jc * 128:(jc + 1) * 128], h1T.t[:, kc, sl],
                                       kc == 0, kc == 7, wgb2.bs + h1T.bs, pb.bs)
                                E("scalar", "activation", pb.bs, [glu.bs[jc]], out=glu.t[:, jc, sl], in_=pb.t[:, :],
                                  func=AF.Gelu_apprx_tanh)
                            ch = hd * 2 + jc
                            o_tl = Tl(None); o_tl.bs = [xc.bs[jc]]
                            i_tl = Tl(None); i_tl.bs = [xpre.bs[jc]]
                            conv("vector", o_tl, xc.t[:, jc, :], i_tl, xpre.t[:, jc, :], (rcw, rcb), rcw.t[:, ch, :],
                                 rcb.t[:, ch:ch + 1], 4, 2, nseq, L)
                            E("scalar", "copy", [xc.bs[jc]], xcb.bs, out=xcb.t[:, jc, :], in_=xc.t[:, jc, :])
                        for jc in range(2):
                            ch = hd * 2 + jc
                            for d in range(2):
                                for g in range(2):
                                    dst = rg if g == 0 else ig
                                    for blk in range(2):
                                        sl = slice(blk * 512, (blk + 1) * 512)
                                        pb = pbanks.next()
                                        for ic in range(2):
                                            MM(pb.t[:, :], gwb.t[:, d * 2 + g, ic, jc * 128:(jc + 1) * 128], xcb.t[:, ic, sl],
                                               ic == 0, ic == 1, gwb.bs + xcb.bs, pb.bs)
                                        E("scalar", "activation", pb.bs + gbT.bs, dst.bs, out=dst.t[:, sl], in_=pb.t[:, :],
                                          func=AF.Sigmoid, bias=gbT.t[:, d, g, ch:ch + 1], scale=1.0)
                                E("scalar", "activation", rg.bs + nsp.bs, at.bs, out=at.t[:], in_=rg.t[:], func=AF.Exp,
                                  scale=nsp.t[:, d, ch:ch + 1])
                                E("gpsimd", "tensor_tensor", at.bs, st_.bs, out=st_.t[:], in0=at.t[:], in1=at.t[:], op=ALU.mult)
                                E("scalar", "activation", st_.bs, st_.bs, out=st_.t[:], in_=st_.t[:], func=AF.Sqrt,
                                  bias=1.0, scale=-1.0)
                                E("gpsimd", "tensor_tensor", ig.bs + [xc.bs[jc]], ig.bs, out=ig.t[:], in0=ig.t[:],
                                  in1=xc.t[:, jc, :], op=ALU.mult)
                                E("vector", "tensor_tensor", ig.bs + st_.bs, ut.bs, out=ut.t[:], in0=ig.t[:], in1=st_.t[:],
                                  op=ALU.mult)
                                hd_t = hdir[d]
                                for sq in range(nseq):
                                    lo, hi = sq * L, (sq + 1) * L
                                    init = h0T.t[:, d, ch:ch + 1] if pi == 1 else 0.0
                                    if d == 0:
                                        o_, a_, u_ = hd_t.t[:, lo:hi], at.t[:, lo:hi], ut.t[:, lo:hi]
                                    else:
                                        o_, a_, u_ = hd_t.t[:, lo:hi][:, ::-1], at.t[:, lo:hi][:, ::-1], ut.t[:, lo:hi][:, ::-1]
                                    E("vector", "tensor_tensor_scan", at.bs + ut.bs + h0T.bs, hd_t.bs, out=o_, data0=a_,
                                      data1=u_, initial=init, op0=ALU.mult, op1=ALU.add)
                                if pi == 0:
                                    hv = hd_t.t[:].rearrange("p (s l) -> p s l", s=4)
                                    col = L - 1 if d == 0 else 0
                                    E("gpsimd", "tensor_copy", hd_t.bs, nst.bs, out=nst.t[:, :, d, ch:ch + 1],
                                      in_=hv[:, :, col:col + 1])
                            E("vector", "tensor_tensor", hdir[0].bs + hdir[1].bs, hdir[0].bs, out=hdir[0].t[:], in0=hdir[0].t[:],
                              in1=hdir[1].t[:], op=ALU.add)
                            E("gpsimd", "tensor_tensor", hdir[0].bs + [glu.bs[jc]], [yaT.bs[ch]], out=yaT.t[:, ch, :],
                              in0=hdir[0].t[:], in1=glu.t[:, jc, :], op=ALU.mult)
                    dbg("yaT%d" % pi, yaT, yaT.t[:], BF16)
                    if pi == 0:
                        dbg("nst", nst, nst.t[:])
                    S.barrier()
                    if stop == (pi, 3): S.enabled = False

                mT = alloc(mxs, "mT", [128, 8, 1024], BF16, nbuf=8)
                with ExitStack() as ph:
                    wbf_r = ring(ph, "mwbf", [128, 8, 256], BF16, 12)
                    gat = ring(ph, "gat", [128, 512], F32, 2)
                    gbt = ring(ph, "gbt", [128, 512], F32, 2)
                    mat = ring(ph, "mat", [128, 512], F32, 2)
                    w_in_v = w_in_bd.rearrange("(kc p) n -> p kc n", p=128)
                    wa_v = w_a_bd.rearrange("(kc p) n -> p kc n", p=128)
                    wb_v = w_b_bd.rearrange("(kc p) n -> p kc n", p=128)
                    for cg in range(4):
                        ws = []
                        for (src, bk) in ((wa_v[:, :, cg * 256:(cg + 1) * 256], "w_a"), (wb_v[:, :, cg * 256:(cg + 1) * 256], "w_b"),
                                          (w_in_v[:, :, 5120 + cg * 256:5120 + (cg + 1) * 256], "w_in"),
                                          (w_in_v[:, :, 6144 + cg * 256:6144 + (cg + 1) * 256], "w_in")):
                            wbf = wbf_r.next()
                            S.dma("sync", wbf.t[:], src, reads=[wcv_b[bk]], writes=wbf.bs)
                            ws.append(wbf)
                        for jc in range(2):
                            c = cg * 2 + jc
                            cs = slice(jc * 128, (jc + 1) * 128)
                            for blk in range(2):
                                sl = slice(blk * 512, (blk + 1) * 512)
                                pya = pbanks.next()
                                for kc in range(8):
                                    MM(pya.t[:, :], ws[0].t[:, kc, cs], yaT.t[:, kc, sl], kc == 0, kc == 7, ws[0].bs + yaT.bs, pya.bs)
                                pyb = pbanks.next()
                                for kc in range(8):
                                    MM(pyb.t[:, :], ws[1].t[:, kc, cs], ybT.t[:, kc, sl], kc == 0, kc == 7, ws[1].bs + ybT.bs, pyb.bs)
                                pga = pbanks.next()
                                for kc in range(8):
                                    MM(pga.t[:, :], ws[2].t[:, kc, cs], h1T.t[:, kc, sl], kc == 0, kc == 7, ws[2].bs + h1T.bs, pga.bs)
                                pgb = pbanks.next()
                                for kc in range(8):
                                    MM(pgb.t[:, :], ws[3].t[:, kc, cs], h1T.t[:, kc, sl], kc == 0, kc == 7, ws[3].bs + h1T.bs, pgb.bs)
                                ga = gat.next(); gb = gbt.next(); ma = mat.next()
                                E("scalar", "activation", pga.bs, ga.bs, out=ga.t[:], in_=pga.t[:, :], func=AF.Sigmoid)
                                E("scalar", "activation", pgb.bs, gb.bs, out=gb.t[:], in_=pgb.t[:, :], func=AF.Sigmoid)
                                E("vector", "tensor_tensor", pya.bs + ga.bs, ma.bs, out=ma.t[:], in0=pya.t[:, :], in1=ga.t[:], op=ALU.mult)
                                E("vector", "tensor_tensor", pyb.bs + gb.bs, gb.bs, out=gb.t[:], in0=pyb.t[:, :], in1=gb.t[:], op=ALU.mult)
                                E("gpsimd", "tensor_tensor", ma.bs + gb.bs, [mT.bs[c]], out=mT.t[:, c, sl], in0=ma.t[:], in1=gb.t[:], op=ALU.add)
                    dbg("mT%d" % pi, mT, mT.t[:], BF16)
                    S.barrier()
                    if stop == (pi, 4): S.enabled = False

                with ExitStack() as ph:
                    wo = alloc(ph, "wo", [128, 8, DM], BF16, nbuf=4)
                    wo_v = w_o_bd.rearrange("(kc p) n -> p kc n", p=128)
                    for cg in range(4):
                        S.dma("sync", wo.t[:, :, cg * 256:(cg + 1) * 256], wo_v[:, :, cg * 256:(cg + 1) * 256],
                              reads=[wcv_b["w_o"]], writes=[wo.bs[cg]])
                    g1r = alloc(ph, "g1r", [128, DM]); load_row(g1r, modrow_d[pi, 0, :], [modrow_b])
                    lg = alloc(ph, "lg", [128, DM]); load_row(lg, ln_d[0])
                    lb = alloc(ph, "lb", [128, DM]); load_row(lb, ln_d[1])
                    xr = ring(ph, "xt3", [128, DM], F32, 2)
                    pr = ring(ph, "pt3", [128, DM], F32, 2)
                    yt_r = ring(ph, "yt3", [128, DM], F32, 2)
                    lnt = (alloc(ph, "stats", [128, 2, 6]), alloc(ph, "mv", [128, 2]), alloc(ph, "rstd", [128, 1]))
                    for i in range(8):
                        xt = xr.next()
                        load("sync", xt, x_d[i * 128:(i + 1) * 128, :])
                        if pi == 1:
                            pt = pr.next()
                            load("sync", pt, pos_d[i * 128:(i + 1) * 128, :])
                            E("gpsimd", "tensor_tensor", xt.bs + pt.bs, xt.bs, out=xt.t[:], in0=xt.t[:], in1=pt.t[:], op=ALU.add)
                        yt = yt_r.next()
                        for hf in range(2):
                            pb = pbanks.next()
                            for kc in range(8):
                                MM(pb.t[:, :], mT.t[:, kc, i * 128:(i + 1) * 128], wo.t[:, kc, hf * 512:(hf + 1) * 512],
                                   kc == 0, kc == 7, mT.bs + wo.bs, pb.bs)
                            E("vector", "tensor_tensor", pb.bs + g1r.bs, yt.bs, out=yt.t[:, hf * 512:(hf + 1) * 512],
                              in0=pb.t[:, :], in1=g1r.t[:, hf * 512:(hf + 1) * 512], op=ALU.mult)
                        E("vector", "scalar_tensor_tensor", xt.bs + yt.bs, yt.bs, out=yt.t[:], in0=xt.t[:],
                          scalar=ALPHA, in1=yt.t[:], op0=ALU.mult, op1=ALU.add)
                        layer_norm_tile(lnt, yt, lg, lb)
                        S.dma("gpsimd", x2_d[pi, i * 128:(i + 1) * 128, :], yt.t[:], reads=yt.bs, writes=[x2_db[pi][i]])
                        if i == 0:
                            dbg("x2_%d" % pi, yt, yt.t[:])
                    S.barrier()
                    if stop == (pi, 5): S.enabled = False

              with ExitStack() as ph:
                  pbanks = pbanks4
                  wqb = alloc(ph, "wqb", [128, 8, 2048], BF16, nbuf=8)
                  UVN = int(os.environ.get('PEER_UVN', '16'))
                  uv_r = ring(ph, "uvg", [128, 2 * DM], BF16, UVN)
                  wq_v = wq_bd.rearrange("(kc p) n -> p kc n", p=128)
                  for cg in range(8):
                      S.dma("sync", wqb.t[:, :, cg * 256:(cg + 1) * 256], wq_v[:, :, cg * 256:(cg + 1) * 256],
                            reads=[wcv_b["wq"]], writes=[wqb.bs[cg]])
                  sh2r = alloc(ph, "sh2r", [128, DM]); load_row(sh2r, modrow_d[pi, 1, :], [modrow_b])
                  sc2r = alloc(ph, "sc2r", [128, DM]); load_row(sc2r, modrow_d[pi, 2, :], [modrow_b])
                  g2r = alloc(ph, "g2r", [128, DM]); load_row(g2r, modrow_d[pi, 3, :], [modrow_b])
                  lg = alloc(ph, "lg2", [128, DM]); load_row(lg, ln_d[2])
                  lb = alloc(ph, "lb2", [128, DM]); load_row(lb, ln_d[3])
                  lnt = (alloc(ph, "stats2", [128, 2, 6]), alloc(ph, "mv2", [128, 2]), alloc(ph, "rstd2", [128, 1]))
                  h2 = alloc(ph, "h2", [128, DM])
                  h2b_l = [alloc(ph, "h2b%d" % q, [128, DM], BF16) for q in range(2)]
                  h2T = alloc(ph, "h2T", [128, 8, 128], BF16)
                  qT = alloc(ph, "qT", [128, 16, 128])
                  sc = alloc(ph, "sc", [128, 16, 128])
                  scw = alloc(ph, "scw", [128, 128]); scw2 = alloc(ph, "scw2", [128, 128])
                  tops = alloc(ph, "tops", [128, 16, 16], nbuf=16); topi = alloc(ph, "topi", [128, 16, 16], U32, nbuf=16)

                  def _sub(tl, k):
                      w = Tl(None); w.bs = [tl.bs[k]]; return w
                  tops_b = [_sub(tops, k) for k in range(16)]; topi_b = [_sub(topi, k) for k in range(16)]
                  topif = alloc(ph, "topif", [128, 16, 16])
                  cand = alloc(ph, "cand", [128, 8, 16, 16]); candw = alloc(ph, "candw", [128, 256]); candw2 = alloc(ph, "candw2", [128, 256])
                  bs_ = alloc(ph, "bests", [128, 8, 16], nbuf=8); bp = alloc(ph, "bestp", [128, 8, 16], U32, nbuf=8)
                  bs_b = [_sub(bs_, k) for k in range(8)]; bp_b = [_sub(bp, k) for k in range(8)]
                  k1 = alloc(ph, "k1", [128, 8, 16], U32); k2 = alloc(ph, "k2", [128, 8, 16], U32)
                  k1f = alloc(ph, "k1f", [128, 8, 16]); k2f = alloc(ph, "k2f", [128, 8, 16])
                  oh = cand; i1f = alloc(ph, "i1f", [128, 8, 16]); i2f = alloc(ph, "i2f", [128, 8, 16])
                  io16 = alloc(ph, "io16", [128, 16]); io16i = alloc(ph, "io16i", [128, 16], I32)
                  E("gpsimd", "iota", [], io16i.bs, out=io16i.t[:], pattern=[[1, 16]], base=0, channel_multiplier=0)
                  E("vector", "tensor_copy", io16i.bs, io16.bs, out=io16.t[:], in_=io16i.t[:])
                  eidx_l = [alloc(ph, "eidx%d" % q, [128, 128], I32) for q in range(2)]
                  gw_l = [alloc(ph, "gw%d" % q, [128, 8, 16]) for q in range(2)]
                  zs = alloc(ph, "zs", [128, 8])
                  act = alloc(ph, "act", [128, 128]); wgt = alloc(ph, "wgt", [128, 128])
                  dg_r = ring(ph, "dg", [128, 128], BF16, 8)
                  ot_r = ring(ph, "ot", [128, DM], F32, 2)
                  x2_r = ring(ph, "x2t", [128, DM], F32, 2)

                  def top16x2(args_a, args_b):
                      chains = [args_a, args_b]
                      for (st, sap, wt, wap, os_, oi, ts, ti) in chains:
                          E("vector", "max", st.bs, ts.bs, out=os_[:, 0:8], in_=sap)
                      yield
                      for (st, sap, wt, wap, os_, oi, ts, ti) in chains:
                          E("vector", "max_index", st.bs + ts.bs, ti.bs, out=oi[:, 0:8], in_max=os_[:, 0:8], in_values=sap)
                      for (st, sap, wt, wap, os_, oi, ts, ti) in chains:
                          E("vector", "match_replace", st.bs + ts.bs, wt.bs, out=wap, in_to_replace=os_[:, 0:8], in_values=sap, imm_value=-1e30)
                      yield
                      for (st, sap, wt, wap, os_, oi, ts, ti) in chains:
                          E("vector", "max", wt.bs, ts.bs, out=os_[:, 8:16], in_=wap)
                      yield
                      for (st, sap, wt, wap, os_, oi, ts, ti) in chains:
                          E("vector", "max_index", wt.bs + ts.bs, ti.bs, out=oi[:, 8:16], in_max=os_[:, 8:16], in_values=wap)
                      yield

                  def top16(src_tl, src_ap, work_tl, work_ap, out_s, out_i, o_tl_s, o_tl_i):
                      E("vector", "max", src_tl.bs, o_tl_s.bs, out=out_s[:, 0:8], in_=src_ap)
                      E("vector", "max_index", src_tl.bs + o_tl_s.bs, o_tl_i.bs, out=out_i[:, 0:8], in_max=out_s[:, 0:8], in_values=src_ap)
                      E("vector", "match_replace", src_tl.bs + o_tl_s.bs, work_tl.bs, out=work_ap, in_to_replace=out_s[:, 0:8],
                        in_values=src_ap, imm_value=-1e30)
                      E("vector", "max", work_tl.bs, o_tl_s.bs, out=out_s[:, 8:16], in_=work_ap)
                      E("vector", "max_index", work_tl.bs + o_tl_s.bs, o_tl_i.bs, out=out_i[:, 8:16], in_max=out_s[:, 8:16], in_values=work_ap)

                  def front(i):
                      sl_ = i % 2
                      h2b = h2b_l[sl_]; eidx = eidx_l[sl_]; gw = gw_l[sl_]
                      x2t = x2_r.next()
                      S.dma("sync", x2t.t[:], x2_d[pi, i * 128:(i + 1) * 128, :], reads=[x2_db[pi][i]], writes=x2t.bs)
                      yield
                      E("vector", "tensor_tensor", x2t.bs + sc2r.bs, h2.bs, out=h2.t[:], in0=x2t.t[:], in1=sc2r.t[:], op=ALU.mult)
                      E("vector", "tensor_tensor", h2.bs + sh2r.bs, h2.bs, out=h2.t[:], in0=h2.t[:], in1=sh2r.t[:], op=ALU.add)
                      yield
                      E("scalar", "copy", h2.bs, h2b.bs, out=h2b.t[:], in_=h2.t[:])
                      yield
                      pt = pbf.next()
                      for c in range(8):
                          TR(pt.t[:, c * 128:(c + 1) * 128], h2b.t[:, c * 128:(c + 1) * 128], identb.t[:], h2b.bs + identb.bs, pt.bs)
                      yield
                      E("scalar", "copy", pt.bs, h2T.bs, out=h2T.t[:].rearrange("p a n -> p (a n)"), in_=pt.t[:, :])
                      yield
                      for g4 in range(4):
                          pb = pbanks.next()
                          for gg in range(4):
                              hp = g4 * 4 + gg
                              for kc in range(8):
                                  MM(pb.t[:, gg * 128:(gg + 1) * 128], wqb.t[:, kc, hp * 128:(hp + 1) * 128], h2T.t[:, kc, :],
                                     kc == 0, kc == 7, wqb.bs + h2T.bs, pb.bs)
                          yield
                          E("scalar", "copy", pb.bs, qT.bs, out=qT.t[:, g4 * 4:(g4 + 1) * 4, :].rearrange("p a n -> p (a n)"), in_=pb.t[:, :])
                          yield
                      for g4 in range(4):
                          pb = pbanks.next()
                          for gg in range(4):
                              hp = g4 * 4 + gg
                              MM(pb.t[:, gg * 128:(gg + 1) * 128], qT.t[:, hp, :], skT.t[:, hp % 2, :], True, True,
                                 qT.bs + skT.bs, pb.bs)
                          yield
                          E("scalar", "copy", pb.bs, sc.bs, out=sc.t[:, g4 * 4:(g4 + 1) * 4, :].rearrange("p a n -> p (a n)"), in_=pb.t[:, :])
                          yield
                      for hp in range(0, 16, 2):
                          yield from top16x2((sc, sc.t[:, hp, :], scw, scw.t[:], tops.t[:, hp, :], topi.t[:, hp, :], tops_b[hp], topi_b[hp]),
                                  (sc, sc.t[:, hp + 1, :], scw2, scw2.t[:], tops.t[:, hp + 1, :], topi.t[:, hp + 1, :], tops_b[hp + 1], topi_b[hp + 1]))
                      E("vector", "tensor_copy", topi.bs, topif.bs, out=topif.t[:], in_=topi.t[:])
                      tv = tops.t[:].rearrange("p (h q) k -> p h q k", q=2)
                      tiv = topif.t[:].rearrange("p (h q) k -> p h q k", q=2)
                      E("vector", "tensor_tensor", tops.bs, cand.bs, out=cand.t[:],
                        in0=tv[:, :, 0, :].unsqueeze(3).to_broadcast([128, 8, 16, 16]),
                        in1=tv[:, :, 1, :].unsqueeze(2).to_broadcast([128, 8, 16, 16]), op=ALU.add)
                      yield
                      for h in range(0, 8, 2):
                          yield from top16x2((cand, cand.t[:, h].rearrange("p a b -> p (a b)"), candw, candw.t[:], bs_.t[:, h, :], bp.t[:, h, :], bs_b[h], bp_b[h]),
                                  (cand, cand.t[:, h + 1].rearrange("p a b -> p (a b)"), candw2, candw2.t[:], bs_.t[:, h + 1, :], bp.t[:, h + 1, :], bs_b[h + 1], bp_b[h + 1]))
                      E("vector", "tensor_single_scalar", bp.bs, k1.bs, out=k1.t[:], in_=bp.t[:], scalar=4, op=ALU.logical_shift_right)
                      E("vector", "tensor_single_scalar", bp.bs, k2.bs, out=k2.t[:], in_=bp.t[:], scalar=15, op=ALU.bitwise_and)
                      E("vector", "tensor_copy", k1.bs, k1f.bs, out=k1f.t[:], in_=k1.t[:])
                      E("vector", "tensor_copy", k2.bs, k2f.bs, out=k2f.t[:], in_=k2.t[:])
                      yield
                      iob = io16.t[:].unsqueeze(1).unsqueeze(1).to_broadcast([128, 8, 16, 16])
                      for (kf_, q, dst) in ((k1f, 0, i1f), (k2f, 1, i2f)):
                          E("vector", "tensor_tensor", kf_.bs + io16.bs, oh.bs, out=oh.t[:],
                            in0=kf_.t[:].unsqueeze(3).to_broadcast([128, 8, 16, 16]), in1=iob, op=ALU.is_equal)
                          yield
                          E("vector", "tensor_tensor", oh.bs + topif.bs, oh.bs, out=oh.t[:], in0=oh.t[:],
                            in1=tiv[:, :, q, :].unsqueeze(2).to_broadcast([128, 8, 16, 16]), op=ALU.mult)
                          yield
                          E("vector", "tensor_reduce", oh.bs, dst.bs, out=dst.t[:], in_=oh.t[:], axis=AX.X, op=ALU.add)
                          yield
                      E("vector", "scalar_tensor_tensor", i1f.bs + i2f.bs, i1f.bs, out=i1f.t[:], in0=i1f.t[:], scalar=128.0,
                        in1=i2f.t[:], op0=ALU.mult, op1=ALU.add)
                      E("vector", "tensor_copy", i1f.bs, eidx.bs, out=eidx.t[:].rearrange("p (h k) -> p h k", h=8), in_=i1f.t[:])
                      yield
                      E("vector", "tensor_tensor", bs_.bs, gw.bs, out=gw.t[:], in0=bs_.t[:],
                        in1=bs_.t[:, :, 0:1].to_broadcast([128, 8, 16]), op=ALU.subtract)
                      yield
                      E("scalar", "activation", gw.bs, gw.bs, out=gw.t[:], in_=gw.t[:], func=AF.Exp)
                      yield
                      E("vector", "tensor_reduce", gw.bs, zs.bs, out=zs.t[:], in_=gw.t[:], axis=AX.X, op=ALU.add)
                      E("vector", "reciprocal", zs.bs, zs.bs, out=zs.t[:], in_=zs.t[:])
                      E("vector", "tensor_tensor", gw.bs + zs.bs, gw.bs, out=gw.t[:], in0=gw.t[:],
                        in1=zs.t[:].unsqueeze(2).to_broadcast([128, 8, 16]), op=ALU.mult)
                      x2_of[i] = x2t
                      yield

                  def back(i, nxt):
                      sl_ = i % 2
                      h2b = h2b_l[sl_]; eidx = eidx_l[sl_]; gw = gw_l[sl_]
                      x2t = x2_of[i]
                      gwf = gw.t[:].rearrange("p h k -> p (h k)")
                      uvs_of = {}
                      GS = PEER_GS
                      NG = 128 // GS
                      SK = PEER_SK
                      assert (SK + 1) * GS + GS - 1 <= UVN + GS - 1 and UVN >= (SK + 2) * GS - 0, "gather ring too shallow for skew"
                      for jg in range(NG + SK):
                          if jg < NG:
                              uvs = []
                              for jj in range(GS):
                                  j = jg * GS + jj
                                  uv = uv_r.next()
                                  uvs.append(uv)
                                  GATHER(uv.t[:], uv_d, eidx.t[:, j:j + 1], eidx.bs + [uv_b], uv.bs)
                                  E("vector", "tensor_tensor", uv.bs + h2b.bs, uv.bs, out=uv.t[:, 0:DM], in0=uv.t[:, 0:DM], in1=h2b.t[:],
                                    op=ALU.mult)
                                  E("scalar", "activation", uv.bs, uv.bs + (act.bs if jj in (0, GS - 1) else []), out=uv.t[:, 0:DM],
                                    in_=uv.t[:, 0:DM], func=AF.Identity, accum_out=act.t[:, j:j + 1])
                              uvs_of[jg] = uvs
                          if nxt is not None and jg < NG:
                              for _ in range(PEER_FS):
                                  next(nxt, None)
                          if jg >= SK:
                              g_ = jg - SK
                              grp = slice(g_ * GS, (g_ + 1) * GS)
                              uvs = uvs_of.pop(g_)
                              E("scalar", "activation", act.bs, wgt.bs, out=wgt.t[:, grp], in_=act.t[:, grp], func=AF.Gelu_apprx_tanh)
                              E("vector", "tensor_tensor", wgt.bs + gw.bs, wgt.bs, out=wgt.t[:, grp], in0=wgt.t[:, grp], in1=gwf[:, grp], op=ALU.mult)
                              for jj in range(GS):
                                  j = g_ * GS + jj
                                  uv = uvs[jj]
                                  dg = dg_r.next()
                                  E("scalar", "activation", identf.bs + wgt.bs, dg.bs, out=dg.t[:], in_=identf.t[:], func=AF.Identity,
                                    scale=wgt.t[:, j:j + 1])
                                  for hf in range(2):
                                      MM(pacc[hf].t[:, :], dg.t[:], uv.t[:, DM + hf * 512:DM + (hf + 1) * 512], j == 0, j == 127,
                                         dg.bs + uv.bs, pacc[hf].bs)
                      ot = ot_r.next()
                      for hf in range(2):
                          E("vector", "tensor_tensor", pacc[hf].bs + g2r.bs, ot.bs, out=ot.t[:, hf * 512:(hf + 1) * 512], in0=pacc[hf].t[:, :],
                            in1=g2r.t[:, hf * 512:(hf + 1) * 512], op=ALU.mult)
                      E("vector", "scalar_tensor_tensor", x2t.bs + ot.bs, ot.bs, out=ot.t[:], in0=x2t.t[:], scalar=ALPHA,
                        in1=ot.t[:], op0=ALU.mult, op1=ALU.add)
                      layer_norm_tile(lnt, ot, lg, lb, eng2="vector")
                      S.dma("sync", yout[pi][i * 128:(i + 1) * 128, :], ot.t[:], reads=ot.bs)

                  x2_of = {}
                  g0 = front(0)
                  for _ in g0:
                      pass
                  for i in range(8):
                      nxt = front(i + 1) if i + 1 < 8 else None
                      back(i, nxt)
                      if nxt is not None:
                          for _ in nxt:
                              pass
                  S.barrier()

        S.enabled = True
        with ExitStack() as ph:
            nso = alloc(ph, "nso", [64, 128])
            if stop is None or stop >= (0, 3):
                pb = pbanks.next()
                TR(pb.t[0:64, 0:128], nst.t[:].rearrange("p s d c -> p (s d c)"), identf.t[:], nst.bs + identf.bs, pb.bs)
                E("vector", "tensor_copy", pb.bs, nso.bs, out=nso.t[:], in_=pb.t[0:64, 0:128])
                S.dma("sync", ns_d, nso.t[:], reads=nso.bs)
            S.emit(top)
    return nc, dbg_names


def _consts():
    bf = ml_dtypes.bfloat16
    c = {}
    c["identf"] = np.eye(128, dtype=np.float32)
    c["identb"] = np.eye(128, dtype=np.float32).astype(bf)
    m0 = np.ones((128, 1), np.float32); m0[0, 0] = 0.0
    c["mask0"] = m0
    t = np.arange(1024)
    r = (t // 64).astype(np.float32); col = (t % 64).astype(np.float32)
    quarter = DM // 4
    omega = (1.0 / (10000.0 ** (np.arange(quarter, dtype=np.float32) / quarter))).astype(np.float32)
    er = r[:, None] * omega[None, :]; ec = col[:, None] * omega[None, :]
    c["pos"] = np.concatenate([np.sin(er), np.cos(er), np.sin(ec), np.cos(ec)], axis=-1).astype(np.float32)
    deltas = np.linspace(math.log(1e-2) / 1.5, math.log(1e-2) / 0.3, DM, dtype=np.float32)
    bands = np.linspace(1e-4, 15, 16, dtype=np.float32)
    for L in (256, 1024):
        ti = np.arange(L, dtype=np.float32)
        tn = ti / max(L - 1, 1)
        w = (2.0 * math.pi * ti / L).astype(np.float32)
        fw = w[:, None] * bands[None, :]
        z = np.concatenate([tn[:, None], np.cos(fw), -np.sin(fw)], axis=-1).astype(np.float32)
        c["zT%d" % L] = np.ascontiguousarray(z.T)
        c["dec%d" % L] = np.exp(-tn[:, None] * np.abs(deltas)[None, :]).astype(np.float32)
        tt = np.arange(L, dtype=np.float64)
        ang = np.pi * np.outer(tt, tt) / L
        C = np.cos(ang)
        Sm = -np.sin(ang)
        Sm[:, 0] = (-1.0) ** tt
        c["C%d" % L] = C.astype(np.float32).astype(bf)
        c["S%d" % L] = Sm.astype(np.float32).astype(bf)
        c["ST%d" % L] = np.ascontiguousarray(Sm.T).astype(np.float32).astype(bf)
        nfc = L // 128
        f = np.arange(L)
        wfre = np.where(f == 0, 1.0 / (2 * L), 1.0 / L)
        wB = np.where(f == 0, 0.0, 1.0 / L)
        mD = np.where(f == 0, 0.0, 1.0 / L)
        m2 = np.where(f == 0, 1.0 / (2 * L), 0.0)
        tab = np.stack([wfre, wB, mD, m2], 0).reshape(4, nfc, 128).transpose(2, 0, 1)
        c["wtab%d" % L] = np.ascontiguousarray(tab).astype(np.float32)
    return c


def _chunkT(v):
    v = np.asarray(v)
    lead = v.shape[:-1]
    n = v.shape[-1] // 128
    v = v.reshape(lead + (n, 128))
    return np.ascontiguousarray(np.moveaxis(v, -1, 0))


def _in_maps(inp):
    f = lambda a: np.ascontiguousarray(np.asarray(a, dtype=np.float32))
    cst = _consts()
    shared = dict(cst)
    shared["w_ada"] = f(inp["w_ada"][0])
    b_ada = f(inp["b_ada"][0])
    shared["b_adaT"] = _chunkT(b_ada[:2048].reshape(2, 1024))
    shared["b_ada_rows"] = np.ascontiguousarray(b_ada[2048:].reshape(4, 1024))
    shared["w_in"] = f(inp["w_in"][0])
    shared["rcw"] = np.ascontiguousarray(_chunkT(f(inp["rnn_conv_w"][0])).transpose(0, 2, 1))
    shared["rcb"] = _chunkT(f(inp["rnn_conv_b"][0]))
    shared["gate_w"] = f(inp["rnn_gate_w"][0])
    shared["gbT"] = _chunkT(f(inp["rnn_gate_b"][0]))
    shared["lamT"] = _chunkT(f(inp["rnn_lambda"][0]))
    shared["hcw"] = np.ascontiguousarray(_chunkT(f(inp["hy_conv_w"][0])).transpose(0, 2, 1))
    shared["hcb"] = _chunkT(f(inp["hy_conv_b"][0]))
    shared["hy_w1"] = f(inp["hy_ffn_w1"][0])
    shared["hy_b1"] = f(inp["hy_ffn_b1"][0]).reshape(64, 1)
    shared["hy_w2"] = f(inp["hy_ffn_w2"][0])
    shared["hy_b2"] = f(inp["hy_ffn_b2"][0]).reshape(64, 1)
    shared["hy_freq"] = f(inp["hy_sin_freq"][0]).reshape(64, 1)
    shared["hy_w3"] = f(inp["hy_ffn_w3"][0])
    shared["hy_b3"] = f(inp["hy_ffn_b3"][0])
    shared["skipT"] = _chunkT(f(inp["hy_skip"][0]))
    shared["w_a"] = f(inp["w_branch_a"][0])
    shared["w_b"] = f(inp["w_branch_b"][0])
    shared["w_o"] = f(inp["w_out"][0])
    shared["ln1_g"] = f(inp["ln1_g"][0]); shared["ln1_b"] = f(inp["ln1_b"][0])
    shared["ln2_g"] = f(inp["ln2_g"][0]); shared["ln2_b"] = f(inp["ln2_b"][0])
    shared["wq"] = f(inp["peer_w_query"][0])
    shared["skT"] = np.ascontiguousarray(f(inp["peer_sub_keys"][0]).transpose(2, 0, 1))
    shared["peer_u"] = f(inp["peer_u"][0])
    shared["peer_v"] = f(inp["peer_v"][0])
    xp = f(inp["x_prompt"]); xs = f(inp["x_sample"]); stt = f(inp["state_rglru"]); cc = f(inp["c"]); cctx = f(inp["c_ctx"])
    maps = []
    for i in range(N_CORES):
        m = dict(shared)
        m["xp"] = np.ascontiguousarray(xp[4 * i:4 * i + 4].reshape(1024, DM))
        m["xs"] = np.ascontiguousarray(xs[i])
        cond = np.stack([cctx, cc[i]], 0)
        m["condT"] = _chunkT(cond).transpose(0, 2, 1).copy()
        m["h0T"] = _chunkT(stt[i, 0])
        maps.append(m)
    return maps


_CACHE = {}


def kernel(**inputs):
    if "nc" not in _CACHE:
        _CACHE["nc"] = build()
    nc, _ = _CACHE["nc"]
    maps = _in_maps(inputs)
    res = run_bass_kernel_spmd(nc, maps, core_ids=list(range(N_CORES)))
    yp = np.zeros((32, 256, DM), np.float32)
    ys = np.zeros((8, 1024, DM), np.float32)
    ns = np.zeros((32, 1, 2, 1024), np.float32)
    for i in range(N_CORES):
        r = res.results[i]
        yp[4 * i:4 * i + 4] = np.asarray(r["yp"]).reshape(4, 256, DM)
        ys[i] = np.asarray(r["ys"])
        ns[4 * i:4 * i + 4, 0] = np.asarray(r["ns"]).reshape(4, 2, 1024)
    return yp, ys, ns
```

```python
import math
import os
HYSKIP = os.environ.get('HYSKIP', '')
PEER_SK = int(os.environ.get('PEER_SK', '2'))
PEER_FS = int(os.environ.get('PEER_FS', '3'))
PEER_GS = int(os.environ.get('PEER_GS', '4'))
from contextlib import ExitStack

import ml_dtypes
import numpy as np

import concourse.bass as bass
import concourse.mybir as mybir
from concourse.bass_utils import run_bass_kernel_spmd

F32 = mybir.dt.float32
BF16 = mybir.dt.bfloat16
I32 = mybir.dt.int32
U32 = mybir.dt.uint32
AF = mybir.ActivationFunctionType
ALU = mybir.AluOpType
AX = mybir.AxisListType

ENGS = ["sync", "scalar", "vector", "gpsimd", "tensor"]
DBGOPS = []
NOSYNC_ENGS = set(os.environ.get('NOSYNC', '').split(',')) - {''}
N_CORES = 8
DM = 1024
ALPHA = 2.0 ** 0.25
LN_EPS = 1e-5
RGLRU_C = 8.0


class Buf:
    __slots__ = ("name", "last_w", "readers")

    def __init__(self, name=""):
        self.name = name
        self.last_w = None
        self.readers = {}


class Op:
    __slots__ = ("eng", "fn", "deps", "key", "pos", "is_dma", "sig", "val", "waits",
                 "nosame", "vc_issue", "vc_done")


class Sched:
    def __init__(self, nc, dma_slots=16, same_engine_sync=True):
        self.nc = nc
        self.ops = []
        self.per_eng = {e: [] for e in ENGS}
        self.npos = {}
        self.dma_slots = dma_slots
        self.dma_count = {e: 0 for e in ENGS}
        self.slot_last = {}
        self.same_engine_sync = same_engine_sync
        self.enabled = True

    def _new(self, eng, fn, dma, nosame):
        o = Op()
        o.eng = eng
        o.fn = fn
        o.is_dma = dma
        o.sig = dma
        o.nosame = nosame
        return o

    def op(self, eng, fn, reads=(), writes=(), dma=False, nosame=False):
        if not self.enabled:
            return None
        o = self._new(eng, fn, dma, nosame)
        deps = []
        for b in reads:
            if b.last_w is not None:
                deps.append(b.last_w)
        for b in writes:
            if b.last_w is not None:
                deps.append(b.last_w)
            deps.extend(b.readers.values())
        if dma:
            slot = self.dma_count[eng] % self.dma_slots
            self.dma_count[eng] += 1
            o.key = ("dma", eng, slot)
            prev = self.slot_last.get(o.key)
            if prev is not None:
                deps.append(prev)
            self.slot_last[o.key] = o
        else:
            o.key = eng
        o.pos = self.npos.get(o.key, 0)
        self.npos[o.key] = o.pos + 1
        o.deps = [d for d in deps if d is not o]
        rk = o.key if not dma else ("dmaop", id(o))
        for b in reads:
            b.readers[rk] = o
        for b in writes:
            b.last_w = o
            b.readers = {}
        self.ops.append(o)
        self.per_eng[eng].append(o)
        return o

    def dma(self, eng, out, in_, reads=(), writes=(), **kw):
        return self.op(eng, lambda e: e.dma_start(out=out, in_=in_, **kw), reads, writes, dma=True)

    def barrier(self):
        if not self.enabled:
            return
        lasts = [self.per_eng[e][-1] for e in ENGS if self.per_eng[e]]
        lasts = [o for o in lasts if o.fn is not None]
        lasts += list(self.slot_last.values())
        for e in ENGS:
            o = self._new(e, None, False, False)
            o.key = e
            o.pos = self.npos.get(e, 0)
            self.npos[e] = o.pos + 1
            o.deps = list(lasts)
            self.ops.append(o)
            self.per_eng[e].append(o)

    def finalize(self):
        last_on_eng = {}
        for o in self.ops:
            vc = {}
            prev = last_on_eng.get(o.eng)
            if prev is not None:
                vc.update(prev.vc_issue)
            waits = []
            best = {}
            for d in o.deps:
                if d.key not in best or best[d.key].pos < d.pos:
                    best[d.key] = d
            for k, d in best.items():
                if (not d.is_dma) and (not o.is_dma) and d.eng == o.eng and (
                        o.nosame or not self.same_engine_sync or o.eng in NOSYNC_ENGS):
                    continue
                if vc.get(k, -1) >= d.pos:
                    continue
                waits.append(d)
                d.sig = True
                for kk, vv in d.vc_done.items():
                    if vc.get(kk, -1) < vv:
                        vc[kk] = vv
            o.waits = waits
            o.vc_issue = vc
            vd = dict(vc)
            if o.fn is not None:
                vd[o.key] = o.pos
            o.vc_done = vd
            last_on_eng[o.eng] = o
        cnt = {}
        for o in self.ops:
            if o.is_dma:
                o.val = 16 * (o.pos + 1)
            elif o.sig:
                cnt[o.key] = cnt.get(o.key, 0) + 1
                o.val = cnt[o.key]

    def emit(self, stack):
        nc = self.nc
        fin = self._new("sync", None, False, False)
        fin.key = "sync"
        fin.pos = self.npos.get("sync", 0)
        fin.deps = list(self.slot_last.values())
        self.ops.append(fin)
        self.per_eng["sync"].append(fin)
        self.finalize()
        sems = {}
        for e in ENGS:
            sems[e] = stack.enter_context(nc.semaphore("s_" + e))
        for k in self.slot_last.keys():
            sems[k] = stack.enter_context(nc.semaphore("d_%s_%d" % (k[1], k[2])))
        per_eng = self.per_eng

        def run(engname, eng):
            for o in per_eng[engname]:
                for d in o.waits:
                    eng.wait_ge(sems[d.key], d.val)
                if o.fn is None:
                    continue
                inst = o.fn(eng)
                if o.sig:
                    inst.then_inc(sems[o.key], 16 if o.is_dma else 1)

        with nc.Block() as block:
            @block.sync
            def _(e):
                run("sync", e)

            @block.scalar
            def _(e):
                run("scalar", e)

            @block.vector
            def _(e):
                run("vector", e)

            @block.gpsimd
            def _(e):
                run("gpsimd", e)

            @block.tensor
            def _(e):
                run("tensor", e)


class Tl:
    def __init__(self, t, nbuf=1, name=""):
        self.t = t
        self.bs = [Buf(name + str(i)) for i in range(nbuf)]
        self.b = self.bs[0]


class Ring:
    def __init__(self, tiles):
        self.tiles = tiles
        self.i = 0

    def next(self):
        t = self.tiles[self.i % len(self.tiles)]
        self.i += 1
        return t


def build(debug=(), stop=None):
    nc = bass.Bass("TRN2", target_bir_lowering=False)
    S = Sched(nc)
    debug = set(debug)
    dbg_names = []

    def din(name, shape, dt=F32):
        return nc.dram_tensor(name, list(shape), dt, kind="ExternalInput").ap()

    def dout(name, shape, dt=F32):
        return nc.dram_tensor(name, list(shape), dt, kind="ExternalOutput").ap()

    xin = [din("xp", [1024, DM]), din("xs", [1024, DM])]
    pos_d = din("pos", [1024, DM])
    condT_d = din("condT", [128, 8, 2])
    h0T_d = din("h0T", [128, 2, 8])
    w_ada_d = din("w_ada", [DM, 6 * DM])
    b_adaT_d = din("b_adaT", [128, 2, 8])
    b_ada_rows_d = din("b_ada_rows", [4, DM])
    w_in_d = din("w_in", [DM, 7168])
    rcw_d = din("rcw", [128, 8, 4])
    rcb_d = din("rcb", [128, 8])
    gate_w_d = din("gate_w", [2, 2, 4, 256, 256])
    gbT_d = din("gbT", [128, 2, 2, 8])
    lamT_d = din("lamT", [128, 2, 8])
    hcw_d = din("hcw", [128, 24, 3])
    hcb_d = din("hcb", [128, 24])
    hy_w1_d = din("hy_w1", [33, 64])
    hy_b1_d = din("hy_b1", [64, 1])
    hy_w2_d = din("hy_w2", [64, 64])
    hy_b2_d = din("hy_b2", [64, 1])
    hy_freq_d = din("hy_freq", [64, 1])
    hy_w3_d = din("hy_w3", [64, 4096])
    hy_b3_d = din("hy_b3", [4096])
    skipT_d = din("skipT", [128, 2, 8])
    w_a_d = din("w_a", [DM, DM])
    w_b_d = din("w_b", [DM, DM])
    w_o_d = din("w_o", [DM, DM])
    ln_d = [din("ln1_g", [DM]), din("ln1_b", [DM]), din("ln2_g", [DM]), din("ln2_b", [DM])]
    wq_d = din("wq", [DM, 2048])
    skT_d = din("skT", [128, 2, 128])
    pu_d = din("peer_u", [16384, DM])
    pv_d = din("peer_v", [16384, DM])
    identf_d = din("identf", [128, 128])
    identb_d = din("identb", [128, 128], BF16)
    zT_d = [din("zT256", [33, 256]), din("zT1024", [33, 1024])]
    dec_d = [din("dec256", [256, DM]), din("dec1024", [1024, DM])]
    C_d = [din("C256", [256, 256], BF16), din("C1024", [1024, 1024], BF16)]
    Sm_d = [din("S256", [256, 256], BF16), din("S1024", [1024, 1024], BF16)]
    ST_d = [din("ST256", [256, 256], BF16), din("ST1024", [1024, 1024], BF16)]
    wtab_d = [din("wtab256", [128, 4, 2]), din("wtab1024", [128, 4, 8])]
    mask0_d = din("mask0", [128, 1])

    yout = [dout("yp", [1024, DM]), dout("ys", [1024, DM])]
    ns_d = dout("ns", [64, 128])
    modrow_d = nc.dram_tensor("modrow", [2, 4, DM], F32, kind="Internal").ap()
    modrow_b = Buf("modrow")
    w_in_bd = nc.dram_tensor("w_in_bf16", [DM, 7168], BF16, kind="Internal").ap()
    w_a_bd = nc.dram_tensor("w_a_bf16", [DM, DM], BF16, kind="Internal").ap()
    w_b_bd = nc.dram_tensor("w_b_bf16", [DM, DM], BF16, kind="Internal").ap()
    w_o_bd = nc.dram_tensor("w_o_bf16", [DM, DM], BF16, kind="Internal").ap()
    wq_bd = nc.dram_tensor("wq_bf16", [DM, 2048], BF16, kind="Internal").ap()
    gate_w_bd = nc.dram_tensor("gate_w_bf16", [2, 2, 4, 256, 256], BF16, kind="Internal").ap()
    wcv_b = {k: Buf("wcv_" + k) for k in ("w_in", "w_a", "w_b", "w_o", "wq", "gate")}
    uv_d = nc.dram_tensor("uv_bf16", [16384, 2 * DM], BF16, kind="Internal").ap()
    uv_b = Buf("uv")
    x2_d = nc.dram_tensor("x2_scratch", [2, 1024, DM], F32, kind="Internal").ap()
    x2_db = [[Buf("x2d") for _ in range(8)] for _ in range(2)]

    top = ExitStack()
    with top:
        uid = [0]

        def alloc(scope, name, shape, dt=F32, nbuf=1):
            uid[0] += 1
            t = scope.enter_context(nc.sbuf_tensor("s%d_%s" % (uid[0], name), list(shape), dt))
            return Tl(t, nbuf, name)

        def ring(scope, name, shape, dt, n):
            return Ring([alloc(scope, "%s_%d" % (name, i), shape, dt) for i in range(n)])

        _pb = [Tl(top.enter_context(nc.psum_tensor("pb%d" % i, [128, 512], F32)), 1, "pb%d" % i) for i in range(6)]
        pbanks6 = Ring(_pb)
        pbanks4 = Ring(_pb[:4])
        pacc = _pb[4:6]
        pbanks = pbanks6
        pbf = Ring([Tl(top.enter_context(nc.psum_tensor("pbf%d" % i, [128, 1024], BF16)), 1, "pbf%d" % i)
                    for i in range(2)])

        def dbg(name, tl, ap, dt=F32):
            if name not in debug:
                return
            o = dout("dbg_" + name, list(ap.shape), dt)
            dbg_names.append("dbg_" + name)
            S.dma("sync", o, ap, reads=tl.bs)

        def E(eng, meth, reads, writes, nosame=False, **kw):
            return S.op(eng, lambda e: getattr(e, meth)(**kw), reads, writes, nosame=nosame)

        def MM(out, lhsT, rhs, start, stop, reads, writes):
            return S.op("tensor", lambda e: e.matmul(out, lhsT=lhsT, rhs=rhs, start=start, stop=stop),
                        reads, writes, nosame=True)

        def GATHER(out, table, idx, reads, writes):
            return S.op("gpsimd", lambda e: e.indirect_dma_start(
                out=out, out_offset=None, in_=table, in_offset=bass.IndirectOffsetOnAxis(ap=idx, axis=0)),
                reads, writes, dma=True)

        def TR(out, in_, ident, reads, writes):
            return S.op("tensor", lambda e: e.transpose(out=out, in_=in_, identity=ident),
                        reads, writes, nosame=True)

        def load(eng, tl, dram_ap, sb_ap=None, extra_reads=()):
            S.dma(eng, sb_ap if sb_ap is not None else tl.t[:], dram_ap, reads=list(extra_reads), writes=tl.bs)

        S.dma("gpsimd", w_in_bd, w_in_d, writes=[wcv_b["w_in"]])
        S.dma("gpsimd", w_b_bd, w_b_d, writes=[wcv_b["w_b"]])
        S.dma("gpsimd", gate_w_bd.rearrange("a b c i j -> (a b c i) j"), gate_w_d.rearrange("a b c i j -> (a b c i) j"),
              writes=[wcv_b["gate"]])
        S.dma("gpsimd", w_a_bd, w_a_d, writes=[wcv_b["w_a"]])
        S.dma("gpsimd", w_o_bd, w_o_d, writes=[wcv_b["w_o"]])
        S.dma("gpsimd", wq_bd, wq_d, writes=[wcv_b["wq"]])
        for cq in range(4):
            r0, r1 = cq * 4096, (cq + 1) * 4096
            S.dma("gpsimd", uv_d[r0:r1, 0:DM], pu_d[r0:r1, :], writes=[uv_b])
            S.dma("gpsimd", uv_d[r0:r1, DM:2 * DM], pv_d[r0:r1, :], writes=[uv_b])
        identf = alloc(top, "identf", [128, 128]); load("sync", identf, identf_d)
        identb = alloc(top, "identb", [128, 128], BF16); load("sync", identb, identb_d)
        mask0 = alloc(top, "mask0", [128, 1]); load("sync", mask0, mask0_d)
        epsb = alloc(top, "epsb", [128, 1])
        E("vector", "memset", [], epsb.bs, ap=epsb.t[:], constant=LN_EPS)
        rcw = alloc(top, "rcw", [128, 8, 4]); load("sync", rcw, rcw_d)
        rcb = alloc(top, "rcb", [128, 8]); load("sync", rcb, rcb_d)
        gbT = alloc(top, "gbT", [128, 2, 2, 8]); load("sync", gbT, gbT_d)
        lamT = alloc(top, "lamT", [128, 2, 8]); load("sync", lamT, lamT_d)
        h0T = alloc(top, "h0T", [128, 2, 8]); load("sync", h0T, h0T_d)
        hcw = alloc(top, "hcw", [128, 24, 3]); load("sync", hcw, hcw_d)
        hcb = alloc(top, "hcb", [128, 24]); load("sync", hcb, hcb_d)
        skipT = alloc(top, "skipT", [128, 2, 8]); load("sync", skipT, skipT_d)
        skT = alloc(top, "skT", [128, 2, 128]); load("sync", skT, skT_d)
        nsp = alloc(top, "nsp", [128, 2, 8])
        E("scalar", "activation", lamT.bs, nsp.bs, out=nsp.t[:], in_=lamT.t[:], func=AF.Exp, scale=-1.0)
        E("scalar", "activation", nsp.bs, nsp.bs, out=nsp.t[:], in_=nsp.t[:], func=AF.Ln, bias=1.0, scale=1.0)
        E("vector", "tensor_scalar", nsp.bs, nsp.bs, out=nsp.t[:], in0=nsp.t[:], scalar1=-RGLRU_C, scalar2=None,
          op0=ALU.mult)
        modT = alloc(top, "modT", [128, 2, 8, 2])
        nst = alloc(top, "nst", [128, 4, 2, 8])

        with ExitStack() as ph:
            condT = alloc(ph, "condT", [128, 8, 2]); load("sync", condT, condT_d)
            condS = alloc(ph, "condS", [128, 8, 2])
            E("scalar", "activation", condT.bs, condS.bs, out=condS.t[:], in_=condT.t[:], func=AF.Silu)
            b_adaT = alloc(ph, "b_adaT", [128, 2, 8]); load("sync", b_adaT, b_adaT_d)
            brow = alloc(ph, "brow", [1, 4, DM])
            load("sync", brow, b_ada_rows_d.rearrange("(o a) n -> o a n", o=1))
            wa_ring = ring(ph, "wa", [128, 8, DM], F32, 2)
            rowt = ring(ph, "rowt", [1, DM], F32, 2)
            for ty in range(6):
                wa = wa_ring.next()
                load("sync", wa, w_ada_d[:, ty * DM:(ty + 1) * DM].rearrange("(kc p) n -> p kc n", p=128))
                if ty < 2:
                    pb = pbanks.next()
                    for m in range(8):
                        for kc in range(8):
                            MM(pb.t[:, m * 2:m * 2 + 2], wa.t[:, kc, m * 128:(m + 1) * 128], condS.t[:, kc, :],
                               kc == 0, kc == 7, wa.bs + condS.bs, pb.bs)
                    E("vector", "tensor_tensor", pb.bs + b_adaT.bs, modT.bs,
                      out=modT.t[:, ty], in0=pb.t[:, 0:16].rearrange("p (m j) -> p m j", j=2),
                      in1=b_adaT.t[:, ty].unsqueeze(2).to_broadcast([128, 8, 2]), op=ALU.add)
                    if ty == 1:
                        E("vector", "tensor_scalar", modT.bs, modT.bs, out=modT.t[:, 1], in0=modT.t[:, 1],
                          scalar1=1.0, scalar2=None, op0=ALU.add)
                else:
                    for j in range(2):
                        rt = rowt.next()
                        for hf in range(2):
                            pb = pbanks.next()
                            for kc in range(8):
                                MM(pb.t[0:1, :], condS.t[:, kc, j:j + 1], wa.t[:, kc, hf * 512:(hf + 1) * 512],
                                   kc == 0, kc == 7, wa.bs + condS.bs, pb.bs)
                            E("vector", "tensor_tensor", pb.bs + brow.bs, rt.bs,
                              out=rt.t[0:1, hf * 512:(hf + 1) * 512], in0=pb.t[0:1, :],
                              in1=brow.t[0:1, ty - 2, hf * 512:(hf + 1) * 512], op=ALU.add)
                        if ty == 4:
                            E("vector", "tensor_scalar", rt.bs, rt.bs, out=rt.t[:], in0=rt.t[:], scalar1=1.0,
                              scalar2=None, op0=ALU.add)
                        S.dma("sync", modrow_d[j, ty - 2:ty - 1, :], rt.t[0:1, :], reads=rt.bs, writes=[modrow_b])
            S.barrier()

        def load_row(tl, dram_row, extra=()):
            S.dma("sync", tl.t[:], dram_row.partition_broadcast(128), reads=list(extra), writes=tl.bs)

        def layer_norm_tile(scope_tiles, xt, g_row, b_row, eng2="gpsimd"):
            stats, mv, rstd = scope_tiles
            E("vector", "bn_stats", xt.bs, stats.bs, out=stats.t[:, 0, :], in_=xt.t[:, 0:512])
            E("vector", "bn_stats", xt.bs, stats.bs, out=stats.t[:, 1, :], in_=xt.t[:, 512:1024])
            E("vector", "bn_aggr", stats.bs, mv.bs, out=mv.t[:], in_=stats.t[:].rearrange("p a b -> p (a b)"))
            E("scalar", "activation", mv.bs + epsb.bs, rstd.bs, out=rstd.t[:], in_=mv.t[:, 1:2], func=AF.Sqrt,
              bias=epsb.t[:], scale=1.0)
            E("vector", "reciprocal", rstd.bs, rstd.bs, out=rstd.t[:], in_=rstd.t[:])
            E("vector", "tensor_scalar", xt.bs + mv.bs + rstd.bs, xt.bs, out=xt.t[:], in0=xt.t[:],
              scalar1=mv.t[:, 0:1], scalar2=rstd.t[:], op0=ALU.subtract, op1=ALU.mult)
            E(eng2, "tensor_tensor", xt.bs + g_row.bs, xt.bs, out=xt.t[:], in0=xt.t[:], in1=g_row.t[:], op=ALU.mult)
            E(eng2, "tensor_tensor", xt.bs + b_row.bs, xt.bs, out=xt.t[:], in0=xt.t[:], in1=b_row.t[:], op=ALU.add)

        def conv(eng, out_tl, out_ap, in_tl, in_ap, w_tl, w_ap, b_ap, ntap, left, nseq, L):
            o3 = out_ap.rearrange("p (s l) -> p s l", s=nseq)
            i3 = in_ap.rearrange("p (s l) -> p s l", s=nseq)
            E(eng, "tensor_scalar", in_tl.bs + w_tl[0].bs + w_tl[1].bs, out_tl.bs, out=out_ap, in0=in_ap,
              scalar1=w_ap[:, left:left + 1], scalar2=b_ap, op0=ALU.mult, op1=ALU.add)
            for j in range(ntap):
                o = j - left
                if o == 0:
                    continue
                lo_out = max(0, -o)
                hi_out = L - max(0, o)
                E(eng, "scalar_tensor_tensor", in_tl.bs + out_tl.bs + w_tl[0].bs, out_tl.bs,
                  out=o3[:, :, lo_out:hi_out], in0=i3[:, :, lo_out + o:hi_out + o], scalar=w_ap[:, j:j + 1],
                  in1=o3[:, :, lo_out:hi_out], op0=ALU.mult, op1=ALU.add)

        for pi in range(2):
            nseq, L = (4, 256) if pi == 0 else (1, 1024)
            pbanks = pbanks6
            ntc = L // 128
            x_d = xin[pi]
            with ExitStack() as pp:
              with ExitStack() as mxs:
                h1T = alloc(mxs, "h1T", [128, 8, 1024], BF16)
                ybT = alloc(mxs, "ybT", [128, 8, 1024], BF16, nbuf=8)

                with ExitStack() as ph:
                    xr = ring(ph, "xt", [128, DM], F32, 2)
                    pr = ring(ph, "pt", [128, DM], F32, 2)
                    for i in range(8):
                        xt = xr.next()
                        load("sync", xt, x_d[i * 128:(i + 1) * 128, :])
                        if pi == 1:
                            pt = pr.next()
                            load("gpsimd", pt, pos_d[i * 128:(i + 1) * 128, :])
                            E("vector", "tensor_tensor", xt.bs + pt.bs, xt.bs, out=xt.t[:], in0=xt.t[:], in1=pt.t[:],
                              op=ALU.add)
                        for hf in range(2):
                            pb = pbanks.next()
                            for cc in range(4):
                                c = hf * 4 + cc
                                TR(pb.t[:, cc * 128:(cc + 1) * 128], xt.t[:, c * 128:(c + 1) * 128], identf.t[:],
                                   xt.bs + identf.bs, pb.bs)
                            for cc in range(4):
                                c = hf * 4 + cc
                                E("scalar", "activation", pb.bs + modT.bs, h1T.bs,
                                  out=h1T.t[:, c, i * 128:(i + 1) * 128], in_=pb.t[:, cc * 128:(cc + 1) * 128],
                                  func=AF.Identity, bias=modT.t[:, 0, c, pi:pi + 1], scale=modT.t[:, 1, c, pi:pi + 1])
                    dbg("h1T%d" % pi, h1T, h1T.t[:], BF16)
                    S.barrier()
                    if stop == (pi, 1): S.enabled = False

                with ExitStack() as ph:
                    li = pi
                    Cm = alloc(ph, "Cm", [128, ntc, L], BF16)
                    Sm = alloc(ph, "Sm", [128, ntc, L], BF16)
                    STm = alloc(ph, "STm", [128, ntc, L], BF16)
                    load("sync", Cm, C_d[li].rearrange("(tc p) f -> p tc f", p=128))
                    load("sync", Sm, Sm_d[li].rearrange("(tc p) f -> p tc f", p=128))
                    load("sync", STm, ST_d[li].rearrange("(tc p) f -> p tc f", p=128))
                    wtab = alloc(ph, "wtab", [128, 4, ntc]); load("sync", wtab, wtab_d[li])
                    hid2 = alloc(ph, "hid2", [64, L])
                    with ExitStack() as sub:
                        zT = alloc(sub, "zT", [33, L]); load("sync", zT, zT_d[li])
                        w1 = alloc(sub, "hw1", [33, 64]); load("sync", w1, hy_w1_d)
                        w2 = alloc(sub, "hw2", [64, 64]); load("sync", w2, hy_w2_d)
                        hb1 = alloc(sub, "hb1", [64, 1]); load("sync", hb1, hy_b1_d)
                        hb2 = alloc(sub, "hb2", [64, 1]); load("sync", hb2, hy_b2_d)
                        hfr = alloc(sub, "hfr", [64, 1]); load("sync", hfr, hy_freq_d)
                        fb = alloc(sub, "fb", [64, 2])
                        E("vector", "tensor_tensor", hb1.bs + hfr.bs, fb.bs, out=fb.t[:, 0:1], in0=hb1.t[:], in1=hfr.t[:],
                          op=ALU.mult)
                        E("vector", "tensor_tensor", hb2.bs + hfr.bs, fb.bs, out=fb.t[:, 1:2], in0=hb2.t[:], in1=hfr.t[:],
                          op=ALU.mult)
                        hid = [alloc(sub, "hid1", [64, L]), hid2]
                        sarg = alloc(sub, "sarg", [64, L])
                        sint = alloc(sub, "sint", [64, L], I32)
                        sflt = alloc(sub, "sflt", [64, L])
                        for layer in range(2):
                            src = zT if layer == 0 else hid[0]
                            wl = w1 if layer == 0 else w2
                            kdim = 33 if layer == 0 else 64
                            blk = min(L, 512)
                            for b0 in range(0, L, blk):
                                pb = pbanks.next()
                                MM(pb.t[0:64, 0:blk], wl.t[0:kdim, :], src.t[0:kdim, b0:b0 + blk], True, True,
                                   wl.bs + src.bs, pb.bs)
                                E("vector", "tensor_scalar", pb.bs + hfr.bs + fb.bs, sarg.bs, out=sarg.t[:, b0:b0 + blk],
                                  in0=pb.t[0:64, 0:blk], scalar1=hfr.t[:], scalar2=fb.t[:, layer:layer + 1],
                                  op0=ALU.mult, op1=ALU.add)
                            E("vector", "tensor_scalar", sarg.bs, sarg.bs, out=sarg.t[:], in0=sarg.t[:],
                              scalar1=float(1.0 / (2 * math.pi)), scalar2=8.0, op0=ALU.mult, op1=ALU.add)
                            E("vector", "tensor_copy", sarg.bs, sint.bs, out=sint.t[:], in_=sarg.t[:])
                            E("vector", "tensor_copy", sint.bs, sflt.bs, out=sflt.t[:], in_=sint.t[:])
                            E("vector", "tensor_tensor", sarg.bs + sflt.bs, sarg.bs, out=sarg.t[:], in0=sarg.t[:],
                              in1=sflt.t[:], op=ALU.subtract)
                            E("vector", "tensor_single_scalar", sarg.bs, sflt.bs, out=sflt.t[:], in_=sarg.t[:], scalar=0.5,
                              op=ALU.is_gt)
                            E("vector", "tensor_tensor", sarg.bs + sflt.bs, sarg.bs, out=sarg.t[:], in0=sarg.t[:],
                              in1=sflt.t[:], op=ALU.subtract)
                            E("scalar", "activation", sarg.bs, hid[layer].bs, out=hid[layer].t[:], in_=sarg.t[:],
                              func=AF.Sin, scale=float(2 * math.pi))
                        S.barrier()
                        dbg("hid2_%d" % pi, hid2, hid2.t[:])
                        if stop == (pi, 1.5): S.enabled = False

                    w3c_r = ring(ph, "w3c", [64, 4, 128], F32, 2)
                    b3c_r = ring(ph, "b3c", [128, 4, 128], F32, 2)
                    dec_r = ring(ph, "decf", [128, ntc, 128], F32, 2)
                    wgb_r = ring(ph, "wgb", [128, 8, 3, 128], BF16, 3)
                    ysz = 2 * ntc * 128 if pi == 1 else 2 * 2 * 4 * 128
                    nset = 2
                    TDT = F32 if pi == 0 else BF16
                    sh_ = (alloc(ph, "kf", [128, 4, 128]), alloc(ph, "kff", [128, 2, 128]), alloc(ph, "kfb", [128, 2, 128]),
                           alloc(ph, "kpm", [128, ntc, 2, 2, 128], BF16),
                           alloc(ph, "tmpd", [128, 2, 128]), alloc(ph, "hpre", [128, 3, 1024], BF16, nbuf=3),
                           alloc(ph, "wb", [128, 1024], BF16), alloc(ph, "wT", [128, 8, 128], BF16),
                           [alloc(ph, "tt%d" % r, [128, 512]) for r in range(4)],
                           alloc(ph, "YT", [128, ysz], BF16), alloc(ph, "tmpz", [128, 1024]), alloc(ph, "z1", [128, 1024]))

                    def mkset(q):
                        (kf_, kff_, kfb_, kpm_, tmpd_, hpre_, wb_, wT_, tt_, YT_, tmpz_, z1_) = sh_
                        return (kf_, kff_, kfb_, kpm_,
                                alloc(ph, "TA%d" % q, [128, ntc, 2, 128], TDT), alloc(ph, "TB%d" % q, [128, ntc, 2, 128], TDT),
                                alloc(ph, "TD0%d" % q, [128, 2, 128]), tmpd_, hpre_,
                                alloc(ph, "hc%d" % q, [128, 3, 1024], F32, nbuf=3),
                                wb_, wT_, tt_, YT_, tmpz_, z1_)
                    bsets = [mkset(q) for q in range(nset)]

                    w_in_bv = w_in_bd.rearrange("(kc p) n -> p kc n", p=128)
                    w3_v = hy_w3_d.rearrange("k (q n) -> k q n", q=4)
                    b3_v = hy_b3_d.rearrange("(q n) -> q n", q=4)
                    dec_v = dec_d[li].rearrange("(tc p) n -> p tc n", p=128)

                    def stA(c):
                        (kf, kff, kfb, kpm, TA, TB, TD0, tmpd, hpre, hc, wb, wT, tt_r, YT, tmpz, z1) = bsets[c % nset]
                        w3c = w3c_r.next(); load("sync", w3c, w3_v[:, :, c * 128:(c + 1) * 128])
                        b3c = b3c_r.next()
                        S.dma("sync", b3c.t[:], b3_v[:, c * 128:(c + 1) * 128].partition_broadcast(128), writes=b3c.bs)
                        decf = dec_r.next(); load("sync", decf, dec_v[:, :, c * 128:(c + 1) * 128])
                        wgb = wgb_r.next()
                        for q in range(3):
                            S.dma("sync", wgb.t[:, :, q, :],
                                  w_in_bv[:, :, 2048 + q * 1024 + c * 128:2048 + q * 1024 + (c + 1) * 128],
                                  reads=[wcv_b["w_in"]], writes=wgb.bs)
                        if c == 0:
                            dbg("b3c%d" % pi, b3c, b3c.t[:])
                            if stop == (pi, 1.55): S.enabled = False
                        for tc in range(ntc):
                            pb = pbanks.next()
                            MM(pb.t[:, :], hid2.t[:, tc * 128:(tc + 1) * 128], w3c.t[:].rearrange("k q n -> k (q n)"),
                               True, True, hid2.bs + w3c.bs, pb.bs)
                            E("vector", "tensor_tensor", pb.bs + b3c.bs, kf.bs, out=kf.t[:].rearrange("p q n -> p (q n)"),
                              in0=pb.t[:, :], in1=b3c.t[:].rearrange("p q n -> p (q n)"), op=ALU.add)
                            dbc = decf.t[:, tc, :].unsqueeze(1).to_broadcast([128, 2, 128])
                            E("gpsimd", "tensor_tensor", kf.bs + decf.bs, kff.bs, out=kff.t[:], in0=kf.t[:, 0:2, :], in1=dbc,
                              op=ALU.mult)
                            E("gpsimd", "tensor_tensor", kf.bs + decf.bs, kfb.bs, out=kfb.t[:], in0=kf.t[:, 2:4, :], in1=dbc,
                              op=ALU.mult)
                            if tc == 0:
                                E("vector", "tensor_scalar", kfb.bs + mask0.bs, kfb.bs, out=kfb.t[:], in0=kfb.t[:],
                                  scalar1=mask0.t[:], scalar2=None, op0=ALU.mult)
                            E("gpsimd", "tensor_tensor", kff.bs + kfb.bs, kpm.bs, out=kpm.t[:, tc, 0, :, :], in0=kff.t[:],
                              in1=kfb.t[:], op=ALU.add)
                            E("gpsimd", "tensor_tensor", kff.bs + kfb.bs, kpm.bs, out=kpm.t[:, tc, 1, :, :], in0=kff.t[:],
                              in1=kfb.t[:], op=ALU.subtract)
                        if c == 0:
                            dbg("kpm%d" % pi, kpm, kpm.t[:], BF16)
                            if stop == (pi, 1.57): S.enabled = False
                        for fc in range(ntc):
                            pa = pbanks.next()
                            for tc in range(ntc):
                                MM(pa.t[:, 0:256], Cm.t[:, tc, fc * 128:(fc + 1) * 128], kpm.t[:, tc, 0, :, :].rearrange("p o n -> p (o n)"),
                                   tc == 0, tc == ntc - 1, Cm.bs + kpm.bs, pa.bs)
                            for tc in range(ntc):
                                MM(pa.t[:, 256:512], Sm.t[:, tc, fc * 128:(fc + 1) * 128], kpm.t[:, tc, 1, :, :].rearrange("p o n -> p (o n)"),
                                   tc == 0, tc == ntc - 1, Sm.bs + kpm.bs, pa.bs)
                            if 'A' not in HYSKIP: E("scalar", "activation", pa.bs + wtab.bs, TA.bs, out=TA.t[:, fc].rearrange("p o n -> p (o n)"),
                              in_=pa.t[:, 0:256], func=AF.Identity,
                              scale=wtab.t[:, 0, fc:fc + 1])
                            if 'B' not in HYSKIP: E("scalar", "activation", pa.bs + wtab.bs, TB.bs, out=TB.t[:, fc].rearrange("p o n -> p (o n)"),
                              in_=pa.t[:, 256:512], func=AF.Identity,
                              scale=wtab.t[:, 1, fc:fc + 1])
                            if fc == 0 and 'D' not in HYSKIP:
                                pd = pbanks.next()
                                for tc in range(ntc):
                                    MM(pd.t[:, 0:256], Sm.t[:, tc, 0:128], kpm.t[:, tc, 0, :, :].rearrange("p o n -> p (o n)"),
                                       tc == 0, tc == ntc - 1, Sm.bs + kpm.bs, pd.bs)
                                E("scalar", "activation", pa.bs + wtab.bs, TD0.bs, out=TD0.t[:].rearrange("p o n -> p (o n)"),
                                  in_=pa.t[:, 0:256], func=AF.Identity, scale=wtab.t[:, 2, 0:1])
                                E("scalar", "activation", pd.bs + wtab.bs, tmpd.bs, out=tmpd.t[:].rearrange("p o n -> p (o n)"),
                                  in_=pd.t[:, 0:256], func=AF.Identity, scale=wtab.t[:, 3, 0:1])
                                E("gpsimd", "tensor_tensor", TD0.bs + tmpd.bs, TD0.bs, out=TD0.t[:], in0=TD0.t[:],
                                  in1=tmpd.t[:], op=ALU.add)
                        if c == 0:
                            dbg("TA%d" % pi, TA, TA.t[:]); dbg("TB%d" % pi, TB, TB.t[:]); dbg("TD0_%d" % pi, TD0, TD0.t[:])
                            if stop == (pi, 1.6): S.enabled = False
                        for q in range(3):
                            for blk in range(2):
                                pb = pbanks.next()
                                for kc in range(8):
                                    MM(pb.t[:, :], wgb.t[:, kc, q, :], h1T.t[:, kc, blk * 512:(blk + 1) * 512],
                                       kc == 0, kc == 7, wgb.bs + h1T.bs, pb.bs)
                                E("scalar", "copy", pb.bs, [hpre.bs[q]], out=hpre.t[:, q, blk * 512:(blk + 1) * 512],
                                  in_=pb.t[:, :])
                            cc = q * 8 + c
                            o_tl = Tl(None); o_tl.bs = [hc.bs[q]]
                            i_tl = Tl(None); i_tl.bs = [hpre.bs[q]]
                            conv("vector", o_tl, hc.t[:, q, :], i_tl, hpre.t[:, q, :],
                                 (hcw, hcb), hcw.t[:, cc, :], hcb.t[:, cc:cc + 1], 3, 1, nseq, L)
                        if c == 0:
                            dbg("hc%d" % pi, hc, hc.t[:])
                            if stop == (pi, 1.7): S.enabled = False
                    def stB(c):
                        (kf, kff, kfb, kpm, TA, TB, TD0, tmpd, hpre, hc, wb, wT, tt_r, YT, tmpz, z1) = bsets[c % nset]
                        for od in range(2):
                            if od == 0:
                                w_ap, w_bs = hc.t[:, 2, :], [hc.bs[2]]
                                x_ap, x_bs = hc.t[:, 0, :], [hc.bs[0]]
                            else:
                                w_ap, w_bs = z1.t[:], z1.bs
                                x_ap, x_bs = hc.t[:, 1, :], [hc.bs[1]]
                            E("scalar", "copy", w_bs, wb.bs, out=wb.t[:], in_=w_ap)
                            pt = pbf.next()
                            for tt in range(8):
                                slot = (tt % 2) * 4 + tt // 2 if pi == 0 else tt
                                TR(pt.t[:, slot * 128:(slot + 1) * 128], wb.t[:, tt * 128:(tt + 1) * 128], identb.t[:],
                                   wb.bs + identb.bs, pt.bs)
                            E("vector", "tensor_copy", pt.bs, wT.bs, out=wT.t[:].rearrange("p a n -> p (a n)"), in_=pt.t[:, :])
                            if pi == 0:
                                YTv = YT.t[:].rearrange("p (fc r s n) -> p fc r s n", fc=2, r=2, s=4)
                                wTv = wT.t[:].rearrange("p (tc s) n -> p tc (s n)", s=4)
                                groups = [(fc, 1) for fc in range(2)]
                            else:
                                YTv = YT.t[:].rearrange("p (fc r n) -> p fc r n", fc=ntc, r=2)
                                groups = [(0, 4), (4, 4)]
                            for (f0, nf) in groups:
                                pre = pbanks.next()
                                pim = pbanks.next()
                                if pi == 0:
                                    fc = f0
                                    for (pbk, Mx) in ((pre, Cm), (pim, Sm)):
                                        for tc in range(ntc):
                                            MM(pbk.t[:, :], Mx.t[:, tc, fc * 128:(fc + 1) * 128], wTv[:, tc, :],
                                               tc == 0, tc == ntc - 1, Mx.bs + wT.bs, pbk.bs)
                                    ta = TA.t[:, fc, od, :].unsqueeze(1).to_broadcast([128, 4, 128])
                                    tb_ = TB.t[:, fc, od, :].unsqueeze(1).to_broadcast([128, 4, 128])
                                    if fc == 0:
                                        td = TD0.t[:, od, :].unsqueeze(1).to_broadcast([128, 4, 128])
                                        td_bs = TD0.bs
                                    else:
                                        td = ta
                                        td_bs = TA.bs
                                    ure = pre.t[:, :].rearrange("p (s n) -> p s n", s=4)
                                    uim = pim.t[:, :].rearrange("p (s n) -> p s n", s=4)
                                    yre = YTv[:, fc, 0, :, :]
                                    yim = YTv[:, fc, 1, :, :]
                                    shp = "p (s n) -> p s n"
                                    tv = [t_.t[:, :].rearrange(shp, s=4) for t_ in tt_r]
                                    E("vector", "tensor_tensor", pre.bs + TA.bs, tt_r[0].bs, out=tv[0], in0=ure, in1=ta, op=ALU.mult)
                                    E("vector", "tensor_tensor", pim.bs + TB.bs, tt_r[1].bs, out=tv[1], in0=uim, in1=tb_, op=ALU.mult)
                                    E("vector", "tensor_tensor", pre.bs + TB.bs, tt_r[2].bs, out=tv[2], in0=ure, in1=tb_, op=ALU.mult)
                                    E("vector", "tensor_tensor", pim.bs + td_bs, tt_r[3].bs, out=tv[3], in0=uim, in1=td, op=ALU.mult)
                                    E("gpsimd", "tensor_tensor", tt_r[0].bs + tt_r[1].bs, YT.bs, out=yre, in0=tv[0], in1=tv[1], op=ALU.subtract)
                                    E("gpsimd", "tensor_tensor", tt_r[2].bs + tt_r[3].bs, YT.bs, out=yim, in0=tv[2], in1=tv[3], op=ALU.add)
                                else:
                                    for ff in range(nf):
                                        fc = f0 + ff
                                        for (pbk, Mx) in ((pre, Cm), (pim, Sm)):
                                            for tc in range(ntc):
                                                MM(pbk.t[:, ff * 128:(ff + 1) * 128], Mx.t[:, tc, fc * 128:(fc + 1) * 128],
                                                   wT.t[:, tc, :], tc == 0, tc == ntc - 1, Mx.bs + wT.bs, pbk.bs)
                                    ta = TA.t[:, f0:f0 + nf, od, :]
                                    tb_ = TB.t[:, f0:f0 + nf, od, :]
                                    ure = pre.t[:, :].rearrange("p (s n) -> p s n", s=4)
                                    uim = pim.t[:, :].rearrange("p (s n) -> p s n", s=4)
                                    yre = YTv[:, f0:f0 + nf, 0, :]
                                    yim = YTv[:, f0:f0 + nf, 1, :]
                                    tv = [t_.t[:, :].rearrange("p (s n) -> p s n", s=4) for t_ in tt_r]
                                    E("vector", "tensor_tensor", pre.bs + TA.bs, tt_r[0].bs, out=tv[0], in0=ure, in1=ta, op=ALU.mult)
                                    E("vector", "tensor_tensor", pim.bs + TB.bs, tt_r[1].bs, out=tv[1], in0=uim, in1=tb_, op=ALU.mult)
                                    E("vector", "tensor_tensor", pre.bs + TB.bs, tt_r[2].bs, out=tv[2], in0=ure, in1=tb_, op=ALU.mult)
                                    E("vector", "tensor_tensor", pim.bs + TA.bs, tt_r[3].bs, out=tv[3], in0=uim, in1=ta, op=ALU.mult)
                                    if f0 == 0:
                                        E("vector", "tensor_tensor", pim.bs + TD0.bs, tt_r[3].bs, out=tt_r[3].t[:, 0:128],
                                          in0=pim.t[:, 0:128], in1=TD0.t[:, od, :], op=ALU.mult)
                                    E("gpsimd", "tensor_tensor", tt_r[0].bs + tt_r[1].bs, YT.bs, out=yre, in0=tv[0], in1=tv[1], op=ALU.subtract)
                                    E("gpsimd", "tensor_tensor", tt_r[2].bs + tt_r[3].bs, YT.bs, out=yim, in0=tv[2], in1=tv[3], op=ALU.add)
                            sk = skipT.t[:, od, c:c + 1]
                            if od == 0:
                                o_ap_full, o_bs = z1.t[:], z1.bs
                            else:
                                o_ap_full, o_bs = ybT.t[:, c, :], [ybT.bs[c]]
                            for blk in range(2):
                                pb = pbanks.next()
                                if pi == 0:
                                    for s2 in range(2):
                                        sq = blk * 2 + s2
                                        k = 0
                                        for fc in range(2):
                                            for r, Mx in ((0, Cm), (1, STm)):
                                                MM(pb.t[:, s2 * 256:(s2 + 1) * 256], YTv[:, fc, r, sq, :], Mx.t[:, fc, :],
                                                   k == 0, k == 3, YT.bs + Mx.bs, pb.bs)
                                                k += 1
                                else:
                                    k = 0
                                    for fc in range(ntc):
                                        for r, Mx in ((0, Cm), (1, STm)):
                                            MM(pb.t[:, :], YTv[:, fc, r, :], Mx.t[:, fc, blk * 512:(blk + 1) * 512],
                                               k == 0, k == 2 * ntc - 1, YT.bs + Mx.bs, pb.bs)
                                            k += 1
                                sl = slice(blk * 512, (blk + 1) * 512)
                                E("vector", "scalar_tensor_tensor", w_bs + skipT.bs + pb.bs, tmpz.bs, out=tmpz.t[:, sl],
                                  in0=w_ap[:, sl], scalar=sk, in1=pb.t[:, :], op0=ALU.mult, op1=ALU.add)
                                E("gpsimd", "tensor_tensor", tmpz.bs + x_bs, o_bs, out=o_ap_full[:, sl], in0=tmpz.t[:, sl],
                                  in1=x_ap[:, sl], op=ALU.mult)
                    if nset == 2:
                        stA(0)
                        for c in range(8):
                            if c + 1 < 8:
                                stA(c + 1)
                            stB(c)
                    else:
                        for c in range(8):
                            stA(c)
                            stB(c)
                    dbg("ybT%d" % pi, ybT, ybT.t[:], BF16)
                    S.barrier()
                    if stop == (pi, 2): S.enabled = False

                yaT = alloc(mxs, "yaT", [128, 8, 1024], BF16, nbuf=8)
                with ExitStack() as ph:
                    wbf_r = ring(ph, "rwbf", [128, 8, 256], BF16, 4)
                    gwb_r = ring(ph, "gwb", [128, 4, 2, 256], BF16, 2)
                    xpre = alloc(ph, "xpre", [128, 2, 1024], F32, nbuf=2)
                    xc = alloc(ph, "xc", [128, 2, 1024], F32, nbuf=2)
                    xcb = alloc(ph, "xcb", [128, 2, 1024], BF16)
                    rg = alloc(ph, "rg", [128, 1024]); ig = alloc(ph, "ig", [128, 1024])
                    at = alloc(ph, "at", [128, 1024]); st_ = alloc(ph, "st", [128, 1024]); ut = alloc(ph, "ut", [128, 1024])
                    hdir = [alloc(ph, "hf", [128, 1024]), alloc(ph, "hb", [128, 1024])]
                    glu = alloc(ph, "glu", [128, 2, 1024], F32, nbuf=2)
                    w_in_bv = w_in_bd.rearrange("(kc p) n -> p kc n", p=128)
                    for hd in range(4):
                        wxb = wbf_r.next()
                        S.dma("sync", wxb.t[:], w_in_bv[:, :, hd * 256:(hd + 1) * 256], reads=[wcv_b["w_in"]], writes=wxb.bs)
                        wgb2 = wbf_r.next()
                        S.dma("sync", wgb2.t[:], w_in_bv[:, :, 1024 + hd * 256:1024 + (hd + 1) * 256], reads=[wcv_b["w_in"]],
                              writes=wgb2.bs)
                        gwb = gwb_r.next()
                        for d in range(2):
                            for g in range(2):
                                S.dma("sync", gwb.t[:, d * 2 + g, :, :],
                                      gate_w_bd[d, g, hd].rearrange("(ic p) j -> p ic j", p=128), reads=[wcv_b["gate"]],
                                      writes=gwb.bs)
                        for jc in range(2):
                            for blk in range(2):
                                sl = slice(blk * 512, (blk + 1) * 512)
                                pb = pbanks.next()
                                for kc in range(8):
                                    MM(pb.t[:, :], wxb.t[:, kc, jc * 128:(jc + 1) * 128], h1T.t[:, kc, sl],
                                       kc == 0, kc == 7, wxb.bs + h1T.bs, pb.bs)
                                E("scalar", "copy", pb.bs, [xpre.bs[jc]], out=xpre.t[:, jc, sl], in_=pb.t[:, :])
                                pb = pbanks.next()
                                for kc in range(8):
                                    MM(pb.t[:, :], wgb2.t[:, kc, jc * 128:(jc + 1) * 128], h1T.t[:, kc, sl],
                                       kc == 0, kc == 7, wgb2.bs + h1T.bs, pb.bs)
                                E("scalar", "activation", pb.bs, [glu.bs[jc]], out=glu.t[:, jc, sl], in_=pb.t[:, :],
                                  func=AF.Gelu_apprx_tanh)
                            ch = hd * 2 + jc
                            o_tl = Tl(None); o_tl.bs = [xc.bs[jc]]
                            i_tl = Tl(None); i_tl.bs = [xpre.bs[jc]]
                            conv("vector", o_tl, xc.t[:, jc, :], i_tl, xpre.t[:, jc, :], (rcw, rcb), rcw.t[:, ch, :],
                                 rcb.t[:, ch:ch + 1], 4, 2, nseq, L)
                            E("scalar", "copy", [xc.bs[jc]], xcb.bs, out=xcb.t[:, jc, :], in_=xc.t[:, jc, :])
                        for jc in range(2):
                            ch = hd * 2 + jc
                            for d in range(2):
                                for g in range(2):
                                    dst = rg if g == 0 else ig
                                    for blk in range(2):
                                        sl = slice(blk * 512, (blk + 1) * 512)
                                        pb = pbanks.next()
                                        for ic in range(2):
                                            MM(pb.t[:, :], gwb.t[:, d * 2 + g, ic, jc * 128:(jc + 1) * 128], xcb.t[:, ic, sl],
                                               ic == 0, ic == 1, gwb.bs + xcb.bs, pb.bs)
                                        E("scalar", "activation", pb.bs + gbT.bs, dst.bs, out=dst.t[:, sl], in_=pb.t[:, :],
                                          func=AF.Sigmoid, bias=gbT.t[:, d, g, ch:ch + 1], scale=1.0)
                                E("scalar", "activation", rg.bs + nsp.bs, at.bs, out=at.t[:], in_=rg.t[:], func=AF.Exp,
                                  scale=nsp.t[:, d, ch:ch + 1])
                                E("gpsimd", "tensor_tensor", at.bs, st_.bs, out=st_.t[:], in0=at.t[:], in1=at.t[:], op=ALU.mult)
                                E("scalar", "activation", st_.bs, st_.bs, out=st_.t[:], in_=st_.t[:], func=AF.Sqrt,
                                  bias=1.0, scale=-1.0)
                                E("gpsimd", "tensor_tensor", ig.bs + [xc.bs[jc]], ig.bs, out=ig.t[:], in0=ig.t[:],
                                  in1=xc.t[:, jc, :], op=ALU.mult)
                                E("vector", "tensor_tensor", ig.bs + st_.bs, ut.bs, out=ut.t[:], in0=ig.t[:], in1=st_.t[:],
                                  op=ALU.mult)
                                hd_t = hdir[d]
                                for sq in range(nseq):
                                    lo, hi = sq * L, (sq + 1) * L
                                    init = h0T.t[:, d, ch:ch + 1] if pi == 1 else 0.0
                                    if d == 0:
                                        o_, a_, u_ = hd_t.t[:, lo:hi], at.t[:, lo:hi], ut.t[:, lo:hi]
                                    else:
                                        o_, a_, u_ = hd_t.t[:, lo:hi][:, ::-1], at.t[:, lo:hi][:, ::-1], ut.t[:, lo:hi][:, ::-1]
                                    E("vector", "tensor_tensor_scan", at.bs + ut.bs + h0T.bs, hd_t.bs, out=o_, data0=a_,
                                      data1=u_, initial=init, op0=ALU.mult, op1=ALU.add)
                                if pi == 0:
                                    hv = hd_t.t[:].rearrange("p (s l) -> p s l", s=4)
                                    col = L - 1 if d == 0 else 0
                                    E("gpsimd", "tensor_copy", hd_t.bs, nst.bs, out=nst.t[:, :, d, ch:ch + 1],
                                      in_=hv[:, :, col:col + 1])
                            E("vector", "tensor_tensor", hdir[0].bs + hdir[1].bs, hdir[0].bs, out=hdir[0].t[:], in0=hdir[0].t[:],
                              in1=hdir[1].t[:], op=ALU.add)
                            E("gpsimd", "tensor_tensor", hdir[0].bs + [glu.bs[jc]], [yaT.bs[ch]], out=yaT.t[:, ch, :],
                              in0=hdir[0].t[:], in1=glu.t[:, jc, :], op=ALU.mult)
                    dbg("yaT%d" % pi, yaT, yaT.t[:], BF16)
                    if pi == 0:
                        dbg("nst", nst, nst.t[:])
                    S.barrier()
                    if stop == (pi, 3): S.enabled = False

                mT = alloc(mxs, "mT", [128, 8, 1024], BF16, nbuf=8)
                with ExitStack() as ph:
                    wbf_r = ring(ph, "mwbf", [128, 8, 256], BF16, 12)
                    gat = ring(ph, "gat", [128, 512], F32, 2)
                    gbt = ring(ph, "gbt", [128, 512], F32, 2)
                    mat = ring(ph, "mat", [128, 512], F32, 2)
                    w_in_v = w_in_bd.rearrange("(kc p) n -> p kc n", p=128)
                    wa_v = w_a_bd.rearrange("(kc p) n -> p kc n", p=128)
                    wb_v = w_b_bd.rearrange("(kc p) n -> p kc n", p=128)
                    for cg in range(4):
                        ws = []
                        for (src, bk) in ((wa_v[:, :, cg * 256:(cg + 1) * 256], "w_a"), (wb_v[:, :, cg * 256:(cg + 1) * 256], "w_b"),
                                          (w_in_v[:, :, 5120 + cg * 256:5120 + (cg + 1) * 256], "w_in"),
                                          (w_in_v[:, :, 6144 + cg * 256:6144 + (cg + 1) * 256], "w_in")):
                            wbf = wbf_r.next()
                            S.dma("sync", wbf.t[:], src, reads=[wcv_b[bk]], writes=wbf.bs)
                            ws.append(wbf)
                        for jc in range(2):
                            c = cg * 2 + jc
                            cs = slice(jc * 128, (jc + 1) * 128)
                            for blk in range(2):
                                sl = slice(blk * 512, (blk + 1) * 512)
                                pya = pbanks.next()
                                for kc in range(8):
                                    MM(pya.t[:, :], ws[0].t[:, kc, cs], yaT.t[:, kc, sl], kc == 0, kc == 7, ws[0].bs + yaT.bs, pya.bs)
                                pyb = pbanks.next()
                                for kc in range(8):
                                    MM(pyb.t[:, :], ws[1].t[:, kc, cs], ybT.t[:, kc, sl], kc == 0, kc == 7, ws[1].bs + ybT.bs, pyb.bs)
                                pga = pbanks.next()
                                for kc in range(8):
                                    MM(pga.t[:, :], ws[2].t[:, kc, cs], h1T.t[:, kc, sl], kc == 0, kc == 7, ws[2].bs + h1T.bs, pga.bs)
                                pgb = pbanks.next()
                                for kc in range(8):
                                    MM(pgb.t[:, :], ws[3].t[:, kc, cs], h1T.t[:, kc, sl], kc == 0, kc == 7, ws[3].bs + h1T.bs, pgb.bs)
                                ga = gat.next(); gb = gbt.next(); ma = mat.next()
                                E("scalar", "activation", pga.bs, ga.bs, out=ga.t[:], in_=pga.t[:, :], func=AF.Sigmoid)
                                E("scalar", "activation", pgb.bs, gb.bs, out=gb.t[:], in_=pgb.t[:, :], func=AF.Sigmoid)
                                E("vector", "tensor_tensor", pya.bs + ga.bs, ma.bs, out=ma.t[:], in0=pya.t[:, :], in1=ga.t[:], op=ALU.mult)
                                E("vector", "tensor_tensor", pyb.bs + gb.bs, gb.bs, out=gb.t[:], in0=pyb.t[:, :], in1=gb.t[:], op=ALU.mult)
                                E("gpsimd", "tensor_tensor", ma.bs + gb.bs, [mT.bs[c]], out=mT.t[:, c, sl], in0=ma.t[:], in1=gb.t[:], op=ALU.add)
                    dbg("mT%d" % pi, mT, mT.t[:], BF16)
                    S.barrier()
                    if stop == (pi, 4): S.enabled = False

                with ExitStack() as ph:
                    wo = alloc(ph, "wo", [128, 8, DM], BF16, nbuf=4)
                    wo_v = w_o_bd.rearrange("(kc p) n -> p kc n", p=128)
                    for cg in range(4):
                        S.dma("sync", wo.t[:, :, cg * 256:(cg + 1) * 256], wo_v[:, :, cg * 256:(cg + 1) * 256],
                              reads=[wcv_b["w_o"]], writes=[wo.bs[cg]])
                    g1r = alloc(ph, "g1r", [128, DM]); load_row(g1r, modrow_d[pi, 0, :], [modrow_b])
                    lg = alloc(ph, "lg", [128, DM]); load_row(lg, ln_d[0])
                    lb = alloc(ph, "lb", [128, DM]); load_row(lb, ln_d[1])
                    xr = ring(ph, "xt3", [128, DM], F32, 2)
                    pr = ring(ph, "pt3", [128, DM], F32, 2)
                    yt_r = ring(ph, "yt3", [128, DM], F32, 2)
                    lnt = (alloc(ph, "stats", [128, 2, 6]), alloc(ph, "mv", [128, 2]), alloc(ph, "rstd", [128, 1]))
                    for i in range(8):
                        xt = xr.next()
                        load("sync", xt, x_d[i * 128:(i + 1) * 128, :])
                        if pi == 1:
                            pt = pr.next()
                            load("sync", pt, pos_d[i * 128:(i + 1) * 128, :])
                            E("gpsimd", "tensor_tensor", xt.bs + pt.bs, xt.bs, out=xt.t[:], in0=xt.t[:], in1=pt.t[:], op=ALU.add)
                        yt = yt_r.next()
                        for hf in range(2):
                            pb = pbanks.next()
                            for kc in range(8):
                                MM(pb.t[:, :], mT.t[:, kc, i * 128:(i + 1) * 128], wo.t[:, kc, hf * 512:(hf + 1) * 512],
                                   kc == 0, kc == 7, mT.bs + wo.bs, pb.bs)
                            E("vector", "tensor_tensor", pb.bs + g1r.bs, yt.bs, out=yt.t[:, hf * 512:(hf + 1) * 512],
                              in0=pb.t[:, :], in1=g1r.t[:, hf * 512:(hf + 1) * 512], op=ALU.mult)
                        E("vector", "scalar_tensor_tensor", xt.bs + yt.bs, yt.bs, out=yt.t[:], in0=xt.t[:],
                          scalar=ALPHA, in1=yt.t[:], op0=ALU.mult, op1=ALU.add)
                        layer_norm_tile(lnt, yt, lg, lb)
                        S.dma("gpsimd", x2_d[pi, i * 128:(i + 1) * 128, :], yt.t[:], reads=yt.bs, writes=[x2_db[pi][i]])
                        if i == 0:
                            dbg("x2_%d" % pi, yt, yt.t[:])
                    S.barrier()
                    if stop == (pi, 5): S.enabled = False

              with ExitStack() as ph:
                  pbanks = pbanks4
                  wqb = alloc(ph, "wqb", [128, 8, 2048], BF16, nbuf=8)
                  UVN = int(os.environ.get('PEER_UVN', '16'))
                  uv_r = ring(ph, "uvg", [128, 2 * DM], BF16, UVN)
                  wq_v = wq_bd.rearrange("(kc p) n -> p kc n", p=128)
                  for cg in range(8):
                      S.dma("sync", wqb.t[:, :, cg * 256:(cg + 1) * 256], wq_v[:, :, cg * 256:(cg + 1) * 256],
                            reads=[wcv_b["wq"]], writes=[wqb.bs[cg]])
                  sh2r = alloc(ph, "sh2r", [128, DM]); load_row(sh2r, modrow_d[pi, 1, :], [modrow_b])
                  sc2r = alloc(ph, "sc2r", [128, DM]); load_row(sc2r, modrow_d[pi, 2, :], [modrow_b])
                  g2r = alloc(ph, "g2r", [128, DM]); load_row(g2r, modrow_d[pi, 3, :], [modrow_b])
                  lg = alloc(ph, "lg2", [128, DM]); load_row(lg, ln_d[2])
                  lb = alloc(ph, "lb2", [128, DM]); load_row(lb, ln_d[3])
                  lnt = (alloc(ph, "stats2", [128, 2, 6]), alloc(ph, "mv2", [128, 2]), alloc(ph, "rstd2", [128, 1]))
                  h2 = alloc(ph, "h2", [128, DM])
                  h2b_l = [alloc(ph, "h2b%d" % q, [128, DM], BF16) for q in range(2)]
                  h2T = alloc(ph, "h2T", [128, 8, 128], BF16)
                  qT = alloc(ph, "qT", [128, 16, 128])
                  sc = alloc(ph, "sc", [128, 16, 128])
                  scw = alloc(ph, "scw", [128, 128]); scw2 = alloc(ph, "scw2", [128, 128])
                  tops = alloc(ph, "tops", [128, 16, 16], nbuf=16); topi = alloc(ph, "topi", [128, 16, 16], U32, nbuf=16)

                  def _sub(tl, k):
                      w = Tl(None); w.bs = [tl.bs[k]]; return w
                  tops_b = [_sub(tops, k) for k in range(16)]; topi_b = [_sub(topi, k) for k in range(16)]
                  topif = alloc(ph, "topif", [128, 16, 16])
                  cand = alloc(ph, "cand", [128, 8, 16, 16]); candw = alloc(ph, "candw", [128, 256]); candw2 = alloc(ph, "candw2", [128, 256])
                  bs_ = alloc(ph, "bests", [128, 8, 16], nbuf=8); bp = alloc(ph, "bestp", [128, 8, 16], U32, nbuf=8)
                  bs_b = [_sub(bs_, k) for k in range(8)]; bp_b = [_sub(bp, k) for k in range(8)]
                  k1 = alloc(ph, "k1", [128, 8, 16], U32); k2 = alloc(ph, "k2", [128, 8, 16], U32)
                  k1f = alloc(ph, "k1f", [128, 8, 16]); k2f = alloc(ph, "k2f", [128, 8, 16])
                  oh = cand; i1f = alloc(ph, "i1f", [128, 8, 16]); i2f = alloc(ph, "i2f", [128, 8, 16])
                  io16 = alloc(ph, "io16", [128, 16]); io16i = alloc(ph, "io16i", [128, 16], I32)
                  E("gpsimd", "iota", [], io16i.bs, out=io16i.t[:], pattern=[[1, 16]], base=0, channel_multiplier=0)
                  E("vector", "tensor_copy", io16i.bs, io16.bs, out=io16.t[:], in_=io16i.t[:])
                  eidx_l = [alloc(ph, "eidx%d" % q, [128, 128], I32) for q in range(2)]
                  gw_l = [alloc(ph, "gw%d" % q, [128, 8, 16]) for q in range(2)]
                  zs = alloc(ph, "zs", [128, 8])
                  act = alloc(ph, "act", [128, 128]); wgt = alloc(ph, "wgt", [128, 128])
                  dg_r = ring(ph, "dg", [128, 128], BF16, 8)
                  ot_r = ring(ph, "ot", [128, DM], F32, 2)
                  x2_r = ring(ph, "x2t", [128, DM], F32, 2)

                  def top16x2(args_a, args_b):
                      chains = [args_a, args_b]
                      for (st, sap, wt, wap, os_, oi, ts, ti) in chains:
                          E("vector", "max", st.bs, ts.bs, out=os_[:, 0:8], in_=sap)
                      yield
                      for (st, sap, wt, wap, os_, oi, ts, ti) in chains:
                          E("vector", "max_index", st.bs + ts.bs, ti.bs, out=oi[:, 0:8], in_max=os_[:, 0:8], in_values=sap)
                      for (st, sap, wt, wap, os_, oi, ts, ti) in chains:
                          E("vector", "match_replace", st.bs + ts.bs, wt.bs, out=wap, in_to_replace=os_[:, 0:8], in_values=sap, imm_value=-1e30)
                      yield
                      for (st, sap, wt, wap, os_, oi, ts, ti) in chains:
                          E("vector", "max", wt.bs, ts.bs, out=os_[:, 8:16], in_=wap)
                      yield
                      for (st, sap, wt, wap, os_, oi, ts, ti) in chains:
                          E("vector", "max_index", wt.bs + ts.bs, ti.bs, out=oi[:, 8:16], in_max=os_[:, 8:16], in_values=wap)
                      yield

                  def top16(src_tl, src_ap, work_tl, work_ap, out_s, out_i, o_tl_s, o_tl_i):
                      E("vector", "max", src_tl.bs, o_tl_s.bs, out=out_s[:, 0:8], in_=src_ap)
                      E("vector", "max_index", src_tl.bs + o_tl_s.bs, o_tl_i.bs, out=out_i[:, 0:8], in_max=out_s[:, 0:8], in_values=src_ap)
                      E("vector", "match_replace", src_tl.bs + o_tl_s.bs, work_tl.bs, out=work_ap, in_to_replace=out_s[:, 0:8],
                        in_values=src_ap, imm_value=-1e30)
                      E("vector", "max", work_tl.bs, o_tl_s.bs, out=out_s[:, 8:16], in_=work_ap)
                      E("vector", "max_index", work_tl.bs + o_tl_s.bs, o_tl_i.bs, out=out_i[:, 8:16], in_max=out_s[:, 8:16], in_values=work_ap)

                  def front(i):
                      sl_ = i % 2
                      h2b = h2b_l[sl_]; eidx = eidx_l[sl_]; gw = gw_l[sl_]
                      x2t = x2_r.next()
                      S.dma("sync", x2t.t[:], x2_d[pi, i * 128:(i + 1) * 128, :], reads=[x2_db[pi][i]], writes=x2t.bs)
                      yield
                      E("vector", "tensor_tensor", x2t.bs + sc2r.bs, h2.bs, out=h2.t[:], in0=x2t.t[:], in1=sc2r.t[:], op=ALU.mult)
                      E("vector", "tensor_tensor", h2.bs + sh2r.bs, h2.bs, out=h2.t[:], in0=h2.t[:], in1=sh2r.t[:], op=ALU.add)
                      yield
                      E("scalar", "copy", h2.bs, h2b.bs, out=h2b.t[:], in_=h2.t[:])
                      yield
                      pt = pbf.next()
                      for c in range(8):
                          TR(pt.t[:, c * 128:(c + 1) * 128], h2b.t[:, c * 128:(c + 1) * 128], identb.t[:], h2b.bs + identb.bs, pt.bs)
                      yield
                      E("scalar", "copy", pt.bs, h2T.bs, out=h2T.t[:].rearrange("p a n -> p (a n)"), in_=pt.t[:, :])
                      yield
                      for g4 in range(4):
                          pb = pbanks.next()
                          for gg in range(4):
                              hp = g4 * 4 + gg
                              for kc in range(8):
                                  MM(pb.t[:, gg * 128:(gg + 1) * 128], wqb.t[:, kc, hp * 128:(hp + 1) * 128], h2T.t[:, kc, :],
                                     kc == 0, kc == 7, wqb.bs + h2T.bs, pb.bs)
                          yield
                          E("scalar", "copy", pb.bs, qT.bs, out=qT.t[:, g4 * 4:(g4 + 1) * 4, :].rearrange("p a n -> p (a n)"), in_=pb.t[:, :])
                          yield
                      for g4 in range(4):
                          pb = pbanks.next()
                          for gg in range(4):
                              hp = g4 * 4 + gg
                              MM(pb.t[:, gg * 128:(gg + 1) * 128], qT.t[:, hp, :], skT.t[:, hp % 2, :], True, True,
                                 qT.bs + skT.bs, pb.bs)
                          yield
                          E("scalar", "copy", pb.bs, sc.bs, out=sc.t[:, g4 * 4:(g4 + 1) * 4, :].rearrange("p a n -> p (a n)"), in_=pb.t[:, :])
                          yield
                      for hp in range(0, 16, 2):
                          yield from top16x2((sc, sc.t[:, hp, :], scw, scw.t[:], tops.t[:, hp, :], topi.t[:, hp, :], tops_b[hp], topi_b[hp]),
                                  (sc, sc.t[:, hp + 1, :], scw2, scw2.t[:], tops.t[:, hp + 1, :], topi.t[:, hp + 1, :], tops_b[hp + 1], topi_b[hp + 1]))
                      E("vector", "tensor_copy", topi.bs, topif.bs, out=topif.t[:], in_=topi.t[:])
                      tv = tops.t[:].rearrange("p (h q) k -> p h q k", q=2)
                      tiv = topif.t[:].rearrange("p (h q) k -> p h q k", q=2)
                      E("vector", "tensor_tensor", tops.bs, cand.bs, out=cand.t[:],
                        in0=tv[:, :, 0, :].unsqueeze(3).to_broadcast([128, 8, 16, 16]),
                        in1=tv[:, :, 1, :].unsqueeze(2).to_broadcast([128, 8, 16, 16]), op=ALU.add)
                      yield
                      for h in range(0, 8, 2):
                          yield from top16x2((cand, cand.t[:, h].rearrange("p a b -> p (a b)"), candw, candw.t[:], bs_.t[:, h, :], bp.t[:, h, :], bs_b[h], bp_b[h]),
                                  (cand, cand.t[:, h + 1].rearrange("p a b -> p (a b)"), candw2, candw2.t[:], bs_.t[:, h + 1, :], bp.t[:, h + 1, :], bs_b[h + 1], bp_b[h + 1]))
                      E("vector", "tensor_single_scalar", bp.bs, k1.bs, out=k1.t[:], in_=bp.t[:], scalar=4, op=ALU.logical_shift_right)
                      E("vector", "tensor_single_scalar", bp.bs, k2.bs, out=k2.t[:], in_=bp.t[:], scalar=15, op=ALU.bitwise_and)
                      E("vector", "tensor_copy", k1.bs, k1f.bs, out=k1f.t[:], in_=k1.t[:])
                      E("vector", "tensor_copy", k2.bs, k2f.bs, out=k2f.t[:], in_=k2.t[:])
                      yield
                      iob = io16.t[:].unsqueeze(1).unsqueeze(1).to_broadcast([128, 8, 16, 16])
                      for (kf_, q, dst) in ((k1f, 0, i1f), (k2f, 1, i2f)):
                          E("vector", "tensor_tensor", kf_.bs + io16.bs, oh.bs, out=oh.t[:],
                            in0=kf_.t[:].unsqueeze(3).to_broadcast([128, 8, 16, 16]), in1=iob, op=ALU.is_equal)
                          yield
                          E("vector", "tensor_tensor", oh.bs + topif.bs, oh.bs, out=oh.t[:], in0=oh.t[:],
                            in1=tiv[:, :, q, :].unsqueeze(2).to_broadcast([128, 8, 16, 16]), op=ALU.mult)
                          yield
                          E("vector", "tensor_reduce", oh.bs, dst.bs, out=dst.t[:], in_=oh.t[:], axis=AX.X, op=ALU.add)
                          yield
                      E("vector", "scalar_tensor_tensor", i1f.bs + i2f.bs, i1f.bs, out=i1f.t[:], in0=i1f.t[:], scalar=128.0,
                        in1=i2f.t[:], op0=ALU.mult, op1=ALU.add)
                      E("vector", "tensor_copy", i1f.bs, eidx.bs, out=eidx.t[:].rearrange("p (h k) -> p h k", h=8), in_=i1f.t[:])
                      yield
                      E("vector", "tensor_tensor", bs_.bs, gw.bs, out=gw.t[:], in0=bs_.t[:],
                        in1=bs_.t[:, :, 0:1].to_broadcast([128, 8, 16]), op=ALU.subtract)
                      yield
                      E("scalar", "activation", gw.bs, gw.bs, out=gw.t[:], in_=gw.t[:], func=AF.Exp)
                      yield
                      E("vector", "tensor_reduce", gw.bs, zs.bs, out=zs.t[:], in_=gw.t[:], axis=AX.X, op=ALU.add)
                      E("vector", "reciprocal", zs.bs, zs.bs, out=zs.t[:], in_=zs.t[:])
                      E("vector", "tensor_tensor", gw.bs + zs.bs, gw.bs, out=gw.t[:], in0=gw.t[:],
                        in1=zs.t[:].unsqueeze(2).to_broadcast([128, 8, 16]), op=ALU.mult)
                      x2_of[i] = x2t
                      yield

                  def back(i, nxt):
                      sl_ = i % 2
                      h2b = h2b_l[sl_]; eidx = eidx_l[sl_]; gw = gw_l[sl_]
                      x2t = x2_of[i]
                      gwf = gw.t[:].rearrange("p h k -> p (h k)")
                      uvs_of = {}
                      GS = PEER_GS
                      NG = 128 // GS
                      SK = PEER_SK
                      assert (SK + 1) * GS + GS - 1 <= UVN + GS - 1 and UVN >= (SK + 2) * GS - 0, "gather ring too shallow for skew"
                      for jg in range(NG + SK):
                          if jg < NG:
                              uvs = []
                              for jj in range(GS):
                                  j = jg * GS + jj
                                  uv = uv_r.next()
                                  uvs.append(uv)
                                  GATHER(uv.t[:], uv_d, eidx.t[:, j:j + 1], eidx.bs + [uv_b], uv.bs)
                                  E("vector", "tensor_tensor", uv.bs + h2b.bs, uv.bs, out=uv.t[:, 0:DM], in0=uv.t[:, 0:DM], in1=h2b.t[:],
                                    op=ALU.mult)
                                  E("scalar", "activation", uv.bs, uv.bs + (act.bs if jj in (0, GS - 1) else []), out=uv.t[:, 0:DM],
                                    in_=uv.t[:, 0:DM], func=AF.Identity, accum_out=act.t[:, j:j + 1])
                              uvs_of[jg] = uvs
                          if nxt is not None and jg < NG:
                              for _ in range(PEER_FS):
                                  next(nxt, None)
                          if jg >= SK:
                              g_ = jg - SK
                              grp = slice(g_ * GS, (g_ + 1) * GS)
                              uvs = uvs_of.pop(g_)
                              E("scalar", "activation", act.bs, wgt.bs, out=wgt.t[:, grp], in_=act.t[:, grp], func=AF.Gelu_apprx_tanh)
                              E("vector", "tensor_tensor", wgt.bs + gw.bs, wgt.bs, out=wgt.t[:, grp], in0=wgt.t[:, grp], in1=gwf[:, grp], op=ALU.mult)
                              for jj in range(GS):
                                  j = g_ * GS + jj
                                  uv = uvs[jj]
                                  dg = dg_r.next()
                                  E("scalar", "activation", identf.bs + wgt.bs, dg.bs, out=dg.t[:], in_=identf.t[:], func=AF.Identity,
                                    scale=wgt.t[:, j:j + 1])
                                  for hf in range(2):
                                      MM(pacc[hf].t[:, :], dg.t[:], uv.t[:, DM + hf * 512:DM + (hf + 1) * 512], j == 0, j == 127,
                                         dg.bs + uv.bs, pacc[hf].bs)
                      ot = ot_r.next()
                      for hf in range(2):
                          E("vector", "tensor_tensor", pacc[hf].bs + g2r.bs, ot.bs, out=ot.t[:, hf * 512:(hf + 1) * 512], in0=pacc[hf].t[:, :],
                            in1=g2r.t[:, hf * 512:(hf + 1) * 512], op=ALU.mult)
                      E("vector", "scalar_tensor_tensor", x2t.bs + ot.bs, ot.bs, out=ot.t[:], in0=x2t.t[:], scalar=ALPHA,
                        in1=ot.t[:], op0=ALU.mult, op1=ALU.add)
                      layer_norm_tile(lnt, ot, lg, lb, eng2="vector")
                      S.dma("sync", yout[pi][i * 128:(i + 1) * 128, :], ot.t[:], reads=ot.bs)

                  x2_of = {}
                  g0 = front(0)
                  for _ in g0:
                      pass
                  for i in range(8):
                      nxt = front(i + 1) if i + 1 < 8 else None
                      back(i, nxt)
                      if nxt is not None:
                          for _ in nxt:
                              pass
                  S.barrier()

        S.enabled = True
        with ExitStack() as ph:
            nso = alloc(ph, "nso", [64, 128])
            if stop is None or stop >= (0, 3):
                pb = pbanks.next()
                TR(pb.t[0:64, 0:128], nst.t[:].rearrange("p s d c -> p (s d c)"), identf.t[:], nst.bs + identf.bs, pb.bs)
                E("vector", "tensor_copy", pb.bs, nso.bs, out=nso.t[:], in_=pb.t[0:64, 0:128])
                S.dma("sync", ns_d, nso.t[:], reads=nso.bs)
            S.emit(top)
    return nc, dbg_names


def _consts():
    bf = ml_dtypes.bfloat16
    c = {}
    c["identf"] = np.eye(128, dtype=np.float32)
    c["identb"] = np.eye(128, dtype=np.float32).astype(bf)
    m0 = np.ones((128, 1), np.float32); m0[0, 0] = 0.0
    c["mask0"] = m0
    t = np.arange(1024)
    r = (t // 64).astype(np.float32); col = (t % 64).astype(np.float32)
    quarter = DM // 4
    omega = (1.0 / (10000.0 ** (np.arange(quarter, dtype=np.float32) / quarter))).astype(np.float32)
    er = r[:, None] * omega[None, :]; ec = col[:, None] * omega[None, :]
    c["pos"] = np.concatenate([np.sin(er), np.cos(er), np.sin(ec), np.cos(ec)], axis=-1).astype(np.float32)
    deltas = np.linspace(math.log(1e-2) / 1.5, math.log(1e-2) / 0.3, DM, dtype=np.float32)
    bands = np.linspace(1e-4, 15, 16, dtype=np.float32)
    for L in (256, 1024):
        ti = np.arange(L, dtype=np.float32)
        tn = ti / max(L - 1, 1)
        w = (2.0 * math.pi * ti / L).astype(np.float32)
        fw = w[:, None] * bands[None, :]
        z = np.concatenate([tn[:, None], np.cos(fw), -np.sin(fw)], axis=-1).astype(np.float32)
        c["zT%d" % L] = np.ascontiguousarray(z.T)
        c["dec%d" % L] = np.exp(-tn[:, None] * np.abs(deltas)[None, :]).astype(np.float32)
        tt = np.arange(L, dtype=np.float64)
        ang = np.pi * np.outer(tt, tt) / L
        C = np.cos(ang)
        Sm = -np.sin(ang)
        Sm[:, 0] = (-1.0) ** tt
        c["C%d" % L] = C.astype(np.float32).astype(bf)
        c["S%d" % L] = Sm.astype(np.float32).astype(bf)
        c["ST%d" % L] = np.ascontiguousarray(Sm.T).astype(np.float32).astype(bf)
        nfc = L // 128
        f = np.arange(L)
        wfre = np.where(f == 0, 1.0 / (2 * L), 1.0 / L)
        wB = np.where(f == 0, 0.0, 1.0 / L)
        mD = np.where(f == 0, 0.0, 1.0 / L)
        m2 = np.where(f == 0, 1.0 / (2 * L), 0.0)
        tab = np.stack([wfre, wB, mD, m2], 0).reshape(4, nfc, 128).transpose(2, 0, 1)
        c["wtab%d" % L] = np.ascontiguousarray(tab).astype(np.float32)
    return c


def _chunkT(v):
    v = np.asarray(v)
    lead = v.shape[:-1]
    n = v.shape[-1] // 128
    v = v.reshape(lead + (n, 128))
    return np.ascontiguousarray(np.moveaxis(v, -1, 0))


def _in_maps(inp):
    f = lambda a: np.ascontiguousarray(np.asarray(a, dtype=np.float32))
    cst = _consts()
    shared = dict(cst)
    shared["w_ada"] = f(inp["w_ada"][0])
    b_ada = f(inp["b_ada"][0])
    shared["b_adaT"] = _chunkT(b_ada[:2048].reshape(2, 1024))
    shared["b_ada_rows"] = np.ascontiguousarray(b_ada[2048:].reshape(4, 1024))
    shared["w_in"] = f(inp["w_in"][0])
    shared["rcw"] = np.ascontiguousarray(_chunkT(f(inp["rnn_conv_w"][0])).transpose(0, 2, 1))
    shared["rcb"] = _chunkT(f(inp["rnn_conv_b"][0]))
    shared["gate_w"] = f(inp["rnn_gate_w"][0])
    shared["gbT"] = _chunkT(f(inp["rnn_gate_b"][0]))
    shared["lamT"] = _chunkT(f(inp["rnn_lambda"][0]))
    shared["hcw"] = np.ascontiguousarray(_chunkT(f(inp["hy_conv_w"][0])).transpose(0, 2, 1))
    shared["hcb"] = _chunkT(f(inp["hy_conv_b"][0]))
    shared["hy_w1"] = f(inp["hy_ffn_w1"][0])
    shared["hy_b1"] = f(inp["hy_ffn_b1"][0]).reshape(64, 1)
    shared["hy_w2"] = f(inp["hy_ffn_w2"][0])
    shared["hy_b2"] = f(inp["hy_ffn_b2"][0]).reshape(64, 1)
    shared["hy_freq"] = f(inp["hy_sin_freq"][0]).reshape(64, 1)
    shared["hy_w3"] = f(inp["hy_ffn_w3"][0])
    shared["hy_b3"] = f(inp["hy_ffn_b3"][0])
    shared["skipT"] = _chunkT(f(inp["hy_skip"][0]))
    shared["w_a"] = f(inp["w_branch_a"][0])
    shared["w_b"] = f(inp["w_branch_b"][0])
    shared["w_o"] = f(inp["w_out"][0])
    shared["ln1_g"] = f(inp["ln1_g"][0]); shared["ln1_b"] = f(inp["ln1_b"][0])
    shared["ln2_g"] = f(inp["ln2_g"][0]); shared["ln2_b"] = f(inp["ln2_b"][0])
    shared["wq"] = f(inp["peer_w_query"][0])
    shared["skT"] = np.ascontiguousarray(f(inp["peer_sub_keys"][0]).transpose(2, 0, 1))
    shared["peer_u"] = f(inp["peer_u"][0])
    shared["peer_v"] = f(inp["peer_v"][0])
    xp = f(inp["x_prompt"]); xs = f(inp["x_sample"]); stt = f(inp["state_rglru"]); cc = f(inp["c"]); cctx = f(inp["c_ctx"])
    maps = []
    for i in range(N_CORES):
        m = dict(shared)
        m["xp"] = np.ascontiguousarray(xp[4 * i:4 * i + 4].reshape(1024, DM))
        m["xs"] = np.ascontiguousarray(xs[i])
        cond = np.stack([cctx, cc[i]], 0)
        m["condT"] = _chunkT(cond).transpose(0, 2, 1).copy()
        m["h0T"] = _chunkT(stt[i, 0])
        maps.append(m)
    return maps


_CACHE = {}


def kernel(**inputs):
    if "nc" not in _CACHE:
        _CACHE["nc"] = build()
    nc, _ = _CACHE["nc"]
    maps = _in_maps(inputs)
    res = run_bass_kernel_spmd(nc, maps, core_ids=list(range(N_CORES)))
    yp = np.zeros((32, 256, DM), np.float32)
    ys = np.zeros((8, 1024, DM), np.float32)
    ns = np.zeros((32, 1, 2, 1024), np.float32)
    for i in range(N_CORES):
        r = res.results[i]
        yp[4 * i:4 * i + 4] = np.asarray(r["yp"]).reshape(4, 256, DM)
        ys[i] = np.asarray(r["ys"])
        ns[4 * i:4 * i + 4, 0] = np.asarray(r["ns"]).reshape(4, 2, 1024)
    return yp, ys, ns
```

```python
import math
import os
HYSKIP = os.environ.get('HYSKIP', '')
PEER_SK = int(os.environ.get('PEER_SK', '3'))
PEER_FS = int(os.environ.get('PEER_FS', '2'))
PEER_GS = int(os.environ.get('PEER_GS', '2'))
from contextlib import ExitStack

import ml_dtypes
import numpy as np

import concourse.bass as bass
import concourse.mybir as mybir
from concourse.bass_utils import run_bass_kernel_spmd

F32 = mybir.dt.float32
BF16 = mybir.dt.bfloat16
I32 = mybir.dt.int32
U32 = mybir.dt.uint32
AF = mybir.ActivationFunctionType
ALU = mybir.AluOpType
AX = mybir.AxisListType

ENGS = ["sync", "scalar", "vector", "gpsimd", "tensor"]
DBGOPS = []
NOSYNC_ENGS = set(os.environ.get('NOSYNC', '').split(',')) - {''}
N_CORES = 8
DM = 1024
ALPHA = 2.0 ** 0.25
LN_EPS = 1e-5
RGLRU_C = 8.0


class Buf:
    __slots__ = ("name", "last_w", "readers")

    def __init__(self, name=""):
        self.name = name
        self.last_w = None
        self.readers = {}


class Op:
    __slots__ = ("eng", "fn", "deps", "key", "pos", "is_dma", "sig", "val", "waits",
                 "nosame", "vc_issue", "vc_done")


class Sched:
    def __init__(self, nc, dma_slots=16, same_engine_sync=True):
        self.nc = nc
        self.ops = []
        self.per_eng = {e: [] for e in ENGS}
        self.npos = {}
        self.dma_slots = dma_slots
        self.dma_count = {e: 0 for e in ENGS}
        self.slot_last = {}
        self.same_engine_sync = same_engine_sync
        self.enabled = True

    def _new(self, eng, fn, dma, nosame):
        o = Op()
        o.eng = eng
        o.fn = fn
        o.is_dma = dma
        o.sig = dma
        o.nosame = nosame
        return o

    def op(self, eng, fn, reads=(), writes=(), dma=False, nosame=False):
        if not self.enabled:
            return None
        o = self._new(eng, fn, dma, nosame)
        deps = []
        for b in reads:
            if b.last_w is not None:
                deps.append(b.last_w)
        for b in writes:
            if b.last_w is not None:
                deps.append(b.last_w)
            deps.extend(b.readers.values())
        if dma:
            slot = self.dma_count[eng] % self.dma_slots
            self.dma_count[eng] += 1
            o.key = ("dma", eng, slot)
            prev = self.slot_last.get(o.key)
            if prev is not None:
                deps.append(prev)
            self.slot_last[o.key] = o
        else:
            o.key = eng
        o.pos = self.npos.get(o.key, 0)
        self.npos[o.key] = o.pos + 1
        o.deps = [d for d in deps if d is not o]
        rk = o.key if not dma else ("dmaop", id(o))
        for b in reads:
            b.readers[rk] = o
        for b in writes:
            b.last_w = o
            b.readers = {}
        self.ops.append(o)
        self.per_eng[eng].append(o)
        return o

    def dma(self, eng, out, in_, reads=(), writes=(), **kw):
        return self.op(eng, lambda e: e.dma_start(out=out, in_=in_, **kw), reads, writes, dma=True)

    def barrier(self):
        if not self.enabled:
            return
        lasts = [self.per_eng[e][-1] for e in ENGS if self.per_eng[e]]
        lasts = [o for o in lasts if o.fn is not None]
        lasts += list(self.slot_last.values())
        for e in ENGS:
            o = self._new(e, None, False, False)
            o.key = e
            o.pos = self.npos.get(e, 0)
            self.npos[e] = o.pos + 1
            o.deps = list(lasts)
            self.ops.append(o)
            self.per_eng[e].append(o)

    def finalize(self):
        last_on_eng = {}
        for o in self.ops:
            vc = {}
            prev = last_on_eng.get(o.eng)
            if prev is not None:
                vc.update(prev.vc_issue)
            waits = []
            best = {}
            for d in o.deps:
                if d.key not in best or best[d.key].pos < d.pos:
                    best[d.key] = d
            for k, d in best.items():
                if (not d.is_dma) and (not o.is_dma) and d.eng == o.eng and (
                        o.nosame or not self.same_engine_sync or o.eng in NOSYNC_ENGS):
                    continue
                if vc.get(k, -1) >= d.pos:
                    continue
                waits.append(d)
                d.sig = True
                for kk, vv in d.vc_done.items():
                    if vc.get(kk, -1) < vv:
                        vc[kk] = vv
            o.waits = waits
            o.vc_issue = vc
            vd = dict(vc)
            if o.fn is not None:
                vd[o.key] = o.pos
            o.vc_done = vd
            last_on_eng[o.eng] = o
        cnt = {}
        for o in self.ops:
            if o.is_dma:
                o.val = 16 * (o.pos + 1)
            elif o.sig:
                cnt[o.key] = cnt.get(o.key, 0) + 1
                o.val = cnt[o.key]

    def emit(self, stack):
        nc = self.nc
        fin = self._new("sync", None, False, False)
        fin.key = "sync"
        fin.pos = self.npos.get("sync", 0)
        fin.deps = list(self.slot_last.values())
        self.ops.append(fin)
        self.per_eng["sync"].append(fin)
        self.finalize()
        sems = {}
        for e in ENGS:
            sems[e] = stack.enter_context(nc.semaphore("s_" + e))
        for k in self.slot_last.keys():
            sems[k] = stack.enter_context(nc.semaphore("d_%s_%d" % (k[1], k[2])))
        per_eng = self.per_eng

        def run(engname, eng):
            for o in per_eng[engname]:
                for d in o.waits:
                    eng.wait_ge(sems[d.key], d.val)
                if o.fn is None:
                    continue
                inst = o.fn(eng)
                if o.sig:
                    inst.then_inc(sems[o.key], 16 if o.is_dma else 1)

        with nc.Block() as block:
            @block.sync
            def _(e):
                run("sync", e)

            @block.scalar
            def _(e):
                run("scalar", e)

            @block.vector
            def _(e):
                run("vector", e)

            @block.gpsimd
            def _(e):
                run("gpsimd", e)

            @block.tensor
            def _(e):
                run("tensor", e)


class Tl:
    def __init__(self, t, nbuf=1, name=""):
        self.t = t
        self.bs = [Buf(name + str(i)) for i in range(nbuf)]
        self.b = self.bs[0]


class Ring:
    def __init__(self, tiles):
        self.tiles = tiles
        self.i = 0

    def next(self):
        t = self.tiles[self.i % len(self.tiles)]
        self.i += 1
        return t


def build(debug=(), stop=None):
    nc = bass.Bass("TRN2", target_bir_lowering=False)
    S = Sched(nc)
    debug = set(debug)
    dbg_names = []

    def din(name, shape, dt=F32):
        return nc.dram_tensor(name, list(shape), dt, kind="ExternalInput").ap()

    def dout(name, shape, dt=F32):
        return nc.dram_tensor(name, list(shape), dt, kind="ExternalOutput").ap()

    xin = [din("xp", [1024, DM]), din("xs", [1024, DM])]
    pos_d = din("pos", [1024, DM])
    condT_d = din("condT", [128, 8, 2])
    h0T_d = din("h0T", [128, 2, 8])
    w_ada_d = din("w_ada", [DM, 6 * DM])
    b_adaT_d = din("b_adaT", [128, 2, 8])
    b_ada_rows_d = din("b_ada_rows", [4, DM])
    w_in_d = din("w_in", [DM, 7168])
    rcw_d = din("rcw", [128, 8, 4])
    rcb_d = din("rcb", [128, 8])
    gate_w_d = din("gate_w", [2, 2, 4, 256, 256])
    gbT_d = din("gbT", [128, 2, 2, 8])
    lamT_d = din("lamT", [128, 2, 8])
    hcw_d = din("hcw", [128, 24, 3])
    hcb_d = din("hcb", [128, 24])
    hy_w1_d = din("hy_w1", [33, 64])
    hy_b1_d = din("hy_b1", [64, 1])
    hy_w2_d = din("hy_w2", [64, 64])
    hy_b2_d = din("hy_b2", [64, 1])
    hy_freq_d = din("hy_freq", [64, 1])
    hy_w3_d = din("hy_w3", [64, 4096])
    hy_b3_d = din("hy_b3", [4096])
    skipT_d = din("skipT", [128, 2, 8])
    w_a_d = din("w_a", [DM, DM])
    w_b_d = din("w_b", [DM, DM])
    w_o_d = din("w_o", [DM, DM])
    ln_d = [din("ln1_g", [DM]), din("ln1_b", [DM]), din("ln2_g", [DM]), din("ln2_b", [DM])]
    wq_d = din("wq", [DM, 2048])
    skT_d = din("skT", [128, 2, 128])
    pu_d = din("peer_u", [16384, DM])
    pv_d = din("peer_v", [16384, DM])
    identf_d = din("identf", [128, 128])
    identb_d = din("identb", [128, 128], BF16)
    zT_d = [din("zT256", [33, 256]), din("zT1024", [33, 1024])]
    dec_d = [din("dec256", [256, DM]), din("dec1024", [1024, DM])]
    C_d = [din("C256", [256, 256], BF16), din("C1024", [1024, 1024], BF16)]
    Sm_d = [din("S256", [256, 256], BF16), din("S1024", [1024, 1024], BF16)]
    ST_d = [din("ST256", [256, 256], BF16), din("ST1024", [1024, 1024], BF16)]
    wtab_d = [din("wtab256", [128, 4, 2]), din("wtab1024", [128, 4, 8])]
    mask0_d = din("mask0", [128, 1])

    yout = [dout("yp", [1024, DM]), dout("ys", [1024, DM])]
    ns_d = dout("ns", [64, 128])
    modrow_d = nc.dram_tensor("modrow", [2, 4, DM], F32, kind="Internal").ap()
    modrow_b = Buf("modrow")
    w_in_bd = nc.dram_tensor("w_in_bf16", [DM, 7168], BF16, kind="Internal").ap()
    w_a_bd = nc.dram_tensor("w_a_bf16", [DM, DM], BF16, kind="Internal").ap()
    w_b_bd = nc.dram_tensor("w_b_bf16", [DM, DM], BF16, kind="Internal").ap()
    w_o_bd = nc.dram_tensor("w_o_bf16", [DM, DM], BF16, kind="Internal").ap()
    wq_bd = nc.dram_tensor("wq_bf16", [DM, 2048], BF16, kind="Internal").ap()
    gate_w_bd = nc.dram_tensor("gate_w_bf16", [2, 2, 4, 256, 256], BF16, kind="Internal").ap()
    wcv_b = {k: Buf("wcv_" + k) for k in ("w_in", "w_a", "w_b", "w_o", "wq", "gate")}
    uv_d = nc.dram_tensor("uv_bf16", [16384, 2 * DM], BF16, kind="Internal").ap()
    uv_b = Buf("uv")
    x2_d = nc.dram_tensor("x2_scratch", [2, 1024, DM], F32, kind="Internal").ap()
    x2_db = [[Buf("x2d") for _ in range(8)] for _ in range(2)]

    top = ExitStack()
    with top:
        uid = [0]

        def alloc(scope, name, shape, dt=F32, nbuf=1):
            uid[0] += 1
            t = scope.enter_context(nc.sbuf_tensor("s%d_%s" % (uid[0], name), list(shape), dt))
            return Tl(t, nbuf, name)

        def ring(scope, name, shape, dt, n):
            return Ring([alloc(scope, "%s_%d" % (name, i), shape, dt) for i in range(n)])

        _pb = [Tl(top.enter_context(nc.psum_tensor("pb%d" % i, [128, 512], F32)), 1, "pb%d" % i) for i in range(6)]
        pbanks6 = Ring(_pb)
        pbanks4 = Ring(_pb[:4])
        pacc = _pb[4:6]
        pbanks = pbanks6
        pbf = Ring([Tl(top.enter_context(nc.psum_tensor("pbf%d" % i, [128, 1024], BF16)), 1, "pbf%d" % i)
                    for i in range(2)])

        def dbg(name, tl, ap, dt=F32):
            if name not in debug:
                return
            o = dout("dbg_" + name, list(ap.shape), dt)
            dbg_names.append("dbg_" + name)
            S.dma("sync", o, ap, reads=tl.bs)

        def E(eng, meth, reads, writes, nosame=False, **kw):
            return S.op(eng, lambda e: getattr(e, meth)(**kw), reads, writes, nosame=nosame)

        def MM(out, lhsT, rhs, start, stop, reads, writes):
            return S.op("tensor", lambda e: e.matmul(out, lhsT=lhsT, rhs=rhs, start=start, stop=stop),
                        reads, writes, nosame=True)

        def GATHER(out, table, idx, reads, writes):
            return S.op("gpsimd", lambda e: e.indirect_dma_start(
                out=out, out_offset=None, in_=table, in_offset=bass.IndirectOffsetOnAxis(ap=idx, axis=0)),
                reads, writes, dma=True)

        def TR(out, in_, ident, reads, writes):
            return S.op("tensor", lambda e: e.transpose(out=out, in_=in_, identity=ident),
                        reads, writes, nosame=True)

        def load(eng, tl, dram_ap, sb_ap=None, extra_reads=()):
            S.dma(eng, sb_ap if sb_ap is not None else tl.t[:], dram_ap, reads=list(extra_reads), writes=tl.bs)

        S.dma("gpsimd", w_in_bd, w_in_d, writes=[wcv_b["w_in"]])
        S.dma("gpsimd", w_b_bd, w_b_d, writes=[wcv_b["w_b"]])
        S.dma("gpsimd", gate_w_bd.rearrange("a b c i j -> (a b c i) j"), gate_w_d.rearrange("a b c i j -> (a b c i) j"),
              writes=[wcv_b["gate"]])
        S.dma("gpsimd", w_a_bd, w_a_d, writes=[wcv_b["w_a"]])
        S.dma("gpsimd", w_o_bd, w_o_d, writes=[wcv_b["w_o"]])
        S.dma("gpsimd", wq_bd, wq_d, writes=[wcv_b["wq"]])
        for cq in range(4):
            r0, r1 = cq * 4096, (cq + 1) * 4096
            S.dma("gpsimd", uv_d[r0:r1, 0:DM], pu_d[r0:r1, :], writes=[uv_b])
            S.dma("gpsimd", uv_d[r0:r1, DM:2 * DM], pv_d[r0:r1, :], writes=[uv_b])
        identf = alloc(top, "identf", [128, 128]); load("sync", identf, identf_d)
        identb = alloc(top, "identb", [128, 128], BF16); load("sync", identb, identb_d)
        mask0 = alloc(top, "mask0", [128, 1]); load("sync", mask0, mask0_d)
        epsb = alloc(top, "epsb", [128, 1])
        E("vector", "memset", [], epsb.bs, ap=epsb.t[:], constant=LN_EPS)
        rcw = alloc(top, "rcw", [128, 8, 4]); load("sync", rcw, rcw_d)
        rcb = alloc(top, "rcb", [128, 8]); load("sync", rcb, rcb_d)
        gbT = alloc(top, "gbT", [128, 2, 2, 8]); load("sync", gbT, gbT_d)
        lamT = alloc(top, "lamT", [128, 2, 8]); load("sync", lamT, lamT_d)
        h0T = alloc(top, "h0T", [128, 2, 8]); load("sync", h0T, h0T_d)
        hcw = alloc(top, "hcw", [128, 24, 3]); load("sync", hcw, hcw_d)
        hcb = alloc(top, "hcb", [128, 24]); load("sync", hcb, hcb_d)
        skipT = alloc(top, "skipT", [128, 2, 8]); load("sync", skipT, skipT_d)
        skT = alloc(top, "skT", [128, 2, 128]); load("sync", skT, skT_d)
        nsp = alloc(top, "nsp", [128, 2, 8])
        E("scalar", "activation", lamT.bs, nsp.bs, out=nsp.t[:], in_=lamT.t[:], func=AF.Exp, scale=-1.0)
        E("scalar", "activation", nsp.bs, nsp.bs, out=nsp.t[:], in_=nsp.t[:], func=AF.Ln, bias=1.0, scale=1.0)
        E("vector", "tensor_scalar", nsp.bs, nsp.bs, out=nsp.t[:], in0=nsp.t[:], scalar1=-RGLRU_C, scalar2=None,
          op0=ALU.mult)
        modT = alloc(top, "modT", [128, 2, 8, 2])
        nst = alloc(top, "nst", [128, 4, 2, 8])

        with ExitStack() as ph:
            condT = alloc(ph, "condT", [128, 8, 2]); load("sync", condT, condT_d)
            condS = alloc(ph, "condS", [128, 8, 2])
            E("scalar", "activation", condT.bs, condS.bs, out=condS.t[:], in_=condT.t[:], func=AF.Silu)
            b_adaT = alloc(ph, "b_adaT", [128, 2, 8]); load("sync", b_adaT, b_adaT_d)
            brow = alloc(ph, "brow", [1, 4, DM])
            load("sync", brow, b_ada_rows_d.rearrange("(o a) n -> o a n", o=1))
            wa_ring = ring(ph, "wa", [128, 8, DM], F32, 2)
            rowt = ring(ph, "rowt", [1, DM], F32, 2)
            for ty in range(6):
                wa = wa_ring.next()
                load("sync", wa, w_ada_d[:, ty * DM:(ty + 1) * DM].rearrange("(kc p) n -> p kc n", p=128))
                if ty < 2:
                    pb = pbanks.next()
                    for m in range(8):
                        for kc in range(8):
                            MM(pb.t[:, m * 2:m * 2 + 2], wa.t[:, kc, m * 128:(m + 1) * 128], condS.t[:, kc, :],
                               kc == 0, kc == 7, wa.bs + condS.bs, pb.bs)
                    E("vector", "tensor_tensor", pb.bs + b_adaT.bs, modT.bs,
                      out=modT.t[:, ty], in0=pb.t[:, 0:16].rearrange("p (m j) -> p m j", j=2),
                      in1=b_adaT.t[:, ty].unsqueeze(2).to_broadcast([128, 8, 2]), op=ALU.add)
                    if ty == 1:
                        E("vector", "tensor_scalar", modT.bs, modT.bs, out=modT.t[:, 1], in0=modT.t[:, 1],
                          scalar1=1.0, scalar2=None, op0=ALU.add)
                else:
                    for j in range(2):
                        rt = rowt.next()
                        for hf in range(2):
                            pb = pbanks.next()
                            for kc in range(8):
                                MM(pb.t[0:1, :], condS.t[:, kc, j:j + 1], wa.t[:, kc, hf * 512:(hf + 1) * 512],
                                   kc == 0, kc == 7, wa.bs + condS.bs, pb.bs)
                            E("vector", "tensor_tensor", pb.bs + brow.bs, rt.bs,
                              out=rt.t[0:1, hf * 512:(hf + 1) * 512], in0=pb.t[0:1, :],
                              in1=brow.t[0:1, ty - 2, hf * 512:(hf + 1) * 512], op=ALU.add)
                        if ty == 4:
                            E("vector", "tensor_scalar", rt.bs, rt.bs, out=rt.t[:], in0=rt.t[:], scalar1=1.0,
                              scalar2=None, op0=ALU.add)
                        S.dma("sync", modrow_d[j, ty - 2:ty - 1, :], rt.t[0:1, :], reads=rt.bs, writes=[modrow_b])
            S.barrier()

        def load_row(tl, dram_row, extra=()):
            S.dma("sync", tl.t[:], dram_row.partition_broadcast(128), reads=list(extra), writes=tl.bs)

        def layer_norm_tile(scope_tiles, xt, g_row, b_row, eng2="gpsimd"):
            stats, mv, rstd = scope_tiles
            E("vector", "bn_stats", xt.bs, stats.bs, out=stats.t[:, 0, :], in_=xt.t[:, 0:512])
            E("vector", "bn_stats", xt.bs, stats.bs, out=stats.t[:, 1, :], in_=xt.t[:, 512:1024])
            E("vector", "bn_aggr", stats.bs, mv.bs, out=mv.t[:], in_=stats.t[:].rearrange("p a b -> p (a b)"))
            E("scalar", "activation", mv.bs + epsb.bs, rstd.bs, out=rstd.t[:], in_=mv.t[:, 1:2], func=AF.Sqrt,
              bias=epsb.t[:], scale=1.0)
            E("vector", "reciprocal", rstd.bs, rstd.bs, out=rstd.t[:], in_=rstd.t[:])
            E("vector", "tensor_scalar", xt.bs + mv.bs + rstd.bs, xt.bs, out=xt.t[:], in0=xt.t[:],
              scalar1=mv.t[:, 0:1], scalar2=rstd.t[:], op0=ALU.subtract, op1=ALU.mult)
            E(eng2, "tensor_tensor", xt.bs + g_row.bs, xt.bs, out=xt.t[:], in0=xt.t[:], in1=g_row.t[:], op=ALU.mult)
            E(eng2, "tensor_tensor", xt.bs + b_row.bs, xt.bs, out=xt.t[:], in0=xt.t[:], in1=b_row.t[:], op=ALU.add)

        def conv(eng, out_tl, out_ap, in_tl, in_ap, w_tl, w_ap, b_ap, ntap, left, nseq, L):
            o3 = out_ap.rearrange("p (s l) -> p s l", s=nseq)
            i3 = in_ap.rearrange("p (s l) -> p s l", s=nseq)
            E(eng, "tensor_scalar", in_tl.bs + w_tl[0].bs + w_tl[1].bs, out_tl.bs, out=out_ap, in0=in_ap,
              scalar1=w_ap[:, left:left + 1], scalar2=b_ap, op0=ALU.mult, op1=ALU.add)
            for j in range(ntap):
                o = j - left
                if o == 0:
                    continue
                lo_out = max(0, -o)
                hi_out = L - max(0, o)
                E(eng, "scalar_tensor_tensor", in_tl.bs + out_tl.bs + w_tl[0].bs, out_tl.bs,
                  out=o3[:, :, lo_out:hi_out], in0=i3[:, :, lo_out + o:hi_out + o], scalar=w_ap[:, j:j + 1],
                  in1=o3[:, :, lo_out:hi_out], op0=ALU.mult, op1=ALU.add)

        for pi in range(2):
            nseq, L = (4, 256) if pi == 0 else (1, 1024)
            pbanks = pbanks6
            ntc = L // 128
            x_d = xin[pi]
            with ExitStack() as pp:
              with ExitStack() as mxs:
                h1T = alloc(mxs, "h1T", [128, 8, 1024], BF16)
                ybT = alloc(mxs, "ybT", [128, 8, 1024], BF16, nbuf=8)

                with ExitStack() as ph:
                    xr = ring(ph, "xt", [128, DM], F32, 2)
                    pr = ring(ph, "pt", [128, DM], F32, 2)
                    for i in range(8):
                        xt = xr.next()
                        load("sync", xt, x_d[i * 128:(i + 1) * 128, :])
                        if pi == 1:
                            pt = pr.next()
                            load("gpsimd", pt, pos_d[i * 128:(i + 1) * 128, :])
                            E("vector", "tensor_tensor", xt.bs + pt.bs, xt.bs, out=xt.t[:], in0=xt.t[:], in1=pt.t[:],
                              op=ALU.add)
                        for hf in range(2):
                            pb = pbanks.next()
                            for cc in range(4):
                                c = hf * 4 + cc
                                TR(pb.t[:, cc * 128:(cc + 1) * 128], xt.t[:, c * 128:(c + 1) * 128], identf.t[:],
                                   xt.bs + identf.bs, pb.bs)
                            for cc in range(4):
                                c = hf * 4 + cc
                                E("scalar", "activation", pb.bs + modT.bs, h1T.bs,
                                  out=h1T.t[:, c, i * 128:(i + 1) * 128], in_=pb.t[:, cc * 128:(cc + 1) * 128],
                                  func=AF.Identity, bias=modT.t[:, 0, c, pi:pi + 1], scale=modT.t[:, 1, c, pi:pi + 1])
                    dbg("h1T%d" % pi, h1T, h1T.t[:], BF16)
                    S.barrier()
                    if stop == (pi, 1): S.enabled = False

                with ExitStack() as ph:
                    li = pi
                    Cm = alloc(ph, "Cm", [128, ntc, L], BF16)
                    Sm = alloc(ph, "Sm", [128, ntc, L], BF16)
                    STm = alloc(ph, "STm", [128, ntc, L], BF16)
                    load("sync", Cm, C_d[li].rearrange("(tc p) f -> p tc f", p=128))
                    load("sync", Sm, Sm_d[li].rearrange("(tc p) f -> p tc f", p=128))
                    load("sync", STm, ST_d[li].rearrange("(tc p) f -> p tc f", p=128))
                    wtab = alloc(ph, "wtab", [128, 4, ntc]); load("sync", wtab, wtab_d[li])
                    hid2 = alloc(ph, "hid2", [64, L])
                    with ExitStack() as sub:
                        zT = alloc(sub, "zT", [33, L]); load("sync", zT, zT_d[li])
                        w1 = alloc(sub, "hw1", [33, 64]); load("sync", w1, hy_w1_d)
                        w2 = alloc(sub, "hw2", [64, 64]); load("sync", w2, hy_w2_d)
                        hb1 = alloc(sub, "hb1", [64, 1]); load("sync", hb1, hy_b1_d)
                        hb2 = alloc(sub, "hb2", [64, 1]); load("sync", hb2, hy_b2_d)
                        hfr = alloc(sub, "hfr", [64, 1]); load("sync", hfr, hy_freq_d)
                        fb = alloc(sub, "fb", [64, 2])
                        E("vector", "tensor_tensor", hb1.bs + hfr.bs, fb.bs, out=fb.t[:, 0:1], in0=hb1.t[:], in1=hfr.t[:],
                          op=ALU.mult)
                        E("vector", "tensor_tensor", hb2.bs + hfr.bs, fb.bs, out=fb.t[:, 1:2], in0=hb2.t[:], in1=hfr.t[:],
                          op=ALU.mult)
                        hid = [alloc(sub, "hid1", [64, L]), hid2]
                        sarg = alloc(sub, "sarg", [64, L])
                        sint = alloc(sub, "sint", [64, L], I32)
                        sflt = alloc(sub, "sflt", [64, L])
                        for layer in range(2):
                            src = zT if layer == 0 else hid[0]
                            wl = w1 if layer == 0 else w2
                            kdim = 33 if layer == 0 else 64
                            blk = min(L, 512)
                            for b0 in range(0, L, blk):
                                pb = pbanks.next()
                                MM(pb.t[0:64, 0:blk], wl.t[0:kdim, :], src.t[0:kdim, b0:b0 + blk], True, True,
                                   wl.bs + src.bs, pb.bs)
                                E("vector", "tensor_scalar", pb.bs + hfr.bs + fb.bs, sarg.bs, out=sarg.t[:, b0:b0 + blk],
                                  in0=pb.t[0:64, 0:blk], scalar1=hfr.t[:], scalar2=fb.t[:, layer:layer + 1],
                                  op0=ALU.mult, op1=ALU.add)
                            E("vector", "tensor_scalar", sarg.bs, sarg.bs, out=sarg.t[:], in0=sarg.t[:],
                              scalar1=float(1.0 / (2 * math.pi)), scalar2=8.0, op0=ALU.mult, op1=ALU.add)
                            E("vector", "tensor_copy", sarg.bs, sint.bs, out=sint.t[:], in_=sarg.t[:])
                            E("vector", "tensor_copy", sint.bs, sflt.bs, out=sflt.t[:], in_=sint.t[:])
                            E("vector", "tensor_tensor", sarg.bs + sflt.bs, sarg.bs, out=sarg.t[:], in0=sarg.t[:],
                              in1=sflt.t[:], op=ALU.subtract)
                            E("vector", "tensor_single_scalar", sarg.bs, sflt.bs, out=sflt.t[:], in_=sarg.t[:], scalar=0.5,
                              op=ALU.is_gt)
                            E("vector", "tensor_tensor", sarg.bs + sflt.bs, sarg.bs, out=sarg.t[:], in0=sarg.t[:],
                              in1=sflt.t[:], op=ALU.subtract)
                            E("scalar", "activation", sarg.bs, hid[layer].bs, out=hid[layer].t[:], in_=sarg.t[:],
                              func=AF.Sin, scale=float(2 * math.pi))
                        S.barrier()
                        dbg("hid2_%d" % pi, hid2, hid2.t[:])
                        if stop == (pi, 1.5): S.enabled = False

                    w3c_r = ring(ph, "w3c", [64, 4, 128], F32, 2)
                    b3c_r = ring(ph, "b3c", [128, 4, 128], F32, 2)
                    dec_r = ring(ph, "decf", [128, ntc, 128], F32, 2)
                    wgb_r = ring(ph, "wgb", [128, 8, 3, 128], BF16, 3)
                    ysz = 2 * ntc * 128 if pi == 1 else 2 * 2 * 4 * 128
                    nset = 2 if pi == 0 else 1

                    def mkset(q):
                        return (alloc(ph, "kf%d" % q, [128, 4, 128]), alloc(ph, "kff%d" % q, [128, 2, 128]),
                                alloc(ph, "kfb%d" % q, [128, 2, 128]),
                                alloc(ph, "kpm%d" % q, [128, ntc, 2, 2, 128], BF16),
                                alloc(ph, "TA%d" % q, [128, ntc, 2, 128]), alloc(ph, "TB%d" % q, [128, ntc, 2, 128]),
                                alloc(ph, "TD0%d" % q, [128, 2, 128]), alloc(ph, "tmpd%d" % q, [128, 2, 128]),
                                alloc(ph, "hpre%d" % q, [128, 3, 1024], BF16, nbuf=3), alloc(ph, "hc%d" % q, [128, 3, 1024], F32, nbuf=3),
                                alloc(ph, "wb%d" % q, [128, 1024], BF16), alloc(ph, "wT%d" % q, [128, 8, 128], BF16),
                                [alloc(ph, "tt%d_%d" % (q, r), [128, 512]) for r in range(4)],
                                alloc(ph, "YT%d" % q, [128, ysz], BF16), alloc(ph, "tmpz%d" % q, [128, 1024]),
                                alloc(ph, "z1%d" % q, [128, 1024]))
                    bsets = [mkset(q) for q in range(nset)]

                    w_in_bv = w_in_bd.rearrange("(kc p) n -> p kc n", p=128)
                    w3_v = hy_w3_d.rearrange("k (q n) -> k q n", q=4)
                    b3_v = hy_b3_d.rearrange("(q n) -> q n", q=4)
                    dec_v = dec_d[li].rearrange("(tc p) n -> p tc n", p=128)

                    def stA(c):
                        (kf, kff, kfb, kpm, TA, TB, TD0, tmpd, hpre, hc, wb, wT, tt_r, YT, tmpz, z1) = bsets[c % nset]
                        w3c = w3c_r.next(); load("sync", w3c, w3_v[:, :, c * 128:(c + 1) * 128])
                        b3c = b3c_r.next()
                        S.dma("sync", b3c.t[:], b3_v[:, c * 128:(c + 1) * 128].partition_broadcast(128), writes=b3c.bs)
                        decf = dec_r.next(); load("sync", decf, dec_v[:, :, c * 128:(c + 1) * 128])
                        wgb = wgb_r.next()
                        for q in range(3):
                            S.dma("sync", wgb.t[:, :, q, :],
                                  w_in_bv[:, :, 2048 + q * 1024 + c * 128:2048 + q * 1024 + (c + 1) * 128],
                                  reads=[wcv_b["w_in"]], writes=wgb.bs)
                        if c == 0:
                            dbg("b3c%d" % pi, b3c, b3c.t[:])
                            if stop == (pi, 1.55): S.enabled = False
                        for tc in range(ntc):
                            pb = pbanks.next()
                            MM(pb.t[:, :], hid2.t[:, tc * 128:(tc + 1) * 128], w3c.t[:].rearrange("k q n -> k (q n)"),
                               True, True, hid2.bs + w3c.bs, pb.bs)
                            E("vector", "tensor_tensor", pb.bs + b3c.bs, kf.bs, out=kf.t[:].rearrange("p q n -> p (q n)"),
                              in0=pb.t[:, :], in1=b3c.t[:].rearrange("p q n -> p (q n)"), op=ALU.add)
                            dbc = decf.t[:, tc, :].unsqueeze(1).to_broadcast([128, 2, 128])
                            E("gpsimd", "tensor_tensor", kf.bs + decf.bs, kff.bs, out=kff.t[:], in0=kf.t[:, 0:2, :], in1=dbc,
                              op=ALU.mult)
                            E("gpsimd", "tensor_tensor", kf.bs + decf.bs, kfb.bs, out=kfb.t[:], in0=kf.t[:, 2:4, :], in1=dbc,
                              op=ALU.mult)
                            if tc == 0:
                                E("vector", "tensor_scalar", kfb.bs + mask0.bs, kfb.bs, out=kfb.t[:], in0=kfb.t[:],
                                  scalar1=mask0.t[:], scalar2=None, op0=ALU.mult)
                            E("gpsimd", "tensor_tensor", kff.bs + kfb.bs, kpm.bs, out=kpm.t[:, tc, 0, :, :], in0=kff.t[:],
                              in1=kfb.t[:], op=ALU.add)
                            E("gpsimd", "tensor_tensor", kff.bs + kfb.bs, kpm.bs, out=kpm.t[:, tc, 1, :, :], in0=kff.t[:],
                              in1=kfb.t[:], op=ALU.subtract)
                        if c == 0:
                            dbg("kpm%d" % pi, kpm, kpm.t[:], BF16)
                            if stop == (pi, 1.57): S.enabled = False
                        for fc in range(ntc):
                            pa = pbanks.next()
                            for tc in range(ntc):
                                MM(pa.t[:, 0:256], Cm.t[:, tc, fc * 128:(fc + 1) * 128], kpm.t[:, tc, 0, :, :].rearrange("p o n -> p (o n)"),
                                   tc == 0, tc == ntc - 1, Cm.bs + kpm.bs, pa.bs)
                            for tc in range(ntc):
                                MM(pa.t[:, 256:512], Sm.t[:, tc, fc * 128:(fc + 1) * 128], kpm.t[:, tc, 1, :, :].rearrange("p o n -> p (o n)"),
                                   tc == 0, tc == ntc - 1, Sm.bs + kpm.bs, pa.bs)
                            if 'A' not in HYSKIP: E("scalar", "activation", pa.bs + wtab.bs, TA.bs, out=TA.t[:, fc].rearrange("p o n -> p (o n)"),
                              in_=pa.t[:, 0:256], func=AF.Identity,
                              scale=wtab.t[:, 0, fc:fc + 1])
                            if 'B' not in HYSKIP: E("scalar", "activation", pa.bs + wtab.bs, TB.bs, out=TB.t[:, fc].rearrange("p o n -> p (o n)"),
                              in_=pa.t[:, 256:512], func=AF.Identity,
                              scale=wtab.t[:, 1, fc:fc + 1])
                            if fc == 0 and 'D' not in HYSKIP:
                                pd = pbanks.next()
                                for tc in range(ntc):
                                    MM(pd.t[:, 0:256], Sm.t[:, tc, 0:128], kpm.t[:, tc, 0, :, :].rearrange("p o n -> p (o n)"),
                                       tc == 0, tc == ntc - 1, Sm.bs + kpm.bs, pd.bs)
                                E("scalar", "activation", pa.bs + wtab.bs, TD0.bs, out=TD0.t[:].rearrange("p o n -> p (o n)"),
                                  in_=pa.t[:, 0:256], func=AF.Identity, scale=wtab.t[:, 2, 0:1])
                                E("scalar", "activation", pd.bs + wtab.bs, tmpd.bs, out=tmpd.t[:].rearrange("p o n -> p (o n)"),
                                  in_=pd.t[:, 0:256], func=AF.Identity, scale=wtab.t[:, 3, 0:1])
                                E("gpsimd", "tensor_tensor", TD0.bs + tmpd.bs, TD0.bs, out=TD0.t[:], in0=TD0.t[:],
                                  in1=tmpd.t[:], op=ALU.add)
                        if c == 0:
                            dbg("TA%d" % pi, TA, TA.t[:]); dbg("TB%d" % pi, TB, TB.t[:]); dbg("TD0_%d" % pi, TD0, TD0.t[:])
                            if stop == (pi, 1.6): S.enabled = False
                        for q in range(3):
                            for blk in range(2):
                                pb = pbanks.next()
                                for kc in range(8):
                                    MM(pb.t[:, :], wgb.t[:, kc, q, :], h1T.t[:, kc, blk * 512:(blk + 1) * 512],
                                       kc == 0, kc == 7, wgb.bs + h1T.bs, pb.bs)
                                E("scalar", "copy", pb.bs, [hpre.bs[q]], out=hpre.t[:, q, blk * 512:(blk + 1) * 512],
                                  in_=pb.t[:, :])
                            cc = q * 8 + c
                            o_tl = Tl(None); o_tl.bs = [hc.bs[q]]
                            i_tl = Tl(None); i_tl.bs = [hpre.bs[q]]
                            conv("vector", o_tl, hc.t[:, q, :], i_tl, hpre.t[:, q, :],
                                 (hcw, hcb), hcw.t[:, cc, :], hcb.t[:, cc:cc + 1], 3, 1, nseq, L)
                        if c == 0:
                            dbg("hc%d" % pi, hc, hc.t[:])
                            if stop == (pi, 1.7): S.enabled = False
                    def stB(c):
                        (kf, kff, kfb, kpm, TA, TB, TD0, tmpd, hpre, hc, wb, wT, tt_r, YT, tmpz, z1) = bsets[c % nset]
                        for od in range(2):
                            if od == 0:
                                w_ap, w_bs = hc.t[:, 2, :], [hc.bs[2]]
                                x_ap, x_bs = hc.t[:, 0, :], [hc.bs[0]]
                            else:
                                w_ap, w_bs = z1.t[:], z1.bs
                                x_ap, x_bs = hc.t[:, 1, :], [hc.bs[1]]
                            E("scalar", "copy", w_bs, wb.bs, out=wb.t[:], in_=w_ap)
                            pt = pbf.next()
                            for tt in range(8):
                                slot = (tt % 2) * 4 + tt // 2 if pi == 0 else tt
                                TR(pt.t[:, slot * 128:(slot + 1) * 128], wb.t[:, tt * 128:(tt + 1) * 128], identb.t[:],
                                   wb.bs + identb.bs, pt.bs)
                            E("vector", "tensor_copy", pt.bs, wT.bs, out=wT.t[:].rearrange("p a n -> p (a n)"), in_=pt.t[:, :])
                            if pi == 0:
                                YTv = YT.t[:].rearrange("p (fc r s n) -> p fc r s n", fc=2, r=2, s=4)
                                wTv = wT.t[:].rearrange("p (tc s) n -> p tc (s n)", s=4)
                                groups = [(fc, 1) for fc in range(2)]
                            else:
                                YTv = YT.t[:].rearrange("p (fc r n) -> p fc r n", fc=ntc, r=2)
                                groups = [(0, 4), (4, 4)]
                            for (f0, nf) in groups:
                                pre = pbanks.next()
                                pim = pbanks.next()
                                if pi == 0:
                                    fc = f0
                                    for (pbk, Mx) in ((pre, Cm), (pim, Sm)):
                                        for tc in range(ntc):
                                            MM(pbk.t[:, :], Mx.t[:, tc, fc * 128:(fc + 1) * 128], wTv[:, tc, :],
                                               tc == 0, tc == ntc - 1, Mx.bs + wT.bs, pbk.bs)
                                    ta = TA.t[:, fc, od, :].unsqueeze(1).to_broadcast([128, 4, 128])
                                    tb_ = TB.t[:, fc, od, :].unsqueeze(1).to_broadcast([128, 4, 128])
                                    if fc == 0:
                                        td = TD0.t[:, od, :].unsqueeze(1).to_broadcast([128, 4, 128])
                                        td_bs = TD0.bs
                                    else:
                                        td = ta
                                        td_bs = TA.bs
                                    ure = pre.t[:, :].rearrange("p (s n) -> p s n", s=4)
                                    uim = pim.t[:, :].rearrange("p (s n) -> p s n", s=4)
                                    yre = YTv[:, fc, 0, :, :]
                                    yim = YTv[:, fc, 1, :, :]
                                    shp = "p (s n) -> p s n"
                                    tv = [t_.t[:, :].rearrange(shp, s=4) for t_ in tt_r]
                                    E("vector", "tensor_tensor", pre.bs + TA.bs, tt_r[0].bs, out=tv[0], in0=ure, in1=ta, op=ALU.mult)
                                    E("vector", "tensor_tensor", pim.bs + TB.bs, tt_r[1].bs, out=tv[1], in0=uim, in1=tb_, op=ALU.mult)
                                    E("vector", "tensor_tensor", pre.bs + TB.bs, tt_r[2].bs, out=tv[2], in0=ure, in1=tb_, op=ALU.mult)
                                    E("vector", "tensor_tensor", pim.bs + td_bs, tt_r[3].bs, out=tv[3], in0=uim, in1=td, op=ALU.mult)
                                    E("gpsimd", "tensor_tensor", tt_r[0].bs + tt_r[1].bs, YT.bs, out=yre, in0=tv[0], in1=tv[1], op=ALU.subtract)
                                    E("gpsimd", "tensor_tensor", tt_r[2].bs + tt_r[3].bs, YT.bs, out=yim, in0=tv[2], in1=tv[3], op=ALU.add)
                                else:
                                    for ff in range(nf):
                                        fc = f0 + ff
                                        for (pbk, Mx) in ((pre, Cm), (pim, Sm)):
                                            for tc in range(ntc):
                                                MM(pbk.t[:, ff * 128:(ff + 1) * 128], Mx.t[:, tc, fc * 128:(fc + 1) * 128],
                                                   wT.t[:, tc, :], tc == 0, tc == ntc - 1, Mx.bs + wT.bs, pbk.bs)
                                    ta = TA.t[:, f0:f0 + nf, od, :]
                                    tb_ = TB.t[:, f0:f0 + nf, od, :]
                                    ure = pre.t[:, :].rearrange("p (s n) -> p s n", s=4)
                                    uim = pim.t[:, :].rearrange("p (s n) -> p s n", s=4)
                                    yre = YTv[:, f0:f0 + nf, 0, :]
                                    yim = YTv[:, f0:f0 + nf, 1, :]
                                    tv = [t_.t[:, :].rearrange("p (s n) -> p s n", s=4) for t_ in tt_r]
                                    E("vector", "tensor_tensor", pre.bs + TA.bs, tt_r[0].bs, out=tv[0], in0=ure, in1=ta, op=ALU.mult)
                                    E("vector", "tensor_tensor", pim.bs + TB.bs, tt_r[1].bs, out=tv[1], in0=uim, in1=tb_, op=ALU.mult)
                                    E("vector", "tensor_tensor", pre.bs + TB.bs, tt_r[2].bs, out=tv[2], in0=ure, in1=tb_, op=ALU.mult)
                                    E("vector", "tensor_tensor", pim.bs + TA.bs, tt_r[3].bs, out=tv[3], in0=uim, in1=ta, op=ALU.mult)
                                    if f0 == 0:
                                        E("vector", "tensor_tensor", pim.bs + TD0.bs, tt_r[3].bs, out=tt_r[3].t[:, 0:128],
                                          in0=pim.t[:, 0:128], in1=TD0.t[:, od, :], op=ALU.mult)
                                    E("gpsimd", "tensor_tensor", tt_r[0].bs + tt_r[1].bs, YT.bs, out=yre, in0=tv[0], in1=tv[1], op=ALU.subtract)
                                    E("gpsimd", "tensor_tensor", tt_r[2].bs + tt_r[3].bs, YT.bs, out=yim, in0=tv[2], in1=tv[3], op=ALU.add)
                            sk = skipT.t[:, od, c:c + 1]
                            if od == 0:
                                o_ap_full, o_bs = z1.t[:], z1.bs
                            else:
                                o_ap_full, o_bs = ybT.t[:, c, :], [ybT.bs[c]]
                            for blk in range(2):
                                pb = pbanks.next()
                                if pi == 0:
                                    for s2 in range(2):
                                        sq = blk * 2 + s2
                                        k = 0
                                        for fc in range(2):
                                            for r, Mx in ((0, Cm), (1, STm)):
                                                MM(pb.t[:, s2 * 256:(s2 + 1) * 256], YTv[:, fc, r, sq, :], Mx.t[:, fc, :],
                                                   k == 0, k == 3, YT.bs + Mx.bs, pb.bs)
                                                k += 1
                                else:
                                    k = 0
                                    for fc in range(ntc):
                                        for r, Mx in ((0, Cm), (1, STm)):
                                            MM(pb.t[:, :], YTv[:, fc, r, :], Mx.t[:, fc, blk * 512:(blk + 1) * 512],
                                               k == 0, k == 2 * ntc - 1, YT.bs + Mx.bs, pb.bs)
                                            k += 1
                                sl = slice(blk * 512, (blk + 1) * 512)
                                E("vector", "scalar_tensor_tensor", w_bs + skipT.bs + pb.bs, tmpz.bs, out=tmpz.t[:, sl],
                                  in0=w_ap[:, sl], scalar=sk, in1=pb.t[:, :], op0=ALU.mult, op1=ALU.add)
                                E("gpsimd", "tensor_tensor", tmpz.bs + x_bs, o_bs, out=o_ap_full[:, sl], in0=tmpz.t[:, sl],
                                  in1=x_ap[:, sl], op=ALU.mult)
                    if nset == 2:
                        stA(0)
                        for c in range(8):
                            if c + 1 < 8:
                                stA(c + 1)
                            stB(c)
                    else:
                        for c in range(8):
                            stA(c)
                            stB(c)
                    dbg("ybT%d" % pi, ybT, ybT.t[:], BF16)
                    S.barrier()
                    if stop == (pi, 2): S.enabled = False

                yaT = alloc(mxs, "yaT", [128, 8, 1024], BF16, nbuf=8)
                with ExitStack() as ph:
                    wbf_r = ring(ph, "rwbf", [128, 8, 256], BF16, 4)
                    gwb_r = ring(ph, "gwb", [128, 4, 2, 256], BF16, 2)
                    xpre = alloc(ph, "xpre", [128, 2, 1024], F32, nbuf=2)
                    xc = alloc(ph, "xc", [128, 2, 1024], F32, nbuf=2)
                    xcb = alloc(ph, "xcb", [128, 2, 1024], BF16)
                    rg = alloc(ph, "rg", [128, 1024]); ig = alloc(ph, "ig", [128, 1024])
                    at = alloc(ph, "at", [128, 1024]); st_ = alloc(ph, "st", [128, 1024]); ut = alloc(ph, "ut", [128, 1024])
                    hdir = [alloc(ph, "hf", [128, 1024]), alloc(ph, "hb", [128, 1024])]
                    glu = alloc(ph, "glu", [128, 2, 1024], F32, nbuf=2)
                    w_in_bv = w_in_bd.rearrange("(kc p) n -> p kc n", p=128)
                    for hd in range(4):
                        wxb = wbf_r.next()
                        S.dma("sync", wxb.t[:], w_in_bv[:, :, hd * 256:(hd + 1) * 256], reads=[wcv_b["w_in"]], writes=wxb.bs)
                        wgb2 = wbf_r.next()
                        S.dma("sync", wgb2.t[:], w_in_bv[:, :, 1024 + hd * 256:1024 + (hd + 1) * 256], reads=[wcv_b["w_in"]],
                              writes=wgb2.bs)
                        gwb = gwb_r.next()
                        for d in range(2):
                            for g in range(2):
                                S.dma("sync", gwb.t[:, d * 2 + g, :, :],
                                      gate_w_bd[d, g, hd].rearrange("(ic p) j -> p ic j", p=128), reads=[wcv_b["gate"]],
                                      writes=gwb.bs)
                        for jc in range(2):
                            for blk in range(2):
                                sl = slice(blk * 512, (blk + 1) * 512)
                                pb = pbanks.next()
                                for kc in range(8):
                                    MM(pb.t[:, :], wxb.t[:, kc, jc * 128:(jc + 1) * 128], h1T.t[:, kc, sl],
                                       kc == 0, kc == 7, wxb.bs + h1T.bs, pb.bs)
                                E("scalar", "copy", pb.bs, [xpre.bs[jc]], out=xpre.t[:, jc, sl], in_=pb.t[:, :])
                                pb = pbanks.next()
                                for kc in range(8):
                                    MM(pb.t[:, :], wgb2.t[:, kc, jc * 128:(jc + 1) * 128], h1T.t[:, kc, sl],
                                       kc == 0, kc == 7, wgb2.bs + h1T.bs, pb.bs)
                                E("scalar", "activation", pb.bs, [glu.bs[jc]], out=glu.t[:, jc, sl], in_=pb.t[:, :],
                                  func=AF.Gelu_apprx_tanh)
                            ch = hd * 2 + jc
                            o_tl = Tl(None); o_tl.bs = [xc.bs[jc]]
                            i_tl = Tl(None); i_tl.bs = [xpre.bs[jc]]
                            conv("vector", o_tl, xc.t[:, jc, :], i_tl, xpre.t[:, jc, :], (rcw, rcb), rcw.t[:, ch, :],
                                 rcb.t[:, ch:ch + 1], 4, 2, nseq, L)
                            E("scalar", "copy", [xc.bs[jc]], xcb.bs, out=xcb.t[:, jc, :], in_=xc.t[:, jc, :])
                        for jc in range(2):
                            ch = hd * 2 + jc
                            for d in range(2):
                                for g in range(2):
                                    dst = rg if g == 0 else ig
                                    for blk in range(2):
                                        sl = slice(blk * 512, (blk + 1) * 512)
                                        pb = pbanks.next()
                                        for ic in range(2):
                                            MM(pb.t[:, :], gwb.t[:, d * 2 + g, ic, jc * 128:(jc + 1) * 128], xcb.t[:, ic, sl],
                                               ic == 0, ic == 1, gwb.bs + xcb.bs, pb.bs)
                                        E("scalar", "activation", pb.bs + gbT.bs, dst.bs, out=dst.t[:, sl], in_=pb.t[:, :],
                                          func=AF.Sigmoid, bias=gbT.t[:, d, g, ch:ch + 1], scale=1.0)
                                E("scalar", "activation", rg.bs + nsp.bs, at.bs, out=at.t[:], in_=rg.t[:], func=AF.Exp,
                                  scale=nsp.t[:, d, ch:ch + 1])
                                E("gpsimd", "tensor_tensor", at.bs, st_.bs, out=st_.t[:], in0=at.t[:], in1=at.t[:], op=ALU.mult)
                                E("scalar", "activation", st_.bs, st_.bs, out=st_.t[:], in_=st_.t[:], func=AF.Sqrt,
                                  bias=1.0, scale=-1.0)
                                E("gpsimd", "tensor_tensor", ig.bs + [xc.bs[jc]], ig.bs, out=ig.t[:], in0=ig.t[:],
                                  in1=xc.t[:, jc, :], op=ALU.mult)
                                E("vector", "tensor_tensor", ig.bs + st_.bs, ut.bs, out=ut.t[:], in0=ig.t[:], in1=st_.t[:],
                                  op=ALU.mult)
                                hd_t = hdir[d]
                                for sq in range(nseq):
                                    lo, hi = sq * L, (sq + 1) * L
                                    init = h0T.t[:, d, ch:ch + 1] if pi == 1 else 0.0
                                    if d == 0:
                                        o_, a_, u_ = hd_t.t[:, lo:hi], at.t[:, lo:hi], ut.t[:, lo:hi]
                                    else:
                                        o_, a_, u_ = hd_t.t[:, lo:hi][:, ::-1], at.t[:, lo:hi][:, ::-1], ut.t[:, lo:hi][:, ::-1]
                                    E("vector", "tensor_tensor_scan", at.bs + ut.bs + h0T.bs, hd_t.bs, out=o_, data0=a_,
                                      data1=u_, initial=init, op0=ALU.mult, op1=ALU.add)
                                if pi == 0:
                                    hv = hd_t.t[:].rearrange("p (s l) -> p s l", s=4)
                                    col = L - 1 if d == 0 else 0
                                    E("gpsimd", "tensor_copy", hd_t.bs, nst.bs, out=nst.t[:, :, d, ch:ch + 1],
                                      in_=hv[:, :, col:col + 1])
                            E("vector", "tensor_tensor", hdir[0].bs + hdir[1].bs, hdir[0].bs, out=hdir[0].t[:], in0=hdir[0].t[:],
                              in1=hdir[1].t[:], op=ALU.add)
                            E("gpsimd", "tensor_tensor", hdir[0].bs + [glu.bs[jc]], [yaT.bs[ch]], out=yaT.t[:, ch, :],
                              in0=hdir[0].t[:], in1=glu.t[:, jc, :], op=ALU.mult)
                    dbg("yaT%d" % pi, yaT, yaT.t[:], BF16)
                    if pi == 0:
                        dbg("nst", nst, nst.t[:])
                    S.barrier()
                    if stop == (pi, 3): S.enabled = False

                mT = alloc(mxs, "mT", [128, 8, 1024], BF16, nbuf=8)
                with ExitStack() as ph:
                    wbf_r = ring(ph, "mwbf", [128, 8, 256], BF16, 12)
                    gat = ring(ph, "gat", [128, 512], F32, 2)
                    gbt = ring(ph, "gbt", [128, 512], F32, 2)
                    mat = ring(ph, "mat", [128, 512], F32, 2)
                    w_in_v = w_in_bd.rearrange("(kc p) n -> p kc n", p=128)
                    wa_v = w_a_bd.rearrange("(kc p) n -> p kc n", p=128)
                    wb_v = w_b_bd.rearrange("(kc p) n -> p kc n", p=128)
                    for cg in range(4):
                        ws = []
                        for (src, bk) in ((wa_v[:, :, cg * 256:(cg + 1) * 256], "w_a"), (wb_v[:, :, cg * 256:(cg + 1) * 256], "w_b"),
                                          (w_in_v[:, :, 5120 + cg * 256:5120 + (cg + 1) * 256], "w_in"),
                                          (w_in_v[:, :, 6144 + cg * 256:6144 + (cg + 1) * 256], "w_in")):
                            wbf = wbf_r.next()
                            S.dma("sync", wbf.t[:], src, reads=[wcv_b[bk]], writes=wbf.bs)
                            ws.append(wbf)
                        for jc in range(2):
                            c = cg * 2 + jc
                            cs = slice(jc * 128, (jc + 1) * 128)
                            for blk in range(2):
                                sl = slice(blk * 512, (blk + 1) * 512)
                                pya = pbanks.next()
                                for kc in range(8):
                                    MM(pya.t[:, :], ws[0].t[:, kc, cs], yaT.t[:, kc, sl], kc == 0, kc == 7, ws[0].bs + yaT.bs, pya.bs)
                                pyb = pbanks.next()
                                for kc in range(8):
                                    MM(pyb.t[:, :], ws[1].t[:, kc, cs], ybT.t[:, kc, sl], kc == 0, kc == 7, ws[1].bs + ybT.bs, pyb.bs)
                                pga = pbanks.next()
                                for kc in range(8):
                                    MM(pga.t[:, :], ws[2].t[:, kc, cs], h1T.t[:, kc, sl], kc == 0, kc == 7, ws[2].bs + h1T.bs, pga.bs)
                                pgb = pbanks.next()
                                for kc in range(8):
                                    MM(pgb.t[:, :], ws[3].t[:, kc, cs], h1T.t[:, kc, sl], kc == 0, kc == 7, ws[3].bs + h1T.bs, pgb.bs)
                                ga = gat.next(); gb = gbt.next(); ma = mat.next()
                                E("scalar", "activation", pga.bs, ga.bs, out=ga.t[:], in_=pga.t[:, :], func=AF.Sigmoid)
                                E("scalar", "activation", pgb.bs, gb.bs, out=gb.t[:], in_=pgb.t[:, :], func=AF.Sigmoid)
                                E("vector", "tensor_tensor", pya.bs + ga.bs, ma.bs, out=ma.t[:], in0=pya.t[:, :], in1=ga.t[:], op=ALU.mult)
                                E("vector", "tensor_tensor", pyb.bs + gb.bs, gb.bs, out=gb.t[:], in0=pyb.t[:, :], in1=gb.t[:], op=ALU.mult)
                                E("gpsimd", "tensor_tensor", ma.bs + gb.bs, [mT.bs[c]], out=mT.t[:, c, sl], in0=ma.t[:], in1=gb.t[:], op=ALU.add)
                    dbg("mT%d" % pi, mT, mT.t[:], BF16)
                    S.barrier()
                    if stop == (pi, 4): S.enabled = False

                with ExitStack() as ph:
                    wo = alloc(ph, "wo", [128, 8, DM], BF16, nbuf=4)
                    wo_v = w_o_bd.rearrange("(kc p) n -> p kc n", p=128)
                    for cg in range(4):
                        S.dma("sync", wo.t[:, :, cg * 256:(cg + 1) * 256], wo_v[:, :, cg * 256:(cg + 1) * 256],
                              reads=[wcv_b["w_o"]], writes=[wo.bs[cg]])
                    g1r = alloc(ph, "g1r", [128, DM]); load_row(g1r, modrow_d[pi, 0, :], [modrow_b])
                    lg = alloc(ph, "lg", [128, DM]); load_row(lg, ln_d[0])
                    lb = alloc(ph, "lb", [128, DM]); load_row(lb, ln_d[1])
                    xr = ring(ph, "xt3", [128, DM], F32, 2)
                    pr = ring(ph, "pt3", [128, DM], F32, 2)
                    yt_r = ring(ph, "yt3", [128, DM], F32, 2)
                    lnt = (alloc(ph, "stats", [128, 2, 6]), alloc(ph, "mv", [128, 2]), alloc(ph, "rstd", [128, 1]))
                    for i in range(8):
                        xt = xr.next()
                        load("sync", xt, x_d[i * 128:(i + 1) * 128, :])
                        if pi == 1:
                            pt = pr.next()
                            load("sync", pt, pos_d[i * 128:(i + 1) * 128, :])
                            E("gpsimd", "tensor_tensor", xt.bs + pt.bs, xt.bs, out=xt.t[:], in0=xt.t[:], in1=pt.t[:], op=ALU.add)
                        yt = yt_r.next()
                        for hf in range(2):
                            pb = pbanks.next()
                            for kc in range(8):
                                MM(pb.t[:, :], mT.t[:, kc, i * 128:(i + 1) * 128], wo.t[:, kc, hf * 512:(hf + 1) * 512],
                                   kc == 0, kc == 7, mT.bs + wo.bs, pb.bs)
                            E("vector", "tensor_tensor", pb.bs + g1r.bs, yt.bs, out=yt.t[:, hf * 512:(hf + 1) * 512],
                              in0=pb.t[:, :], in1=g1r.t[:, hf * 512:(hf + 1) * 512], op=ALU.mult)
                        E("vector", "scalar_tensor_tensor", xt.bs + yt.bs, yt.bs, out=yt.t[:], in0=xt.t[:],
                          scalar=ALPHA, in1=yt.t[:], op0=ALU.mult, op1=ALU.add)
                        layer_norm_tile(lnt, yt, lg, lb)
                        S.dma("gpsimd", x2_d[pi, i * 128:(i + 1) * 128, :], yt.t[:], reads=yt.bs, writes=[x2_db[pi][i]])
                        if i == 0:
                            dbg("x2_%d" % pi, yt, yt.t[:])
                    S.barrier()
                    if stop == (pi, 5): S.enabled = False

              with ExitStack() as ph:
                  pbanks = pbanks4
                  wqb = alloc(ph, "wqb", [128, 8, 2048], BF16, nbuf=8)
                  UVN = int(os.environ.get('PEER_UVN', '16'))
                  uv_r = ring(ph, "uvg", [128, 2 * DM], BF16, UVN)
                  wq_v = wq_bd.rearrange("(kc p) n -> p kc n", p=128)
                  for cg in range(8):
                      S.dma("sync", wqb.t[:, :, cg * 256:(cg + 1) * 256], wq_v[:, :, cg * 256:(cg + 1) * 256],
                            reads=[wcv_b["wq"]], writes=[wqb.bs[cg]])
                  sh2r = alloc(ph, "sh2r", [128, DM]); load_row(sh2r, modrow_d[pi, 1, :], [modrow_b])
                  sc2r = alloc(ph, "sc2r", [128, DM]); load_row(sc2r, modrow_d[pi, 2, :], [modrow_b])
                  g2r = alloc(ph, "g2r", [128, DM]); load_row(g2r, modrow_d[pi, 3, :], [modrow_b])
                  lg = alloc(ph, "lg2", [128, DM]); load_row(lg, ln_d[2])
                  lb = alloc(ph, "lb2", [128, DM]); load_row(lb, ln_d[3])
                  lnt = (alloc(ph, "stats2", [128, 2, 6]), alloc(ph, "mv2", [128, 2]), alloc(ph, "rstd2", [128, 1]))
                  h2 = alloc(ph, "h2", [128, DM])
                  h2b_l = [alloc(ph, "h2b%d" % q, [128, DM], BF16) for q in range(2)]
                  h2T = alloc(ph, "h2T", [128, 8, 128], BF16)
                  qT = alloc(ph, "qT", [128, 16, 128])
                  sc = alloc(ph, "sc", [128, 16, 128])
                  scw = alloc(ph, "scw", [128, 128]); scw2 = alloc(ph, "scw2", [128, 128])
                  tops = alloc(ph, "tops", [128, 16, 16], nbuf=16); topi = alloc(ph, "topi", [128, 16, 16], U32, nbuf=16)

                  def _sub(tl, k):
                      w = Tl(None); w.bs = [tl.bs[k]]; return w
                  tops_b = [_sub(tops, k) for k in range(16)]; topi_b = [_sub(topi, k) for k in range(16)]
                  topif = alloc(ph, "topif", [128, 16, 16])
                  cand = alloc(ph, "cand", [128, 8, 16, 16]); candw = alloc(ph, "candw", [128, 256]); candw2 = alloc(ph, "candw2", [128, 256])
                  bs_ = alloc(ph, "bests", [128, 8, 16], nbuf=8); bp = alloc(ph, "bestp", [128, 8, 16], U32, nbuf=8)
                  bs_b = [_sub(bs_, k) for k in range(8)]; bp_b = [_sub(bp, k) for k in range(8)]
                  k1 = alloc(ph, "k1", [128, 8, 16], U32); k2 = alloc(ph, "k2", [128, 8, 16], U32)
                  k1f = alloc(ph, "k1f", [128, 8, 16]); k2f = alloc(ph, "k2f", [128, 8, 16])
                  oh = cand; i1f = alloc(ph, "i1f", [128, 8, 16]); i2f = alloc(ph, "i2f", [128, 8, 16])
                  io16 = alloc(ph, "io16", [128, 16]); io16i = alloc(ph, "io16i", [128, 16], I32)
                  E("gpsimd", "iota", [], io16i.bs, out=io16i.t[:], pattern=[[1, 16]], base=0, channel_multiplier=0)
                  E("vector", "tensor_copy", io16i.bs, io16.bs, out=io16.t[:], in_=io16i.t[:])
                  eidx_l = [alloc(ph, "eidx%d" % q, [128, 128], I32) for q in range(2)]
                  gw_l = [alloc(ph, "gw%d" % q, [128, 8, 16]) for q in range(2)]
                  zs = alloc(ph, "zs", [128, 8])
                  act = alloc(ph, "act", [128, 128]); wgt = alloc(ph, "wgt", [128, 128])
                  dg_r = ring(ph, "dg", [128, 128], BF16, 8)
                  ot_r = ring(ph, "ot", [128, DM], F32, 2)
                  x2_r = ring(ph, "x2t", [128, DM], F32, 2)

                  def top16x2(args_a, args_b):
                      chains = [args_a, args_b]
                      for (st, sap, wt, wap, os_, oi, ts, ti) in chains:
                          E("vector", "max", st.bs, ts.bs, out=os_[:, 0:8], in_=sap)
                      yield
                      for (st, sap, wt, wap, os_, oi, ts, ti) in chains:
                          E("vector", "max_index", st.bs + ts.bs, ti.bs, out=oi[:, 0:8], in_max=os_[:, 0:8], in_values=sap)
                      for (st, sap, wt, wap, os_, oi, ts, ti) in chains:
                          E("vector", "match_replace", st.bs + ts.bs, wt.bs, out=wap, in_to_replace=os_[:, 0:8], in_values=sap, imm_value=-1e30)
                      yield
                      for (st, sap, wt, wap, os_, oi, ts, ti) in chains:
                          E("vector", "max", wt.bs, ts.bs, out=os_[:, 8:16], in_=wap)
                      yield
                      for (st, sap, wt, wap, os_, oi, ts, ti) in chains:
                          E("vector", "max_index", wt.bs + ts.bs, ti.bs, out=oi[:, 8:16], in_max=os_[:, 8:16], in_values=wap)
                      yield

                  def top16(src_tl, src_ap, work_tl, work_ap, out_s, out_i, o_tl_s, o_tl_i):
                      E("vector", "max", src_tl.bs, o_tl_s.bs, out=out_s[:, 0:8], in_=src_ap)
                      E("vector", "max_index", src_tl.bs + o_tl_s.bs, o_tl_i.bs, out=out_i[:, 0:8], in_max=out_s[:, 0:8], in_values=src_ap)
                      E("vector", "match_replace", src_tl.bs + o_tl_s.bs, work_tl.bs, out=work_ap, in_to_replace=out_s[:, 0:8],
                        in_values=src_ap, imm_value=-1e30)
                      E("vector", "max", work_tl.bs, o_tl_s.bs, out=out_s[:, 8:16], in_=work_ap)
                      E("vector", "max_index", work_tl.bs + o_tl_s.bs, o_tl_i.bs, out=out_i[:, 8:16], in_max=out_s[:, 8:16], in_values=work_ap)

                  def front(i):
                      sl_ = i % 2
                      h2b = h2b_l[sl_]; eidx = eidx_l[sl_]; gw = gw_l[sl_]
                      x2t = x2_r.next()
                      S.dma("sync", x2t.t[:], x2_d[pi, i * 128:(i + 1) * 128, :], reads=[x2_db[pi][i]], writes=x2t.bs)
                      yield
                      E("vector", "tensor_tensor", x2t.bs + sc2r.bs, h2.bs, out=h2.t[:], in0=x2t.t[:], in1=sc2r.t[:], op=ALU.mult)
                      E("vector", "tensor_tensor", h2.bs + sh2r.bs, h2.bs, out=h2.t[:], in0=h2.t[:], in1=sh2r.t[:], op=ALU.add)
                      yield
                      E("scalar", "copy", h2.bs, h2b.bs, out=h2b.t[:], in_=h2.t[:])
                      yield
                      pt = pbf.next()
                      for c in range(8):
                          TR(pt.t[:, c * 128:(c + 1) * 128], h2b.t[:, c * 128:(c + 1) * 128], identb.t[:], h2b.bs + identb.bs, pt.bs)
                      yield
                      E("scalar", "copy", pt.bs, h2T.bs, out=h2T.t[:].rearrange("p a n -> p (a n)"), in_=pt.t[:, :])
                      yield
                      for g4 in range(4):
                          pb = pbanks.next()
                          for gg in range(4):
                              hp = g4 * 4 + gg
                              for kc in range(8):
                                  MM(pb.t[:, gg * 128:(gg + 1) * 128], wqb.t[:, kc, hp * 128:(hp + 1) * 128], h2T.t[:, kc, :],
                                     kc == 0, kc == 7, wqb.bs + h2T.bs, pb.bs)
                          yield
                          E("scalar", "copy", pb.bs, qT.bs, out=qT.t[:, g4 * 4:(g4 + 1) * 4, :].rearrange("p a n -> p (a n)"), in_=pb.t[:, :])
                          yield
                      for g4 in range(4):
                          pb = pbanks.next()
                          for gg in range(4):
                              hp = g4 * 4 + gg
                              MM(pb.t[:, gg * 128:(gg + 1) * 128], qT.t[:, hp, :], skT.t[:, hp % 2, :], True, True,
                                 qT.bs + skT.bs, pb.bs)
                          yield
                          E("scalar", "copy", pb.bs, sc.bs, out=sc.t[:, g4 * 4:(g4 + 1) * 4, :].rearrange("p a n -> p (a n)"), in_=pb.t[:, :])
                          yield
                      for hp in range(0, 16, 2):
                          yield from top16x2((sc, sc.t[:, hp, :], scw, scw.t[:], tops.t[:, hp, :], topi.t[:, hp, :], tops_b[hp], topi_b[hp]),
                                  (sc, sc.t[:, hp + 1, :], scw2, scw2.t[:], tops.t[:, hp + 1, :], topi.t[:, hp + 1, :], tops_b[hp + 1], topi_b[hp + 1]))
                      E("vector", "tensor_copy", topi.bs, topif.bs, out=topif.t[:], in_=topi.t[:])
                      tv = tops.t[:].rearrange("p (h q) k -> p h q k", q=2)
                      tiv = topif.t[:].rearrange("p (h q) k -> p h q k", q=2)
                      E("vector", "tensor_tensor", tops.bs, cand.bs, out=cand.t[:],
                        in0=tv[:, :, 0, :].unsqueeze(3).to_broadcast([128, 8, 16, 16]),
                        in1=tv[:, :, 1, :].unsqueeze(2).to_broadcast([128, 8, 16, 16]), op=ALU.add)
                      yield
                      for h in range(0, 8, 2):
                          yield from top16x2((cand, cand.t[:, h].rearrange("p a b -> p (a b)"), candw, candw.t[:], bs_.t[:, h, :], bp.t[:, h, :], bs_b[h], bp_b[h]),
                                  (cand, cand.t[:, h + 1].rearrange("p a b -> p (a b)"), candw2, candw2.t[:], bs_.t[:, h + 1, :], bp.t[:, h + 1, :], bs_b[h + 1], bp_b[h + 1]))
                      E("vector", "tensor_single_scalar", bp.bs, k1.bs, out=k1.t[:], in_=bp.t[:], scalar=4, op=ALU.logical_shift_right)
                      E("vector", "tensor_single_scalar", bp.bs, k2.bs, out=k2.t[:], in_=bp.t[:], scalar=15, op=ALU.bitwise_and)
                      E("vector", "tensor_copy", k1.bs, k1f.bs, out=k1f.t[:], in_=k1.t[:])
                      E("vector", "tensor_copy", k2.bs, k2f.bs, out=k2f.t[:], in_=k2.t[:])
                      yield
                      iob = io16.t[:].unsqueeze(1).unsqueeze(1).to_broadcast([128, 8, 16, 16])
                      for (kf_, q, dst) in ((k1f, 0, i1f), (k2f, 1, i2f)):
                          E("vector", "tensor_tensor", kf_.bs + io16.bs, oh.bs, out=oh.t[:],
                            in0=kf_.t[:].unsqueeze(3).to_broadcast([128, 8, 16, 16]), in1=iob, op=ALU.is_equal)
                          yield
                          E("vector", "tensor_tensor", oh.bs + topif.bs, oh.bs, out=oh.t[:], in0=oh.t[:],
                            in1=tiv[:, :, q, :].unsqueeze(2).to_broadcast([128, 8, 16, 16]), op=ALU.mult)
                          yield
                          E("vector", "tensor_reduce", oh.bs, dst.bs, out=dst.t[:], in_=oh.t[:], axis=AX.X, op=ALU.add)
                          yield
                      E("vector", "scalar_tensor_tensor", i1f.bs + i2f.bs, i1f.bs, out=i1f.t[:], in0=i1f.t[:], scalar=128.0,
                        in1=i2f.t[:], op0=ALU.mult, op1=ALU.add)
                      E("vector", "tensor_copy", i1f.bs, eidx.bs, out=eidx.t[:].rearrange("p (h k) -> p h k", h=8), in_=i1f.t[:])
                      yield
                      E("vector", "tensor_tensor", bs_.bs, gw.bs, out=gw.t[:], in0=bs_.t[:],
                        in1=bs_.t[:, :, 0:1].to_broadcast([128, 8, 16]), op=ALU.subtract)
                      yield
                      E("scalar", "activation", gw.bs, gw.bs, out=gw.t[:], in_=gw.t[:], func=AF.Exp)
                      yield
                      E("vector", "tensor_reduce", gw.bs, zs.bs, out=zs.t[:], in_=gw.t[:], axis=AX.X, op=ALU.add)
                      E("vector", "reciprocal", zs.bs, zs.bs, out=zs.t[:], in_=zs.t[:])
                      E("vector", "tensor_tensor", gw.bs + zs.bs, gw.bs, out=gw.t[:], in0=gw.t[:],
                        in1=zs.t[:].unsqueeze(2).to_broadcast([128, 8, 16]), op=ALU.mult)
                      x2_of[i] = x2t
                      yield

                  def back(i, nxt):
                      sl_ = i % 2
                      h2b = h2b_l[sl_]; eidx = eidx_l[sl_]; gw = gw_l[sl_]
                      x2t = x2_of[i]
                      gwf = gw.t[:].rearrange("p h k -> p (h k)")
                      uvs_of = {}
                      GS = PEER_GS
                      NG = 128 // GS
                      SK = PEER_SK
                      assert (SK + 1) * GS + GS - 1 <= UVN + GS - 1 and UVN >= (SK + 2) * GS - 0, "gather ring too shallow for skew"
                      for jg in range(NG + SK):
                          if jg < NG:
                              uvs = []
                              for jj in range(GS):
                                  j = jg * GS + jj
                                  uv = uv_r.next()
                                  uvs.append(uv)
                                  GATHER(uv.t[:], uv_d, eidx.t[:, j:j + 1], eidx.bs + [uv_b], uv.bs)
                                  E("vector", "tensor_tensor", uv.bs + h2b.bs, uv.bs, out=uv.t[:, 0:DM], in0=uv.t[:, 0:DM], in1=h2b.t[:],
                                    op=ALU.mult)
                                  E("scalar", "activation", uv.bs, uv.bs + (act.bs if jj in (0, GS - 1) else []), out=uv.t[:, 0:DM],
                                    in_=uv.t[:, 0:DM], func=AF.Identity, accum_out=act.t[:, j:j + 1])
                              uvs_of[jg] = uvs
                          if nxt is not None and jg < NG:
                              for _ in range(PEER_FS):
                                  next(nxt, None)
                          if jg >= SK:
                              g_ = jg - SK
                              grp = slice(g_ * GS, (g_ + 1) * GS)
                              uvs = uvs_of.pop(g_)
                              E("scalar", "activation", act.bs, wgt.bs, out=wgt.t[:, grp], in_=act.t[:, grp], func=AF.Gelu_apprx_tanh)
                              E("vector", "tensor_tensor", wgt.bs + gw.bs, wgt.bs, out=wgt.t[:, grp], in0=wgt.t[:, grp], in1=gwf[:, grp], op=ALU.mult)
                              for jj in range(GS):
                                  j = g_ * GS + jj
                                  uv = uvs[jj]
                                  dg = dg_r.next()
                                  E("scalar", "activation", identf.bs + wgt.bs, dg.bs, out=dg.t[:], in_=identf.t[:], func=AF.Identity,
                                    scale=wgt.t[:, j:j + 1])
                                  for hf in range(2):
                                      MM(pacc[hf].t[:, :], dg.t[:], uv.t[:, DM + hf * 512:DM + (hf + 1) * 512], j == 0, j == 127,
                                         dg.bs + uv.bs, pacc[hf].bs)
                      ot = ot_r.next()
                      for hf in range(2):
                          E("vector", "tensor_tensor", pacc[hf].bs + g2r.bs, ot.bs, out=ot.t[:, hf * 512:(hf + 1) * 512], in0=pacc[hf].t[:, :],
                            in1=g2r.t[:, hf * 512:(hf + 1) * 512], op=ALU.mult)
                      E("vector", "scalar_tensor_tensor", x2t.bs + ot.bs, ot.bs, out=ot.t[:], in0=x2t.t[:], scalar=ALPHA,
                        in1=ot.t[:], op0=ALU.mult, op1=ALU.add)
                      layer_norm_tile(lnt, ot, lg, lb, eng2="vector")
                      S.dma("sync", yout[pi][i * 128:(i + 1) * 128, :], ot.t[:], reads=ot.bs)

                  x2_of = {}
                  g0 = front(0)
                  for _ in g0:
                      pass
                  for i in range(8):
                      nxt = front(i + 1) if i + 1 < 8 else None
                      back(i, nxt)
                      if nxt is not None:
                          for _ in nxt:
                              pass
                  S.barrier()

        S.enabled = True
        with ExitStack() as ph:
            nso = alloc(ph, "nso", [64, 128])
            if stop is None or stop >= (0, 3):
                pb = pbanks.next()
                TR(pb.t[0:64, 0:128], nst.t[:].rearrange("p s d c -> p (s d c)"), identf.t[:], nst.bs + identf.bs, pb.bs)
                E("vector", "tensor_copy", pb.bs, nso.bs, out=nso.t[:], in_=pb.t[0:64, 0:128])
                S.dma("sync", ns_d, nso.t[:], reads=nso.bs)
            S.emit(top)
    return nc, dbg_names


def _consts():
    bf = ml_dtypes.bfloat16
    c = {}
    c["identf"] = np.eye(128, dtype=np.float32)
    c["identb"] = np.eye(128, dtype=np.float32).astype(bf)
    m0 = np.ones((128, 1), np.float32); m0[0, 0] = 0.0
    c["mask0"] = m0
    t = np.arange(1024)
    r = (t // 64).astype(np.float32); col = (t % 64).astype(np.float32)
    quarter = DM // 4
    omega = (1.0 / (10000.0 ** (np.arange(quarter, dtype=np.float32) / quarter))).astype(np.float32)
    er = r[:, None] * omega[None, :]; ec = col[:, None] * omega[None, :]
    c["pos"] = np.concatenate([np.sin(er), np.cos(er), np.sin(ec), np.cos(ec)], axis=-1).astype(np.float32)
    deltas = np.linspace(math.log(1e-2) / 1.5, math.log(1e-2) / 0.3, DM, dtype=np.float32)
    bands = np.linspace(1e-4, 15, 16, dtype=np.float32)
    for L in (256, 1024):
        ti = np.arange(L, dtype=np.float32)
        tn = ti / max(L - 1, 1)
        w = (2.0 * math.pi * ti / L).astype(np.float32)
        fw = w[:, None] * bands[None, :]
        z = np.concatenate([tn[:, None], np.cos(fw), -np.sin(fw)], axis=-1).astype(np.float32)
        c["zT%d" % L] = np.ascontiguousarray(z.T)
        c["dec%d" % L] = np.exp(-tn[:, None] * np.abs(deltas)[None, :]).astype(np.float32)
        tt = np.arange(L, dtype=np.float64)
        ang = np.pi * np.outer(tt, tt) / L
        C = np.cos(ang)
        Sm = -np.sin(ang)
        Sm[:, 0] = (-1.0) ** tt
        c["C%d" % L] = C.astype(np.float32).astype(bf)
        c["S%d" % L] = Sm.astype(np.float32).astype(bf)
        c["ST%d" % L] = np.ascontiguousarray(Sm.T).astype(np.float32).astype(bf)
        nfc = L // 128
        f = np.arange(L)
        wfre = np.where(f == 0, 1.0 / (2 * L), 1.0 / L)
        wB = np.where(f == 0, 0.0, 1.0 / L)
        mD = np.where(f == 0, 0.0, 1.0 / L)
        m2 = np.where(f == 0, 1.0 / (2 * L), 0.0)
        tab = np.stack([wfre, wB, mD, m2], 0).reshape(4, nfc, 128).transpose(2, 0, 1)
        c["wtab%d" % L] = np.ascontiguousarray(tab).astype(np.float32)
    return c


def _chunkT(v):
    v = np.asarray(v)
    lead = v.shape[:-1]
    n = v.shape[-1] // 128
    v = v.reshape(lead + (n, 128))
    return np.ascontiguousarray(np.moveaxis(v, -1, 0))


def _in_maps(inp):
    f = lambda a: np.ascontiguousarray(np.asarray(a, dtype=np.float32))
    cst = _consts()
    shared = dict(cst)
    shared["w_ada"] = f(inp["w_ada"][0])
    b_ada = f(inp["b_ada"][0])
    shared["b_adaT"] = _chunkT(b_ada[:2048].reshape(2, 1024))
    shared["b_ada_rows"] = np.ascontiguousarray(b_ada[2048:].reshape(4, 1024))
    shared["w_in"] = f(inp["w_in"][0])
    shared["rcw"] = np.ascontiguousarray(_chunkT(f(inp["rnn_conv_w"][0])).transpose(0, 2, 1))
    shared["rcb"] = _chunkT(f(inp["rnn_conv_b"][0]))
    shared["gate_w"] = f(inp["rnn_gate_w"][0])
    shared["gbT"] = _chunkT(f(inp["rnn_gate_b"][0]))
    shared["lamT"] = _chunkT(f(inp["rnn_lambda"][0]))
    shared["hcw"] = np.ascontiguousarray(_chunkT(f(inp["hy_conv_w"][0])).transpose(0, 2, 1))
    shared["hcb"] = _chunkT(f(inp["hy_conv_b"][0]))
    shared["hy_w1"] = f(inp["hy_ffn_w1"][0])
    shared["hy_b1"] = f(inp["hy_ffn_b1"][0]).reshape(64, 1)
    shared["hy_w2"] = f(inp["hy_ffn_w2"][0])
    shared["hy_b2"] = f(inp["hy_ffn_b2"][0]).reshape(64, 1)
    shared["hy_freq"] = f(inp["hy_sin_freq"][0]).reshape(64, 1)
    shared["hy_w3"] = f(inp["hy_ffn_w3"][0])
    shared["hy_b3"] = f(inp["hy_ffn_b3"][0])
    shared["skipT"] = _chunkT(f(inp["hy_skip"][0]))
    shared["w_a"] = f(inp["w_branch_a"][0])
    shared["w_b"] = f(inp["w_branch_b"][0])
    shared["w_o"] = f(inp["w_out"][0])
    shared["ln1_g"] = f(inp["ln1_g"][0]); shared["ln1_b"] = f(inp["ln1_b"][0])
    shared["ln2_g"] = f(inp["ln2_g"][0]); shared["ln2_b"] = f(inp["ln2_b"][0])
    shared["wq"] = f(inp["peer_w_query"][0])
    shared["skT"] = np.ascontiguousarray(f(inp["peer_sub_keys"][0]).transpose(2, 0, 1))
    shared["peer_u"] = f(inp["peer_u"][0])
    shared["peer_v"] = f(inp["peer_v"][0])
    xp = f(inp["x_prompt"]); xs = f(inp["x_sample"]); stt = f(inp["state_rglru"]); cc = f(inp["c"]); cctx = f(inp["c_ctx"])
    maps = []
    for i in range(N_CORES):
        m = dict(shared)
        m["xp"] = np.ascontiguousarray(xp[4 * i:4 * i + 4].reshape(1024, DM))
        m["xs"] = np.ascontiguousarray(xs[i])
        cond = np.stack([cctx, cc[i]], 0)
        m["condT"] = _chunkT(cond).transpose(0, 2, 1).copy()
        m["h0T"] = _chunkT(stt[i, 0])
        maps.append(m)
    return maps


_CACHE = {}


def kernel(**inputs):
    if "nc" not in _CACHE:
        _CACHE["nc"] = build()
    nc, _ = _CACHE["nc"]
    maps = _in_maps(inputs)
    res = run_bass_kernel_spmd(nc, maps, core_ids=list(range(N_CORES)))
    yp = np.zeros((32, 256, DM), np.float32)
    ys = np.zeros((8, 1024, DM), np.float32)
    ns = np.zeros((32, 1, 2, 1024), np.float32)
    for i in range(N_CORES):
        r = res.results[i]
        yp[4 * i:4 * i + 4] = np.asarray(r["yp"]).reshape(4, 256, DM)
        ys[i] = np.asarray(r["ys"])
        ns[4 * i:4 * i + 4, 0] = np.asarray(r["ns"]).reshape(4, 2, 1024)
    return yp, ys, ns
```
